# Optimizing a Trainium2 kernel written in Bass

```python
import math
import numpy as np
import jax
import jax.numpy as jnp
from jax import lax

D_MODEL = 1024
BATCH = 16
SEQ = 2048
DEPTH = 4

CTX_LEN = 256
GRID_W = 64
N_GROUPS = 4
GROUP_W = D_MODEL // N_GROUPS
N_HEADS_GROUP = 4
HEAD_DIM = GROUP_W // N_HEADS_GROUP
GQA_KV_HEADS = 2
DIFF_QK_DIM = HEAD_DIM // 2
RWKV_DECAY_LORA = 32
RWKV_A_LORA = 32
RWKV_GATE_LORA = 64
D_FF = -(-8 * D_MODEL // (3 * 256)) * 256
BLOCK_Q = 128
RET_CHUNK = 128
ROPE_THETA = 10000.0
LN_EPS = 1e-5
QK_NORM_EPS = 1e-6
RWKV_GN_EPS = 64e-5
ADA_INIT = 0.5
DEEPNORM_ALPHA = (2 * DEPTH) ** 0.25
DEEPNORM_BETA = (8 * DEPTH) ** -0.25

RWKV_SPLITS = (GROUP_W, GROUP_W, GROUP_W, 2 * RWKV_DECAY_LORA, 2 * RWKV_A_LORA, RWKV_GATE_LORA)
GQA_SPLITS = (GROUP_W, GQA_KV_HEADS * HEAD_DIM, GQA_KV_HEADS * HEAD_DIM)
DIFF_SPLITS = (GROUP_W, GROUP_W, GROUP_W)
RET_SPLITS = (GROUP_W, GROUP_W, GROUP_W, GROUP_W)
IN_SPLITS = (sum(RWKV_SPLITS), sum(GQA_SPLITS), sum(DIFF_SPLITS), sum(RET_SPLITS))
IN_W = sum(IN_SPLITS)

kernel_name = 'hybrid_rwkv7_gqa_diffattn_retention_dit'


def _split(t, sizes):
    idx = np.cumsum(sizes)[:-1].tolist()
    return jnp.split(t, idx, axis=-1)


def layer_norm(x, g, b, eps=LN_EPS):
    xf = x.astype(jnp.float32)
    mu = jnp.mean(xf, -1, keepdims=True)
    var = jnp.mean(jnp.square(xf - mu), -1, keepdims=True)
    return ((xf - mu) * lax.rsqrt(var + eps) * g + b).astype(x.dtype)


def head_norm(x, eps):
    xf = x.astype(jnp.float32)
    mu = jnp.mean(xf, -1, keepdims=True)
    var = jnp.mean(jnp.square(xf - mu), -1, keepdims=True)
    return (xf - mu) * lax.rsqrt(var + eps)


def rms_norm(x, g, eps=QK_NORM_EPS):
    xf = x.astype(jnp.float32)
    return (xf * lax.rsqrt(jnp.mean(jnp.square(xf), -1, keepdims=True) + eps) * g).astype(x.dtype)


def modulate(h, shift, scale):
    return h * (1.0 + scale) + shift


def grid_positions(T):
    rows = T // GRID_W
    row = jnp.repeat(jnp.arange(rows, dtype=jnp.int32), GRID_W)
    col = jnp.tile(jnp.arange(GRID_W, dtype=jnp.int32), rows)
    return row, col


def rope_1d(x, pos):
    n = x.shape[-1]
    half = n // 2
    inv = ROPE_THETA ** (-jnp.arange(half, dtype=jnp.float32) * 2.0 / n)
    ang = pos.astype(jnp.float32)[:, None] * inv[None, :]
    shape = (pos.shape[0],) + (1,) * (x.ndim - 3) + (half,)
    cos = jnp.cos(ang).reshape(shape)
    sin = jnp.sin(ang).reshape(shape)
    x1, x2 = x[..., :half], x[..., half:]
    return jnp.concatenate([x1 * cos - x2 * sin, x1 * sin + x2 * cos], -1).astype(x.dtype)


def rope_axial(x, row, col):
    h = x.shape[-1] // 2
    return jnp.concatenate([rope_1d(x[..., :h], row), rope_1d(x[..., h:], col)], -1)


def query_blocks(fn, q):
    B, T = q.shape[:2]
    nb = T // BLOCK_Q
    qb = jnp.moveaxis(q.reshape((B, nb, BLOCK_Q) + q.shape[2:]), 1, 0)
    ob = lax.map(fn, qb)
    return jnp.moveaxis(ob, 0, 1).reshape((B, T) + ob.shape[3:])


def centred_shift(p, mu):
    zero = jnp.zeros_like(p[:, :1])
    nb = 0.5 * (jnp.concatenate([zero, p[:, :-1]], 1) + jnp.concatenate([p[:, 1:], zero], 1))
    return p + mu * (nb - p)


def rwkv7_scan(r, w, k, v, a, b, s0):
    xs = tuple(jnp.moveaxis(t.astype(jnp.float32), 2, 0) for t in (r, w, k, v, a, b))

    def step(s, inp):
        r_t, w_t, k_t, v_t, a_t, b_t = inp
        sa = jnp.einsum('dbhvk,dbhk->dbhv', s, a_t)
        s = s * w_t[..., None, :] + sa[..., :, None] * b_t[..., None, :] + v_t[..., :, None] * k_t[..., None, :]
        y = jnp.einsum('dbhvk,dbhk->dbhv', s, r_t)
        return s, y

    s_fin, ys = lax.scan(step, s0, xs)
    return jnp.moveaxis(ys, 0, 2), s_fin


def rwkv_mixer(p, prm, s0):
    w0, w_up, a0, a_up, g_up, k_k, k_a, r_k, ln_g, ln_b = prm
    B, T, _ = p.shape
    H, N = N_HEADS_GROUP, HEAD_DIM
    pr, pk, pv, pw, pa, pg = _split(p, RWKV_SPLITS)
    w = w0 + jnp.einsum('btdr,drc->btdc', jnp.tanh(pw.reshape(B, T, 2, RWKV_DECAY_LORA)), w_up)
    decay = jnp.exp(-jnp.exp((-jax.nn.softplus(-w) - 0.5).astype(jnp.float32)))
    a = jax.nn.sigmoid(a0 + jnp.einsum('btdr,drc->btdc', pa.reshape(B, T, 2, RWKV_A_LORA), a_up))
    g = jax.nn.sigmoid(pg) @ g_up
    kk = (pk * k_k).reshape(B, T, H, N).astype(jnp.float32)
    kk = kk / jnp.maximum(jnp.linalg.norm(kk, axis=-1, keepdims=True), 1e-12)
    k_dir = pk[:, :, None, :] * (1.0 + (a - 1.0) * k_a)
    both = lambda t: jnp.stack([t, jnp.flip(t, 1)])
    per_dir = lambda t: jnp.stack([t[:, :, 0], jnp.flip(t[:, :, 1], 1)]).reshape(2, B, T, H, N)
    r_d = both(pr.reshape(B, T, H, N))
    v_d = both(pv.reshape(B, T, H, N))
    kk_d = both(kk)
    w_d, k_d, a_d = per_dir(decay), per_dir(k_dir), per_dir(a)
    y_d, s_fin = rwkv7_scan(r_d, w_d, k_d, v_d, -kk_d, kk_d * a_d, s0)
    bonus = jnp.sum(r_d * k_d * r_k, -1, keepdims=True) * v_d
    y = y_d[0] + jnp.flip(y_d[1], 1)
    bonus = bonus[0] + jnp.flip(bonus[1], 1)
    out = head_norm(y, RWKV_GN_EPS).reshape(B, T, GROUP_W) * ln_g + ln_b
    out = (out + bonus.reshape(B, T, GROUP_W)) * g
    return out, s_fin


def retention_dir(q, k, v, gamma, s0):
    B, T, H, dk = q.shape
    C = RET_CHUNK
    nc = T // C
    lg = jnp.log(gamma)
    qc = q.reshape(B, nc, C, H, dk).astype(jnp.float32)
    kc = k.reshape(B, nc, C, H, dk).astype(jnp.float32)
    vc = v.reshape(B, nc, C, H, -1).astype(jnp.float32)
    i = jnp.arange(C, dtype=jnp.float32)
    dif = i[:, None] - i[None, :]
    dmat = jnp.where(dif >= 0, jnp.exp(jnp.maximum(dif, 0.0)[None] * lg[:, None, None]), 0.0)
    intra = jnp.einsum('bnhij,bnjhe->bnihe', jnp.einsum('bnihd,bnjhd->bnhij', qc, kc) * dmat, vc)
    k_dec = kc * jnp.exp((C - 1.0 - i)[:, None] * lg[None, :])[:, :, None]
    kv = jnp.einsum('bnjhd,bnjhe->nbhde', k_dec, vc)
    g_chunk = jnp.exp(C * lg)[None, :, None, None]

    def step(R, kv_n):
        return g_chunk * R + kv_n, R

    s_fin, Rs = lax.scan(step, s0, kv)
    q_dec = qc * jnp.exp((i + 1.0)[:, None] * lg[None, :])[:, :, None]
    inter = jnp.einsum('bnihd,nbhde->bnihe', q_dec, Rs)
    return (intra + inter).reshape(B, T, H, -1), s_fin


def retention_mixer(p, pos, s0):
    B, T, _ = p.shape
    H, N = N_HEADS_GROUP, HEAD_DIM
    pq, pk, pv, pg = _split(p, RET_SPLITS)
    q = pq.reshape(B, T, H, N)
    k = pk.reshape(B, T, H, N)
    v = pv.reshape(B, T, H, N)
    if pos is not None:
        q = rope_1d(q, pos)
        k = rope_1d(k, pos)
    k = k * N ** -0.5
    gam_f = 1.0 - 2.0 ** (-5.0 - jnp.arange(H, dtype=jnp.float32))
    gam_b = gam_f[::-1]
    o_f, s_f = retention_dir(q, k, v, gam_f, s0[0])
    o_b, s_b = retention_dir(jnp.flip(q, 1), jnp.flip(k, 1), jnp.flip(v, 1), gam_b, s0[1])
    o = head_norm(o_f + jnp.flip(o_b, 1), LN_EPS).reshape(B, T, GROUP_W)
    return jax.nn.silu(pg) * o, (s_f, s_b)


def gqa_heads(pq, pk, pv, qn, kn, row, col):
    B, T, _ = pq.shape
    q = rms_norm(pq.reshape(B, T, N_HEADS_GROUP, HEAD_DIM), qn)
    k = rms_norm(pk.reshape(B, T, GQA_KV_HEADS, HEAD_DIM), kn)
    v = pv.reshape(B, T, GQA_KV_HEADS, HEAD_DIM)
    if row is not None:
        q = rope_axial(q, row, col)
        k = rope_axial(k, row, col)
    return q, k, v


def gqa_attend(q, k, v):
    B, Tq, H, d = q.shape
    hkv = k.shape[2]
    qg = q.reshape(B, Tq, hkv, H // hkv, d)
    s = jnp.einsum('bqhgd,bkhd->bhgqk', qg, k) * d ** -0.5
    p = jax.nn.softmax(s.astype(jnp.float32), axis=-1).astype(v.dtype)
    return jnp.einsum('bhgqk,bkhd->bqhgd', p, v).reshape(B, Tq, H, d)


def diff_heads(pq, pk, pv, row, col):
    B, T, _ = pq.shape
    q = pq.reshape(B, T, N_HEADS_GROUP, 2, DIFF_QK_DIM)
    k = pk.reshape(B, T, N_HEADS_GROUP, 2, DIFF_QK_DIM)
    v = pv.reshape(B, T, N_HEADS_GROUP, HEAD_DIM)
    if row is not None:
        q = rope_axial(q, row, col)
        k = rope_axial(k, row, col)
    return q, k, v


def diff_attend(q, k, v, lam):
    s = jnp.einsum('bqhmd,bkhmd->bhmqk', q, k) * DIFF_QK_DIM ** -0.5
    p = jax.nn.softmax(s.astype(jnp.float32), axis=-1)
    a = p[:, :, 0] - lam * p[:, :, 1]
    return jnp.einsum('bhqk,bkhe->bqhe', a.astype(v.dtype), v)


def swiglu(h, w1, w2):
    u, gt = jnp.split(h @ w1, 2, axis=-1)
    return (jax.nn.silu(u) * gt) @ w2


def setup_inputs(seed: int = 0) -> dict:
    key = jax.random.key(seed)
    ks = jax.random.split(key, 40)
    f32 = jnp.float32
    L, D, C = DEPTH, D_MODEL, GROUP_W
    nrm = lambda k, shape, s: s * jax.random.normal(k, shape, f32)
    near_one = lambda k, shape: 1.0 + 0.05 * jax.random.normal(k, shape, f32)
    return {
        'x': nrm(ks[0], (BATCH, SEQ, D), 1.0),
        'c': nrm(ks[1], (BATCH, D), 1.0),
        'ctx': nrm(ks[2], (BATCH, CTX_LEN, D), 1.0),
        'c_ctx': nrm(ks[3], (D,), 1.0),
        'ada_w': nrm(ks[4], (L, D, 6 * D), ADA_INIT * D ** -0.5),
        'ada_b': nrm(ks[5], (L, 6 * D), 0.02),
        'w_in': nrm(ks[6], (L, D, IN_W), D ** -0.5),
        'rwkv_mu': jax.random.uniform(ks[7], (L, IN_SPLITS[0]), f32),
        'rwkv_w0': jax.random.uniform(ks[8], (L, 2, C), f32, -6.5, -1.5),
        'rwkv_w_up': nrm(ks[9], (L, 2, RWKV_DECAY_LORA, C), 0.5 * RWKV_DECAY_LORA ** -0.5),
        'rwkv_a0': nrm(ks[10], (L, 2, C), 0.1),
        'rwkv_a_up': nrm(ks[11], (L, 2, RWKV_A_LORA, C), 0.5 * RWKV_A_LORA ** -0.5),
        'rwkv_g_up': nrm(ks[12], (L, RWKV_GATE_LORA, C), RWKV_GATE_LORA ** -0.5),
        'rwkv_k_k': 0.85 + nrm(ks[13], (L, C), 0.02),
        'rwkv_k_a': near_one(ks[14], (L, C)),
        'rwkv_r_k': nrm(ks[15], (L, N_HEADS_GROUP, HEAD_DIM), 0.1),
        'rwkv_ln_g': near_one(ks[16], (L, C)),
        'rwkv_ln_b': nrm(ks[17], (L, C), 0.02),
        'gqa_q_norm': near_one(ks[18], (L, HEAD_DIM)),
        'gqa_k_norm': near_one(ks[19], (L, HEAD_DIM)),
        'diff_lambda': nrm(ks[20], (L, 4, DIFF_QK_DIM), 0.1),
        'diff_norm': near_one(ks[21], (L, HEAD_DIM)),
        'w_out': nrm(ks[22], (L, N_GROUPS * GROUP_W, D), DEEPNORM_BETA * (N_GROUPS * GROUP_W) ** -0.5),
        'post1_g': near_one(ks[23], (L, D)),
        'post1_b': nrm(ks[24], (L, D), 0.02),
        'ffn_w_in': nrm(ks[25], (L, D, 2 * D_FF), D ** -0.5),
        'ffn_w_out': nrm(ks[26], (L, D_FF, D), DEEPNORM_BETA * D_FF ** -0.5),
        'post2_g': near_one(ks[27], (L, D)),
        'post2_b': nrm(ks[28], (L, D), 0.02),
    }


def reference(x, c, ctx, c_ctx, ada_w, ada_b, w_in, rwkv_mu, rwkv_w0, rwkv_w_up, rwkv_a0,
              rwkv_a_up, rwkv_g_up, rwkv_k_k, rwkv_k_a, rwkv_r_k, rwkv_ln_g, rwkv_ln_b,
              gqa_q_norm, gqa_k_norm, diff_lambda, diff_norm, w_out, post1_g, post1_b,
              ffn_w_in, ffn_w_out, post2_g, post2_b):
    B, S, D = x.shape
    H, N = N_HEADS_GROUP, HEAD_DIM
    row, col = grid_positions(S)
    pos = jnp.arange(S, dtype=jnp.int32)
    for l in range(DEPTH):
        need_ctx = l < DEPTH - 1
        mod = jax.nn.silu(c) @ ada_w[l] + ada_b[l]
        mod_c = jax.nn.silu(c_ctx) @ ada_w[l] + ada_b[l]
        sh1, sc1, g1, sh2, sc2, g2 = jnp.split(mod[:, None, :], 6, axis=-1)
        csh1, csc1, cg1, csh2, csc2, cg2 = jnp.split(mod_c, 6, axis=-1)
        p_x = modulate(x, sh1, sc1) @ w_in[l]
        p_c = modulate(ctx, csh1, csc1) @ w_in[l]
        rw_x, gq_x, df_x, rt_x = _split(p_x, IN_SPLITS)
        rw_c, gq_c, df_c, rt_c = _split(p_c, IN_SPLITS)

        rprm = (rwkv_w0[l], rwkv_w_up[l], rwkv_a0[l], rwkv_a_up[l], rwkv_g_up[l], rwkv_k_k[l],
                rwkv_k_a[l], rwkv_r_k[l], rwkv_ln_g[l], rwkv_ln_b[l])
        s0 = jnp.zeros((2, B, H, N, N), jnp.float32)
        y_rw_c, s_rw = rwkv_mixer(centred_shift(rw_c, rwkv_mu[l]), rprm, s0)
        y_rw_x, _ = rwkv_mixer(centred_shift(rw_x, rwkv_mu[l]), rprm, s_rw)

        qx, kx, vx = gqa_heads(*_split(gq_x, GQA_SPLITS), gqa_q_norm[l], gqa_k_norm[l], row, col)
        qc, kc, vc = gqa_heads(*_split(gq_c, GQA_SPLITS), gqa_q_norm[l], gqa_k_norm[l], None, None)
        k_all = jnp.concatenate([kc, kx], 1)
        v_all = jnp.concatenate([vc, vx], 1)
        y_gq_x = query_blocks(lambda qb: gqa_attend(qb, k_all, v_all), qx).reshape(B, S, GROUP_W)

        lam_init = 0.8 - 0.6 * math.exp(-0.3 * l)
        dl = diff_lambda[l].astype(jnp.float32)
        lam = jnp.exp(jnp.sum(dl[0] * dl[1])) - jnp.exp(jnp.sum(dl[2] * dl[3])) + lam_init
        dqx, dkx, dvx = diff_heads(*_split(df_x, DIFF_SPLITS), row, col)
        dqc, dkc, dvc = diff_heads(*_split(df_c, DIFF_SPLITS), None, None)
        dk_all = jnp.concatenate([dkc, dkx], 1)
        dv_all = jnp.concatenate([dvc, dvx], 1)
        y_df_x = query_blocks(lambda qb: diff_attend(qb, dk_all, dv_all, lam), dqx)
        y_df_x = (rms_norm(y_df_x, diff_norm[l]) * (1.0 - lam_init)).reshape(B, S, GROUP_W)

        r0 = (jnp.zeros((B, H, N, N), jnp.float32), jnp.zeros((B, H, N, N), jnp.float32))
        y_rt_c, s_rt = retention_mixer(rt_c, None, r0)
        y_rt_x, _ = retention_mixer(rt_x, pos, s_rt)

        y_x = jnp.concatenate([y_rw_x, y_gq_x, y_df_x, y_rt_x], -1) @ w_out[l]
        x = layer_norm(DEEPNORM_ALPHA * x + g1 * y_x, post1_g[l], post1_b[l])
        f_x = swiglu(modulate(x, sh2, sc2), ffn_w_in[l], ffn_w_out[l])
        x = layer_norm(DEEPNORM_ALPHA * x + g2 * f_x, post2_g[l], post2_b[l])

        if need_ctx:
            y_gq_c = gqa_attend(qc, kc, vc).reshape(B, CTX_LEN, GROUP_W)
            y_df_c = diff_attend(dqc, dkc, dvc, lam)
            y_df_c = (rms_norm(y_df_c, diff_norm[l]) * (1.0 - lam_init)).reshape(B, CTX_LEN, GROUP_W)
            y_c = jnp.concatenate([y_rw_c, y_gq_c, y_df_c, y_rt_c], -1) @ w_out[l]
            ctx = layer_norm(DEEPNORM_ALPHA * ctx + cg1 * y_c, post1_g[l], post1_b[l])
            f_c = swiglu(modulate(ctx, csh2, csc2), ffn_w_in[l], ffn_w_out[l])
            ctx = layer_norm(DEEPNORM_ALPHA * ctx + cg2 * f_c, post2_g[l], post2_b[l])
    return x
```

```python
import math, os
KSKIP = os.environ.get('KSKIP', '')
import numpy as np
import ml_dtypes
import concourse.bass as bass
import concourse.mybir as mybir
from concourse.bass_utils import run_bass_kernel_spmd
from contextlib import ExitStack

F32 = mybir.dt.float32
BF16 = mybir.dt.bfloat16
AF = mybir.ActivationFunctionType
ALU = mybir.AluOpType
AX = mybir.AxisListType

D = 1024
NB = 2
CT = 256
SX = 2048
T = CT + SX
NT = T // 128
INW = 3264
DFF = 2816
NFF = DFF // 128
DEPTH = 4
ALPHA = (2 * DEPTH) ** 0.25
LN_EPS = 1e-5
QK_EPS = 1e-6
GN_EPS = 64e-5
CS = 8
O_RW, O_GQ, O_DF, O_RT = 0, 960, 1472, 2240


class Sched:
    def __init__(self, nc, es):
        self.nc = nc
        self.es = es
        self.eng = {"pe": nc.tensor, "act": nc.scalar, "dve": nc.vector, "pool": nc.gpsimd, "sp": nc.sync}
        self.sem = {}
        self.cnt = {}
        self.nsem = 0
        for k in self.eng:
            self.sem[k] = self._newsem()
            self.cnt[k] = 0
        self.waited = {k: {} for k in self.eng}
        self.NSLOT = 8
        self.dsem = {}
        self.dcnt = {}
        self.dnext = {}
        for q in ("sp", "pool", "act"):
            self.dsem[q] = [self._newsem() for i in range(self.NSLOT)]
            self.dcnt[q] = [0] * self.NSLOT
            self.dnext[q] = 0
        self.ninstr = 0

    def _newsem(self):
        self.nsem += 1
        return self.es.enter_context(self.nc.semaphore("sem%d" % self.nsem))

    def _wait(self, e, tok, raw=False):
        if tok is None:
            return
        sem, val, owner = tok
        if owner == e and (e == "pe" or not raw):
            return
        key = id(sem)
        w = self.waited[e]
        if w.get(key, 0) >= val:
            return
        self.eng[e].wait_ge(sem, val)
        w[key] = val

    def deps(self, e, reads, writes):
        for t in reads:
            self._wait(e, t.lastw, raw=True)
        for t in writes:
            self._wait(e, t.lastw)
            for r in t.readers.values():
                self._wait(e, r)

    def done(self, tok, reads, writes):
        for t in reads:
            t.readers[tok[2] + str(id(tok[0]))] = tok
        for t in writes:
            t.lastw = tok
            t.readers = {}

    def op(self, e, fn, reads=(), writes=()):
        self.deps(e, reads, writes)
        ins = fn(self.eng[e])
        self.cnt[e] += 1
        ins.then_inc(self.sem[e], 1)
        tok = (self.sem[e], self.cnt[e], e)
        self.done(tok, reads, writes)
        self.ninstr += 1
        return tok

    def dma(self, q, out, in_, reads=(), writes=(), **kw):
        s = self.dnext[q]
        self.dnext[q] = (s + 1) % self.NSLOT
        sem = self.dsem[q][s]
        if self.dcnt[q][s] > 0:
            self._wait(q, (sem, 16 * self.dcnt[q][s], "dma"))
        self.deps(q, reads, writes)
        ins = self.eng[q].dma_start(out=out, in_=in_, **kw)
        self.dcnt[q][s] += 1
        ins.then_inc(sem, 16)
        tok = (sem, 16 * self.dcnt[q][s], "dma")
        self.done(tok, reads, writes)
        self.ninstr += 1
        return tok

    def barrier(self):
        toks = []
        for k in self.eng:
            if self.cnt[k] > 0:
                toks.append((self.sem[k], self.cnt[k], k))
        for q in self.dsem:
            for s in range(self.NSLOT):
                if self.dcnt[q][s] > 0:
                    toks.append((self.dsem[q][s], 16 * self.dcnt[q][s], "dma"))
        for e in self.eng:
            for t in toks:
                self._wait(e, t)
        for k in self.eng:
            if self.cnt[k] > 12000:
                self.sem[k] = self._newsem()
                self.cnt[k] = 0
        for q in self.dsem:
            for s in range(self.NSLOT):
                if self.dcnt[q][s] > 1500:
                    self.dsem[q][s] = self._newsem()
                    self.dcnt[q][s] = 0


class TT:
    def __init__(self, ap):
        self.ap = ap
        self.lastw = None
        self.readers = {}

    def __getitem__(self, k):
        return self.ap[k]


class Scope:
    def __init__(self, kb):
        self.kb = kb
        self.es = ExitStack()

    def __enter__(self):
        self.es.__enter__()
        return self

    def __exit__(self, *a):
        self.kb.S.barrier()
        return self.es.__exit__(*a)

    def sb(self, shape, dt=F32, name="t"):
        self.kb.uid += 1
        return TT(self.es.enter_context(self.kb.nc.sbuf_tensor("%s_%d" % (name, self.kb.uid), list(shape), dt)))

    def ps(self, shape, dt=F32, name="p"):
        self.kb.uid += 1
        return TT(self.es.enter_context(self.kb.nc.psum_tensor("%s_%d" % (name, self.kb.uid), list(shape), dt)))


def bc(ap, n):
    return ap.partition_broadcast(n)


class KB:
    def __init__(self, NL=DEPTH, dbg=(), stages=None):
        self.NL = NL
        self.dbg = set(dbg)
        self.stages = stages
        self.uid = 0
        nc = self.nc = bass.Bass("TRN2", target_bir_lowering=False)
        self.I = {}
        self.scr = {}

        def inp(name, shape, dt=F32):
            self.I[name] = nc.dram_tensor(name, list(shape), dt, kind="ExternalInput").ap()

        inp("xin", [NB, T, D])
        inp("cvec", [3, D])
        L = DEPTH
        inp("ada_w", [L, D, 6 * D]); inp("ada_b", [L, 6 * D]); inp("w_in", [L, D, INW])
        inp("rwkv_mu", [L, 960]); inp("rwkv_w0", [L, 512]); inp("rwkv_w_up", [L, 64, 256])
        inp("rwkv_a0", [L, 512]); inp("rwkv_a_up", [L, 64, 256]); inp("rwkv_g_up", [L, 64, 256])
        inp("rwkv_k_k", [L, 256]); inp("rwkv_k_a", [L, 256]); inp("rwkv_r_k", [L, 256])
        inp("rwkv_ln_g", [L, 256]); inp("rwkv_ln_b", [L, 256])
        inp("gqa_q_norm", [L, 64]); inp("gqa_k_norm", [L, 64]); inp("diff_lambda", [L, 128]); inp("diff_norm", [L, 64])
        inp("w_out", [L, D, D]); inp("post1_g", [L, D]); inp("post1_b", [L, D])
        inp("ffn_w_in", [L, D, 2 * DFF]); inp("ffn_w_out", [L, DFF, D]); inp("post2_g", [L, D]); inp("post2_b", [L, D])
        inp("c_ident", [128, 128]); inp("c_ropeG", [SX, 2, 64]); inp("c_ropeD", [SX, 2, 32]); inp("c_ropeR", [SX, 2, 64])
        inp("c_maskJ", [8, 256]); inp("c_retM", [36, 4, 128, 512], BF16); inp("c_zero", [128, 2048])
        self.out = nc.dram_tensor("y", [NB, SX, D], F32, kind="ExternalOutput").ap()

        def scr(name, shape, dt=F32):
            kind = "ExternalOutput" if name in self.dbg else "Internal"
            self.scr[name] = nc.dram_tensor(name, list(shape), dt, kind=kind).ap()

        scr("P", [NB, T, INW]); scr("Xs", [NB, T, D]); scr("X1s", [NB, T, D]); scr("modD", [3, 6 * D])
        for m in "gr":
            scr("QT" + m, [NB, 64, 4, T], BF16)
            scr("KT" + m, [NB, 64, 4, T], BF16)
        scr("QTd", [NB, 32, 8, T], BF16)
        scr("KTd", [NB, 32, 8, T], BF16)
        scr("As", [128, T, 8]); scr("Rs", [128, T, 8]); scr("Ws", [2, 128, T, 4])
        scr("LBs", [2, 8, T, 128]); scr("LKs", [2, 8, T, 128]); scr("RVs", [8, T, 256]); scr("Yfull", [2, 8, T, 256])
        scr("Gt", [NB, T, 256]); scr("Bon", [NB, T, 256]); scr("Ycat", [NB, T, D])

    def build(self):
        nc = self.nc
        with ExitStack() as es:
            self.S = S = Sched(nc, es)
            with Scope(self) as g:
                self.g = g
                self.ident = g.sb([128, 128], F32, "ident")
                S.dma("sp", self.ident[:], self.I["c_ident"], writes=[self.ident])
                self.siluT = g.sb([128, 8, 3], F32, "siluT")
                self.cvals = [64.0 * QK_EPS, GN_EPS, LN_EPS, QK_EPS, 0.0]
                self.cst = g.sb([128, 8], F32, "cst")
                for i_, v_ in enumerate(self.cvals):
                    S.op("dve", lambda e: e.memset(self.cst[:, i_:i_ + 1], float(v_)), writes=[self.cst])
                self.stage_init()
                for l in range(self.NL):
                    self.layer(l)
                S.barrier()
            print("ninstr", S.ninstr, "nsem", S.nsem)
        return nc

    def rpow(self, ot, o_ap, it, i_ap, scale, bias, p):
        S = self.S
        assert p == -0.5
        bcol = self.cvals.index(float(bias))
        np_ = o_ap.shape[0]
        S.op("act", lambda e: e.activation(o_ap, i_ap, AF.Sqrt, bias=self.cst[0:np_, bcol:bcol + 1], scale=float(scale)), reads=[it, self.cst], writes=[ot])
        S.op("dve", lambda e: e.reciprocal(o_ap, o_ap), reads=[ot], writes=[ot])

    def on(self, name):
        return self.stages is None or name in self.stages

    def stage_init(self):
        S, I = self.S, self.I
        with Scope(self) as sc:
            cv = sc.sb([3, D])
            S.dma("sp", cv[:], I["cvec"], writes=[cv])
            sv = sc.sb([3, D])
            S.op("act", lambda e: e.activation(sv[:], cv[:], AF.Silu), reads=[cv], writes=[sv])
            pT = sc.ps([128, 8, 3])
            for kc in range(8):
                S.op("pe", lambda e: e.transpose(pT[:, kc, :], sv[:, kc * 128:(kc + 1) * 128], self.ident[0:3, 0:3]), reads=[sv, self.ident], writes=[pT])
            S.op("dve", lambda e: e.tensor_copy(self.siluT[:], pT[:]), reads=[pT], writes=[self.siluT])
            z = sc.sb([128, 2048])
            S.dma("sp", z[:], I["c_zero"], writes=[z])
            for name in ("LBs", "LKs"):
                flat = self.scr[name].rearrange("d m t k -> (d m t k)").rearrange("(a p f) -> a p f", p=128, f=2048)
                for a in range(flat.shape[0]):
                    S.dma("sp", flat[a], z[:], reads=[z])
            flat = self.scr["RVs"].rearrange("m t k -> (m t k)").rearrange("(a p f) -> a p f", p=128, f=2048)
            for a in range(flat.shape[0]):
                S.dma("sp", flat[a], z[:], reads=[z])

    def layer(self, l):
        self.l = l
        self.last = (l == DEPTH - 1)
        self.Xsrc = self.I["xin"] if l == 0 else self.scr["Xs"]
        if self.on("mod"):
            self.stage_mod(l)
        with Scope(self) as ls:
            self.ls = ls
            S = self.S
            self.modT = ls.sb([128, 48, 3], F32, "modT")
            for v in range(3):
                S.dma("sp", self.modT[:, :, v], self.scr["modD"][v].rearrange("(c p) -> p c", p=128), writes=[self.modT], allow_slow_non_contiguous=True)
            for c0 in (8, 32):
                S.op("dve", lambda e: e.tensor_scalar(self.modT[:, c0:c0 + 8, :], self.modT[:, c0:c0 + 8, :], 1.0, None, ALU.add), reads=[self.modT], writes=[self.modT])
            if self.on("inproj"):
                self.stage_inproj(l)
            if self.on("rwprep"):
                self.stage_rwprep(l)
            if self.on("scan"):
                self.stage_scan(l)
            if self.on("rwpost"):
                self.stage_rwpost(l)
            if self.on("attn"):
                self.stage_attn(l)
            if self.on("mix"):
                self.stage_mix(l)
            if self.on("ffn"):
                self.stage_ffn(l)

    def stage_mod(self, l):
        S, I = self.S, self.I
        with Scope(self) as sc:
            adab = sc.sb([3, 6 * D])
            S.dma("sp", adab[:], bc(I["ada_b"][l], 3), writes=[adab])
            modsb = sc.sb([3, 6 * D])
            wb = [sc.sb([128, 8, 512]) for _ in range(2)]
            pm = [sc.ps([3, 512]) for _ in range(2)]
            for n in range(12):
                w = wb[n % 2]
                S.dma("sp" if n % 2 == 0 else "pool", w[:], I["ada_w"][l, :, n * 512:(n + 1) * 512].rearrange("(c p) n -> p c n", p=128), writes=[w])
                p = pm[n % 2]
                for kc in range(8):
                    S.op("pe", lambda e: e.matmul(p[:], self.siluT[:, kc, :], w[:, kc, :], start=(kc == 0), stop=(kc == 7)), reads=[self.siluT, w], writes=[p])
                S.op("dve", lambda e: e.tensor_tensor(modsb[:, n * 512:(n + 1) * 512], p[:], adab[:, n * 512:(n + 1) * 512], ALU.add), reads=[p, adab], writes=[modsb])
            S.dma("sp", self.scr["modD"], modsb[:], reads=[modsb])

    def xT_modulated(self, sc, src_tile, pT, xmT, col0, width, v, which):
        S = self.S
        shc, scc = (0, 8) if which == 0 else (24, 32)
        for kc in range(8):
            pp = pT[kc % 2]
            S.op("pe", lambda e: e.transpose(pp[:], src_tile[:, kc * 128:(kc + 1) * 128], self.ident[:]), reads=[src_tile, self.ident], writes=[pp])
            eng = "dve" if kc % 2 == 0 else "pool"
            eng = "dve"
            S.op(eng, lambda e: e.tensor_scalar(xmT[:, kc, col0:col0 + 128], pp[:], self.modT[:, scc + kc, v:v + 1], self.modT[:, shc + kc, v:v + 1], ALU.mult, ALU.add), reads=[pp, self.modT], writes=[xmT])

    def rope(self, eng2, out, src, tab, nh, nblk, half, tmp1, tmp2, reads, tabT):
        S = self.S
        w = nblk * 2 * half
        Cb = tab[:, 0, :].unsqueeze(1).to_broadcast([128, nh, w])
        S.op("dve", lambda e: e.tensor_tensor(tmp1, src, Cb, ALU.mult), reads=reads + [tabT], writes=[self._t1])
        v5 = lambda ap: ap.rearrange("p h (b two f) -> p h b two f", b=nblk, two=2)
        sgv = tab[:, 1, :].rearrange("p (b two f) -> p b two f", b=nblk, two=2)
        for hf in range(2):
            o = v5(tmp2)[:, :, :, hf, :]
            i = v5(src)[:, :, :, 1 - hf, :]
            sg = sgv[:, :, hf, :].unsqueeze(1).to_broadcast([128, nh, nblk, half])
            S.op(eng2, lambda e: e.tensor_tensor(o, i, sg, ALU.mult), reads=reads + [tabT], writes=[self._t2])
        S.op("dve", lambda e: e.tensor_tensor(out, tmp1, tmp2, ALU.add), reads=[self._t1, self._t2], writes=[self._ro])

    def stage_inproj(self, l):
        S, I = self.S, self.I
        with Scope(self) as sc:
            win = sc.sb([128, 8, INW], BF16, "win")
            for kc in range(8):
                S.dma("pool", win[:, kc, :], I["w_in"][l, kc * 128:(kc + 1) * 128, :], writes=[win])
            rG = sc.sb([128, 16, 2, 64]); rD = sc.sb([128, 16, 2, 32]); rR = sc.sb([128, 16, 2, 64])
            S.dma("sp", rG[:], I["c_ropeG"].rearrange("(n p) a w -> p n a w", p=128), writes=[rG])
            S.dma("sp", rD[:], I["c_ropeD"].rearrange("(n p) a w -> p n a w", p=128), writes=[rD])
            S.dma("sp", rR[:], I["c_ropeR"].rearrange("(n p) a w -> p n a w", p=128), writes=[rR])
            GW = sc.sb([128, 6, 64])
            for h in range(4):
                S.dma("sp", GW[:, h, :], bc(I["gqa_q_norm"][l], 128), writes=[GW])
            for h in range(4, 6):
                S.dma("sp", GW[:, h, :], bc(I["gqa_k_norm"][l], 128), writes=[GW])
            S.op("dve", lambda e: e.tensor_scalar(GW[:, 4:6, :], GW[:, 4:6, :], 8.0, None, ALU.mult), reads=[GW], writes=[GW])
            xt = [sc.sb([128, D]) for _ in range(2)]
            pT = [sc.ps([128, 128]) for _ in range(2)]
            xmT = [sc.sb([128, 8, 128], BF16) for _ in range(2)]
            pP = [sc.ps([128, 512]) for _ in range(2)]
            Psb = [sc.sb([128, INW]) for _ in range(2)]
            pQ = [sc.ps([64, 4, 128]) for _ in range(2)]
            sq = sc.sb([128, 6, 64]); ss = sc.sb([128, 6]); r1 = sc.sb([128, 6]); qn = sc.sb([128, 6, 64])
            t1 = sc.sb([128, 512]); t2 = sc.sb([128, 512]); ro = sc.sb([128, 512])
            self._t1, self._t2, self._ro = t1, t2, ro
            QTs = [sc.sb([64, 4, 128], BF16) for _ in range(6)]
            it = 0
            for b in range(NB):
                for n in range(NT):
                    v = 2 if n < 2 else b
                    isx = n >= 2
                    rows = slice(n * 128, (n + 1) * 128)
                    x = xt[it % 2]; xm = xmT[it % 2]; P = Psb[it % 2]
                    S.dma("sp", x[:], self.Xsrc[b, rows, :], writes=[x])
                    self.xT_modulated(sc, x, pT, xm, 0, 128, v, 0)
                    for c in range(7):
                        c0 = c * 512
                        w = min(512, INW - c0)
                        pp = pP[c % 2]
                        for kc in range(8):
                            S.op("pe", lambda e: e.matmul(pp[:, :w], xm[:, kc, :], win[:, kc, c0:c0 + w], start=(kc == 0), stop=(kc == 7)), reads=[xm, win], writes=[pp])
                        S.op("act", lambda e: e.activation(P[:, c0:c0 + w], pp[:, :w], AF.Copy), reads=[pp], writes=[P])
                    S.dma("pool", self.scr["P"][b, rows, :], P[:], reads=[P])
                    qk = P[:, O_GQ:O_GQ + 384].rearrange("p (h w) -> p h w", w=64)
                    S.op("pool", lambda e: e.tensor_tensor(sq[:], qk, qk, ALU.mult), reads=[P], writes=[sq])
                    S.op("dve", lambda e: e.tensor_reduce(ss[:], sq[:], AX.X, ALU.add), reads=[sq], writes=[ss])
                    self.rpow(r1, r1[:], ss, ss[:], 1.0, 64.0 * QK_EPS, -0.5)
                    S.op("dve", lambda e: e.tensor_tensor(qn[:], qk, r1[:].unsqueeze(2).to_broadcast([128, 6, 64]), ALU.mult), reads=[P, r1], writes=[qn])
                    S.op("dve", lambda e: e.tensor_tensor(qn[:], qn[:], GW[:], ALU.mult), reads=[qn, GW], writes=[qn])
                    v3 = lambda t, nh, w: t[:, 0:nh * w].rearrange("p (h w) -> p h w", w=w)
                    if isx:
                        self.rope("pool", v3(ro, 6, 64), qn[:], rG[:, n - 2], 6, 2, 16, v3(t1, 6, 64), v3(t2, 6, 64), [qn], rG)
                        src, srcT = v3(ro, 6, 64), ro
                    else:
                        src, srcT = qn[:], qn
                    self.emit_T(src, srcT, 6, pQ, QTs, [("QTg", 0, 4, 1.0), ("KTg", 4, 2, 1.0)], b, rows)
                    dq = P[:, O_DF:O_DF + 512].rearrange("p (h w) -> p h w", w=32)
                    if isx:
                        self.rope("pool", v3(ro, 16, 32), dq, rD[:, n - 2], 16, 2, 8, v3(t1, 16, 32), v3(t2, 16, 32), [P], rD)
                        src, srcT = v3(ro, 16, 32), ro
                    else:
                        src, srcT = dq, P
                    self.emit_T(src, srcT, 16, pQ, QTs, [("QTd", 0, 8, 32 ** -0.5), ("KTd", 8, 8, 1.0)], b, rows, width=32)
                    rq = P[:, O_RT:O_RT + 512].rearrange("p (h w) -> p h w", w=64)
                    if isx:
                        self.rope("pool", v3(ro, 8, 64), rq, rR[:, n - 2], 8, 1, 32, v3(t1, 8, 64), v3(t2, 8, 64), [P], rR)
                        src, srcT = v3(ro, 8, 64), ro
                    else:
                        src, srcT = rq, P
                    self.emit_T(src, srcT, 8, pQ, QTs, [("QTr", 0, 4, 1.0), ("KTr", 4, 4, 0.125)], b, rows)
                    it += 1

    def emit_T(self, src, srcT, nh, pQ, QTs, outs, b, rows, width=64):
        S = self.S
        for (name, h0, cnt_all, scale) in outs:
            for g0 in range(0, cnt_all, 4):
                cnt = min(4, cnt_all - g0)
                self._qi = getattr(self, "_qi", 0) + 1
                pq = pQ[self._qi % 2]
                st = QTs[self._qi % 6]
                for j in range(cnt):
                    S.op("pe", lambda e: e.transpose(pq[0:width, j, :], src[:, h0 + g0 + j, :], self.ident[:]), reads=[srcT, self.ident], writes=[pq])
                S.op("act", lambda e: e.activation(st[0:width, 0:cnt, :], pq[0:width, 0:cnt, :], AF.Copy, scale=float(scale)), reads=[pq], writes=[st])
                S.dma("pool", self.scr[name][b, :, g0:g0 + cnt, rows], st[0:width, 0:cnt, :], reads=[st])

    def stage_rwprep(self, l):
        S, I = self.S, self.I
        with Scope(self) as sc:
            def btile(key, w, nrep=1):
                t = sc.sb([128, nrep * w])
                for r in range(nrep):
                    S.dma("sp", t[:, r * w:(r + 1) * w], bc(I[key][l], 128), writes=[t])
                return t
            MU = btile("rwkv_mu", 960); W0 = btile("rwkv_w0", 512); A0 = btile("rwkv_a0", 512)
            KK = btile("rwkv_k_k", 256); KA = btile("rwkv_k_a", 256); RK = btile("rwkv_r_k", 256)
            WUP = sc.sb([32, 2, 256]); AUP = sc.sb([32, 2, 256]); GUP = sc.sb([64, 256])
            S.dma("sp", WUP[:], I["rwkv_w_up"][l].rearrange("(d r) c -> r d c", d=2), writes=[WUP])
            S.dma("sp", AUP[:], I["rwkv_a_up"][l].rearrange("(d r) c -> r d c", d=2), writes=[AUP])
            S.dma("sp", GUP[:], I["rwkv_g_up"][l], writes=[GUP])
            cur = sc.sb([128, 960]); prv = sc.sb([128, 960]); nxt = sc.sb([128, 960])
            tt = sc.sb([128, 960])
            pst = sc.sb([128, 1024])
            S.op("dve", lambda e: e.memset(pst[:, 0:64], 0.0), writes=[pst])
            lor = sc.sb([128, 3, 64]); lorT = sc.sb([32, 4, 128]); lorTg = sc.sb([64, 128])
            pL = sc.ps([32, 4, 128]); pW = sc.ps([128, 512]); pAl = sc.ps([128, 512]); pG = sc.ps([128, 256])
            wt = sc.sb([128, 512]); e1 = sc.sb([128, 512])
            decp = sc.sb([128, 64 + 512])
            S.op("dve", lambda e: e.memset(decp[:, 0:64], 0.0), writes=[decp])
            asig = sc.sb([128, 512]); gt = sc.sb([128, 256])
            kkp = sc.sb([128, 64 + 256])
            S.op("dve", lambda e: e.memset(kkp[:, 0:64], 0.0), writes=[kkp])
            kq = sc.sb([128, 256]); ks = sc.sb([128, 4]); kr = sc.sb([128, 4])
            bd = sc.sb([128, 512]); kd = sc.sb([128, 512]); tk = sc.sb([128, 512])
            rk = sc.sb([128, 256]); pr = sc.sb([128, 512]); s8 = sc.sb([128, 8]); s4 = sc.sb([128, 4]); bon = sc.sb([128, 256])
            pA = sc.ps([128, 4, 128]); pR = sc.ps([128, 4, 128]); pWt = [sc.ps([128, 4, 128]) for _ in range(2)]
            Ast = sc.sb([128, 128, 8]); Rst = sc.sb([128, 128, 8]); Wst = [sc.sb([128, 128, 4]) for _ in range(2)]
            S.op("pool", lambda e: e.memset(Ast[:], 0.0), writes=[Ast])
            S.op("pool", lambda e: e.memset(Rst[:], 0.0), writes=[Rst])
            P = self.scr["P"]
            for n in range(NT):
                rows = slice(n * 128, (n + 1) * 128)
                r0 = n * 128
                for b in range(NB):
                    S.dma("sp", cur[:], P[b, rows, 0:960], writes=[cur])
                    if n in (0, 2):
                        S.op("pool", lambda e: e.memset(prv[0:32, :], 0.0), writes=[prv])
                        S.dma("sp", prv[1:128, :], P[b, r0:r0 + 127, 0:960], writes=[prv])
                    else:
                        S.dma("sp", prv[:], P[b, r0 - 1:r0 + 127, 0:960], writes=[prv])
                    if n in (1, NT - 1):
                        S.op("pool", lambda e: e.memset(nxt[96:128, :], 0.0), writes=[nxt])
                        S.dma("sp", nxt[0:127, :], P[b, r0 + 1:r0 + 128, 0:960], writes=[nxt])
                    else:
                        S.dma("sp", nxt[:], P[b, r0 + 1:r0 + 129, 0:960], writes=[nxt])
                    S.op("pool", lambda e: e.tensor_tensor(tt[:], prv[:], nxt[:], ALU.add), reads=[prv, nxt], writes=[tt])
                    S.op("dve", lambda e: e.scalar_tensor_tensor(tt[:], tt[:], 0.5, cur[:], ALU.mult, ALU.subtract), reads=[tt, cur], writes=[tt])
                    S.op("pool", lambda e: e.tensor_tensor(tt[:], tt[:], MU[:], ALU.mult), reads=[tt, MU], writes=[tt])
                    S.op("dve", lambda e: e.tensor_tensor(pst[:, 64:1024], tt[:], cur[:], ALU.add), reads=[tt, cur], writes=[pst])
                    p_r = pst[:, 64:320]; p_k = pst[:, 320:576]; p_v = pst[:, 576:832]
                    S.op("act", lambda e: e.activation(lor[:, 0, :], pst[:, 832:896], AF.Tanh), reads=[pst], writes=[lor])
                    S.op("act", lambda e: e.activation(lor[:, 2, :], pst[:, 960:1024], AF.Sigmoid), reads=[pst], writes=[lor])
                    S.op("pool", lambda e: e.tensor_copy(lor[:, 1, :], pst[:, 896:960]), reads=[pst], writes=[lor])
                    if 'A' in KSKIP:
                        continue
                    for j in range(4):
                        S.op("pe", lambda e: e.transpose(pL[:, j, :], lor[:, j // 2, 32 * (j % 2):32 * (j % 2) + 32], self.ident[:]), reads=[lor, self.ident], writes=[pL])
                    S.op("dve", lambda e: e.tensor_copy(lorT[:], pL[:]), reads=[pL], writes=[lorT])
                    S.op("pe", lambda e: e.transpose(pWt[0][0:64, 0, :], lor[:, 2, :], self.ident[:]), reads=[lor, self.ident], writes=[pWt[0]])
                    S.op("dve", lambda e: e.tensor_copy(lorTg[:], pWt[0][0:64, 0, :]), reads=[pWt[0]], writes=[lorTg])
                    for d in range(2):
                        S.op("pe", lambda e: e.matmul(pW[:, d * 256:(d + 1) * 256], lorT[:, d, :], WUP[:, d, :], start=True, stop=True), reads=[lorT, WUP], writes=[pW])
                        S.op("pe", lambda e: e.matmul(pAl[:, d * 256:(d + 1) * 256], lorT[:, 2 + d, :], AUP[:, d, :], start=True, stop=True), reads=[lorT, AUP], writes=[pAl])
                    S.op("pe", lambda e: e.matmul(pG[:], lorTg[:], GUP[:], start=True, stop=True), reads=[lorTg, GUP], writes=[pG])
                    if 'B' in KSKIP:
                        continue
                    S.op("dve", lambda e: e.tensor_tensor(wt[:], pW[:], W0[:], ALU.add), reads=[pW, W0], writes=[wt])
                    S.op("act", lambda e: e.activation(e1[:], wt[:], AF.Exp, scale=-1.0), reads=[wt], writes=[e1])
                    S.op("dve", lambda e: e.tensor_scalar(e1[:], e1[:], 1.0, None, ALU.add), reads=[e1], writes=[e1])
                    S.op("dve", lambda e: e.reciprocal(e1[:], e1[:]), reads=[e1], writes=[e1])
                    S.op("act", lambda e: e.activation(decp[:, 64:576], e1[:], AF.Exp, scale=-math.exp(-0.5)), reads=[e1], writes=[decp])
                    S.op("dve", lambda e: e.tensor_tensor(wt[:], pAl[:], A0[:], ALU.add), reads=[pAl, A0], writes=[wt])
                    S.op("act", lambda e: e.activation(e1[:], wt[:], AF.Exp, scale=-1.0), reads=[wt], writes=[e1])
                    S.op("dve", lambda e: e.tensor_scalar(e1[:], e1[:], 1.0, None, ALU.add), reads=[e1], writes=[e1])
                    S.op("dve", lambda e: e.reciprocal(asig[:], e1[:]), reads=[e1], writes=[asig])
                    S.op("act", lambda e: e.activation(gt[:], pG[:], AF.Copy), reads=[pG], writes=[gt])
                    S.dma("pool", self.scr["Gt"][b, rows, :], gt[:], reads=[gt])
                    kk = kkp[:, 64:320]
                    S.op("pool", lambda e: e.tensor_tensor(kk, p_k, KK[:], ALU.mult), reads=[pst, KK], writes=[kkp])
                    S.op("pool", lambda e: e.tensor_tensor(kq[:], kk, kk, ALU.mult), reads=[kkp], writes=[kq])
                    S.op("dve", lambda e: e.tensor_reduce(ks[:], kq[:].rearrange("p (h w) -> p h w", w=64), AX.X, ALU.add), reads=[kq], writes=[ks])
                    S.op("dve", lambda e: e.tensor_scalar(kr[:], ks[:], 1e-24, None, ALU.max), reads=[ks], writes=[kr])
                    self.rpow(kr, kr[:], kr, kr[:], 1.0, 0.0, -0.5)
                    kk3 = kk.rearrange("p (h w) -> p h w", w=64)
                    S.op("dve", lambda e: e.tensor_tensor(kk3, kk3, kr[:].unsqueeze(2).to_broadcast([128, 4, 64]), ALU.mult), reads=[kkp, kr], writes=[kkp])
                    d3 = lambda t: t[:].rearrange("p (d c) -> p d c", d=2)
                    b2 = lambda ap: ap.unsqueeze(1).to_broadcast([128, 2, 256])
                    S.op("dve", lambda e: e.tensor_tensor(d3(bd), d3(asig), b2(kk), ALU.mult), reads=[asig, kkp], writes=[bd])
                    S.op("dve", lambda e: e.scalar_tensor_tensor(d3(tk), d3(asig), -1.0, b2(KA[:]), ALU.add, ALU.mult), reads=[asig, KA], writes=[tk])
                    S.op("dve", lambda e: e.scalar_tensor_tensor(d3(kd), d3(tk), 1.0, b2(p_k), ALU.add, ALU.mult), reads=[tk, pst], writes=[kd])
                    S.op("pool", lambda e: e.tensor_tensor(rk[:], p_r, RK[:], ALU.mult), reads=[pst, RK], writes=[rk])
                    S.op("dve", lambda e: e.tensor_tensor(d3(pr), d3(kd), b2(rk[:]), ALU.mult), reads=[kd, rk], writes=[pr])
                    S.op("dve", lambda e: e.tensor_reduce(s8[:], pr[:].rearrange("p (g w) -> p g w", w=64), AX.X, ALU.add), reads=[pr], writes=[s8])
                    S.op("dve", lambda e: e.tensor_tensor(s4[:], s8[:, 0:4], s8[:, 4:8], ALU.add), reads=[s8], writes=[s4])
                    S.op("dve", lambda e: e.tensor_tensor(bon[:].rearrange("p (h w) -> p h w", w=64), p_v.rearrange("p (h w) -> p h w", w=64), s4[:].unsqueeze(2).to_broadcast([128, 4, 64]), ALU.mult), reads=[pst, s4], writes=[bon])
                    S.dma("pool", self.scr["Bon"][b, rows, :], bon[:], reads=[bon])
                    if 'C' in KSKIP:
                        continue
                    lo, hi = b * 64, (b + 1) * 64
                    for h in range(4):
                        if b == 0:
                            ink = kkp[:, 64 + 64 * h:128 + 64 * h]; inr = pst[:, 64 + 64 * h:128 + 64 * h]
                            S.op("pe", lambda e: e.transpose(pA[0:64, h, :], ink, self.ident[:]), reads=[kkp, self.ident], writes=[pA])
                            S.op("pe", lambda e: e.transpose(pR[0:64, h, :], inr, self.ident[:]), reads=[pst, self.ident], writes=[pR])
                        else:
                            ink = kkp[:, 64 * h:128 + 64 * h]; inr = pst[:, 64 * h:128 + 64 * h]
                            S.op("pe", lambda e: e.transpose(pA[:, h, :], ink, self.ident[:]), reads=[kkp, self.ident], writes=[pA])
                            S.op("pe", lambda e: e.transpose(pR[:, h, :], inr, self.ident[:]), reads=[pst, self.ident], writes=[pR])
                    S.op("act", lambda e: e.activation(Ast[lo:hi, :, 4 * b:4 * b + 4].rearrange("p t h -> p h t"), pA[lo:hi, :, :], AF.Copy, scale=-1.0), reads=[pA], writes=[Ast])
                    S.op("act", lambda e: e.activation(Rst[lo:hi, :, 4 * b:4 * b + 4].rearrange("p t h -> p h t"), pR[lo:hi, :, :], AF.Copy), reads=[pR], writes=[Rst])
                    for d in range(2):
                        for h in range(4):
                            c0 = 64 + 256 * d + 64 * h
                            if b == 0:
                                S.op("pe", lambda e: e.transpose(pWt[d][0:64, h, :], decp[:, c0:c0 + 64], self.ident[:]), reads=[decp, self.ident], writes=[pWt[d]])
                            else:
                                S.op("pe", lambda e: e.transpose(pWt[d][:, h, :], decp[:, c0 - 64:c0 + 64], self.ident[:]), reads=[decp, self.ident], writes=[pWt[d]])
                        S.op("dve", lambda e: e.tensor_copy(Wst[d][lo:hi, :, :].rearrange("p t h -> p h t"), pWt[d][lo:hi, :, :]), reads=[pWt[d]], writes=[Wst[d]])
                    if 'D' in KSKIP:
                        continue
                    for d in range(2):
                        if 'E' in KSKIP:
                            continue
                        S.dma("sp", self.scr["LBs"][d, 4 * b:4 * b + 4, rows, lo:hi].rearrange("h t k -> t h k"), bd[:, 256 * d:256 * d + 256].rearrange("p (h k) -> p h k", k=64), reads=[bd])
                        S.dma("sp", self.scr["LKs"][d, 4 * b:4 * b + 4, rows, lo:hi].rearrange("h t k -> t h k"), kd[:, 256 * d:256 * d + 256].rearrange("p (h k) -> p h k", k=64), reads=[kd])
                    rv = self.scr["RVs"]
                    dst = bass.AP(rv.tensor, (4 * b) * T * 256 + r0 * 256, [[256, 128], [T * 256 + 64, 4], [1, 64]])
                    if 'F' not in KSKIP:
                        S.dma("sp", dst, p_v.rearrange("p (h k) -> p h k", k=64), reads=[pst])
                S.dma("pool", self.scr["As"][:, rows, :], Ast[:], reads=[Ast])
                S.dma("pool", self.scr["Rs"][:, rows, :], Rst[:], reads=[Rst])
                for d in range(2):
                    S.dma("pool", self.scr["Ws"][d, :, rows, :], Wst[d][:], reads=[Wst[d]])

    def stage_scan(self, l):
        S, I = self.S, self.I
        with Scope(self) as sc:
            MJ = sc.sb([8, 256])
            S.dma("sp", MJ[:], I["c_maskJ"], writes=[MJ])
            St = [sc.sb([128, 256]) for _ in range(2)]
            Tmp = [sc.sb([128, 256]) for _ in range(2)]
            SAm = [sc.sb([8, 256]) for _ in range(2)]
            for d in range(2):
                S.op("dve", lambda e: e.memset(St[d][:], 0.0), writes=[St[d]])
            pSAt = [sc.ps([8, 2, 256]) for _ in range(2)]; pUt = [sc.ps([128, 2, 256]) for _ in range(2)]; pYt = [sc.ps([8, 2, 256]) for _ in range(2)]
            pSA = [[TT(pSAt[i][:, d, :]) for d in range(2)] for i in range(2)]
            pU = [[TT(pUt[i][:, d, :]) for d in range(2)] for i in range(2)]
            pY = [[TT(pYt[i][:, d, :]) for d in range(2)] for i in range(2)]
            NBUF = 2
            bufs = []
            for i in range(NBUF):
                bb = []
                for d in range(2):
                    bb.append(dict(A=sc.sb([128, CS, 8]), R=sc.sb([128, CS, 8]), W=sc.sb([128, CS, 4]),
                                   LB=sc.sb([8, CS, 128]), LK=sc.sb([8, CS, 128]), RV=sc.sb([8, CS, 256]), Y=sc.sb([8, CS, 256])))
                bufs.append(bb)
            NCH = T // CS

            def rowbase(d, c):
                if d == 0:
                    return c * CS
                t0 = c * CS
                if t0 < CT:
                    return CT - CS - t0
                return (T + CT - CS) - t0

            def load(c):
                bb = bufs[c % NBUF]
                for d in range(2):
                    ra = rowbase(d, c)
                    rs = slice(ra, ra + CS)
                    q = "sp"
                    B = bb[d]
                    S.dma(q, B["A"][:], self.scr["As"][:, rs, :], writes=[B["A"]])
                    S.dma(q, B["R"][:], self.scr["Rs"][:, rs, :], writes=[B["R"]])
                    S.dma(q, B["W"][:], self.scr["Ws"][d, :, rs, :], writes=[B["W"]])
                    S.dma(q, B["LB"][:], self.scr["LBs"][d, :, rs, :], writes=[B["LB"]])
                    S.dma(q, B["LK"][:], self.scr["LKs"][d, :, rs, :], writes=[B["LK"]])
                    S.dma(q, B["RV"][:], self.scr["RVs"][:, rs, :], writes=[B["RV"]])

            load(0)
            step = 0
            for c in range(NCH):
                if c + 1 < NCH:
                    load(c + 1)
                bb = bufs[c % NBUF]
                for s in range(CS):
                    pi = step % 2
                    for d in range(2):
                        B = bb[d]
                        i = s if d == 0 else CS - 1 - s
                        st, tmp, sam = St[d], Tmp[d], SAm[d]
                        psa, pu, py = pSA[pi][d], pU[pi][d], pY[pi][d]
                        S.op("pe", lambda e: e.matmul(psa[:], B["A"][:, i, :], st[:], start=True, stop=True), reads=[B["A"], st], writes=[psa])
                        S.op("dve", lambda e: e.tensor_tensor(sam[:], psa[:], MJ[:], ALU.mult), reads=[psa, MJ], writes=[sam])
                        S.op("pool", lambda e: e.tensor_tensor(tmp[:].rearrange("p (j v) -> p j v", j=4), st[:].rearrange("p (j v) -> p j v", j=4), B["W"][:, i, :].unsqueeze(2).to_broadcast([128, 4, 64]), ALU.mult), reads=[st, B["W"]], writes=[tmp])
                        S.op("pe", lambda e: e.matmul(pu[:], B["LK"][:, i, :], B["RV"][:, i, :], start=True, stop=False), reads=[B["LK"], B["RV"]], writes=[pu])
                        S.op("pe", lambda e: e.matmul(pu[:], B["LB"][:, i, :], sam[:], start=False, stop=True), reads=[B["LB"], sam], writes=[pu])
                        S.op("dve", lambda e: e.tensor_tensor(st[:], tmp[:], pu[:], ALU.add), reads=[tmp, pu], writes=[st])
                        S.op("pe", lambda e: e.matmul(py[:], B["R"][:, i, :], st[:], start=True, stop=True), reads=[B["R"], st], writes=[py])
                        S.op("act", lambda e: e.activation(B["Y"][:, i, :], py[:], AF.Copy), reads=[py], writes=[B["Y"]])
                    step += 1
                for d in range(2):
                    ra = rowbase(d, c)
                    S.dma("pool", self.scr["Yfull"][d, :, ra:ra + CS, :], bb[d]["Y"][:], reads=[bb[d]["Y"]])

    def stage_rwpost(self, l):
        S, I = self.S, self.I
        with Scope(self) as sc:
            LNG = sc.sb([128, 256]); LNB = sc.sb([128, 256])
            S.dma("sp", LNG[:], bc(I["rwkv_ln_g"][l], 128), writes=[LNG])
            S.dma("sp", LNB[:], bc(I["rwkv_ln_b"][l], 128), writes=[LNB])
            yf = [sc.sb([128, 256]) for _ in range(2)]; yb = [sc.sb([128, 256]) for _ in range(2)]
            bo = [sc.sb([128, 256]) for _ in range(2)]; gg = [sc.sb([128, 256]) for _ in range(2)]
            y = sc.sb([128, 256]); sq = sc.sb([128, 256]); s1 = sc.sb([128, 4]); s2 = sc.sb([128, 4]); o = [sc.sb([128, 256]) for _ in range(2)]
            h3 = lambda t: t[:].rearrange("p (h w) -> p h w", w=64)
            b3 = lambda t: t[:].unsqueeze(2).to_broadcast([128, 4, 64])
            yfull = self.scr["Yfull"]
            it = 0
            for b in range(NB):
                for n in range(NT):
                    if self.last and n < 2:
                        continue
                    rows = slice(n * 128, (n + 1) * 128)
                    i = it % 2
                    for d, dstt in ((0, yf[i]), (1, yb[i])):
                        src = bass.AP(yfull.tensor, d * 8 * T * 256 + (4 * b) * T * 256 + n * 128 * 256, [[256, 128], [T * 256 + 64, 4], [1, 64]])
                        S.dma("sp", h3(dstt), src, writes=[dstt])
                    S.dma("sp", bo[i][:], self.scr["Bon"][b, rows, :], writes=[bo[i]])
                    S.dma("sp", gg[i][:], self.scr["Gt"][b, rows, :], writes=[gg[i]])
                    S.op("pool", lambda e: e.tensor_tensor(y[:], yf[i][:], yb[i][:], ALU.add), reads=[yf[i], yb[i]], writes=[y])
                    self.head_norm(y, sq, s1, s2, GN_EPS)
                    S.op("dve", lambda e: e.tensor_tensor(y[:], y[:], LNG[:], ALU.mult), reads=[y, LNG], writes=[y])
                    S.op("pool", lambda e: e.tensor_tensor(y[:], y[:], LNB[:], ALU.add), reads=[y, LNB], writes=[y])
                    S.op("pool", lambda e: e.tensor_tensor(y[:], y[:], bo[i][:], ALU.add), reads=[y, bo[i]], writes=[y])
                    S.op("dve", lambda e: e.tensor_tensor(o[i][:], y[:], gg[i][:], ALU.mult), reads=[y, gg[i]], writes=[o[i]])
                    S.dma("pool", self.scr["Ycat"][b, rows, 0:256], o[i][:], reads=[o[i]])
                    it += 1

    def head_norm(self, y, sq, s1, s2, eps, nh=4):
        S = self.S
        h3 = lambda t: t[:, 0:nh * 64].rearrange("p (h w) -> p h w", w=64)
        b3 = lambda t: t[:, 0:nh].unsqueeze(2).to_broadcast([128, nh, 64])
        S.op("dve", lambda e: e.tensor_reduce(s1[:, 0:nh], h3(y), AX.X, ALU.add), reads=[y], writes=[s1])
        S.op("dve", lambda e: e.tensor_scalar(s1[:, 0:nh], s1[:, 0:nh], -1.0 / 64, None, ALU.mult), reads=[s1], writes=[s1])
        S.op("dve", lambda e: e.tensor_tensor(h3(y), h3(y), b3(s1), ALU.add), reads=[y, s1], writes=[y])
        S.op("pool", lambda e: e.tensor_tensor(h3(sq), h3(y), h3(y), ALU.mult), reads=[y], writes=[sq])
        S.op("dve", lambda e: e.tensor_reduce(s2[:, 0:nh], h3(sq), AX.X, ALU.add), reads=[sq], writes=[s2])
        self.rpow(s2, s2[:, 0:nh], s2, s2[:, 0:nh], 1.0 / 64, eps, -0.5)
        S.op("dve", lambda e: e.tensor_tensor(h3(y), h3(y), b3(s2), ALU.mult), reads=[y, s2], writes=[y])

    def stage_attn(self, l):
        S, I = self.S, self.I
        lam_init = 0.8 - 0.6 * math.exp(-0.3 * l)
        with Scope(self) as sc:
            DL = sc.sb([128, 128]); dp = sc.sb([128, 128]); ds = sc.sb([128, 2]); lam = sc.sb([128, 1]); nlam = sc.sb([128, 1])
            S.dma("sp", DL[:], bc(I["diff_lambda"][l], 128), writes=[DL])
            S.op("dve", lambda e: e.tensor_tensor(dp[:, 0:32], DL[:, 0:32], DL[:, 32:64], ALU.mult), reads=[DL], writes=[dp])
            S.op("dve", lambda e: e.tensor_tensor(dp[:, 32:64], DL[:, 64:96], DL[:, 96:128], ALU.mult), reads=[DL], writes=[dp])
            S.op("dve", lambda e: e.tensor_reduce(ds[:], dp[:, 0:64].rearrange("p (a w) -> p a w", w=32), AX.X, ALU.add), reads=[dp], writes=[ds])
            S.op("act", lambda e: e.activation(ds[:], ds[:], AF.Exp), reads=[ds], writes=[ds])
            S.op("dve", lambda e: e.tensor_tensor(lam[:], ds[:, 0:1], ds[:, 1:2], ALU.subtract), reads=[ds], writes=[lam])
            S.op("dve", lambda e: e.tensor_scalar(nlam[:], lam[:], lam_init, -1.0, ALU.add, ALU.mult), reads=[lam], writes=[nlam])
            DN = sc.sb([128, 64])
            S.dma("sp", DN[:], bc(I["diff_norm"][l], 128), writes=[DN])
            S.op("dve", lambda e: e.tensor_scalar(DN[:], DN[:], 1.0 - lam_init, None, ALU.mult), reads=[DN], writes=[DN])
            KT = sc.sb([64, 4, T], BF16); QT = sc.sb([64, 4, T], BF16)
            KTd = sc.sb([32, 8, T], BF16); QTd = sc.sb([32, 8, T], BF16)
            V = sc.sb([128, NT, 4, 65], BF16)
            S.op("pool", lambda e: e.memset(V[:, :, :, 64:65], 1.0), writes=[V])
            pS = [sc.ps([128, 512]) for _ in range(2)]
            pO = [sc.ps([128, 4, 65]) for _ in range(4)]
            Pball = [sc.sb([128, NT, 512], BF16) for _ in range(2)]
            Mk = [sc.sb([128, 512], BF16) for _ in range(3)]
            rec = sc.sb([128, 4, 1]); o1 = sc.sb([128, 4, 64]); o2 = sc.sb([128, 4, 64])
            sq = sc.sb([128, 256]); s1 = sc.sb([128, 4]); s2 = sc.sb([128, 4])
            gate = sc.sb([128, 4, 64]); gs = sc.sb([128, 4, 64])
            osb = [sc.sb([128, 4, 64]) for _ in range(2)]
            Pd = self.scr["P"]
            cnt = {"u": 0, "o": 0, "p": 0}
            for m in "gdr":
                nk = 2 if m == "g" else 4
                nv = 2 if m == "g" else 4
                vcol = {"g": O_GQ + 384, "d": O_DF + 512, "r": O_RT + 512}[m]
                ocol = {"g": 256, "d": 512, "r": 768}[m]
                for b in range(NB):
                    if m == "d":
                        S.dma("sp", KTd[:], self.scr["KTd"][b], writes=[KTd])
                        S.dma("sp", QTd[:], self.scr["QTd"][b], writes=[QTd])
                    else:
                        S.dma("sp", KT[:, 0:nk, :], self.scr["KT" + m][b, :, 0:nk, :], writes=[KT])
                        S.dma("sp", QT[:], self.scr["QT" + m][b], writes=[QT])
                    for hv in range(nv):
                        S.dma("pool", V[:, :, hv, 0:64], Pd[b, :, vcol + hv * 64:vcol + hv * 64 + 64].rearrange("(n p) w -> p n w", p=128), writes=[V])
                    chunks = ([] if self.last else [(0, 256, 0, 2)]) + [(256 + 512 * q, 512, 0, NT) for q in range(4)]
                    for h in range(4):
                        for (q0, w, k0, k1) in chunks:
                            nj = w // 128
                            isx = q0 >= 256
                            units = [(0, 0)] if m != "d" else [(0, 0), (1, 0)]
                            pos = []
                            for (mi, rbase) in units:
                                po = pO[cnt["o"] % 4]; cnt["o"] += 1
                                pos.append(po)
                                rws = slice(rbase, rbase + (64 if m != "d" else 32))
                                hk = h // 2 if m == "g" else h
                                hvv = h // 2 if m == "g" else h
                                PB_ = Pball[cnt["o"] % 2]
                                for kt in range(k0, k1):
                                    ps_ = pS[cnt["u"] % 2]; cnt["u"] += 1
                                    pb = PB_[:, kt, :]
                                    if m == "d":
                                        S.op("pe", lambda e: e.matmul(ps_[:, :w], KTd[:, 2 * h + mi, kt * 128:(kt + 1) * 128], QTd[:, 2 * h + mi, q0:q0 + w], start=True, stop=True), reads=[KTd, QTd], writes=[ps_])
                                    else:
                                        S.op("pe", lambda e: e.matmul(ps_[:, :w], KT[rws, hk, kt * 128:(kt + 1) * 128], QT[rws, h, q0:q0 + w], start=True, stop=True), reads=[KT, QT], writes=[ps_])
                                    if m == "r":
                                        mk = Mk[cnt["p"] % 3]; cnt["p"] += 1
                                        if not isx:
                                            ti = {0: 15, 1: 14}[kt]
                                        elif kt < 2:
                                            ti = 28 + kt * 4 + (q0 - 256) // 512
                                        else:
                                            off = 4 * ((q0 - 256) // 512) - (kt - 2)
                                            ti = off + 15
                                        S.dma("sp", mk[:], I["c_retM"][ti, h], writes=[mk])
                                        S.op("dve", lambda e: e.tensor_tensor(pb[:, :w], ps_[:, :w], mk[:, :w], ALU.mult), reads=[ps_, mk], writes=[PB_])
                                    else:
                                        S.op("act", lambda e: e.activation(pb[:, :w], ps_[:, :w], AF.Exp), reads=[ps_], writes=[PB_])
                                for j in range(nj):
                                    for kt in range(k0, k1):
                                        S.op("pe", lambda e: e.matmul(po[:, j, :], PB_[:, kt, j * 128:(j + 1) * 128], V[:, kt, hvv, :], start=(kt == k0), stop=(kt == k1 - 1)), reads=[PB_, V], writes=[po])
                            ob = osb[cnt["o"] % 2]
                            dst = self.scr["Ycat"][b, q0:q0 + w, ocol + h * 64:ocol + h * 64 + 64].rearrange("(j t) v -> t j v", t=128)
                            if m == "g":
                                po = pos[0]
                                S.op("dve", lambda e: e.reciprocal(rec[:, 0:nj, :], po[:, 0:nj, 64:65]), reads=[po], writes=[rec])
                                S.op("dve", lambda e: e.tensor_tensor(ob[:, 0:nj, :], po[:, 0:nj, 0:64], rec[:, 0:nj, :].to_broadcast([128, nj, 64]), ALU.mult), reads=[po, rec], writes=[ob])
                            elif m == "d":
                                for (po, ot) in ((pos[0], o1), (pos[1], o2)):
                                    S.op("dve", lambda e: e.reciprocal(rec[:, 0:nj, :], po[:, 0:nj, 64:65]), reads=[po], writes=[rec])
                                    S.op("dve", lambda e: e.tensor_tensor(ot[:, 0:nj, :], po[:, 0:nj, 0:64], rec[:, 0:nj, :].to_broadcast([128, nj, 64]), ALU.mult), reads=[po, rec], writes=[ot])
                                S.op("dve", lambda e: e.scalar_tensor_tensor(o1[:, 0:nj, :], o2[:, 0:nj, :], nlam[:, 0:1], o1[:, 0:nj, :], ALU.mult, ALU.add), reads=[o1, o2, nlam], writes=[o1])
                                S.op("pool", lambda e: e.tensor_tensor(o2[:, 0:nj, :], o1[:, 0:nj, :], o1[:, 0:nj, :], ALU.mult), reads=[o1], writes=[o2])
                                S.op("dve", lambda e: e.tensor_reduce(s1[:, 0:nj], o2[:, 0:nj, :], AX.X, ALU.add), reads=[o2], writes=[s1])
                                self.rpow(s1, s1[:, 0:nj], s1, s1[:, 0:nj], 1.0 / 64, QK_EPS, -0.5)
                                S.op("dve", lambda e: e.tensor_tensor(o1[:, 0:nj, :], o1[:, 0:nj, :], s1[:, 0:nj].unsqueeze(2).to_broadcast([128, nj, 64]), ALU.mult), reads=[o1, s1], writes=[o1])
                                S.op("dve", lambda e: e.tensor_tensor(ob[:, 0:nj, :], o1[:, 0:nj, :], DN[:].unsqueeze(1).to_broadcast([128, nj, 64]), ALU.mult), reads=[o1, DN], writes=[ob])
                            else:
                                po = pos[0]
                                S.dma("sp", gate[:, 0:nj, :], Pd[b, q0:q0 + w, O_RT + 768 + h * 64:O_RT + 768 + h * 64 + 64].rearrange("(j t) v -> t j v", t=128), writes=[gate])
                                S.op("act", lambda e: e.activation(gs[:, 0:nj, :], gate[:, 0:nj, :], AF.Silu), reads=[gate], writes=[gs])
                                yv = TT(o1[:].rearrange("p j w -> p (j w)"))
                                S.op("act", lambda e: e.activation(o1[:, 0:nj, :], po[:, 0:nj, 0:64], AF.Copy), reads=[po], writes=[o1])
                                yv.lastw = o1.lastw
                                self.head_norm(yv, sq, s1, s2, LN_EPS, nh=nj)
                                o1.lastw = yv.lastw
                                S.op("dve", lambda e: e.tensor_tensor(ob[:, 0:nj, :], o1[:, 0:nj, :], gs[:, 0:nj, :], ALU.mult), reads=[o1, gs], writes=[ob])
                            S.dma("sp", dst, ob[:, 0:nj, :], reads=[ob])

    def layer_norm_out(self, sc, x1, G, Bt, out, tmps):
        S = self.S
        s1, s2, sq = tmps
        S.op("dve", lambda e: e.tensor_reduce(s1[:], x1[:], AX.X, ALU.add), reads=[x1], writes=[s1])
        S.op("dve", lambda e: e.tensor_scalar(s1[:], s1[:], -1.0 / D, None, ALU.mult), reads=[s1], writes=[s1])
        S.op("dve", lambda e: e.tensor_scalar(x1[:], x1[:], s1[:, 0:1], None, ALU.add), reads=[x1, s1], writes=[x1])
        S.op("pool", lambda e: e.tensor_tensor(sq[:], x1[:], x1[:], ALU.mult), reads=[x1], writes=[sq])
        S.op("dve", lambda e: e.tensor_reduce(s2[:], sq[:], AX.X, ALU.add), reads=[sq], writes=[s2])
        self.rpow(s2, s2[:], s2, s2[:], 1.0 / D, LN_EPS, -0.5)
        S.op("dve", lambda e: e.scalar_tensor_tensor(x1[:], x1[:], s2[:, 0:1], G[:], ALU.mult, ALU.mult), reads=[x1, s2, G], writes=[x1])
        S.op("pool", lambda e: e.tensor_tensor(out[:], x1[:], Bt[:], ALU.add), reads=[x1, Bt], writes=[out])

    def stage_mix(self, l):
        S, I = self.S, self.I
        with Scope(self) as sc:
            wo = sc.sb([128, 8, D], BF16)
            for kc in range(8):
                S.dma("pool", wo[:, kc, :], I["w_out"][l, kc * 128:(kc + 1) * 128, :], writes=[wo])
            G1 = sc.sb([128, 3, D])
            for v in range(3):
                S.dma("sp", G1[:, v, :], bc(self.scr["modD"][v, 2 * D:3 * D], 128), writes=[G1])
            PG = sc.sb([128, D]); PB = sc.sb([128, D])
            S.dma("sp", PG[:], bc(I["post1_g"][l], 128), writes=[PG])
            S.dma("sp", PB[:], bc(I["post1_b"][l], 128), writes=[PB])
            yc = [sc.sb([128, D]) for _ in range(2)]; xt = [sc.sb([128, D]) for _ in range(2)]
            pT = [sc.ps([128, 128]) for _ in range(2)]
            yT = [sc.sb([128, 8, 128], BF16) for _ in range(2)]
            pO = [sc.ps([128, 512]) for _ in range(2)]
            t = sc.sb([128, D]); x1 = sc.sb([128, D]); outt = [sc.sb([128, D]) for _ in range(2)]
            s1 = sc.sb([128, 1]); s2 = sc.sb([128, 1]); sq = sc.sb([128, D])
            it = 0
            for b in range(NB):
                for n in range(NT):
                    if self.last and n < 2:
                        continue
                    v = 2 if n < 2 else b
                    rows = slice(n * 128, (n + 1) * 128)
                    i = it % 2
                    S.dma("sp", yc[i][:], self.scr["Ycat"][b, rows, :], writes=[yc[i]])
                    S.dma("sp", xt[i][:], self.Xsrc[b, rows, :], writes=[xt[i]])
                    for kc in range(8):
                        pp = pT[kc % 2]
                        S.op("pe", lambda e: e.transpose(pp[:], yc[i][:, kc * 128:(kc + 1) * 128], self.ident[:]), reads=[yc[i], self.ident], writes=[pp])
                        S.op("act", lambda e: e.activation(yT[i][:, kc, :], pp[:], AF.Copy), reads=[pp], writes=[yT[i]])
                    for c in range(2):
                        po = pO[c]
                        for kc in range(8):
                            S.op("pe", lambda e: e.matmul(po[:], yT[i][:, kc, :], wo[:, kc, c * 512:(c + 1) * 512], start=(kc == 0), stop=(kc == 7)), reads=[yT[i], wo], writes=[po])
                        S.op("dve", lambda e: e.tensor_tensor(t[:, c * 512:(c + 1) * 512], po[:], G1[:, v, c * 512:(c + 1) * 512], ALU.mult), reads=[po, G1], writes=[t])
                    S.op("dve", lambda e: e.scalar_tensor_tensor(x1[:], xt[i][:], ALPHA, t[:], ALU.mult, ALU.add), reads=[xt[i], t], writes=[x1])
                    self.layer_norm_out(sc, x1, PG, PB, outt[i], (s1, s2, sq))
                    S.dma("pool", self.scr["X1s"][b, rows, :], outt[i][:], reads=[outt[i]])
                    it += 1

    def stage_ffn(self, l):
        S, I = self.S, self.I
        with Scope(self) as sc:
            w1 = sc.sb([128, 8, 2 * DFF], BF16, "w1")
            for kc in range(8):
                S.dma("pool", w1[:, kc, :], I["ffn_w_in"][l, kc * 128:(kc + 1) * 128, :], writes=[w1])
            w2 = sc.sb([128, NFF, D], BF16, "w2")
            for fc in range(NFF):
                S.dma("pool", w2[:, fc, :], I["ffn_w_out"][l, fc * 128:(fc + 1) * 128, :], writes=[w2])
            G2 = sc.sb([128, D])
            PG = sc.sb([128, D]); PB = sc.sb([128, D])
            S.dma("sp", PG[:], bc(I["post2_g"][l], 128), writes=[PG])
            S.dma("sp", PB[:], bc(I["post2_b"][l], 128), writes=[PB])
            xt = [sc.sb([128, D]) for _ in range(2)]
            pT = [sc.ps([128, 128]) for _ in range(2)]
            x1T = sc.sb([128, 8, 512], BF16)
            aT = sc.sb([128, NFF, 512], BF16)
            pU = [sc.ps([128, 512]) for _ in range(2)]; pGt = [sc.ps([128, 512]) for _ in range(2)]
            su = [sc.sb([128, 512]) for _ in range(2)]
            pF = [sc.ps([128, 512]) for _ in range(2)]
            t = sc.sb([128, D]); x2 = sc.sb([128, D]); outt = [sc.sb([128, D]) for _ in range(2)]
            s1 = sc.sb([128, 1]); s2 = sc.sb([128, 1]); sq = sc.sb([128, D])
            it = 0
            for b in range(NB):
                groups = ([] if self.last else [(0, 2, 2)]) + [(2 + 4 * q, 4, b) for q in range(4)]
                for (n0, nt, v) in groups:
                    w = nt * 128
                    S.dma("sp", G2[:], bc(self.scr["modD"][v, 5 * D:6 * D], 128), writes=[G2])
                    for j in range(nt):
                        rows = slice((n0 + j) * 128, (n0 + j + 1) * 128)
                        x = xt[it % 2]; it += 1
                        S.dma("sp", x[:], self.scr["X1s"][b, rows, :], writes=[x])
                        self.xT_modulated(sc, x, pT, x1T, j * 128, 128, v, 1)
                    for fc in range(NFF):
                        pu = pU[fc % 2]; pg = pGt[fc % 2]; s_ = su[fc % 2]
                        for kc in range(8):
                            S.op("pe", lambda e: e.matmul(pu[:, :w], w1[:, kc, fc * 128:(fc + 1) * 128], x1T[:, kc, :w], start=(kc == 0), stop=(kc == 7)), reads=[w1, x1T], writes=[pu])
                        for kc in range(8):
                            S.op("pe", lambda e: e.matmul(pg[:, :w], w1[:, kc, DFF + fc * 128:DFF + (fc + 1) * 128], x1T[:, kc, :w], start=(kc == 0), stop=(kc == 7)), reads=[w1, x1T], writes=[pg])
                        S.op("act", lambda e: e.activation(s_[:, :w], pu[:, :w], AF.Silu), reads=[pu], writes=[s_])
                        S.op("dve", lambda e: e.tensor_tensor(aT[:, fc, :w], s_[:, :w], pg[:, :w], ALU.mult), reads=[s_, pg], writes=[aT])
                    for j in range(nt):
                        rows = slice((n0 + j) * 128, (n0 + j + 1) * 128)
                        x = xt[it % 2]; it += 1
                        S.dma("sp", x[:], self.scr["X1s"][b, rows, :], writes=[x])
                        for c in range(2):
                            pf = pF[c]
                            for fc in range(NFF):
                                S.op("pe", lambda e: e.matmul(pf[:], aT[:, fc, j * 128:(j + 1) * 128], w2[:, fc, c * 512:(c + 1) * 512], start=(fc == 0), stop=(fc == NFF - 1)), reads=[aT, w2], writes=[pf])
                            S.op("dve", lambda e: e.tensor_tensor(t[:, c * 512:(c + 1) * 512], pf[:], G2[:, c * 512:(c + 1) * 512], ALU.mult), reads=[pf, G2], writes=[t])
                        S.op("dve", lambda e: e.scalar_tensor_tensor(x2[:], x[:], ALPHA, t[:], ALU.mult, ALU.add), reads=[x, t], writes=[x2])
                        o = outt[j % 2]
                        self.layer_norm_out(sc, x2, PG, PB, o, (s1, s2, sq))
                        if self.last:
                            xr = (n0 + j) * 128 - CT
                            S.dma("pool", self.out[b, xr:xr + 128, :], o[:], reads=[o])
                        else:
                            S.dma("pool", self.scr["Xs"][b, rows, :], o[:], reads=[o])


def host_consts():
    c = {}
    c["c_ident"] = np.eye(128, dtype=np.float32)
    t = np.arange(SX)
    row = (t // 64).astype(np.float32)
    col = (t % 64).astype(np.float32)
    pos = t.astype(np.float32)

    def tab(p, n):
        half = n // 2
        inv = np.power(np.float32(10000.0), -(np.arange(half, dtype=np.float32) * np.float32(2.0) / np.float32(n))).astype(np.float32)
        ang = (p[:, None] * inv[None, :]).astype(np.float32)
        cs, sn = np.cos(ang).astype(np.float32), np.sin(ang).astype(np.float32)
        return np.concatenate([cs, cs], 1), np.concatenate([-sn, sn], 1)

    def axial(n):
        h = n // 2
        c1, s1 = tab(row, h)
        c2, s2 = tab(col, h)
        return np.stack([np.concatenate([c1, c2], 1), np.concatenate([s1, s2], 1)], 1).astype(np.float32)

    c["c_ropeG"] = axial(64)
    c["c_ropeD"] = axial(32)
    cr, sr = tab(pos, 64)
    c["c_ropeR"] = np.stack([cr, sr], 1).astype(np.float32)
    mj = np.zeros((8, 256), np.float32)
    for m in range(8):
        h = m % 4
        mj[m, h * 64:(h + 1) * 64] = 1.0
    c["c_maskJ"] = mj
    c["c_zero"] = np.zeros((128, 2048), np.float32)
    gf = 1.0 - 2.0 ** (-5.0 - np.arange(4, dtype=np.float64))
    gb = gf[::-1]
    M = np.zeros((36, 4, 128, 512), np.float64)
    p = np.arange(128)[:, None]
    j = np.arange(512)[None, :]
    for h in range(4):
        lf, lb = math.log(gf[h]), math.log(gb[h])
        for off in range(-15, 13):
            dlt = (128 * off + j - p).astype(np.float64)
            m = np.where(dlt > 0, np.exp(lf * np.maximum(dlt, 0)), 0.0) + np.where(dlt < 0, np.exp(lb * np.maximum(-dlt, 0)), 0.0) + np.where(dlt == 0, 2.0, 0.0)
            M[off + 15, h] = m
        for kc in range(2):
            for qc in range(4):
                cc = 128 * kc + p
                ii = 512 * qc + j
                M[28 + kc * 4 + qc, h] = np.exp(lf * (256 + ii - cc)) + np.exp(lb * (2048 - ii + cc))
    c["c_retM"] = M.astype(np.float32).astype(ml_dtypes.bfloat16)
    return c


_CACHE = {}


def kernel(**inputs):
    f = lambda k: np.ascontiguousarray(np.asarray(inputs[k], dtype=np.float32))
    L = DEPTH
    shared = {}
    for k in ("ada_w", "ada_b", "w_in", "rwkv_mu", "rwkv_g_up", "rwkv_k_k", "rwkv_k_a", "rwkv_ln_g", "rwkv_ln_b",
              "gqa_q_norm", "gqa_k_norm", "diff_norm", "w_out", "post1_g", "post1_b", "ffn_w_in", "ffn_w_out", "post2_g", "post2_b"):
        shared[k] = f(k)
    shared["rwkv_w0"] = f("rwkv_w0").reshape(L, 512)
    shared["rwkv_a0"] = f("rwkv_a0").reshape(L, 512)
    shared["rwkv_w_up"] = f("rwkv_w_up").reshape(L, 64, 256)
    shared["rwkv_a_up"] = f("rwkv_a_up").reshape(L, 64, 256)
    shared["rwkv_r_k"] = f("rwkv_r_k").reshape(L, 256)
    shared["diff_lambda"] = f("diff_lambda").reshape(L, 128)
    shared.update(host_consts())
    x, c, ctx, c_ctx = f("x"), f("c"), f("ctx"), f("c_ctx")
    in_maps = []
    for core in range(8):
        bs = slice(core * NB, (core + 1) * NB)
        m = dict(shared)
        m["xin"] = np.ascontiguousarray(np.concatenate([ctx[bs], x[bs]], axis=1))
        m["cvec"] = np.ascontiguousarray(np.concatenate([c[bs], c_ctx[None, :]], axis=0))
        in_maps.append(m)
    if "nc" not in _CACHE:
        _CACHE["nc"] = KB().build()
    res = run_bass_kernel_spmd(_CACHE["nc"], in_maps, core_ids=list(range(8)))
    return np.concatenate([r["y"] for r in res.results], axis=0).astype(np.float32)
```

```python
import math, os
KSKIP = os.environ.get('KSKIP', '')
import numpy as np
import ml_dtypes
import concourse.bass as bass
import concourse.mybir as mybir
from concourse.bass_utils import run_bass_kernel_spmd
from contextlib import ExitStack

F32 = mybir.dt.float32
BF16 = mybir.dt.bfloat16
AF = mybir.ActivationFunctionType
ALU = mybir.AluOpType
AX = mybir.AxisListType

D = 1024
NB = 2
CT = 256
SX = 2048
T = CT + SX
NT = T // 128
INW = 3264
DFF = 2816
NFF = DFF // 128
DEPTH = 4
ALPHA = (2 * DEPTH) ** 0.25
LN_EPS = 1e-5
QK_EPS = 1e-6
GN_EPS = 64e-5
CS = 16
O_RW, O_GQ, O_DF, O_RT = 0, 960, 1472, 2240


class Sched:
    def __init__(self, nc, es):
        self.nc = nc
        self.es = es
        self.eng = {"pe": nc.tensor, "act": nc.scalar, "dve": nc.vector, "pool": nc.gpsimd, "sp": nc.sync}
        self.sem = {}
        self.cnt = {}
        self.nsem = 0
        for k in self.eng:
            self.sem[k] = self._newsem()
            self.cnt[k] = 0
        self.waited = {k: {} for k in self.eng}
        self.NSLOT = 8
        self.dsem = {}
        self.dcnt = {}
        self.dnext = {}
        for q in ("sp", "pool", "act"):
            self.dsem[q] = [self._newsem() for i in range(self.NSLOT)]
            self.dcnt[q] = [0] * self.NSLOT
            self.dnext[q] = 0
        self.ninstr = 0

    def _newsem(self):
        self.nsem += 1
        return self.es.enter_context(self.nc.semaphore("sem%d" % self.nsem))

    def _wait(self, e, tok, raw=False):
        if tok is None:
            return
        sem, val, owner = tok
        if owner == e and (e == "pe" or not raw):
            return
        key = id(sem)
        w = self.waited[e]
        if w.get(key, 0) >= val:
            return
        self.eng[e].wait_ge(sem, val)
        w[key] = val

    def deps(self, e, reads, writes):
        for t in reads:
            self._wait(e, t.lastw, raw=True)
        for t in writes:
            self._wait(e, t.lastw)
            for r in t.readers.values():
                self._wait(e, r)

    def done(self, tok, reads, writes):
        for t in reads:
            t.readers[tok[2] + str(id(tok[0]))] = tok
        for t in writes:
            t.lastw = tok
            t.readers = {}

    def op(self, e, fn, reads=(), writes=()):
        self.deps(e, reads, writes)
        ins = fn(self.eng[e])
        self.cnt[e] += 1
        ins.then_inc(self.sem[e], 1)
        tok = (self.sem[e], self.cnt[e], e)
        self.done(tok, reads, writes)
        self.ninstr += 1
        return tok

    def dma(self, q, out, in_, reads=(), writes=(), **kw):
        s = self.dnext[q]
        self.dnext[q] = (s + 1) % self.NSLOT
        sem = self.dsem[q][s]
        if self.dcnt[q][s] > 0:
            self._wait(q, (sem, 16 * self.dcnt[q][s], "dma"))
        self.deps(q, reads, writes)
        ins = self.eng[q].dma_start(out=out, in_=in_, **kw)
        self.dcnt[q][s] += 1
        ins.then_inc(sem, 16)
        tok = (sem, 16 * self.dcnt[q][s], "dma")
        self.done(tok, reads, writes)
        self.ninstr += 1
        return tok

    def barrier(self):
        toks = []
        for k in self.eng:
            if self.cnt[k] > 0:
                toks.append((self.sem[k], self.cnt[k], k))
        for q in self.dsem:
            for s in range(self.NSLOT):
                if self.dcnt[q][s] > 0:
                    toks.append((self.dsem[q][s], 16 * self.dcnt[q][s], "dma"))
        for e in self.eng:
            for t in toks:
                self._wait(e, t)
        for k in self.eng:
            if self.cnt[k] > 12000:
                self.sem[k] = self._newsem()
                self.cnt[k] = 0
        for q in self.dsem:
            for s in range(self.NSLOT):
                if self.dcnt[q][s] > 1500:
                    self.dsem[q][s] = self._newsem()
                    self.dcnt[q][s] = 0


class TT:
    def __init__(self, ap):
        self.ap = ap
        self.lastw = None
        self.readers = {}

    def __getitem__(self, k):
        return self.ap[k]


class Scope:
    def __init__(self, kb):
        self.kb = kb
        self.es = ExitStack()

    def __enter__(self):
        self.es.__enter__()
        return self

    def __exit__(self, *a):
        self.kb.S.barrier()
        return self.es.__exit__(*a)

    def sb(self, shape, dt=F32, name="t"):
        self.kb.uid += 1
        return TT(self.es.enter_context(self.kb.nc.sbuf_tensor("%s_%d" % (name, self.kb.uid), list(shape), dt)))

    def ps(self, shape, dt=F32, name="p"):
        self.kb.uid += 1
        assert dt == F32
        shape = list(shape)
        free = int(np.prod(shape[1:]))
        assert free <= 512
        t = self.es.enter_context(self.kb.nc.psum_tensor("%s_%d" % (name, self.kb.uid), [128, 512], dt))
        v = t[0:shape[0], 0:free]
        if len(shape) == 3:
            v = v.rearrange("p (a b) -> p a b", a=shape[1])
        return TT(v)


def bc(ap, n):
    return ap.partition_broadcast(n)


class KB:
    def __init__(self, NL=DEPTH, dbg=(), stages=None):
        self.NL = NL
        self.dbg = set(dbg)
        self.stages = stages
        self.uid = 0
        nc = self.nc = bass.Bass("TRN2", target_bir_lowering=False)
        self.I = {}
        self.scr = {}

        def inp(name, shape, dt=F32):
            self.I[name] = nc.dram_tensor(name, list(shape), dt, kind="ExternalInput").ap()

        inp("xin", [NB, T, D])
        inp("cvec", [3, D])
        L = DEPTH
        inp("ada_w", [L, D, 6 * D]); inp("ada_b", [L, 6 * D]); inp("w_in", [L, D, INW])
        inp("rwkv_mu", [L, 960]); inp("rwkv_w0", [L, 512]); inp("rwkv_w_up", [L, 64, 256])
        inp("rwkv_a0", [L, 512]); inp("rwkv_a_up", [L, 64, 256]); inp("rwkv_g_up", [L, 64, 256])
        inp("rwkv_k_k", [L, 256]); inp("rwkv_k_a", [L, 256]); inp("rwkv_r_k", [L, 256])
        inp("rwkv_ln_g", [L, 256]); inp("rwkv_ln_b", [L, 256])
        inp("gqa_q_norm", [L, 64]); inp("gqa_k_norm", [L, 64]); inp("diff_lambda", [L, 128]); inp("diff_norm", [L, 64])
        inp("w_out", [L, D, D]); inp("post1_g", [L, D]); inp("post1_b", [L, D])
        inp("ffn_w_in", [L, D, 2 * DFF]); inp("ffn_w_out", [L, DFF, D]); inp("post2_g", [L, D]); inp("post2_b", [L, D])
        inp("c_ident", [128, 128]); inp("c_ropeG", [SX, 2, 64]); inp("c_ropeD", [SX, 2, 32]); inp("c_ropeR", [SX, 2, 64])
        inp("c_maskJ", [8, 256]); inp("c_retM", [36, 4, 128, 512], BF16); inp("c_zero", [128, 4096], BF16)
        self.out = nc.dram_tensor("y", [NB, SX, D], F32, kind="ExternalOutput").ap()

        def scr(name, shape, dt=F32):
            kind = "ExternalOutput" if name in self.dbg else "Internal"
            self.scr[name] = nc.dram_tensor(name, list(shape), dt, kind=kind).ap()

        scr("P", [NB, T, INW]); scr("Xs", [NB, T, D]); scr("X1s", [NB, T, D]); scr("modD", [3, 6 * D])
        for m in "gr":
            scr("QT" + m, [NB, 64, 4, T], BF16)
            scr("KT" + m, [NB, 64, 4, T], BF16)
        scr("QTd", [NB, 32, 8, T], BF16)
        scr("KTd", [NB, 32, 8, T], BF16)
        scr("As", [128, T, 8], BF16); scr("Rs", [128, T, 8], BF16); scr("Ws", [2, 128, T, 4])
        scr("LBs", [2, 8, T, 128], BF16); scr("LKs", [2, 8, T, 128], BF16); scr("RVs", [8, T, 256], BF16); scr("Yfull", [2, 8, T, 256])
        scr("Gt", [NB, T, 256]); scr("Bon", [NB, T, 256]); scr("Ycat", [NB, T, D])

    def build(self):
        nc = self.nc
        with ExitStack() as es:
            self.S = S = Sched(nc, es)
            with Scope(self) as g:
                self.g = g
                self.ident = g.sb([128, 128], F32, "ident")
                S.dma("sp", self.ident[:], self.I["c_ident"], writes=[self.ident])
                self.siluT = g.sb([128, 8, 3], F32, "siluT")
                self.cvals = [64.0 * QK_EPS, GN_EPS, LN_EPS, QK_EPS, 0.0]
                self.cst = g.sb([128, 8], F32, "cst")
                for i_, v_ in enumerate(self.cvals):
                    S.op("dve", lambda e: e.memset(self.cst[:, i_:i_ + 1], float(v_)), writes=[self.cst])
                self.stage_init()
                for l in range(self.NL):
                    self.layer(l)
                S.barrier()
            print("ninstr", S.ninstr, "nsem", S.nsem)
        return nc

    def rpow(self, ot, o_ap, it, i_ap, scale, bias, p):
        S = self.S
        assert p == -0.5
        bcol = self.cvals.index(float(bias))
        np_ = o_ap.shape[0]
        S.op("act", lambda e: e.activation(o_ap, i_ap, AF.Sqrt, bias=self.cst[0:np_, bcol:bcol + 1], scale=float(scale)), reads=[it, self.cst], writes=[ot])
        S.op("dve", lambda e: e.reciprocal(o_ap, o_ap), reads=[ot], writes=[ot])

    def on(self, name):
        return self.stages is None or name in self.stages

    def stage_init(self):
        S, I = self.S, self.I
        with Scope(self) as sc:
            cv = sc.sb([3, D])
            S.dma("sp", cv[:], I["cvec"], writes=[cv])
            sv = sc.sb([3, D])
            S.op("act", lambda e: e.activation(sv[:], cv[:], AF.Silu), reads=[cv], writes=[sv])
            pT = sc.ps([128, 8, 3])
            for kc in range(8):
                S.op("pe", lambda e: e.transpose(pT[:, kc, :], sv[:, kc * 128:(kc + 1) * 128], self.ident[0:3, 0:3]), reads=[sv, self.ident], writes=[pT])
            S.op("dve", lambda e: e.tensor_copy(self.siluT[:], pT[:]), reads=[pT], writes=[self.siluT])
            z = sc.sb([128, 4096], BF16)
            S.dma("sp", z[:], I["c_zero"], writes=[z])
            for name in ("LBs", "LKs"):
                flat = self.scr[name].rearrange("d m t k -> (d m t k)").rearrange("(a p f) -> a p f", p=128, f=4096)
                for a in range(flat.shape[0]):
                    S.dma("sp", flat[a], z[:], reads=[z])
            flat = self.scr["RVs"].rearrange("m t k -> (m t k)").rearrange("(a p f) -> a p f", p=128, f=4096)
            for a in range(flat.shape[0]):
                S.dma("sp", flat[a], z[:], reads=[z])

    def layer(self, l):
        self.l = l
        self.last = (l == DEPTH - 1)
        self.Xsrc = self.I["xin"] if l == 0 else self.scr["Xs"]
        if self.on("mod"):
            self.stage_mod(l)
        with Scope(self) as ls:
            self.ls = ls
            S = self.S
            self.modT = ls.sb([128, 48, 3], F32, "modT")
            for v in range(3):
                S.dma("sp", self.modT[:, :, v], self.scr["modD"][v].rearrange("(c p) -> p c", p=128), writes=[self.modT], allow_slow_non_contiguous=True)
            for c0 in (8, 32):
                S.op("dve", lambda e: e.tensor_scalar(self.modT[:, c0:c0 + 8, :], self.modT[:, c0:c0 + 8, :], 1.0, None, ALU.add), reads=[self.modT], writes=[self.modT])
            if self.on("inproj"):
                self.stage_inproj(l)
            if self.on("rwprep"):
                self.stage_rwprep(l)
            if self.on("scan"):
                self.stage_scan(l)
            if self.on("rwpost"):
                self.stage_rwpost(l)
            if self.on("attn"):
                self.stage_attn(l)
            if self.on("mix"):
                self.stage_mix(l)
            if self.on("ffn"):
                self.stage_ffn(l)

    def stage_mod(self, l):
        S, I = self.S, self.I
        with Scope(self) as sc:
            adab = sc.sb([3, 6 * D])
            S.dma("sp", adab[:], bc(I["ada_b"][l], 3), writes=[adab])
            modsb = sc.sb([3, 6 * D])
            wb = [sc.sb([128, 8, 512]) for _ in range(2)]
            pm = [sc.ps([3, 512]) for _ in range(2)]
            for n in range(12):
                w = wb[n % 2]
                S.dma("sp" if n % 2 == 0 else "pool", w[:], I["ada_w"][l, :, n * 512:(n + 1) * 512].rearrange("(c p) n -> p c n", p=128), writes=[w])
                p = pm[n % 2]
                for kc in range(8):
                    S.op("pe", lambda e: e.matmul(p[:], self.siluT[:, kc, :], w[:, kc, :], start=(kc == 0), stop=(kc == 7)), reads=[self.siluT, w], writes=[p])
                S.op("dve", lambda e: e.tensor_tensor(modsb[:, n * 512:(n + 1) * 512], p[:], adab[:, n * 512:(n + 1) * 512], ALU.add), reads=[p, adab], writes=[modsb])
            S.dma("sp", self.scr["modD"], modsb[:], reads=[modsb])

    def xT_modulated(self, sc, src_tile, pT, xmT, col0, width, v, which):
        S = self.S
        shc, scc = (0, 8) if which == 0 else (24, 32)
        for kc in range(8):
            pp = pT[kc % 2]
            S.op("pe", lambda e: e.transpose(pp[:], src_tile[:, kc * 128:(kc + 1) * 128], self.ident[:]), reads=[src_tile, self.ident], writes=[pp])
            eng = "dve" if kc % 2 == 0 else "pool"
            eng = "dve"
            S.op(eng, lambda e: e.tensor_scalar(xmT[:, kc, col0:col0 + 128], pp[:], self.modT[:, scc + kc, v:v + 1], self.modT[:, shc + kc, v:v + 1], ALU.mult, ALU.add), reads=[pp, self.modT], writes=[xmT])

    def rope(self, eng2, out, src, tab, nh, nblk, half, tmp1, tmp2, reads, tabT):
        S = self.S
        w = nblk * 2 * half
        Cb = tab[:, 0, :].unsqueeze(1).to_broadcast([128, nh, w])
        S.op("dve", lambda e: e.tensor_tensor(tmp1, src, Cb, ALU.mult), reads=reads + [tabT], writes=[self._t1])
        v5 = lambda ap: ap.rearrange("p h (b two f) -> p h b two f", b=nblk, two=2)
        sgv = tab[:, 1, :].rearrange("p (b two f) -> p b two f", b=nblk, two=2)
        for hf in range(2):
            o = v5(tmp2)[:, :, :, hf, :]
            i = v5(src)[:, :, :, 1 - hf, :]
            sg = sgv[:, :, hf, :].unsqueeze(1).to_broadcast([128, nh, nblk, half])
            S.op(eng2, lambda e: e.tensor_tensor(o, i, sg, ALU.mult), reads=reads + [tabT], writes=[self._t2])
        S.op("dve", lambda e: e.tensor_tensor(out, tmp1, tmp2, ALU.add), reads=[self._t1, self._t2], writes=[self._ro])

    def stage_inproj(self, l):
        S, I = self.S, self.I
        with Scope(self) as sc:
            win = sc.sb([128, 8, INW], BF16, "win")
            for kc in range(8):
                S.dma("pool", win[:, kc, :], I["w_in"][l, kc * 128:(kc + 1) * 128, :], writes=[win])
            rG = sc.sb([128, 16, 2, 64]); rD = sc.sb([128, 16, 2, 32]); rR = sc.sb([128, 16, 2, 64])
            S.dma("sp", rG[:], I["c_ropeG"].rearrange("(n p) a w -> p n a w", p=128), writes=[rG])
            S.dma("sp", rD[:], I["c_ropeD"].rearrange("(n p) a w -> p n a w", p=128), writes=[rD])
            S.dma("sp", rR[:], I["c_ropeR"].rearrange("(n p) a w -> p n a w", p=128), writes=[rR])
            GW = sc.sb([128, 6, 64])
            for h in range(4):
                S.dma("sp", GW[:, h, :], bc(I["gqa_q_norm"][l], 128), writes=[GW])
            for h in range(4, 6):
                S.dma("sp", GW[:, h, :], bc(I["gqa_k_norm"][l], 128), writes=[GW])
            S.op("dve", lambda e: e.tensor_scalar(GW[:, 4:6, :], GW[:, 4:6, :], 8.0, None, ALU.mult), reads=[GW], writes=[GW])
            xt = [sc.sb([128, D]) for _ in range(2)]
            pT = [sc.ps([128, 128]) for _ in range(2)]
            xmT = [sc.sb([128, 8, 128], BF16) for _ in range(2)]
            pP = [sc.ps([128, 512]) for _ in range(2)]
            Psb = [sc.sb([128, INW]) for _ in range(2)]
            pQ = [sc.ps([64, 4, 128]) for _ in range(2)]
            sq = sc.sb([128, 6, 64]); ss = sc.sb([128, 6]); r1 = sc.sb([128, 6]); qn = sc.sb([128, 6, 64])
            t1 = sc.sb([128, 512]); t2 = sc.sb([128, 512]); ro = sc.sb([128, 512])
            self._t1, self._t2, self._ro = t1, t2, ro
            QTs = [sc.sb([64, 4, 128], BF16) for _ in range(6)]
            it = 0
            for b in range(NB):
                for n in range(NT):
                    v = 2 if n < 2 else b
                    isx = n >= 2
                    rows = slice(n * 128, (n + 1) * 128)
                    x = xt[it % 2]; xm = xmT[it % 2]; P = Psb[it % 2]
                    S.dma("sp", x[:], self.Xsrc[b, rows, :], writes=[x])
                    self.xT_modulated(sc, x, pT, xm, 0, 128, v, 0)
                    for c in range(7):
                        c0 = c * 512
                        w = min(512, INW - c0)
                        pp = pP[c % 2]
                        for kc in range(8):
                            S.op("pe", lambda e: e.matmul(pp[:, :w], xm[:, kc, :], win[:, kc, c0:c0 + w], start=(kc == 0), stop=(kc == 7)), reads=[xm, win], writes=[pp])
                        S.op("act", lambda e: e.activation(P[:, c0:c0 + w], pp[:, :w], AF.Copy), reads=[pp], writes=[P])
                    S.dma("pool", self.scr["P"][b, rows, :], P[:], reads=[P])
                    qk = P[:, O_GQ:O_GQ + 384].rearrange("p (h w) -> p h w", w=64)
                    S.op("pool", lambda e: e.tensor_tensor(sq[:], qk, qk, ALU.mult), reads=[P], writes=[sq])
                    S.op("dve", lambda e: e.tensor_reduce(ss[:], sq[:], AX.X, ALU.add), reads=[sq], writes=[ss])
                    self.rpow(r1, r1[:], ss, ss[:], 1.0, 64.0 * QK_EPS, -0.5)
                    S.op("dve", lambda e: e.tensor_tensor(qn[:], qk, r1[:].unsqueeze(2).to_broadcast([128, 6, 64]), ALU.mult), reads=[P, r1], writes=[qn])
                    S.op("dve", lambda e: e.tensor_tensor(qn[:], qn[:], GW[:], ALU.mult), reads=[qn, GW], writes=[qn])
                    v3 = lambda t, nh, w: t[:, 0:nh * w].rearrange("p (h w) -> p h w", w=w)
                    if isx:
                        self.rope("pool", v3(ro, 6, 64), qn[:], rG[:, n - 2], 6, 2, 16, v3(t1, 6, 64), v3(t2, 6, 64), [qn], rG)
                        src, srcT = v3(ro, 6, 64), ro
                    else:
                        src, srcT = qn[:], qn
                    self.emit_T(src, srcT, 6, pQ, QTs, [("QTg", 0, 4, 1.0), ("KTg", 4, 2, 1.0)], b, rows)
                    dq = P[:, O_DF:O_DF + 512].rearrange("p (h w) -> p h w", w=32)
                    if isx:
                        self.rope("pool", v3(ro, 16, 32), dq, rD[:, n - 2], 16, 2, 8, v3(t1, 16, 32), v3(t2, 16, 32), [P], rD)
                        src, srcT = v3(ro, 16, 32), ro
                    else:
                        src, srcT = dq, P
                    self.emit_T(src, srcT, 16, pQ, QTs, [("QTd", 0, 8, 32 ** -0.5), ("KTd", 8, 8, 1.0)], b, rows, width=32)
                    rq = P[:, O_RT:O_RT + 512].rearrange("p (h w) -> p h w", w=64)
                    if isx:
                        self.rope("pool", v3(ro, 8, 64), rq, rR[:, n - 2], 8, 1, 32, v3(t1, 8, 64), v3(t2, 8, 64), [P], rR)
                        src, srcT = v3(ro, 8, 64), ro
                    else:
                        src, srcT = rq, P
                    self.emit_T(src, srcT, 8, pQ, QTs, [("QTr", 0, 4, 1.0), ("KTr", 4, 4, 0.125)], b, rows)
                    it += 1

    def emit_T(self, src, srcT, nh, pQ, QTs, outs, b, rows, width=64):
        S = self.S
        for (name, h0, cnt_all, scale) in outs:
            for g0 in range(0, cnt_all, 4):
                cnt = min(4, cnt_all - g0)
                self._qi = getattr(self, "_qi", 0) + 1
                pq = pQ[self._qi % 2]
                st = QTs[self._qi % 6]
                for j in range(cnt):
                    S.op("pe", lambda e: e.transpose(pq[0:width, j, :], src[:, h0 + g0 + j, :], self.ident[:]), reads=[srcT, self.ident], writes=[pq])
                S.op("act", lambda e: e.activation(st[0:width, 0:cnt, :], pq[0:width, 0:cnt, :], AF.Copy, scale=float(scale)), reads=[pq], writes=[st])
                S.dma("pool", self.scr[name][b, :, g0:g0 + cnt, rows], st[0:width, 0:cnt, :], reads=[st])

    def stage_rwprep(self, l):
        S, I = self.S, self.I
        with Scope(self) as sc:
            def btile(key, w, nrep=1):
                t = sc.sb([128, nrep * w])
                for r in range(nrep):
                    S.dma("sp", t[:, r * w:(r + 1) * w], bc(I[key][l], 128), writes=[t])
                return t
            MU = btile("rwkv_mu", 960); W0 = btile("rwkv_w0", 512); A0 = btile("rwkv_a0", 512)
            KK = btile("rwkv_k_k", 256); KA = btile("rwkv_k_a", 256); RK = btile("rwkv_r_k", 256)
            WUP = sc.sb([32, 2, 256]); AUP = sc.sb([32, 2, 256]); GUP = sc.sb([64, 256])
            S.dma("sp", WUP[:], I["rwkv_w_up"][l].rearrange("(d r) c -> r d c", d=2), writes=[WUP])
            S.dma("sp", AUP[:], I["rwkv_a_up"][l].rearrange("(d r) c -> r d c", d=2), writes=[AUP])
            S.dma("sp", GUP[:], I["rwkv_g_up"][l], writes=[GUP])
            cur = sc.sb([128, 960]); prv = sc.sb([128, 960]); nxt = sc.sb([128, 960])
            tt = sc.sb([128, 960])
            pst = sc.sb([128, 1024])
            S.op("dve", lambda e: e.memset(pst[:, 0:64], 0.0), writes=[pst])
            lor = sc.sb([128, 3, 64]); lorT = sc.sb([32, 4, 128]); lorTg = sc.sb([64, 128])
            pL = sc.ps([32, 4, 128]); pW = sc.ps([128, 512]); pAl = sc.ps([128, 512]); pG = sc.ps([128, 256])
            wt = sc.sb([128, 512]); e1 = sc.sb([128, 512])
            decp = sc.sb([128, 64 + 512])
            S.op("dve", lambda e: e.memset(decp[:, 0:64], 0.0), writes=[decp])
            asig = sc.sb([128, 512]); gt = sc.sb([128, 256])
            kkp = sc.sb([128, 64 + 256])
            S.op("dve", lambda e: e.memset(kkp[:, 0:64], 0.0), writes=[kkp])
            kq = sc.sb([128, 256]); ks = sc.sb([128, 4]); kr = sc.sb([128, 4])
            bd = sc.sb([128, 512], BF16); kd = sc.sb([128, 512]); tk = sc.sb([128, 512]); kd16 = sc.sb([128, 512], BF16); v16 = sc.sb([128, 256], BF16)
            rk = sc.sb([128, 256]); pr = sc.sb([128, 512]); s8 = sc.sb([128, 8]); s4 = sc.sb([128, 4]); bon = sc.sb([128, 256])
            pA = sc.ps([128, 4, 128]); pR = sc.ps([128, 4, 128]); pWt = [sc.ps([128, 4, 128]) for _ in range(2)]
            Ast = sc.sb([128, 128, 8], BF16); Rst = sc.sb([128, 128, 8], BF16); Wst = [sc.sb([128, 128, 4]) for _ in range(2)]
            S.op("pool", lambda e: e.memset(Ast[:], 0.0), writes=[Ast])
            S.op("pool", lambda e: e.memset(Rst[:], 0.0), writes=[Rst])
            P = self.scr["P"]
            for n in range(NT):
                rows = slice(n * 128, (n + 1) * 128)
                r0 = n * 128
                for b in range(NB):
                    S.dma("sp", cur[:], P[b, rows, 0:960], writes=[cur])
                    if n in (0, 2):
                        S.op("pool", lambda e: e.memset(prv[0:32, :], 0.0), writes=[prv])
                        S.dma("sp", prv[1:128, :], P[b, r0:r0 + 127, 0:960], writes=[prv])
                    else:
                        S.dma("sp", prv[:], P[b, r0 - 1:r0 + 127, 0:960], writes=[prv])
                    if n in (1, NT - 1):
                        S.op("pool", lambda e: e.memset(nxt[96:128, :], 0.0), writes=[nxt])
                        S.dma("sp", nxt[0:127, :], P[b, r0 + 1:r0 + 128, 0:960], writes=[nxt])
                    else:
                        S.dma("sp", nxt[:], P[b, r0 + 1:r0 + 129, 0:960], writes=[nxt])
                    S.op("pool", lambda e: e.tensor_tensor(tt[:], prv[:], nxt[:], ALU.add), reads=[prv, nxt], writes=[tt])
                    S.op("dve", lambda e: e.scalar_tensor_tensor(tt[:], tt[:], 0.5, cur[:], ALU.mult, ALU.subtract), reads=[tt, cur], writes=[tt])
                    S.op("pool", lambda e: e.tensor_tensor(tt[:], tt[:], MU[:], ALU.mult), reads=[tt, MU], writes=[tt])
                    S.op("dve", lambda e: e.tensor_tensor(pst[:, 64:1024], tt[:], cur[:], ALU.add), reads=[tt, cur], writes=[pst])
                    p_r = pst[:, 64:320]; p_k = pst[:, 320:576]; p_v = pst[:, 576:832]
                    S.op("act", lambda e: e.activation(lor[:, 0, :], pst[:, 832:896], AF.Tanh), reads=[pst], writes=[lor])
                    S.op("act", lambda e: e.activation(lor[:, 2, :], pst[:, 960:1024], AF.Sigmoid), reads=[pst], writes=[lor])
                    S.op("pool", lambda e: e.tensor_copy(lor[:, 1, :], pst[:, 896:960]), reads=[pst], writes=[lor])
                    if 'A' in KSKIP:
                        continue
                    for j in range(4):
                        S.op("pe", lambda e: e.transpose(pL[:, j, :], lor[:, j // 2, 32 * (j % 2):32 * (j % 2) + 32], self.ident[:]), reads=[lor, self.ident], writes=[pL])
                    S.op("dve", lambda e: e.tensor_copy(lorT[:], pL[:]), reads=[pL], writes=[lorT])
                    S.op("pe", lambda e: e.transpose(pWt[0][0:64, 0, :], lor[:, 2, :], self.ident[:]), reads=[lor, self.ident], writes=[pWt[0]])
                    S.op("dve", lambda e: e.tensor_copy(lorTg[:], pWt[0][0:64, 0, :]), reads=[pWt[0]], writes=[lorTg])
                    for d in range(2):
                        S.op("pe", lambda e: e.matmul(pW[:, d * 256:(d + 1) * 256], lorT[:, d, :], WUP[:, d, :], start=True, stop=True), reads=[lorT, WUP], writes=[pW])
                        S.op("pe", lambda e: e.matmul(pAl[:, d * 256:(d + 1) * 256], lorT[:, 2 + d, :], AUP[:, d, :], start=True, stop=True), reads=[lorT, AUP], writes=[pAl])
                    S.op("pe", lambda e: e.matmul(pG[:], lorTg[:], GUP[:], start=True, stop=True), reads=[lorTg, GUP], writes=[pG])
                    if 'B' in KSKIP:
                        continue
                    S.op("dve", lambda e: e.tensor_tensor(wt[:], pW[:], W0[:], ALU.add), reads=[pW, W0], writes=[wt])
                    S.op("act", lambda e: e.activation(e1[:], wt[:], AF.Exp, scale=-1.0), reads=[wt], writes=[e1])
                    S.op("dve", lambda e: e.tensor_scalar(e1[:], e1[:], 1.0, None, ALU.add), reads=[e1], writes=[e1])
                    S.op("dve", lambda e: e.reciprocal(e1[:], e1[:]), reads=[e1], writes=[e1])
                    S.op("act", lambda e: e.activation(decp[:, 64:576], e1[:], AF.Exp, scale=-math.exp(-0.5)), reads=[e1], writes=[decp])
                    S.op("dve", lambda e: e.tensor_tensor(wt[:], pAl[:], A0[:], ALU.add), reads=[pAl, A0], writes=[wt])
                    S.op("act", lambda e: e.activation(e1[:], wt[:], AF.Exp, scale=-1.0), reads=[wt], writes=[e1])
                    S.op("dve", lambda e: e.tensor_scalar(e1[:], e1[:], 1.0, None, ALU.add), reads=[e1], writes=[e1])
                    S.op("dve", lambda e: e.reciprocal(asig[:], e1[:]), reads=[e1], writes=[asig])
                    S.op("act", lambda e: e.activation(gt[:], pG[:], AF.Copy), reads=[pG], writes=[gt])
                    S.dma("pool", self.scr["Gt"][b, rows, :], gt[:], reads=[gt])
                    kk = kkp[:, 64:320]
                    S.op("pool", lambda e: e.tensor_tensor(kk, p_k, KK[:], ALU.mult), reads=[pst, KK], writes=[kkp])
                    S.op("pool", lambda e: e.tensor_tensor(kq[:], kk, kk, ALU.mult), reads=[kkp], writes=[kq])
                    S.op("dve", lambda e: e.tensor_reduce(ks[:], kq[:].rearrange("p (h w) -> p h w", w=64), AX.X, ALU.add), reads=[kq], writes=[ks])
                    S.op("dve", lambda e: e.tensor_scalar(kr[:], ks[:], 1e-24, None, ALU.max), reads=[ks], writes=[kr])
                    self.rpow(kr, kr[:], kr, kr[:], 1.0, 0.0, -0.5)
                    kk3 = kk.rearrange("p (h w) -> p h w", w=64)
                    S.op("dve", lambda e: e.tensor_tensor(kk3, kk3, kr[:].unsqueeze(2).to_broadcast([128, 4, 64]), ALU.mult), reads=[kkp, kr], writes=[kkp])
                    d3 = lambda t: t[:].rearrange("p (d c) -> p d c", d=2)
                    b2 = lambda ap: ap.unsqueeze(1).to_broadcast([128, 2, 256])
                    S.op("dve", lambda e: e.tensor_tensor(d3(bd), d3(asig), b2(kk), ALU.mult), reads=[asig, kkp], writes=[bd])
                    S.op("dve", lambda e: e.scalar_tensor_tensor(d3(tk), d3(asig), -1.0, b2(KA[:]), ALU.add, ALU.mult), reads=[asig, KA], writes=[tk])
                    S.op("dve", lambda e: e.scalar_tensor_tensor(d3(kd), d3(tk), 1.0, b2(p_k), ALU.add, ALU.mult), reads=[tk, pst], writes=[kd])
                    S.op("pool", lambda e: e.tensor_copy(kd16[:], kd[:]), reads=[kd], writes=[kd16])
                    S.op("pool", lambda e: e.tensor_copy(v16[:], p_v), reads=[pst], writes=[v16])
                    S.op("pool", lambda e: e.tensor_tensor(rk[:], p_r, RK[:], ALU.mult), reads=[pst, RK], writes=[rk])
                    S.op("dve", lambda e: e.tensor_tensor(d3(pr), d3(kd), b2(rk[:]), ALU.mult), reads=[kd, rk], writes=[pr])
                    S.op("dve", lambda e: e.tensor_reduce(s8[:], pr[:].rearrange("p (g w) -> p g w", w=64), AX.X, ALU.add), reads=[pr], writes=[s8])
                    S.op("dve", lambda e: e.tensor_tensor(s4[:], s8[:, 0:4], s8[:, 4:8], ALU.add), reads=[s8], writes=[s4])
                    S.op("dve", lambda e: e.tensor_tensor(bon[:].rearrange("p (h w) -> p h w", w=64), p_v.rearrange("p (h w) -> p h w", w=64), s4[:].unsqueeze(2).to_broadcast([128, 4, 64]), ALU.mult), reads=[pst, s4], writes=[bon])
                    S.dma("pool", self.scr["Bon"][b, rows, :], bon[:], reads=[bon])
                    if 'C' in KSKIP:
                        continue
                    lo, hi = b * 64, (b + 1) * 64
                    for h in range(4):
                        if b == 0:
                            ink = kkp[:, 64 + 64 * h:128 + 64 * h]; inr = pst[:, 64 + 64 * h:128 + 64 * h]
                            S.op("pe", lambda e: e.transpose(pA[0:64, h, :], ink, self.ident[:]), reads=[kkp, self.ident], writes=[pA])
                            S.op("pe", lambda e: e.transpose(pR[0:64, h, :], inr, self.ident[:]), reads=[pst, self.ident], writes=[pR])
                        else:
                            ink = kkp[:, 64 * h:128 + 64 * h]; inr = pst[:, 64 * h:128 + 64 * h]
                            S.op("pe", lambda e: e.transpose(pA[:, h, :], ink, self.ident[:]), reads=[kkp, self.ident], writes=[pA])
                            S.op("pe", lambda e: e.transpose(pR[:, h, :], inr, self.ident[:]), reads=[pst, self.ident], writes=[pR])
                    S.op("act", lambda e: e.activation(Ast[lo:hi, :, 4 * b:4 * b + 4].rearrange("p t h -> p h t"), pA[lo:hi, :, :], AF.Copy, scale=-1.0), reads=[pA], writes=[Ast])
                    S.op("act", lambda e: e.activation(Rst[lo:hi, :, 4 * b:4 * b + 4].rearrange("p t h -> p h t"), pR[lo:hi, :, :], AF.Copy), reads=[pR], writes=[Rst])
                    for d in range(2):
                        for h in range(4):
                            c0 = 64 + 256 * d + 64 * h
                            if b == 0:
                                S.op("pe", lambda e: e.transpose(pWt[d][0:64, h, :], decp[:, c0:c0 + 64], self.ident[:]), reads=[decp, self.ident], writes=[pWt[d]])
                            else:
                                S.op("pe", lambda e: e.transpose(pWt[d][:, h, :], decp[:, c0 - 64:c0 + 64], self.ident[:]), reads=[decp, self.ident], writes=[pWt[d]])
                        S.op("dve", lambda e: e.tensor_copy(Wst[d][lo:hi, :, :].rearrange("p t h -> p h t"), pWt[d][lo:hi, :, :]), reads=[pWt[d]], writes=[Wst[d]])
                    if 'D' in KSKIP:
                        continue
                    for d in range(2):
                        if 'E' in KSKIP:
                            continue
                        S.dma("sp", self.scr["LBs"][d, 4 * b:4 * b + 4, rows, lo:hi].rearrange("h t k -> t h k"), bd[:, 256 * d:256 * d + 256].rearrange("p (h k) -> p h k", k=64), reads=[bd])
                        S.dma("sp", self.scr["LKs"][d, 4 * b:4 * b + 4, rows, lo:hi].rearrange("h t k -> t h k"), kd16[:, 256 * d:256 * d + 256].rearrange("p (h k) -> p h k", k=64), reads=[kd16])
                    rv = self.scr["RVs"]
                    dst = bass.AP(rv.tensor, (4 * b) * T * 256 + r0 * 256, [[256, 128], [T * 256 + 64, 4], [1, 64]])
                    if 'F' not in KSKIP:
                        S.dma("sp", dst, v16[:].rearrange("p (h k) -> p h k", k=64), reads=[v16])
                S.dma("pool", self.scr["As"][:, rows, :], Ast[:], reads=[Ast])
                S.dma("pool", self.scr["Rs"][:, rows, :], Rst[:], reads=[Rst])
                for d in range(2):
                    S.dma("pool", self.scr["Ws"][d, :, rows, :], Wst[d][:], reads=[Wst[d]])

    def stage_scan(self, l):
        S, I = self.S, self.I
        with Scope(self) as sc:
            MJ = sc.sb([8, 256])
            S.dma("sp", MJ[:], I["c_maskJ"], writes=[MJ])
            St = [sc.sb([128, 256]) for _ in range(2)]
            S16 = [sc.sb([128, 256], BF16) for _ in range(2)]
            Tmp = [sc.sb([128, 256]) for _ in range(2)]
            SAm = [sc.sb([8, 256], BF16) for _ in range(2)]
            for d in range(2):
                S.op("dve", lambda e: e.memset(St[d][:], 0.0), writes=[St[d]])
                S.op("dve", lambda e: e.memset(S16[d][:], 0.0), writes=[S16[d]])
            pSA1 = [sc.ps([8, 512]) for _ in range(2)]; pU1 = [sc.ps([128, 512]) for _ in range(2)]; pY1 = [sc.ps([8, 512]) for _ in range(2)]
            pSA = [[TT(pSA1[d][:, 0:256]) for d in range(2)] for i in range(2)]
            pU = [[TT(pU1[d][:, 0:256]) for d in range(2)] for i in range(2)]
            pY = [[TT(pY1[d][:, 0:256]) for d in range(2)] for i in range(2)]
            for i in (1,):
                for d in range(2):
                    pSA[i][d] = pSA[0][d]; pU[i][d] = pU[0][d]; pY[i][d] = pY[0][d]
            NBUF = 2
            bufs = []
            for i in range(NBUF):
                bb = []
                for d in range(2):
                    bb.append(dict(A=sc.sb([128, CS, 8], BF16), R=sc.sb([128, CS, 8], BF16), W=sc.sb([128, CS, 4]),
                                   LB=sc.sb([8, CS, 128], BF16), LK=sc.sb([8, CS, 128], BF16), RV=sc.sb([8, CS, 256], BF16), Y=sc.sb([8, CS, 256])))
                bufs.append(bb)
            NCH = T // CS
            if 'Q' in KSKIP:
                NCH = int(os.environ.get('KNCH', '4'))

            def rowbase(d, c):
                if d == 0:
                    return c * CS
                t0 = c * CS
                if t0 < CT:
                    return CT - CS - t0
                return (T + CT - CS) - t0

            def load(c):
                bb = bufs[c % NBUF]
                for d in range(2):
                    ra = rowbase(d, c)
                    rs = slice(ra, ra + CS)
                    q = "sp"
                    B = bb[d]
                    S.dma(q, B["A"][:], self.scr["As"][:, rs, :], writes=[B["A"]])
                    S.dma(q, B["R"][:], self.scr["Rs"][:, rs, :], writes=[B["R"]])
                    S.dma(q, B["W"][:], self.scr["Ws"][d, :, rs, :], writes=[B["W"]])
                    S.dma(q, B["LB"][:], self.scr["LBs"][d, :, rs, :], writes=[B["LB"]])
                    S.dma(q, B["LK"][:], self.scr["LKs"][d, :, rs, :], writes=[B["LK"]])
                    S.dma(q, B["RV"][:], self.scr["RVs"][:, rs, :], writes=[B["RV"]])

            load(0)
            step = 0
            pend = []
            pend_store = None
            W3 = lambda B, i: B["W"][:, i, :].unsqueeze(2).to_broadcast([128, 4, 64])
            j4 = lambda t: t[:].rearrange("p (j v) -> p j v", j=4)

            def flush_y():
                for (d, B, i, pi) in pend:
                    py = pY[pi][d]
                    S.op("pe", lambda e: e.matmul(py[:], B["R"][:, i, :], S16[d][:], start=True, stop=True), reads=[B["R"], S16[d]], writes=[py])
                    S.op("act", lambda e: e.activation(B["Y"][:, i, :], py[:], AF.Copy), reads=[py], writes=[B["Y"]])
                pend.clear()

            for c in range(NCH):
                bb = bufs[c % NBUF]
                for s in range(CS):
                    pi = step % 2
                    idx = [s, CS - 1 - s]
                    for d in range(2):
                        psa = pSA[pi][d]
                        S.op("pe", lambda e: e.matmul(psa[:], bb[d]["A"][:, idx[d], :], S16[d][:], start=True, stop=True), reads=[bb[d]["A"], S16[d]], writes=[psa])
                    flush_y()
                    if s == 0:
                        if pend_store is not None:
                            pc, pbb = pend_store
                            for d in range(2):
                                ra = rowbase(d, pc)
                                S.dma("pool", self.scr["Yfull"][d, :, ra:ra + CS, :], pbb[d]["Y"][:], reads=[pbb[d]["Y"]])
                            pend_store = None
                        if c + 1 < NCH:
                            load(c + 1)
                    for d in range(2):
                        psa = pSA[pi][d]
                        S.op("dve", lambda e: e.tensor_tensor(SAm[d][:], psa[:], MJ[:], ALU.mult), reads=[psa, MJ], writes=[SAm[d]])
                    for d in range(2):
                        S.op("pool", lambda e: e.tensor_tensor(j4(Tmp[d]), j4(St[d]), W3(bb[d], idx[d]), ALU.mult), reads=[St[d], bb[d]["W"]], writes=[Tmp[d]])
                    for d in range(2):
                        B = bb[d]; i = idx[d]; pu = pU[pi][d]
                        S.op("pe", lambda e: e.matmul(pu[:], B["LK"][:, i, :], B["RV"][:, i, :], start=True, stop=False), reads=[B["LK"], B["RV"]], writes=[pu])
                        S.op("pe", lambda e: e.matmul(pu[:], B["LB"][:, i, :], SAm[d][:], start=False, stop=True), reads=[B["LB"], SAm[d]], writes=[pu])
                    for d in range(2):
                        pu = pU[pi][d]
                        S.op("dve", lambda e: e.tensor_tensor(S16[d][:], Tmp[d][:], pu[:], ALU.add), reads=[Tmp[d], pu], writes=[S16[d]])
                    for d in range(2):
                        pu = pU[pi][d]
                        S.op("dve", lambda e: e.tensor_tensor(St[d][:], Tmp[d][:], pu[:], ALU.add), reads=[Tmp[d], pu], writes=[St[d]])
                    for d in range(2):
                        pend.append((d, bb[d], idx[d], pi))
                    if 'Y' in KSKIP:
                        flush_y()
                    step += 1
                pend_store = (c, bb)
            flush_y()
            pc, pbb = pend_store
            for d in range(2):
                ra = rowbase(d, pc)
                S.dma("pool", self.scr["Yfull"][d, :, ra:ra + CS, :], pbb[d]["Y"][:], reads=[pbb[d]["Y"]])

    def stage_rwpost(self, l):
        S, I = self.S, self.I
        with Scope(self) as sc:
            LNG = sc.sb([128, 256]); LNB = sc.sb([128, 256])
            S.dma("sp", LNG[:], bc(I["rwkv_ln_g"][l], 128), writes=[LNG])
            S.dma("sp", LNB[:], bc(I["rwkv_ln_b"][l], 128), writes=[LNB])
            yf = [sc.sb([128, 256]) for _ in range(2)]; yb = [sc.sb([128, 256]) for _ in range(2)]
            bo = [sc.sb([128, 256]) for _ in range(2)]; gg = [sc.sb([128, 256]) for _ in range(2)]
            y = sc.sb([128, 256]); sq = sc.sb([128, 256]); s1 = sc.sb([128, 4]); s2 = sc.sb([128, 4]); o = [sc.sb([128, 256]) for _ in range(2)]
            h3 = lambda t: t[:].rearrange("p (h w) -> p h w", w=64)
            b3 = lambda t: t[:].unsqueeze(2).to_broadcast([128, 4, 64])
            yfull = self.scr["Yfull"]
            it = 0
            for b in range(NB):
                for n in range(NT):
                    if self.last and n < 2:
                        continue
                    rows = slice(n * 128, (n + 1) * 128)
                    i = it % 2
                    for d, dstt in ((0, yf[i]), (1, yb[i])):
                        src = bass.AP(yfull.tensor, d * 8 * T * 256 + (4 * b) * T * 256 + n * 128 * 256, [[256, 128], [T * 256 + 64, 4], [1, 64]])
                        S.dma("sp", h3(dstt), src, writes=[dstt])
                    S.dma("sp", bo[i][:], self.scr["Bon"][b, rows, :], writes=[bo[i]])
                    S.dma("sp", gg[i][:], self.scr["Gt"][b, rows, :], writes=[gg[i]])
                    S.op("pool", lambda e: e.tensor_tensor(y[:], yf[i][:], yb[i][:], ALU.add), reads=[yf[i], yb[i]], writes=[y])
                    self.head_norm(y, sq, s1, s2, GN_EPS)
                    S.op("dve", lambda e: e.tensor_tensor(y[:], y[:], LNG[:], ALU.mult), reads=[y, LNG], writes=[y])
                    S.op("pool", lambda e: e.tensor_tensor(y[:], y[:], LNB[:], ALU.add), reads=[y, LNB], writes=[y])
                    S.op("pool", lambda e: e.tensor_tensor(y[:], y[:], bo[i][:], ALU.add), reads=[y, bo[i]], writes=[y])
                    S.op("dve", lambda e: e.tensor_tensor(o[i][:], y[:], gg[i][:], ALU.mult), reads=[y, gg[i]], writes=[o[i]])
                    S.dma("pool", self.scr["Ycat"][b, rows, 0:256], o[i][:], reads=[o[i]])
                    it += 1

    def head_norm(self, y, sq, s1, s2, eps, nh=4):
        S = self.S
        h3 = lambda t: t[:, 0:nh * 64].rearrange("p (h w) -> p h w", w=64)
        b3 = lambda t: t[:, 0:nh].unsqueeze(2).to_broadcast([128, nh, 64])
        S.op("dve", lambda e: e.tensor_reduce(s1[:, 0:nh], h3(y), AX.X, ALU.add), reads=[y], writes=[s1])
        S.op("dve", lambda e: e.tensor_scalar(s1[:, 0:nh], s1[:, 0:nh], -1.0 / 64, None, ALU.mult), reads=[s1], writes=[s1])
        S.op("dve", lambda e: e.tensor_tensor(h3(y), h3(y), b3(s1), ALU.add), reads=[y, s1], writes=[y])
        S.op("pool", lambda e: e.tensor_tensor(h3(sq), h3(y), h3(y), ALU.mult), reads=[y], writes=[sq])
        S.op("dve", lambda e: e.tensor_reduce(s2[:, 0:nh], h3(sq), AX.X, ALU.add), reads=[sq], writes=[s2])
        self.rpow(s2, s2[:, 0:nh], s2, s2[:, 0:nh], 1.0 / 64, eps, -0.5)
        S.op("dve", lambda e: e.tensor_tensor(h3(y), h3(y), b3(s2), ALU.mult), reads=[y, s2], writes=[y])

    def stage_attn(self, l):
        S, I = self.S, self.I
        lam_init = 0.8 - 0.6 * math.exp(-0.3 * l)
        with Scope(self) as sc:
            DL = sc.sb([128, 128]); dp = sc.sb([128, 128]); ds = sc.sb([128, 2]); lam = sc.sb([128, 1]); nlam = sc.sb([128, 1])
            S.dma("sp", DL[:], bc(I["diff_lambda"][l], 128), writes=[DL])
            S.op("dve", lambda e: e.tensor_tensor(dp[:, 0:32], DL[:, 0:32], DL[:, 32:64], ALU.mult), reads=[DL], writes=[dp])
            S.op("dve", lambda e: e.tensor_tensor(dp[:, 32:64], DL[:, 64:96], DL[:, 96:128], ALU.mult), reads=[DL], writes=[dp])
            S.op("dve", lambda e: e.tensor_reduce(ds[:], dp[:, 0:64].rearrange("p (a w) -> p a w", w=32), AX.X, ALU.add), reads=[dp], writes=[ds])
            S.op("act", lambda e: e.activation(ds[:], ds[:], AF.Exp), reads=[ds], writes=[ds])
            S.op("dve", lambda e: e.tensor_tensor(lam[:], ds[:, 0:1], ds[:, 1:2], ALU.subtract), reads=[ds], writes=[lam])
            S.op("dve", lambda e: e.tensor_scalar(nlam[:], lam[:], lam_init, -1.0, ALU.add, ALU.mult), reads=[lam], writes=[nlam])
            DN = sc.sb([128, 64])
            S.dma("sp", DN[:], bc(I["diff_norm"][l], 128), writes=[DN])
            S.op("dve", lambda e: e.tensor_scalar(DN[:], DN[:], 1.0 - lam_init, None, ALU.mult), reads=[DN], writes=[DN])
            KT = sc.sb([64, 4, T], BF16); QT = sc.sb([64, 4, T], BF16)
            KTd = sc.sb([32, 8, T], BF16); QTd = sc.sb([32, 8, T], BF16)
            V = sc.sb([128, NT, 4, 65], BF16)
            S.op("pool", lambda e: e.memset(V[:, :, :, 64:65], 1.0), writes=[V])
            pS = [sc.ps([128, 512]) for _ in range(2)]
            pO = [sc.ps([128, 4, 65]) for _ in range(4)]
            Pball = [sc.sb([128, NT, 512], BF16) for _ in range(2)]
            Mk = [sc.sb([128, 512], BF16) for _ in range(3)]
            rec = sc.sb([128, 4, 1]); o1 = sc.sb([128, 4, 64]); o2 = sc.sb([128, 4, 64])
            sq = sc.sb([128, 256]); s1 = sc.sb([128, 4]); s2 = sc.sb([128, 4])
            gate = sc.sb([128, 4, 64]); gs = sc.sb([128, 4, 64])
            osb = [sc.sb([128, 4, 64]) for _ in range(2)]
            Pd = self.scr["P"]
            cnt = {"u": 0, "o": 0, "p": 0}
            for m in "gdr":
                nk = 2 if m == "g" else 4
                nv = 2 if m == "g" else 4
                vcol = {"g": O_GQ + 384, "d": O_DF + 512, "r": O_RT + 512}[m]
                ocol = {"g": 256, "d": 512, "r": 768}[m]
                for b in range(NB):
                    if m == "d":
                        S.dma("sp", KTd[:], self.scr["KTd"][b], writes=[KTd])
                        S.dma("sp", QTd[:], self.scr["QTd"][b], writes=[QTd])
                    else:
                        S.dma("sp", KT[:, 0:nk, :], self.scr["KT" + m][b, :, 0:nk, :], writes=[KT])
                        S.dma("sp", QT[:], self.scr["QT" + m][b], writes=[QT])
                    for hv in range(nv):
                        S.dma("pool", V[:, :, hv, 0:64], Pd[b, :, vcol + hv * 64:vcol + hv * 64 + 64].rearrange("(n p) w -> p n w", p=128), writes=[V])
                    chunks = ([] if self.last else [(0, 256, 0, 2)]) + [(256 + 512 * q, 512, 0, NT) for q in range(4)]
                    for h in range(4):
                        for (q0, w, k0, k1) in chunks:
                            nj = w // 128
                            isx = q0 >= 256
                            units = [(0, 0)] if m != "d" else [(0, 0), (1, 0)]
                            pos = []
                            for (mi, rbase) in units:
                                po = pO[cnt["o"] % 4]; cnt["o"] += 1
                                pos.append(po)
                                rws = slice(rbase, rbase + (64 if m != "d" else 32))
                                hk = h // 2 if m == "g" else h
                                hvv = h // 2 if m == "g" else h
                                PB_ = Pball[cnt["o"] % 2]
                                for kt in range(k0, k1):
                                    ps_ = pS[cnt["u"] % 2]; cnt["u"] += 1
                                    pb = PB_[:, kt, :]
                                    if m == "d":
                                        S.op("pe", lambda e: e.matmul(ps_[:, :w], KTd[:, 2 * h + mi, kt * 128:(kt + 1) * 128], QTd[:, 2 * h + mi, q0:q0 + w], start=True, stop=True), reads=[KTd, QTd], writes=[ps_])
                                    else:
                                        S.op("pe", lambda e: e.matmul(ps_[:, :w], KT[rws, hk, kt * 128:(kt + 1) * 128], QT[rws, h, q0:q0 + w], start=True, stop=True), reads=[KT, QT], writes=[ps_])
                                    if m == "r":
                                        mk = Mk[cnt["p"] % 3]; cnt["p"] += 1
                                        if not isx:
                                            ti = {0: 15, 1: 14}[kt]
                                        elif kt < 2:
                                            ti = 28 + kt * 4 + (q0 - 256) // 512
                                        else:
                                            off = 4 * ((q0 - 256) // 512) - (kt - 2)
                                            ti = off + 15
                                        S.dma("sp", mk[:], I["c_retM"][ti, h], writes=[mk])
                                        S.op("dve", lambda e: e.tensor_tensor(pb[:, :w], ps_[:, :w], mk[:, :w], ALU.mult), reads=[ps_, mk], writes=[PB_])
                                    else:
                                        S.op("act", lambda e: e.activation(pb[:, :w], ps_[:, :w], AF.Exp), reads=[ps_], writes=[PB_])
                                for j in range(nj):
                                    for kt in range(k0, k1):
                                        S.op("pe", lambda e: e.matmul(po[:, j, :], PB_[:, kt, j * 128:(j + 1) * 128], V[:, kt, hvv, :], start=(kt == k0), stop=(kt == k1 - 1)), reads=[PB_, V], writes=[po])
                            ob = osb[cnt["o"] % 2]
                            dst = self.scr["Ycat"][b, q0:q0 + w, ocol + h * 64:ocol + h * 64 + 64].rearrange("(j t) v -> t j v", t=128)
                            if m == "g":
                                po = pos[0]
                                S.op("dve", lambda e: e.reciprocal(rec[:, 0:nj, :], po[:, 0:nj, 64:65]), reads=[po], writes=[rec])
                                S.op("dve", lambda e: e.tensor_tensor(ob[:, 0:nj, :], po[:, 0:nj, 0:64], rec[:, 0:nj, :].to_broadcast([128, nj, 64]), ALU.mult), reads=[po, rec], writes=[ob])
                            elif m == "d":
                                for (po, ot) in ((pos[0], o1), (pos[1], o2)):
                                    S.op("dve", lambda e: e.reciprocal(rec[:, 0:nj, :], po[:, 0:nj, 64:65]), reads=[po], writes=[rec])
                                    S.op("dve", lambda e: e.tensor_tensor(ot[:, 0:nj, :], po[:, 0:nj, 0:64], rec[:, 0:nj, :].to_broadcast([128, nj, 64]), ALU.mult), reads=[po, rec], writes=[ot])
                                S.op("dve", lambda e: e.scalar_tensor_tensor(o1[:, 0:nj, :], o2[:, 0:nj, :], nlam[:, 0:1], o1[:, 0:nj, :], ALU.mult, ALU.add), reads=[o1, o2, nlam], writes=[o1])
                                S.op("pool", lambda e: e.tensor_tensor(o2[:, 0:nj, :], o1[:, 0:nj, :], o1[:, 0:nj, :], ALU.mult), reads=[o1], writes=[o2])
                                S.op("dve", lambda e: e.tensor_reduce(s1[:, 0:nj], o2[:, 0:nj, :], AX.X, ALU.add), reads=[o2], writes=[s1])
                                self.rpow(s1, s1[:, 0:nj], s1, s1[:, 0:nj], 1.0 / 64, QK_EPS, -0.5)
                                S.op("dve", lambda e: e.tensor_tensor(o1[:, 0:nj, :], o1[:, 0:nj, :], s1[:, 0:nj].unsqueeze(2).to_broadcast([128, nj, 64]), ALU.mult), reads=[o1, s1], writes=[o1])
                                S.op("dve", lambda e: e.tensor_tensor(ob[:, 0:nj, :], o1[:, 0:nj, :], DN[:].unsqueeze(1).to_broadcast([128, nj, 64]), ALU.mult), reads=[o1, DN], writes=[ob])
                            else:
                                po = pos[0]
                                S.dma("sp", gate[:, 0:nj, :], Pd[b, q0:q0 + w, O_RT + 768 + h * 64:O_RT + 768 + h * 64 + 64].rearrange("(j t) v -> t j v", t=128), writes=[gate])
                                S.op("act", lambda e: e.activation(gs[:, 0:nj, :], gate[:, 0:nj, :], AF.Silu), reads=[gate], writes=[gs])
                                yv = TT(o1[:].rearrange("p j w -> p (j w)"))
                                S.op("act", lambda e: e.activation(o1[:, 0:nj, :], po[:, 0:nj, 0:64], AF.Copy), reads=[po], writes=[o1])
                                yv.lastw = o1.lastw
                                self.head_norm(yv, sq, s1, s2, LN_EPS, nh=nj)
                                o1.lastw = yv.lastw
                                S.op("dve", lambda e: e.tensor_tensor(ob[:, 0:nj, :], o1[:, 0:nj, :], gs[:, 0:nj, :], ALU.mult), reads=[o1, gs], writes=[ob])
                            S.dma("sp", dst, ob[:, 0:nj, :], reads=[ob])

    def layer_norm_out(self, sc, x1, G, Bt, out, tmps):
        S = self.S
        s1, s2, sq = tmps
        S.op("dve", lambda e: e.tensor_reduce(s1[:], x1[:], AX.X, ALU.add), reads=[x1], writes=[s1])
        S.op("dve", lambda e: e.tensor_scalar(s1[:], s1[:], -1.0 / D, None, ALU.mult), reads=[s1], writes=[s1])
        S.op("dve", lambda e: e.tensor_scalar(x1[:], x1[:], s1[:, 0:1], None, ALU.add), reads=[x1, s1], writes=[x1])
        S.op("pool", lambda e: e.tensor_tensor(sq[:], x1[:], x1[:], ALU.mult), reads=[x1], writes=[sq])
        S.op("dve", lambda e: e.tensor_reduce(s2[:], sq[:], AX.X, ALU.add), reads=[sq], writes=[s2])
        self.rpow(s2, s2[:], s2, s2[:], 1.0 / D, LN_EPS, -0.5)
        S.op("dve", lambda e: e.scalar_tensor_tensor(x1[:], x1[:], s2[:, 0:1], G[:], ALU.mult, ALU.mult), reads=[x1, s2, G], writes=[x1])
        S.op("pool", lambda e: e.tensor_tensor(out[:], x1[:], Bt[:], ALU.add), reads=[x1, Bt], writes=[out])

    def stage_mix(self, l):
        S, I = self.S, self.I
        with Scope(self) as sc:
            wo = sc.sb([128, 8, D], BF16)
            for kc in range(8):
                S.dma("pool", wo[:, kc, :], I["w_out"][l, kc * 128:(kc + 1) * 128, :], writes=[wo])
            G1 = sc.sb([128, 3, D])
            for v in range(3):
                S.dma("sp", G1[:, v, :], bc(self.scr["modD"][v, 2 * D:3 * D], 128), writes=[G1])
            PG = sc.sb([128, D]); PB = sc.sb([128, D])
            S.dma("sp", PG[:], bc(I["post1_g"][l], 128), writes=[PG])
            S.dma("sp", PB[:], bc(I["post1_b"][l], 128), writes=[PB])
            yc = [sc.sb([128, D]) for _ in range(2)]; xt = [sc.sb([128, D]) for _ in range(2)]
            pT = [sc.ps([128, 128]) for _ in range(2)]
            yT = [sc.sb([128, 8, 128], BF16) for _ in range(2)]
            pO = [sc.ps([128, 512]) for _ in range(2)]
            t = sc.sb([128, D]); x1 = sc.sb([128, D]); outt = [sc.sb([128, D]) for _ in range(2)]
            s1 = sc.sb([128, 1]); s2 = sc.sb([128, 1]); sq = sc.sb([128, D])
            it = 0
            for b in range(NB):
                for n in range(NT):
                    if self.last and n < 2:
                        continue
                    v = 2 if n < 2 else b
                    rows = slice(n * 128, (n + 1) * 128)
                    i = it % 2
                    S.dma("sp", yc[i][:], self.scr["Ycat"][b, rows, :], writes=[yc[i]])
                    S.dma("sp", xt[i][:], self.Xsrc[b, rows, :], writes=[xt[i]])
                    for kc in range(8):
                        pp = pT[kc % 2]
                        S.op("pe", lambda e: e.transpose(pp[:], yc[i][:, kc * 128:(kc + 1) * 128], self.ident[:]), reads=[yc[i], self.ident], writes=[pp])
                        S.op("act", lambda e: e.activation(yT[i][:, kc, :], pp[:], AF.Copy), reads=[pp], writes=[yT[i]])
                    for c in range(2):
                        po = pO[c]
                        for kc in range(8):
                            S.op("pe", lambda e: e.matmul(po[:], yT[i][:, kc, :], wo[:, kc, c * 512:(c + 1) * 512], start=(kc == 0), stop=(kc == 7)), reads=[yT[i], wo], writes=[po])
                        S.op("dve", lambda e: e.tensor_tensor(t[:, c * 512:(c + 1) * 512], po[:], G1[:, v, c * 512:(c + 1) * 512], ALU.mult), reads=[po, G1], writes=[t])
                    S.op("dve", lambda e: e.scalar_tensor_tensor(x1[:], xt[i][:], ALPHA, t[:], ALU.mult, ALU.add), reads=[xt[i], t], writes=[x1])
                    self.layer_norm_out(sc, x1, PG, PB, outt[i], (s1, s2, sq))
                    S.dma("pool", self.scr["X1s"][b, rows, :], outt[i][:], reads=[outt[i]])
                    it += 1

    def stage_ffn(self, l):
        S, I = self.S, self.I
        with Scope(self) as sc:
            w1 = sc.sb([128, 8, 2 * DFF], BF16, "w1")
            for kc in range(8):
                S.dma("pool", w1[:, kc, :], I["ffn_w_in"][l, kc * 128:(kc + 1) * 128, :], writes=[w1])
            w2 = sc.sb([128, NFF, D], BF16, "w2")
            for fc in range(NFF):
                S.dma("pool", w2[:, fc, :], I["ffn_w_out"][l, fc * 128:(fc + 1) * 128, :], writes=[w2])
            G2 = sc.sb([128, D])
            PG = sc.sb([128, D]); PB = sc.sb([128, D])
            S.dma("sp", PG[:], bc(I["post2_g"][l], 128), writes=[PG])
            S.dma("sp", PB[:], bc(I["post2_b"][l], 128), writes=[PB])
            xt = [sc.sb([128, D]) for _ in range(2)]
            pT = [sc.ps([128, 128]) for _ in range(2)]
            x1T = sc.sb([128, 8, 512], BF16)
            aT = sc.sb([128, NFF, 512], BF16)
            pU = [sc.ps([128, 512]) for _ in range(2)]; pGt = [sc.ps([128, 512]) for _ in range(2)]
            su = [sc.sb([128, 512]) for _ in range(2)]
            pF = [sc.ps([128, 512]) for _ in range(2)]
            t = sc.sb([128, D]); x2 = sc.sb([128, D]); outt = [sc.sb([128, D]) for _ in range(2)]
            s1 = sc.sb([128, 1]); s2 = sc.sb([128, 1]); sq = sc.sb([128, D])
            it = 0
            for b in range(NB):
                groups = ([] if self.last else [(0, 2, 2)]) + [(2 + 4 * q, 4, b) for q in range(4)]
                for (n0, nt, v) in groups:
                    w = nt * 128
                    S.dma("sp", G2[:], bc(self.scr["modD"][v, 5 * D:6 * D], 128), writes=[G2])
                    for j in range(nt):
                        rows = slice((n0 + j) * 128, (n0 + j + 1) * 128)
                        x = xt[it % 2]; it += 1
                        S.dma("sp", x[:], self.scr["X1s"][b, rows, :], writes=[x])
                        self.xT_modulated(sc, x, pT, x1T, j * 128, 128, v, 1)
                    for fc in range(NFF):
                        pu = pU[fc % 2]; pg = pGt[fc % 2]; s_ = su[fc % 2]
                        for kc in range(8):
                            S.op("pe", lambda e: e.matmul(pu[:, :w], w1[:, kc, fc * 128:(fc + 1) * 128], x1T[:, kc, :w], start=(kc == 0), stop=(kc == 7)), reads=[w1, x1T], writes=[pu])
                        for kc in range(8):
                            S.op("pe", lambda e: e.matmul(pg[:, :w], w1[:, kc, DFF + fc * 128:DFF + (fc + 1) * 128], x1T[:, kc, :w], start=(kc == 0), stop=(kc == 7)), reads=[w1, x1T], writes=[pg])
                        S.op("act", lambda e: e.activation(s_[:, :w], pu[:, :w], AF.Silu), reads=[pu], writes=[s_])
                        S.op("dve", lambda e: e.tensor_tensor(aT[:, fc, :w], s_[:, :w], pg[:, :w], ALU.mult), reads=[s_, pg], writes=[aT])
                    for j in range(nt):
                        rows = slice((n0 + j) * 128, (n0 + j + 1) * 128)
                        x = xt[it % 2]; it += 1
                        S.dma("sp", x[:], self.scr["X1s"][b, rows, :], writes=[x])
                        for c in range(2):
                            pf = pF[c]
                            for fc in range(NFF):
                                S.op("pe", lambda e: e.matmul(pf[:], aT[:, fc, j * 128:(j + 1) * 128], w2[:, fc, c * 512:(c + 1) * 512], start=(fc == 0), stop=(fc == NFF - 1)), reads=[aT, w2], writes=[pf])
                            S.op("dve", lambda e: e.tensor_tensor(t[:, c * 512:(c + 1) * 512], pf[:], G2[:, c * 512:(c + 1) * 512], ALU.mult), reads=[pf, G2], writes=[t])
                        S.op("dve", lambda e: e.scalar_tensor_tensor(x2[:], x[:], ALPHA, t[:], ALU.mult, ALU.add), reads=[x, t], writes=[x2])
                        o = outt[j % 2]
                        self.layer_norm_out(sc, x2, PG, PB, o, (s1, s2, sq))
                        if self.last:
                            xr = (n0 + j) * 128 - CT
                            S.dma("pool", self.out[b, xr:xr + 128, :], o[:], reads=[o])
                        else:
                            S.dma("pool", self.scr["Xs"][b, rows, :], o[:], reads=[o])


def host_consts():
    c = {}
    c["c_ident"] = np.eye(128, dtype=np.float32)
    t = np.arange(SX)
    row = (t // 64).astype(np.float32)
    col = (t % 64).astype(np.float32)
    pos = t.astype(np.float32)

    def tab(p, n):
        half = n // 2
        inv = np.power(np.float32(10000.0), -(np.arange(half, dtype=np.float32) * np.float32(2.0) / np.float32(n))).astype(np.float32)
        ang = (p[:, None] * inv[None, :]).astype(np.float32)
        cs, sn = np.cos(ang).astype(np.float32), np.sin(ang).astype(np.float32)
        return np.concatenate([cs, cs], 1), np.concatenate([-sn, sn], 1)

    def axial(n):
        h = n // 2
        c1, s1 = tab(row, h)
        c2, s2 = tab(col, h)
        return np.stack([np.concatenate([c1, c2], 1), np.concatenate([s1, s2], 1)], 1).astype(np.float32)

    c["c_ropeG"] = axial(64)
    c["c_ropeD"] = axial(32)
    cr, sr = tab(pos, 64)
    c["c_ropeR"] = np.stack([cr, sr], 1).astype(np.float32)
    mj = np.zeros((8, 256), np.float32)
    for m in range(8):
        h = m % 4
        mj[m, h * 64:(h + 1) * 64] = 1.0
    c["c_maskJ"] = mj
    c["c_zero"] = np.zeros((128, 4096), ml_dtypes.bfloat16)
    gf = 1.0 - 2.0 ** (-5.0 - np.arange(4, dtype=np.float64))
    gb = gf[::-1]
    M = np.zeros((36, 4, 128, 512), np.float64)
    p = np.arange(128)[:, None]
    j = np.arange(512)[None, :]
    for h in range(4):
        lf, lb = math.log(gf[h]), math.log(gb[h])
        for off in range(-15, 13):
            dlt = (128 * off + j - p).astype(np.float64)
            m = np.where(dlt > 0, np.exp(lf * np.maximum(dlt, 0)), 0.0) + np.where(dlt < 0, np.exp(lb * np.maximum(-dlt, 0)), 0.0) + np.where(dlt == 0, 2.0, 0.0)
            M[off + 15, h] = m
        for kc in range(2):
            for qc in range(4):
                cc = 128 * kc + p
                ii = 512 * qc + j
                M[28 + kc * 4 + qc, h] = np.exp(lf * (256 + ii - cc)) + np.exp(lb * (2048 - ii + cc))
    c["c_retM"] = M.astype(np.float32).astype(ml_dtypes.bfloat16)
    return c


_CACHE = {}


def kernel(**inputs):
    f = lambda k: np.ascontiguousarray(np.asarray(inputs[k], dtype=np.float32))
    L = DEPTH
    shared = {}
    for k in ("ada_w", "ada_b", "w_in", "rwkv_mu", "rwkv_g_up", "rwkv_k_k", "rwkv_k_a", "rwkv_ln_g", "rwkv_ln_b",
              "gqa_q_norm", "gqa_k_norm", "diff_norm", "w_out", "post1_g", "post1_b", "ffn_w_in", "ffn_w_out", "post2_g", "post2_b"):
        shared[k] = f(k)
    shared["rwkv_w0"] = f("rwkv_w0").reshape(L, 512)
    shared["rwkv_a0"] = f("rwkv_a0").reshape(L, 512)
    shared["rwkv_w_up"] = f("rwkv_w_up").reshape(L, 64, 256)
    shared["rwkv_a_up"] = f("rwkv_a_up").reshape(L, 64, 256)
    shared["rwkv_r_k"] = f("rwkv_r_k").reshape(L, 256)
    shared["diff_lambda"] = f("diff_lambda").reshape(L, 128)
    shared.update(host_consts())
    x, c, ctx, c_ctx = f("x"), f("c"), f("ctx"), f("c_ctx")
    in_maps = []
    for core in range(8):
        bs = slice(core * NB, (core + 1) * NB)
        m = dict(shared)
        m["xin"] = np.ascontiguousarray(np.concatenate([ctx[bs], x[bs]], axis=1))
        m["cvec"] = np.ascontiguousarray(np.concatenate([c[bs], c_ctx[None, :]], axis=0))
        in_maps.append(m)
    if "nc" not in _CACHE:
        _CACHE["nc"] = KB().build()
    res = run_bass_kernel_spmd(_CACHE["nc"], in_maps, core_ids=list(range(8)))
    return np.concatenate([r["y"] for r in res.results], axis=0).astype(np.float32)
```

```python
import math, os
KSKIP = os.environ.get('KSKIP', '')
import numpy as np
import ml_dtypes
import concourse.bass as bass
import concourse.mybir as mybir
from concourse.bass_utils import run_bass_kernel_spmd
from contextlib import ExitStack

F32 = mybir.dt.float32
BF16 = mybir.dt.bfloat16
AF = mybir.ActivationFunctionType
ALU = mybir.AluOpType
AX = mybir.AxisListType

D = 1024
NB = 2
CT = 256
SX = 2048
T = CT + SX
NT = T // 128
INW = 3264
DFF = 2816
NFF = DFF // 128
DEPTH = 4
ALPHA = (2 * DEPTH) ** 0.25
LN_EPS = 1e-5
QK_EPS = 1e-6
GN_EPS = 64e-5
CS = 16
O_RW, O_GQ, O_DF, O_RT = 0, 960, 1472, 2240


class Sched:
    def __init__(self, nc, es):
        self.nc = nc
        self.es = es
        self.eng = {"pe": nc.tensor, "act": nc.scalar, "dve": nc.vector, "pool": nc.gpsimd, "sp": nc.sync}
        self.sem = {}
        self.cnt = {}
        self.nsem = 0
        for k in self.eng:
            self.sem[k] = self._newsem()
            self.cnt[k] = 0
        self.waited = {k: {} for k in self.eng}
        self.NSLOT = 8
        self.dsem = {}
        self.dcnt = {}
        self.dnext = {}
        for q in ("sp", "pool", "act"):
            self.dsem[q] = [self._newsem() for i in range(self.NSLOT)]
            self.dcnt[q] = [0] * self.NSLOT
            self.dnext[q] = 0
        self.ninstr = 0

    def _newsem(self):
        self.nsem += 1
        return self.es.enter_context(self.nc.semaphore("sem%d" % self.nsem))

    def _wait(self, e, tok, raw=False):
        if tok is None:
            return
        sem, val, owner = tok
        if owner == e and (e == "pe" or not raw):
            return
        key = id(sem)
        w = self.waited[e]
        if w.get(key, 0) >= val:
            return
        self.eng[e].wait_ge(sem, val)
        w[key] = val

    def deps(self, e, reads, writes):
        for t in reads:
            self._wait(e, t.lastw, raw=True)
        for t in writes:
            self._wait(e, t.lastw)
            for r in t.readers.values():
                self._wait(e, r)

    def done(self, tok, reads, writes):
        for t in reads:
            t.readers[tok[2] + str(id(tok[0]))] = tok
        for t in writes:
            t.lastw = tok
            t.readers = {}

    def op(self, e, fn, reads=(), writes=()):
        self.deps(e, reads, writes)
        ins = fn(self.eng[e])
        self.cnt[e] += 1
        ins.then_inc(self.sem[e], 1)
        tok = (self.sem[e], self.cnt[e], e)
        self.done(tok, reads, writes)
        self.ninstr += 1
        return tok

    def dma(self, q, out, in_, reads=(), writes=(), **kw):
        s = self.dnext[q]
        self.dnext[q] = (s + 1) % self.NSLOT
        sem = self.dsem[q][s]
        if self.dcnt[q][s] > 0:
            self._wait(q, (sem, 16 * self.dcnt[q][s], "dma"))
        self.deps(q, reads, writes)
        ins = self.eng[q].dma_start(out=out, in_=in_, **kw)
        self.dcnt[q][s] += 1
        ins.then_inc(sem, 16)
        tok = (sem, 16 * self.dcnt[q][s], "dma")
        self.done(tok, reads, writes)
        self.ninstr += 1
        return tok

    def barrier(self):
        toks = []
        for k in self.eng:
            if self.cnt[k] > 0:
                toks.append((self.sem[k], self.cnt[k], k))
        for q in self.dsem:
            for s in range(self.NSLOT):
                if self.dcnt[q][s] > 0:
                    toks.append((self.dsem[q][s], 16 * self.dcnt[q][s], "dma"))
        for e in self.eng:
            for t in toks:
                self._wait(e, t)
        for k in self.eng:
            if self.cnt[k] > 12000:
                self.sem[k] = self._newsem()
                self.cnt[k] = 0
        for q in self.dsem:
            for s in range(self.NSLOT):
                if self.dcnt[q][s] > 1500:
                    self.dsem[q][s] = self._newsem()
                    self.dcnt[q][s] = 0


class TT:
    def __init__(self, ap):
        self.ap = ap
        self.lastw = None
        self.readers = {}

    def __getitem__(self, k):
        return self.ap[k]


class Scope:
    def __init__(self, kb):
        self.kb = kb
        self.es = ExitStack()

    def __enter__(self):
        self.es.__enter__()
        return self

    def __exit__(self, *a):
        self.kb.S.barrier()
        return self.es.__exit__(*a)

    def sb(self, shape, dt=F32, name="t"):
        self.kb.uid += 1
        return TT(self.es.enter_context(self.kb.nc.sbuf_tensor("%s_%d" % (name, self.kb.uid), list(shape), dt)))

    def ps(self, shape, dt=F32, name="p"):
        self.kb.uid += 1
        assert dt == F32
        shape = list(shape)
        free = int(np.prod(shape[1:]))
        assert free <= 512
        t = self.es.enter_context(self.kb.nc.psum_tensor("%s_%d" % (name, self.kb.uid), [128, 512], dt))
        v = t[0:shape[0], 0:free]
        if len(shape) == 3:
            v = v.rearrange("p (a b) -> p a b", a=shape[1])
        return TT(v)


def bc(ap, n):
    return ap.partition_broadcast(n)


class KB:
    def __init__(self, NL=DEPTH, dbg=(), stages=None):
        self.NL = NL
        self.dbg = set(dbg)
        self.stages = stages
        self.uid = 0
        nc = self.nc = bass.Bass("TRN2", target_bir_lowering=False)
        self.I = {}
        self.scr = {}

        def inp(name, shape, dt=F32):
            self.I[name] = nc.dram_tensor(name, list(shape), dt, kind="ExternalInput").ap()

        inp("xin", [NB, T, D])
        inp("cvec", [3, D])
        L = DEPTH
        inp("ada_w", [L, D, 6 * D]); inp("ada_b", [L, 6 * D]); inp("w_in", [L, D, INW])
        inp("rwkv_mu", [L, 960]); inp("rwkv_w0", [L, 512]); inp("rwkv_w_up", [L, 64, 256])
        inp("rwkv_a0", [L, 512]); inp("rwkv_a_up", [L, 64, 256]); inp("rwkv_g_up", [L, 64, 256])
        inp("rwkv_k_k", [L, 256]); inp("rwkv_k_a", [L, 256]); inp("rwkv_r_k", [L, 256])
        inp("rwkv_ln_g", [L, 256]); inp("rwkv_ln_b", [L, 256])
        inp("gqa_q_norm", [L, 64]); inp("gqa_k_norm", [L, 64]); inp("diff_lambda", [L, 128]); inp("diff_norm", [L, 64])
        inp("w_out", [L, D, D]); inp("post1_g", [L, D]); inp("post1_b", [L, D])
        inp("ffn_w_in", [L, D, 2 * DFF]); inp("ffn_w_out", [L, DFF, D]); inp("post2_g", [L, D]); inp("post2_b", [L, D])
        inp("c_ident", [128, 128]); inp("c_ropeG", [SX, 2, 64]); inp("c_ropeD", [SX, 2, 32]); inp("c_ropeR", [SX, 2, 64])
        inp("c_maskJ", [8, 256]); inp("c_retM", [36, 4, 128, 512], BF16); inp("c_zero", [128, 4096], BF16)
        self.out = nc.dram_tensor("y", [NB, SX, D], F32, kind="ExternalOutput").ap()

        def scr(name, shape, dt=F32):
            kind = "ExternalOutput" if name in self.dbg else "Internal"
            self.scr[name] = nc.dram_tensor(name, list(shape), dt, kind=kind).ap()

        scr("P", [NB, T, INW]); scr("Xs", [NB, T, D]); scr("X1s", [NB, T, D]); scr("modD", [3, 6 * D])
        for m in "gr":
            scr("QT" + m, [NB, 64, 4, T], BF16)
            scr("KT" + m, [NB, 64, 4, T], BF16)
        scr("QTd", [NB, 32, 8, T], BF16)
        scr("KTd", [NB, 32, 8, T], BF16)
        scr("As", [128, T, 8], BF16); scr("Rs", [128, T, 8], BF16); scr("Ws", [2, 128, T, 4])
        scr("LBs", [2, 8, T, 128], BF16); scr("LKs", [2, 8, T, 128], BF16); scr("RVs", [8, T, 256], BF16); scr("Yfull", [2, 8, T, 256])
        scr("Gt", [NB, T, 256]); scr("Bon", [NB, T, 256]); scr("Ycat", [NB, T, D])

    def build(self):
        nc = self.nc
        with ExitStack() as es:
            self.S = S = Sched(nc, es)
            with Scope(self) as g:
                self.g = g
                self.ident = g.sb([128, 128], F32, "ident")
                S.dma("sp", self.ident[:], self.I["c_ident"], writes=[self.ident])
                self.siluT = g.sb([128, 8, 3], F32, "siluT")
                self.cvals = [64.0 * QK_EPS, GN_EPS, LN_EPS, QK_EPS, 0.0]
                self.cst = g.sb([128, 8], F32, "cst")
                for i_, v_ in enumerate(self.cvals):
                    S.op("dve", lambda e: e.memset(self.cst[:, i_:i_ + 1], float(v_)), writes=[self.cst])
                self.stage_init()
                for l in range(self.NL):
                    self.layer(l)
                S.barrier()
            print("ninstr", S.ninstr, "nsem", S.nsem)
        return nc

    def rpow(self, ot, o_ap, it, i_ap, scale, bias, p):
        S = self.S
        assert p == -0.5
        bcol = self.cvals.index(float(bias))
        np_ = o_ap.shape[0]
        S.op("act", lambda e: e.activation(o_ap, i_ap, AF.Sqrt, bias=self.cst[0:np_, bcol:bcol + 1], scale=float(scale)), reads=[it, self.cst], writes=[ot])
        S.op("dve", lambda e: e.reciprocal(o_ap, o_ap), reads=[ot], writes=[ot])

    def on(self, name):
        return self.stages is None or name in self.stages

    def stage_init(self):
        S, I = self.S, self.I
        with Scope(self) as sc:
            cv = sc.sb([3, D])
            S.dma("sp", cv[:], I["cvec"], writes=[cv])
            sv = sc.sb([3, D])
            S.op("act", lambda e: e.activation(sv[:], cv[:], AF.Silu), reads=[cv], writes=[sv])
            pT = sc.ps([128, 8, 3])
            for kc in range(8):
                S.op("pe", lambda e: e.transpose(pT[:, kc, :], sv[:, kc * 128:(kc + 1) * 128], self.ident[0:3, 0:3]), reads=[sv, self.ident], writes=[pT])
            S.op("dve", lambda e: e.tensor_copy(self.siluT[:], pT[:]), reads=[pT], writes=[self.siluT])
            z = sc.sb([128, 4096], BF16)
            S.dma("sp", z[:], I["c_zero"], writes=[z])
            for name in ("LBs", "LKs"):
                flat = self.scr[name].rearrange("d m t k -> (d m t k)").rearrange("(a p f) -> a p f", p=128, f=4096)
                for a in range(flat.shape[0]):
                    S.dma("sp", flat[a], z[:], reads=[z])
            flat = self.scr["RVs"].rearrange("m t k -> (m t k)").rearrange("(a p f) -> a p f", p=128, f=4096)
            for a in range(flat.shape[0]):
                S.dma("sp", flat[a], z[:], reads=[z])

    def layer(self, l):
        self.l = l
        self.last = (l == DEPTH - 1)
        self.Xsrc = self.I["xin"] if l == 0 else self.scr["Xs"]
        if self.on("mod"):
            self.stage_mod(l)
        with Scope(self) as ls:
            self.ls = ls
            S = self.S
            self.modT = ls.sb([128, 48, 3], F32, "modT")
            for v in range(3):
                S.dma("sp", self.modT[:, :, v], self.scr["modD"][v].rearrange("(c p) -> p c", p=128), writes=[self.modT], allow_slow_non_contiguous=True)
            for c0 in (8, 32):
                S.op("dve", lambda e: e.tensor_scalar(self.modT[:, c0:c0 + 8, :], self.modT[:, c0:c0 + 8, :], 1.0, None, ALU.add), reads=[self.modT], writes=[self.modT])
            if self.on("inproj"):
                self.stage_inproj(l)
            if self.on("rwprep"):
                self.stage_rwprep(l)
            if self.on("scan"):
                self.stage_scan(l)
            if self.on("rwpost"):
                self.stage_rwpost(l)
            if self.on("attn"):
                self.stage_attn(l)
            if self.on("mix"):
                self.stage_mix(l)
            if self.on("ffn"):
                self.stage_ffn(l)

    def stage_mod(self, l):
        S, I = self.S, self.I
        with Scope(self) as sc:
            adab = sc.sb([3, 6 * D])
            S.dma("sp", adab[:], bc(I["ada_b"][l], 3), writes=[adab])
            modsb = sc.sb([3, 6 * D])
            wb = [sc.sb([128, 8, 512]) for _ in range(2)]
            pm = [sc.ps([3, 512]) for _ in range(2)]
            for n in range(12):
                w = wb[n % 2]
                S.dma("sp" if n % 2 == 0 else "pool", w[:], I["ada_w"][l, :, n * 512:(n + 1) * 512].rearrange("(c p) n -> p c n", p=128), writes=[w])
                p = pm[n % 2]
                for kc in range(8):
                    S.op("pe", lambda e: e.matmul(p[:], self.siluT[:, kc, :], w[:, kc, :], start=(kc == 0), stop=(kc == 7)), reads=[self.siluT, w], writes=[p])
                S.op("dve", lambda e: e.tensor_tensor(modsb[:, n * 512:(n + 1) * 512], p[:], adab[:, n * 512:(n + 1) * 512], ALU.add), reads=[p, adab], writes=[modsb])
            S.dma("sp", self.scr["modD"], modsb[:], reads=[modsb])

    def xT_modulated(self, sc, src_tile, pT, xmT, col0, width, v, which):
        S = self.S
        shc, scc = (0, 8) if which == 0 else (24, 32)
        for kc in range(8):
            pp = pT[kc % 2]
            S.op("pe", lambda e: e.transpose(pp[:], src_tile[:, kc * 128:(kc + 1) * 128], self.ident[:]), reads=[src_tile, self.ident], writes=[pp])
            eng = "dve" if kc % 2 == 0 else "pool"
            eng = "dve"
            S.op(eng, lambda e: e.tensor_scalar(xmT[:, kc, col0:col0 + 128], pp[:], self.modT[:, scc + kc, v:v + 1], self.modT[:, shc + kc, v:v + 1], ALU.mult, ALU.add), reads=[pp, self.modT], writes=[xmT])

    def rope(self, eng2, out, src, tab, nh, nblk, half, tmp1, tmp2, reads, tabT):
        S = self.S
        w = nblk * 2 * half
        Cb = tab[:, 0, :].unsqueeze(1).to_broadcast([128, nh, w])
        S.op("dve", lambda e: e.tensor_tensor(tmp1, src, Cb, ALU.mult), reads=reads + [tabT], writes=[self._t1])
        v5 = lambda ap: ap.rearrange("p h (b two f) -> p h b two f", b=nblk, two=2)
        sgv = tab[:, 1, :].rearrange("p (b two f) -> p b two f", b=nblk, two=2)
        for hf in range(2):
            o = v5(tmp2)[:, :, :, hf, :]
            i = v5(src)[:, :, :, 1 - hf, :]
            sg = sgv[:, :, hf, :].unsqueeze(1).to_broadcast([128, nh, nblk, half])
            S.op(eng2, lambda e: e.tensor_tensor(o, i, sg, ALU.mult), reads=reads + [tabT], writes=[self._t2])
        S.op("dve", lambda e: e.tensor_tensor(out, tmp1, tmp2, ALU.add), reads=[self._t1, self._t2], writes=[self._ro])

    def stage_inproj(self, l):
        S, I = self.S, self.I
        with Scope(self) as sc:
            win = sc.sb([128, 8, INW], BF16, "win")
            for kc in range(8):
                S.dma("pool", win[:, kc, :], I["w_in"][l, kc * 128:(kc + 1) * 128, :], writes=[win])
            rG = sc.sb([128, 16, 2, 64]); rD = sc.sb([128, 16, 2, 32]); rR = sc.sb([128, 16, 2, 64])
            S.dma("sp", rG[:], I["c_ropeG"].rearrange("(n p) a w -> p n a w", p=128), writes=[rG])
            S.dma("sp", rD[:], I["c_ropeD"].rearrange("(n p) a w -> p n a w", p=128), writes=[rD])
            S.dma("sp", rR[:], I["c_ropeR"].rearrange("(n p) a w -> p n a w", p=128), writes=[rR])
            GW = sc.sb([128, 6, 64])
            for h in range(4):
                S.dma("sp", GW[:, h, :], bc(I["gqa_q_norm"][l], 128), writes=[GW])
            for h in range(4, 6):
                S.dma("sp", GW[:, h, :], bc(I["gqa_k_norm"][l], 128), writes=[GW])
            S.op("dve", lambda e: e.tensor_scalar(GW[:, 4:6, :], GW[:, 4:6, :], 8.0, None, ALU.mult), reads=[GW], writes=[GW])
            xt = [sc.sb([128, D]) for _ in range(2)]
            pT = [sc.ps([128, 128]) for _ in range(2)]
            xmT = [sc.sb([128, 8, 128], BF16) for _ in range(2)]
            pP = [sc.ps([128, 512]) for _ in range(2)]
            Psb = [sc.sb([128, INW]) for _ in range(2)]
            pQ = [sc.ps([64, 4, 128]) for _ in range(2)]
            sq = sc.sb([128, 6, 64]); ss = sc.sb([128, 6]); r1 = sc.sb([128, 6]); qn = sc.sb([128, 6, 64])
            t1 = sc.sb([128, 512]); t2 = sc.sb([128, 512]); ro = sc.sb([128, 512])
            self._t1, self._t2, self._ro = t1, t2, ro
            QTs = [sc.sb([64, 4, 128], BF16) for _ in range(6)]
            it = 0
            for b in range(NB):
                for n in range(NT):
                    v = 2 if n < 2 else b
                    isx = n >= 2
                    rows = slice(n * 128, (n + 1) * 128)
                    x = xt[it % 2]; xm = xmT[it % 2]; P = Psb[it % 2]
                    S.dma("sp", x[:], self.Xsrc[b, rows, :], writes=[x])
                    self.xT_modulated(sc, x, pT, xm, 0, 128, v, 0)
                    for c in range(7):
                        c0 = c * 512
                        w = min(512, INW - c0)
                        pp = pP[c % 2]
                        for kc in range(8):
                            S.op("pe", lambda e: e.matmul(pp[:, :w], xm[:, kc, :], win[:, kc, c0:c0 + w], start=(kc == 0), stop=(kc == 7)), reads=[xm, win], writes=[pp])
                        S.op("act", lambda e: e.activation(P[:, c0:c0 + w], pp[:, :w], AF.Copy), reads=[pp], writes=[P])
                    S.dma("pool", self.scr["P"][b, rows, :], P[:], reads=[P])
                    qk = P[:, O_GQ:O_GQ + 384].rearrange("p (h w) -> p h w", w=64)
                    S.op("pool", lambda e: e.tensor_tensor(sq[:], qk, qk, ALU.mult), reads=[P], writes=[sq])
                    S.op("dve", lambda e: e.tensor_reduce(ss[:], sq[:], AX.X, ALU.add), reads=[sq], writes=[ss])
                    self.rpow(r1, r1[:], ss, ss[:], 1.0, 64.0 * QK_EPS, -0.5)
                    S.op("dve", lambda e: e.tensor_tensor(qn[:], qk, r1[:].unsqueeze(2).to_broadcast([128, 6, 64]), ALU.mult), reads=[P, r1], writes=[qn])
                    S.op("dve", lambda e: e.tensor_tensor(qn[:], qn[:], GW[:], ALU.mult), reads=[qn, GW], writes=[qn])
                    v3 = lambda t, nh, w: t[:, 0:nh * w].rearrange("p (h w) -> p h w", w=w)
                    if isx:
                        self.rope("pool", v3(ro, 6, 64), qn[:], rG[:, n - 2], 6, 2, 16, v3(t1, 6, 64), v3(t2, 6, 64), [qn], rG)
                        src, srcT = v3(ro, 6, 64), ro
                    else:
                        src, srcT = qn[:], qn
                    self.emit_T(src, srcT, 6, pQ, QTs, [("QTg", 0, 4, 1.0), ("KTg", 4, 2, 1.0)], b, rows)
                    dq = P[:, O_DF:O_DF + 512].rearrange("p (h w) -> p h w", w=32)
                    if isx:
                        self.rope("pool", v3(ro, 16, 32), dq, rD[:, n - 2], 16, 2, 8, v3(t1, 16, 32), v3(t2, 16, 32), [P], rD)
                        src, srcT = v3(ro, 16, 32), ro
                    else:
                        src, srcT = dq, P
                    self.emit_T(src, srcT, 16, pQ, QTs, [("QTd", 0, 8, 32 ** -0.5), ("KTd", 8, 8, 1.0)], b, rows, width=32)
                    rq = P[:, O_RT:O_RT + 512].rearrange("p (h w) -> p h w", w=64)
                    if isx:
                        self.rope("pool", v3(ro, 8, 64), rq, rR[:, n - 2], 8, 1, 32, v3(t1, 8, 64), v3(t2, 8, 64), [P], rR)
                        src, srcT = v3(ro, 8, 64), ro
                    else:
                        src, srcT = rq, P
                    self.emit_T(src, srcT, 8, pQ, QTs, [("QTr", 0, 4, 1.0), ("KTr", 4, 4, 0.125)], b, rows)
                    it += 1

    def emit_T(self, src, srcT, nh, pQ, QTs, outs, b, rows, width=64):
        S = self.S
        for (name, h0, cnt_all, scale) in outs:
            for g0 in range(0, cnt_all, 4):
                cnt = min(4, cnt_all - g0)
                self._qi = getattr(self, "_qi", 0) + 1
                pq = pQ[self._qi % 2]
                st = QTs[self._qi % 6]
                for j in range(cnt):
                    S.op("pe", lambda e: e.transpose(pq[0:width, j, :], src[:, h0 + g0 + j, :], self.ident[:]), reads=[srcT, self.ident], writes=[pq])
                S.op("act", lambda e: e.activation(st[0:width, 0:cnt, :], pq[0:width, 0:cnt, :], AF.Copy, scale=float(scale)), reads=[pq], writes=[st])
                S.dma("pool", self.scr[name][b, :, g0:g0 + cnt, rows], st[0:width, 0:cnt, :], reads=[st])

    def stage_rwprep(self, l):
        S, I = self.S, self.I
        with Scope(self) as sc:
            def btile(key, w, nrep=1):
                t = sc.sb([128, nrep * w])
                for r in range(nrep):
                    S.dma("sp", t[:, r * w:(r + 1) * w], bc(I[key][l], 128), writes=[t])
                return t
            MU = btile("rwkv_mu", 960); W0 = btile("rwkv_w0", 512); A0 = btile("rwkv_a0", 512)
            KK = btile("rwkv_k_k", 256); KA = btile("rwkv_k_a", 256); RK = btile("rwkv_r_k", 256)
            WUP = sc.sb([32, 2, 256]); AUP = sc.sb([32, 2, 256]); GUP = sc.sb([64, 256])
            S.dma("sp", WUP[:], I["rwkv_w_up"][l].rearrange("(d r) c -> r d c", d=2), writes=[WUP])
            S.dma("sp", AUP[:], I["rwkv_a_up"][l].rearrange("(d r) c -> r d c", d=2), writes=[AUP])
            S.dma("sp", GUP[:], I["rwkv_g_up"][l], writes=[GUP])
            def mkset():
                cur = sc.sb([128, 960]); prv = sc.sb([128, 960]); nxt = sc.sb([128, 960])
                tt = sc.sb([128, 960])
                pst = sc.sb([128, 1024])
                S.op("dve", lambda e: e.memset(pst[:, 0:64], 0.0), writes=[pst])
                lor = sc.sb([128, 3, 64]); lorT = sc.sb([32, 4, 128]); lorTg = sc.sb([64, 128])
                wt = sc.sb([128, 512]); e1 = sc.sb([128, 512])
                decp = sc.sb([128, 64 + 512])
                S.op("dve", lambda e: e.memset(decp[:, 0:64], 0.0), writes=[decp])
                asig = sc.sb([128, 512]); gt = sc.sb([128, 256])
                kkp = sc.sb([128, 64 + 256])
                S.op("dve", lambda e: e.memset(kkp[:, 0:64], 0.0), writes=[kkp])
                kq = sc.sb([128, 256]); ks = sc.sb([128, 4]); kr = sc.sb([128, 4])
                bd = sc.sb([128, 512], BF16); kd = sc.sb([128, 512]); tk = sc.sb([128, 512]); kd16 = sc.sb([128, 512], BF16); v16 = sc.sb([128, 256], BF16)
                rk = sc.sb([128, 256]); pr = sc.sb([128, 512]); s8 = sc.sb([128, 8]); s4 = sc.sb([128, 4]); bon = sc.sb([128, 256])
                return (cur, prv, nxt, tt, pst, lor, lorT, lorTg, wt, e1, decp, asig, gt, kkp, kq, ks, kr, bd, kd, tk, kd16, v16, rk, pr, s8, s4, bon)
            tsets = [mkset() for _ in range(2)]
            pL = sc.ps([32, 4, 128]); pW = sc.ps([128, 512]); pAl = sc.ps([128, 512]); pG = sc.ps([128, 256])
            pA = sc.ps([128, 4, 128]); pR = sc.ps([128, 4, 128]); pWt = [sc.ps([128, 4, 128]) for _ in range(2)]
            nsets = []
            for _ in range(2):
                Ast = sc.sb([128, 128, 8], BF16); Rst = sc.sb([128, 128, 8], BF16); Wst = [sc.sb([128, 128, 4]) for _ in range(2)]
                S.op("pool", lambda e: e.memset(Ast[:], 0.0), writes=[Ast])
                S.op("pool", lambda e: e.memset(Rst[:], 0.0), writes=[Rst])
                nsets.append((Ast, Rst, Wst))
            P = self.scr["P"]
            for n in range(NT):
                rows = slice(n * 128, (n + 1) * 128)
                r0 = n * 128
                Ast, Rst, Wst = nsets[n % 2]
                for b in range(NB):
                    (cur, prv, nxt, tt, pst, lor, lorT, lorTg, wt, e1, decp, asig, gt, kkp, kq, ks, kr, bd, kd, tk, kd16, v16, rk, pr, s8, s4, bon) = tsets[b]

                    def issue_loads(n_, b_):
                        cur_, prv_, nxt_ = tsets[b_][0:3]
                        q0_ = n_ * 128
                        S.dma("sp", cur_[:], P[b_, q0_:q0_ + 128, 0:960], writes=[cur_])
                        if n_ in (0, 2):
                            S.op("pool", lambda e: e.memset(prv_[0:32, :], 0.0), writes=[prv_])
                            S.dma("sp", prv_[1:128, :], P[b_, q0_:q0_ + 127, 0:960], writes=[prv_])
                        else:
                            S.dma("sp", prv_[:], P[b_, q0_ - 1:q0_ + 127, 0:960], writes=[prv_])
                        if n_ in (1, NT - 1):
                            S.op("pool", lambda e: e.memset(nxt_[96:128, :], 0.0), writes=[nxt_])
                            S.dma("sp", nxt_[0:127, :], P[b_, q0_ + 1:q0_ + 128, 0:960], writes=[nxt_])
                        else:
                            S.dma("sp", nxt_[:], P[b_, q0_ + 1:q0_ + 129, 0:960], writes=[nxt_])

                    if n == 0 and b == 0:
                        issue_loads(0, 0)
                        issue_loads(0, 1)
                    S.op("pool", lambda e: e.tensor_tensor(tt[:], prv[:], nxt[:], ALU.add), reads=[prv, nxt], writes=[tt])
                    S.op("dve", lambda e: e.scalar_tensor_tensor(tt[:], tt[:], 0.5, cur[:], ALU.mult, ALU.subtract), reads=[tt, cur], writes=[tt])
                    S.op("pool", lambda e: e.tensor_tensor(tt[:], tt[:], MU[:], ALU.mult), reads=[tt, MU], writes=[tt])
                    S.op("dve", lambda e: e.tensor_tensor(pst[:, 64:1024], tt[:], cur[:], ALU.add), reads=[tt, cur], writes=[pst])
                    if n + 1 < NT:
                        issue_loads(n + 1, b)
                    p_r = pst[:, 64:320]; p_k = pst[:, 320:576]; p_v = pst[:, 576:832]
                    S.op("act", lambda e: e.activation(lor[:, 0, :], pst[:, 832:896], AF.Tanh), reads=[pst], writes=[lor])
                    S.op("act", lambda e: e.activation(lor[:, 2, :], pst[:, 960:1024], AF.Sigmoid), reads=[pst], writes=[lor])
                    S.op("pool", lambda e: e.tensor_copy(lor[:, 1, :], pst[:, 896:960]), reads=[pst], writes=[lor])
                    if 'A' in KSKIP:
                        continue
                    for j in range(4):
                        S.op("pe", lambda e: e.transpose(pL[:, j, :], lor[:, j // 2, 32 * (j % 2):32 * (j % 2) + 32], self.ident[:]), reads=[lor, self.ident], writes=[pL])
                    S.op("dve", lambda e: e.tensor_copy(lorT[:], pL[:]), reads=[pL], writes=[lorT])
                    S.op("pe", lambda e: e.transpose(pWt[0][0:64, 0, :], lor[:, 2, :], self.ident[:]), reads=[lor, self.ident], writes=[pWt[0]])
                    S.op("dve", lambda e: e.tensor_copy(lorTg[:], pWt[0][0:64, 0, :]), reads=[pWt[0]], writes=[lorTg])
                    for d in range(2):
                        S.op("pe", lambda e: e.matmul(pW[:, d * 256:(d + 1) * 256], lorT[:, d, :], WUP[:, d, :], start=True, stop=True), reads=[lorT, WUP], writes=[pW])
                        S.op("pe", lambda e: e.matmul(pAl[:, d * 256:(d + 1) * 256], lorT[:, 2 + d, :], AUP[:, d, :], start=True, stop=True), reads=[lorT, AUP], writes=[pAl])
                    S.op("pe", lambda e: e.matmul(pG[:], lorTg[:], GUP[:], start=True, stop=True), reads=[lorTg, GUP], writes=[pG])
                    if 'B' in KSKIP:
                        continue
                    S.op("dve", lambda e: e.tensor_tensor(wt[:], pW[:], W0[:], ALU.add), reads=[pW, W0], writes=[wt])
                    S.op("act", lambda e: e.activation(e1[:], wt[:], AF.Exp, scale=-1.0), reads=[wt], writes=[e1])
                    S.op("dve", lambda e: e.tensor_scalar(e1[:], e1[:], 1.0, None, ALU.add), reads=[e1], writes=[e1])
                    S.op("dve", lambda e: e.reciprocal(e1[:], e1[:]), reads=[e1], writes=[e1])
                    S.op("act", lambda e: e.activation(decp[:, 64:576], e1[:], AF.Exp, scale=-math.exp(-0.5)), reads=[e1], writes=[decp])
                    S.op("dve", lambda e: e.tensor_tensor(wt[:], pAl[:], A0[:], ALU.add), reads=[pAl, A0], writes=[wt])
                    S.op("act", lambda e: e.activation(e1[:], wt[:], AF.Exp, scale=-1.0), reads=[wt], writes=[e1])
                    S.op("dve", lambda e: e.tensor_scalar(e1[:], e1[:], 1.0, None, ALU.add), reads=[e1], writes=[e1])
                    S.op("dve", lambda e: e.reciprocal(asig[:], e1[:]), reads=[e1], writes=[asig])
                    S.op("act", lambda e: e.activation(gt[:], pG[:], AF.Copy), reads=[pG], writes=[gt])
                    S.dma("pool", self.scr["Gt"][b, rows, :], gt[:], reads=[gt])
                    kk = kkp[:, 64:320]
                    S.op("pool", lambda e: e.tensor_tensor(kk, p_k, KK[:], ALU.mult), reads=[pst, KK], writes=[kkp])
                    S.op("pool", lambda e: e.tensor_tensor(kq[:], kk, kk, ALU.mult), reads=[kkp], writes=[kq])
                    S.op("dve", lambda e: e.tensor_reduce(ks[:], kq[:].rearrange("p (h w) -> p h w", w=64), AX.X, ALU.add), reads=[kq], writes=[ks])
                    S.op("dve", lambda e: e.tensor_scalar(kr[:], ks[:], 1e-24, None, ALU.max), reads=[ks], writes=[kr])
                    self.rpow(kr, kr[:], kr, kr[:], 1.0, 0.0, -0.5)
                    kk3 = kk.rearrange("p (h w) -> p h w", w=64)
                    S.op("dve", lambda e: e.tensor_tensor(kk3, kk3, kr[:].unsqueeze(2).to_broadcast([128, 4, 64]), ALU.mult), reads=[kkp, kr], writes=[kkp])
                    d3 = lambda t: t[:].rearrange("p (d c) -> p d c", d=2)
                    b2 = lambda ap: ap.unsqueeze(1).to_broadcast([128, 2, 256])
                    S.op("dve", lambda e: e.tensor_tensor(d3(bd), d3(asig), b2(kk), ALU.mult), reads=[asig, kkp], writes=[bd])
                    S.op("dve", lambda e: e.scalar_tensor_tensor(d3(tk), d3(asig), -1.0, b2(KA[:]), ALU.add, ALU.mult), reads=[asig, KA], writes=[tk])
                    S.op("dve", lambda e: e.scalar_tensor_tensor(d3(kd), d3(tk), 1.0, b2(p_k), ALU.add, ALU.mult), reads=[tk, pst], writes=[kd])
                    S.op("pool", lambda e: e.tensor_copy(kd16[:], kd[:]), reads=[kd], writes=[kd16])
                    S.op("pool", lambda e: e.tensor_copy(v16[:], p_v), reads=[pst], writes=[v16])
                    S.op("pool", lambda e: e.tensor_tensor(rk[:], p_r, RK[:], ALU.mult), reads=[pst, RK], writes=[rk])
                    S.op("dve", lambda e: e.tensor_tensor(d3(pr), d3(kd), b2(rk[:]), ALU.mult), reads=[kd, rk], writes=[pr])
                    S.op("dve", lambda e: e.tensor_reduce(s8[:], pr[:].rearrange("p (g w) -> p g w", w=64), AX.X, ALU.add), reads=[pr], writes=[s8])
                    S.op("dve", lambda e: e.tensor_tensor(s4[:], s8[:, 0:4], s8[:, 4:8], ALU.add), reads=[s8], writes=[s4])
                    S.op("dve", lambda e: e.tensor_tensor(bon[:].rearrange("p (h w) -> p h w", w=64), p_v.rearrange("p (h w) -> p h w", w=64), s4[:].unsqueeze(2).to_broadcast([128, 4, 64]), ALU.mult), reads=[pst, s4], writes=[bon])
                    S.dma("pool", self.scr["Bon"][b, rows, :], bon[:], reads=[bon])
                    if 'C' in KSKIP:
                        continue
                    lo, hi = b * 64, (b + 1) * 64
                    for h in range(4):
                        if b == 0:
                            ink = kkp[:, 64 + 64 * h:128 + 64 * h]; inr = pst[:, 64 + 64 * h:128 + 64 * h]
                            S.op("pe", lambda e: e.transpose(pA[0:64, h, :], ink, self.ident[:]), reads=[kkp, self.ident], writes=[pA])
                            S.op("pe", lambda e: e.transpose(pR[0:64, h, :], inr, self.ident[:]), reads=[pst, self.ident], writes=[pR])
                        else:
                            ink = kkp[:, 64 * h:128 + 64 * h]; inr = pst[:, 64 * h:128 + 64 * h]
                            S.op("pe", lambda e: e.transpose(pA[:, h, :], ink, self.ident[:]), reads=[kkp, self.ident], writes=[pA])
                            S.op("pe", lambda e: e.transpose(pR[:, h, :], inr, self.ident[:]), reads=[pst, self.ident], writes=[pR])
                    S.op("act", lambda e: e.activation(Ast[lo:hi, :, 4 * b:4 * b + 4].rearrange("p t h -> p h t"), pA[lo:hi, :, :], AF.Copy, scale=-1.0), reads=[pA], writes=[Ast])
                    S.op("act", lambda e: e.activation(Rst[lo:hi, :, 4 * b:4 * b + 4].rearrange("p t h -> p h t"), pR[lo:hi, :, :], AF.Copy), reads=[pR], writes=[Rst])
                    for d in range(2):
                        for h in range(4):
                            c0 = 64 + 256 * d + 64 * h
                            if b == 0:
                                S.op("pe", lambda e: e.transpose(pWt[d][0:64, h, :], decp[:, c0:c0 + 64], self.ident[:]), reads=[decp, self.ident], writes=[pWt[d]])
                            else:
                                S.op("pe", lambda e: e.transpose(pWt[d][:, h, :], decp[:, c0 - 64:c0 + 64], self.ident[:]), reads=[decp, self.ident], writes=[pWt[d]])
                        S.op("dve", lambda e: e.tensor_copy(Wst[d][lo:hi, :, :].rearrange("p t h -> p h t"), pWt[d][lo:hi, :, :]), reads=[pWt[d]], writes=[Wst[d]])
                    if 'D' in KSKIP:
                        continue
                    for d in range(2):
                        if 'E' in KSKIP:
                            continue
                        S.dma("sp", self.scr["LBs"][d, 4 * b:4 * b + 4, rows, lo:hi].rearrange("h t k -> t h k"), bd[:, 256 * d:256 * d + 256].rearrange("p (h k) -> p h k", k=64), reads=[bd])
                        S.dma("sp", self.scr["LKs"][d, 4 * b:4 * b + 4, rows, lo:hi].rearrange("h t k -> t h k"), kd16[:, 256 * d:256 * d + 256].rearrange("p (h k) -> p h k", k=64), reads=[kd16])
                    rv = self.scr["RVs"]
                    dst = bass.AP(rv.tensor, (4 * b) * T * 256 + r0 * 256, [[256, 128], [T * 256 + 64, 4], [1, 64]])
                    if 'F' not in KSKIP:
                        S.dma("sp", dst, v16[:].rearrange("p (h k) -> p h k", k=64), reads=[v16])
                S.dma("pool", self.scr["As"][:, rows, :], Ast[:], reads=[Ast])
                S.dma("pool", self.scr["Rs"][:, rows, :], Rst[:], reads=[Rst])
                for d in range(2):
                    S.dma("pool", self.scr["Ws"][d, :, rows, :], Wst[d][:], reads=[Wst[d]])

    def stage_scan(self, l):
        S, I = self.S, self.I
        with Scope(self) as sc:
            MJ = sc.sb([8, 256])
            S.dma("sp", MJ[:], I["c_maskJ"], writes=[MJ])
            St = [sc.sb([128, 256]) for _ in range(2)]
            S16 = [sc.sb([128, 256], BF16) for _ in range(2)]
            Tmp = [sc.sb([128, 256]) for _ in range(2)]
            SAm = [sc.sb([8, 256], BF16) for _ in range(2)]
            for d in range(2):
                S.op("dve", lambda e: e.memset(St[d][:], 0.0), writes=[St[d]])
                S.op("dve", lambda e: e.memset(S16[d][:], 0.0), writes=[S16[d]])
            pSA1 = [sc.ps([8, 512]) for _ in range(2)]; pU1 = [sc.ps([128, 512]) for _ in range(2)]; pY1 = [sc.ps([8, 512]) for _ in range(2)]
            pSA = [[TT(pSA1[d][:, 0:256]) for d in range(2)] for i in range(2)]
            pU = [[TT(pU1[d][:, 0:256]) for d in range(2)] for i in range(2)]
            pY = [[TT(pY1[d][:, 0:256]) for d in range(2)] for i in range(2)]
            for i in (1,):
                for d in range(2):
                    pSA[i][d] = pSA[0][d]; pU[i][d] = pU[0][d]; pY[i][d] = pY[0][d]
            NBUF = 2
            bufs = []
            for i in range(NBUF):
                bb = []
                for d in range(2):
                    bb.append(dict(A=sc.sb([128, CS, 8], BF16), R=sc.sb([128, CS, 8], BF16), W=sc.sb([128, CS, 4]),
                                   LB=sc.sb([8, CS, 128], BF16), LK=sc.sb([8, CS, 128], BF16), RV=sc.sb([8, CS, 256], BF16), Y=sc.sb([8, CS, 256])))
                bufs.append(bb)
            NCH = T // CS
            if 'Q' in KSKIP:
                NCH = int(os.environ.get('KNCH', '4'))

            def rowbase(d, c):
                if d == 0:
                    return c * CS
                t0 = c * CS
                if t0 < CT:
                    return CT - CS - t0
                return (T + CT - CS) - t0

            def load(c):
                bb = bufs[c % NBUF]
                for d in range(2):
                    ra = rowbase(d, c)
                    rs = slice(ra, ra + CS)
                    q = "sp"
                    B = bb[d]
                    S.dma(q, B["A"][:], self.scr["As"][:, rs, :], writes=[B["A"]])
                    S.dma(q, B["R"][:], self.scr["Rs"][:, rs, :], writes=[B["R"]])
                    S.dma(q, B["W"][:], self.scr["Ws"][d, :, rs, :], writes=[B["W"]])
                    S.dma(q, B["LB"][:], self.scr["LBs"][d, :, rs, :], writes=[B["LB"]])
                    S.dma(q, B["LK"][:], self.scr["LKs"][d, :, rs, :], writes=[B["LK"]])
                    S.dma(q, B["RV"][:], self.scr["RVs"][:, rs, :], writes=[B["RV"]])

            load(0)
            step = 0
            pend = []
            pend_store = None
            W3 = lambda B, i: B["W"][:, i, :].unsqueeze(2).to_broadcast([128, 4, 64])
            j4 = lambda t: t[:].rearrange("p (j v) -> p j v", j=4)
            hi16 = lambda t: t[:].bitcast(BF16).rearrange("p (n two) -> p n two", two=2)[:, :, 1]

            def flush_y():
                for (d, B, i, pi) in pend:
                    py = pY[pi][d]
                    S.op("pe", lambda e: e.matmul(py[:], B["R"][:, i, :], hi16(St[d]), start=True, stop=True), reads=[B["R"], St[d]], writes=[py])
                    S.op("act", lambda e: e.activation(B["Y"][:, i, :], py[:], AF.Copy), reads=[py], writes=[B["Y"]])
                pend.clear()

            for c in range(NCH):
                bb = bufs[c % NBUF]
                for s in range(CS):
                    pi = step % 2
                    idx = [s, CS - 1 - s]
                    for d in range(2):
                        psa = pSA[pi][d]
                        S.op("pe", lambda e: e.matmul(psa[:], bb[d]["A"][:, idx[d], :], hi16(St[d]), start=True, stop=True), reads=[bb[d]["A"], St[d]], writes=[psa])
                    flush_y()
                    if s == 0:
                        if pend_store is not None:
                            pc, pbb = pend_store
                            for d in range(2):
                                ra = rowbase(d, pc)
                                S.dma("pool", self.scr["Yfull"][d, :, ra:ra + CS, :], pbb[d]["Y"][:], reads=[pbb[d]["Y"]])
                            pend_store = None
                        if c + 1 < NCH:
                            load(c + 1)
                    for d in range(2):
                        psa = pSA[pi][d]
                        S.op("dve", lambda e: e.tensor_tensor(SAm[d][:], psa[:], MJ[:], ALU.mult), reads=[psa, MJ], writes=[SAm[d]])
                    for d in range(2):
                        S.op("pool", lambda e: e.tensor_tensor(j4(Tmp[d]), j4(St[d]), W3(bb[d], idx[d]), ALU.mult), reads=[St[d], bb[d]["W"]], writes=[Tmp[d]])
                    for d in range(2):
                        B = bb[d]; i = idx[d]; pu = pU[pi][d]
                        S.op("pe", lambda e: e.matmul(pu[:], B["LK"][:, i, :], B["RV"][:, i, :], start=True, stop=False), reads=[B["LK"], B["RV"]], writes=[pu])
                        S.op("pe", lambda e: e.matmul(pu[:], B["LB"][:, i, :], SAm[d][:], start=False, stop=True), reads=[B["LB"], SAm[d]], writes=[pu])
                    for d in range(2):
                        pu = pU[pi][d]
                        S.op("dve", lambda e: e.tensor_tensor(St[d][:], Tmp[d][:], pu[:], ALU.add), reads=[Tmp[d], pu], writes=[St[d]])
                    for d in range(2):
                        pend.append((d, bb[d], idx[d], pi))
                    if 'Y' in KSKIP:
                        flush_y()
                    step += 1
                pend_store = (c, bb)
            flush_y()
            pc, pbb = pend_store
            for d in range(2):
                ra = rowbase(d, pc)
                S.dma("pool", self.scr["Yfull"][d, :, ra:ra + CS, :], pbb[d]["Y"][:], reads=[pbb[d]["Y"]])

    def stage_rwpost(self, l):
        S, I = self.S, self.I
        with Scope(self) as sc:
            LNG = sc.sb([128, 256]); LNB = sc.sb([128, 256])
            S.dma("sp", LNG[:], bc(I["rwkv_ln_g"][l], 128), writes=[LNG])
            S.dma("sp", LNB[:], bc(I["rwkv_ln_b"][l], 128), writes=[LNB])
            yf = [sc.sb([128, 256]) for _ in range(2)]; yb = [sc.sb([128, 256]) for _ in range(2)]
            bo = [sc.sb([128, 256]) for _ in range(2)]; gg = [sc.sb([128, 256]) for _ in range(2)]
            y = sc.sb([128, 256]); sq = sc.sb([128, 256]); s1 = sc.sb([128, 4]); s2 = sc.sb([128, 4]); o = [sc.sb([128, 256]) for _ in range(2)]
            h3 = lambda t: t[:].rearrange("p (h w) -> p h w", w=64)
            b3 = lambda t: t[:].unsqueeze(2).to_broadcast([128, 4, 64])
            yfull = self.scr["Yfull"]
            it = 0
            for b in range(NB):
                for n in range(NT):
                    if self.last and n < 2:
                        continue
                    rows = slice(n * 128, (n + 1) * 128)
                    i = it % 2
                    for d, dstt in ((0, yf[i]), (1, yb[i])):
                        src = bass.AP(yfull.tensor, d * 8 * T * 256 + (4 * b) * T * 256 + n * 128 * 256, [[256, 128], [T * 256 + 64, 4], [1, 64]])
                        S.dma("sp", h3(dstt), src, writes=[dstt])
                    S.dma("sp", bo[i][:], self.scr["Bon"][b, rows, :], writes=[bo[i]])
                    S.dma("sp", gg[i][:], self.scr["Gt"][b, rows, :], writes=[gg[i]])
                    S.op("pool", lambda e: e.tensor_tensor(y[:], yf[i][:], yb[i][:], ALU.add), reads=[yf[i], yb[i]], writes=[y])
                    self.head_norm(y, sq, s1, s2, GN_EPS)
                    S.op("dve", lambda e: e.tensor_tensor(y[:], y[:], LNG[:], ALU.mult), reads=[y, LNG], writes=[y])
                    S.op("pool", lambda e: e.tensor_tensor(y[:], y[:], LNB[:], ALU.add), reads=[y, LNB], writes=[y])
                    S.op("pool", lambda e: e.tensor_tensor(y[:], y[:], bo[i][:], ALU.add), reads=[y, bo[i]], writes=[y])
                    S.op("dve", lambda e: e.tensor_tensor(o[i][:], y[:], gg[i][:], ALU.mult), reads=[y, gg[i]], writes=[o[i]])
                    S.dma("pool", self.scr["Ycat"][b, rows, 0:256], o[i][:], reads=[o[i]])
                    it += 1

    def head_norm(self, y, sq, s1, s2, eps, nh=4):
        S = self.S
        h3 = lambda t: t[:, 0:nh * 64].rearrange("p (h w) -> p h w", w=64)
        b3 = lambda t: t[:, 0:nh].unsqueeze(2).to_broadcast([128, nh, 64])
        S.op("dve", lambda e: e.tensor_reduce(s1[:, 0:nh], h3(y), AX.X, ALU.add), reads=[y], writes=[s1])
        S.op("dve", lambda e: e.tensor_scalar(s1[:, 0:nh], s1[:, 0:nh], -1.0 / 64, None, ALU.mult), reads=[s1], writes=[s1])
        S.op("dve", lambda e: e.tensor_tensor(h3(y), h3(y), b3(s1), ALU.add), reads=[y, s1], writes=[y])
        S.op("pool", lambda e: e.tensor_tensor(h3(sq), h3(y), h3(y), ALU.mult), reads=[y], writes=[sq])
        S.op("dve", lambda e: e.tensor_reduce(s2[:, 0:nh], h3(sq), AX.X, ALU.add), reads=[sq], writes=[s2])
        self.rpow(s2, s2[:, 0:nh], s2, s2[:, 0:nh], 1.0 / 64, eps, -0.5)
        S.op("dve", lambda e: e.tensor_tensor(h3(y), h3(y), b3(s2), ALU.mult), reads=[y, s2], writes=[y])

    def stage_attn(self, l):
        S, I = self.S, self.I
        lam_init = 0.8 - 0.6 * math.exp(-0.3 * l)
        with Scope(self) as sc:
            DL = sc.sb([128, 128]); dp = sc.sb([128, 128]); ds = sc.sb([128, 2]); lam = sc.sb([128, 1]); nlam = sc.sb([128, 1])
            S.dma("sp", DL[:], bc(I["diff_lambda"][l], 128), writes=[DL])
            S.op("dve", lambda e: e.tensor_tensor(dp[:, 0:32], DL[:, 0:32], DL[:, 32:64], ALU.mult), reads=[DL], writes=[dp])
            S.op("dve", lambda e: e.tensor_tensor(dp[:, 32:64], DL[:, 64:96], DL[:, 96:128], ALU.mult), reads=[DL], writes=[dp])
            S.op("dve", lambda e: e.tensor_reduce(ds[:], dp[:, 0:64].rearrange("p (a w) -> p a w", w=32), AX.X, ALU.add), reads=[dp], writes=[ds])
            S.op("act", lambda e: e.activation(ds[:], ds[:], AF.Exp), reads=[ds], writes=[ds])
            S.op("dve", lambda e: e.tensor_tensor(lam[:], ds[:, 0:1], ds[:, 1:2], ALU.subtract), reads=[ds], writes=[lam])
            S.op("dve", lambda e: e.tensor_scalar(nlam[:], lam[:], lam_init, -1.0, ALU.add, ALU.mult), reads=[lam], writes=[nlam])
            DN = sc.sb([128, 64])
            S.dma("sp", DN[:], bc(I["diff_norm"][l], 128), writes=[DN])
            S.op("dve", lambda e: e.tensor_scalar(DN[:], DN[:], 1.0 - lam_init, None, ALU.mult), reads=[DN], writes=[DN])
            KT = sc.sb([64, 4, T], BF16); QT = sc.sb([64, 4, T], BF16)
            KTd = sc.sb([32, 8, T], BF16); QTd = sc.sb([32, 8, T], BF16)
            V = sc.sb([128, NT, 4, 65], BF16)
            S.op("pool", lambda e: e.memset(V[:, :, :, 64:65], 1.0), writes=[V])
            pS = [sc.ps([128, 512]) for _ in range(2)]
            pO = [sc.ps([128, 4, 65]) for _ in range(4)]
            Pball = [sc.sb([128, NT, 512], BF16) for _ in range(2)]
            Mk = [sc.sb([128, 512], BF16) for _ in range(3)]
            rec = sc.sb([128, 4, 1]); o1 = sc.sb([128, 4, 64]); o2 = sc.sb([128, 4, 64])
            sq = sc.sb([128, 256]); s1 = sc.sb([128, 4]); s2 = sc.sb([128, 4])
            gate = sc.sb([128, 4, 64]); gs = sc.sb([128, 4, 64])
            osb = [sc.sb([128, 4, 64]) for _ in range(2)]
            Pd = self.scr["P"]
            cnt = {"u": 0, "o": 0, "p": 0}
            for m in "gdr":
                nk = 2 if m == "g" else 4
                nv = 2 if m == "g" else 4
                vcol = {"g": O_GQ + 384, "d": O_DF + 512, "r": O_RT + 512}[m]
                ocol = {"g": 256, "d": 512, "r": 768}[m]
                for b in range(NB):
                    if m == "d":
                        S.dma("sp", KTd[:], self.scr["KTd"][b], writes=[KTd])
                        S.dma("sp", QTd[:], self.scr["QTd"][b], writes=[QTd])
                    else:
                        S.dma("sp", KT[:, 0:nk, :], self.scr["KT" + m][b, :, 0:nk, :], writes=[KT])
                        S.dma("sp", QT[:], self.scr["QT" + m][b], writes=[QT])
                    for hv in range(nv):
                        S.dma("pool", V[:, :, hv, 0:64], Pd[b, :, vcol + hv * 64:vcol + hv * 64 + 64].rearrange("(n p) w -> p n w", p=128), writes=[V])
                    chunks = ([] if self.last else [(0, 256, 0, 2)]) + [(256 + 512 * q, 512, 0, NT) for q in range(4)]
                    for h in range(4):
                        for (q0, w, k0, k1) in chunks:
                            nj = w // 128
                            isx = q0 >= 256
                            units = [(0, 0)] if m != "d" else [(0, 0), (1, 0)]
                            pos = []
                            for (mi, rbase) in units:
                                po = pO[cnt["o"] % 4]; cnt["o"] += 1
                                pos.append(po)
                                rws = slice(rbase, rbase + (64 if m != "d" else 32))
                                hk = h // 2 if m == "g" else h
                                hvv = h // 2 if m == "g" else h
                                PB_ = Pball[cnt["o"] % 2]
                                for kt in range(k0, k1):
                                    ps_ = pS[cnt["u"] % 2]; cnt["u"] += 1
                                    pb = PB_[:, kt, :]
                                    if m == "d":
                                        S.op("pe", lambda e: e.matmul(ps_[:, :w], KTd[:, 2 * h + mi, kt * 128:(kt + 1) * 128], QTd[:, 2 * h + mi, q0:q0 + w], start=True, stop=True), reads=[KTd, QTd], writes=[ps_])
                                    else:
                                        S.op("pe", lambda e: e.matmul(ps_[:, :w], KT[rws, hk, kt * 128:(kt + 1) * 128], QT[rws, h, q0:q0 + w], start=True, stop=True), reads=[KT, QT], writes=[ps_])
                                    if m == "r":
                                        mk = Mk[cnt["p"] % 3]; cnt["p"] += 1
                                        if not isx:
                                            ti = {0: 15, 1: 14}[kt]
                                        elif kt < 2:
                                            ti = 28 + kt * 4 + (q0 - 256) // 512
                                        else:
                                            off = 4 * ((q0 - 256) // 512) - (kt - 2)
                                            ti = off + 15
                                        S.dma("sp", mk[:], I["c_retM"][ti, h], writes=[mk])
                                        S.op("dve", lambda e: e.tensor_tensor(pb[:, :w], ps_[:, :w], mk[:, :w], ALU.mult), reads=[ps_, mk], writes=[PB_])
                                    else:
                                        S.op("act", lambda e: e.activation(pb[:, :w], ps_[:, :w], AF.Exp), reads=[ps_], writes=[PB_])
                                for j in range(nj):
                                    for kt in range(k0, k1):
                                        S.op("pe", lambda e: e.matmul(po[:, j, :], PB_[:, kt, j * 128:(j + 1) * 128], V[:, kt, hvv, :], start=(kt == k0), stop=(kt == k1 - 1)), reads=[PB_, V], writes=[po])
                            ob = osb[cnt["o"] % 2]
                            dst = self.scr["Ycat"][b, q0:q0 + w, ocol + h * 64:ocol + h * 64 + 64].rearrange("(j t) v -> t j v", t=128)
                            if m == "g":
                                po = pos[0]
                                S.op("dve", lambda e: e.reciprocal(rec[:, 0:nj, :], po[:, 0:nj, 64:65]), reads=[po], writes=[rec])
                                S.op("dve", lambda e: e.tensor_tensor(ob[:, 0:nj, :], po[:, 0:nj, 0:64], rec[:, 0:nj, :].to_broadcast([128, nj, 64]), ALU.mult), reads=[po, rec], writes=[ob])
                            elif m == "d":
                                for (po, ot) in ((pos[0], o1), (pos[1], o2)):
                                    S.op("dve", lambda e: e.reciprocal(rec[:, 0:nj, :], po[:, 0:nj, 64:65]), reads=[po], writes=[rec])
                                    S.op("dve", lambda e: e.tensor_tensor(ot[:, 0:nj, :], po[:, 0:nj, 0:64], rec[:, 0:nj, :].to_broadcast([128, nj, 64]), ALU.mult), reads=[po, rec], writes=[ot])
                                S.op("dve", lambda e: e.scalar_tensor_tensor(o1[:, 0:nj, :], o2[:, 0:nj, :], nlam[:, 0:1], o1[:, 0:nj, :], ALU.mult, ALU.add), reads=[o1, o2, nlam], writes=[o1])
                                S.op("pool", lambda e: e.tensor_tensor(o2[:, 0:nj, :], o1[:, 0:nj, :], o1[:, 0:nj, :], ALU.mult), reads=[o1], writes=[o2])
                                S.op("dve", lambda e: e.tensor_reduce(s1[:, 0:nj], o2[:, 0:nj, :], AX.X, ALU.add), reads=[o2], writes=[s1])
                                self.rpow(s1, s1[:, 0:nj], s1, s1[:, 0:nj], 1.0 / 64, QK_EPS, -0.5)
                                S.op("dve", lambda e: e.tensor_tensor(o1[:, 0:nj, :], o1[:, 0:nj, :], s1[:, 0:nj].unsqueeze(2).to_broadcast([128, nj, 64]), ALU.mult), reads=[o1, s1], writes=[o1])
                                S.op("dve", lambda e: e.tensor_tensor(ob[:, 0:nj, :], o1[:, 0:nj, :], DN[:].unsqueeze(1).to_broadcast([128, nj, 64]), ALU.mult), reads=[o1, DN], writes=[ob])
                            else:
                                po = pos[0]
                                S.dma("sp", gate[:, 0:nj, :], Pd[b, q0:q0 + w, O_RT + 768 + h * 64:O_RT + 768 + h * 64 + 64].rearrange("(j t) v -> t j v", t=128), writes=[gate])
                                S.op("act", lambda e: e.activation(gs[:, 0:nj, :], gate[:, 0:nj, :], AF.Silu), reads=[gate], writes=[gs])
                                yv = TT(o1[:].rearrange("p j w -> p (j w)"))
                                S.op("act", lambda e: e.activation(o1[:, 0:nj, :], po[:, 0:nj, 0:64], AF.Copy), reads=[po], writes=[o1])
                                yv.lastw = o1.lastw
                                self.head_norm(yv, sq, s1, s2, LN_EPS, nh=nj)
                                o1.lastw = yv.lastw
                                S.op("dve", lambda e: e.tensor_tensor(ob[:, 0:nj, :], o1[:, 0:nj, :], gs[:, 0:nj, :], ALU.mult), reads=[o1, gs], writes=[ob])
                            S.dma("sp", dst, ob[:, 0:nj, :], reads=[ob])

    def layer_norm_out(self, sc, x1, G, Bt, out, tmps):
        S = self.S
        s1, s2, sq = tmps
        S.op("dve", lambda e: e.tensor_reduce(s1[:], x1[:], AX.X, ALU.add), reads=[x1], writes=[s1])
        S.op("dve", lambda e: e.tensor_scalar(s1[:], s1[:], -1.0 / D, None, ALU.mult), reads=[s1], writes=[s1])
        S.op("dve", lambda e: e.tensor_scalar(x1[:], x1[:], s1[:, 0:1], None, ALU.add), reads=[x1, s1], writes=[x1])
        S.op("pool", lambda e: e.tensor_tensor(sq[:], x1[:], x1[:], ALU.mult), reads=[x1], writes=[sq])
        S.op("dve", lambda e: e.tensor_reduce(s2[:], sq[:], AX.X, ALU.add), reads=[sq], writes=[s2])
        self.rpow(s2, s2[:], s2, s2[:], 1.0 / D, LN_EPS, -0.5)
        S.op("dve", lambda e: e.scalar_tensor_tensor(x1[:], x1[:], s2[:, 0:1], G[:], ALU.mult, ALU.mult), reads=[x1, s2, G], writes=[x1])
        S.op("pool", lambda e: e.tensor_tensor(out[:], x1[:], Bt[:], ALU.add), reads=[x1, Bt], writes=[out])

    def stage_mix(self, l):
        S, I = self.S, self.I
        with Scope(self) as sc:
            wo = sc.sb([128, 8, D], BF16)
            for kc in range(8):
                S.dma("pool", wo[:, kc, :], I["w_out"][l, kc * 128:(kc + 1) * 128, :], writes=[wo])
            G1 = sc.sb([128, 3, D])
            for v in range(3):
                S.dma("sp", G1[:, v, :], bc(self.scr["modD"][v, 2 * D:3 * D], 128), writes=[G1])
            PG = sc.sb([128, D]); PB = sc.sb([128, D])
            S.dma("sp", PG[:], bc(I["post1_g"][l], 128), writes=[PG])
            S.dma("sp", PB[:], bc(I["post1_b"][l], 128), writes=[PB])
            yc = [sc.sb([128, D]) for _ in range(2)]; xt = [sc.sb([128, D]) for _ in range(2)]
            pT = [sc.ps([128, 128]) for _ in range(2)]
            yT = [sc.sb([128, 8, 128], BF16) for _ in range(2)]
            pO = [sc.ps([128, 512]) for _ in range(2)]
            t = sc.sb([128, D]); x1 = sc.sb([128, D]); outt = [sc.sb([128, D]) for _ in range(2)]
            s1 = sc.sb([128, 1]); s2 = sc.sb([128, 1]); sq = sc.sb([128, D])
            it = 0
            for b in range(NB):
                for n in range(NT):
                    if self.last and n < 2:
                        continue
                    v = 2 if n < 2 else b
                    rows = slice(n * 128, (n + 1) * 128)
                    i = it % 2
                    S.dma("sp", yc[i][:], self.scr["Ycat"][b, rows, :], writes=[yc[i]])
                    S.dma("sp", xt[i][:], self.Xsrc[b, rows, :], writes=[xt[i]])
                    for kc in range(8):
                        pp = pT[kc % 2]
                        S.op("pe", lambda e: e.transpose(pp[:], yc[i][:, kc * 128:(kc + 1) * 128], self.ident[:]), reads=[yc[i], self.ident], writes=[pp])
                        S.op("act", lambda e: e.activation(yT[i][:, kc, :], pp[:], AF.Copy), reads=[pp], writes=[yT[i]])
                    for c in range(2):
                        po = pO[c]
                        for kc in range(8):
                            S.op("pe", lambda e: e.matmul(po[:], yT[i][:, kc, :], wo[:, kc, c * 512:(c + 1) * 512], start=(kc == 0), stop=(kc == 7)), reads=[yT[i], wo], writes=[po])
                        S.op("dve", lambda e: e.tensor_tensor(t[:, c * 512:(c + 1) * 512], po[:], G1[:, v, c * 512:(c + 1) * 512], ALU.mult), reads=[po, G1], writes=[t])
                    S.op("dve", lambda e: e.scalar_tensor_tensor(x1[:], xt[i][:], ALPHA, t[:], ALU.mult, ALU.add), reads=[xt[i], t], writes=[x1])
                    self.layer_norm_out(sc, x1, PG, PB, outt[i], (s1, s2, sq))
                    S.dma("pool", self.scr["X1s"][b, rows, :], outt[i][:], reads=[outt[i]])
                    it += 1

    def stage_ffn(self, l):
        S, I = self.S, self.I
        with Scope(self) as sc:
            w1 = sc.sb([128, 8, 2 * DFF], BF16, "w1")
            for kc in range(8):
                S.dma("pool", w1[:, kc, :], I["ffn_w_in"][l, kc * 128:(kc + 1) * 128, :], writes=[w1])
            w2 = sc.sb([128, NFF, D], BF16, "w2")
            for fc in range(NFF):
                S.dma("pool", w2[:, fc, :], I["ffn_w_out"][l, fc * 128:(fc + 1) * 128, :], writes=[w2])
            G2 = sc.sb([128, D])
            PG = sc.sb([128, D]); PB = sc.sb([128, D])
            S.dma("sp", PG[:], bc(I["post2_g"][l], 128), writes=[PG])
            S.dma("sp", PB[:], bc(I["post2_b"][l], 128), writes=[PB])
            xt = [sc.sb([128, D]) for _ in range(2)]
            pT = [sc.ps([128, 128]) for _ in range(2)]
            x1T = sc.sb([128, 8, 512], BF16)
            aT = sc.sb([128, NFF, 512], BF16)
            pU = [sc.ps([128, 512]) for _ in range(2)]; pGt = [sc.ps([128, 512]) for _ in range(2)]
            su = [sc.sb([128, 512]) for _ in range(2)]
            pF = [sc.ps([128, 512]) for _ in range(2)]
            t = sc.sb([128, D]); x2 = sc.sb([128, D]); outt = [sc.sb([128, D]) for _ in range(2)]
            s1 = sc.sb([128, 1]); s2 = sc.sb([128, 1]); sq = sc.sb([128, D])
            it = 0
            for b in range(NB):
                groups = ([] if self.last else [(0, 2, 2)]) + [(2 + 4 * q, 4, b) for q in range(4)]
                for (n0, nt, v) in groups:
                    w = nt * 128
                    S.dma("sp", G2[:], bc(self.scr["modD"][v, 5 * D:6 * D], 128), writes=[G2])
                    for j in range(nt):
                        rows = slice((n0 + j) * 128, (n0 + j + 1) * 128)
                        x = xt[it % 2]; it += 1
                        S.dma("sp", x[:], self.scr["X1s"][b, rows, :], writes=[x])
                        self.xT_modulated(sc, x, pT, x1T, j * 128, 128, v, 1)
                    for fc in range(NFF):
                        pu = pU[fc % 2]; pg = pGt[fc % 2]; s_ = su[fc % 2]
                        for kc in range(8):
                            S.op("pe", lambda e: e.matmul(pu[:, :w], w1[:, kc, fc * 128:(fc + 1) * 128], x1T[:, kc, :w], start=(kc == 0), stop=(kc == 7)), reads=[w1, x1T], writes=[pu])
                        for kc in range(8):
                            S.op("pe", lambda e: e.matmul(pg[:, :w], w1[:, kc, DFF + fc * 128:DFF + (fc + 1) * 128], x1T[:, kc, :w], start=(kc == 0), stop=(kc == 7)), reads=[w1, x1T], writes=[pg])
                        S.op("act", lambda e: e.activation(s_[:, :w], pu[:, :w], AF.Silu), reads=[pu], writes=[s_])
                        S.op("dve", lambda e: e.tensor_tensor(aT[:, fc, :w], s_[:, :w], pg[:, :w], ALU.mult), reads=[s_, pg], writes=[aT])
                    for j in range(nt):
                        rows = slice((n0 + j) * 128, (n0 + j + 1) * 128)
                        x = xt[it % 2]; it += 1
                        S.dma("sp", x[:], self.scr["X1s"][b, rows, :], writes=[x])
                        for c in range(2):
                            pf = pF[c]
                            for fc in range(NFF):
                                S.op("pe", lambda e: e.matmul(pf[:], aT[:, fc, j * 128:(j + 1) * 128], w2[:, fc, c * 512:(c + 1) * 512], start=(fc == 0), stop=(fc == NFF - 1)), reads=[aT, w2], writes=[pf])
                            S.op("dve", lambda e: e.tensor_tensor(t[:, c * 512:(c + 1) * 512], pf[:], G2[:, c * 512:(c + 1) * 512], ALU.mult), reads=[pf, G2], writes=[t])
                        S.op("dve", lambda e: e.scalar_tensor_tensor(x2[:], x[:], ALPHA, t[:], ALU.mult, ALU.add), reads=[x, t], writes=[x2])
                        o = outt[j % 2]
                        self.layer_norm_out(sc, x2, PG, PB, o, (s1, s2, sq))
                        if self.last:
                            xr = (n0 + j) * 128 - CT
                            S.dma("pool", self.out[b, xr:xr + 128, :], o[:], reads=[o])
                        else:
                            S.dma("pool", self.scr["Xs"][b, rows, :], o[:], reads=[o])


def host_consts():
    c = {}
    c["c_ident"] = np.eye(128, dtype=np.float32)
    t = np.arange(SX)
    row = (t // 64).astype(np.float32)
    col = (t % 64).astype(np.float32)
    pos = t.astype(np.float32)

    def tab(p, n):
        half = n // 2
        inv = np.power(np.float32(10000.0), -(np.arange(half, dtype=np.float32) * np.float32(2.0) / np.float32(n))).astype(np.float32)
        ang = (p[:, None] * inv[None, :]).astype(np.float32)
        cs, sn = np.cos(ang).astype(np.float32), np.sin(ang).astype(np.float32)
        return np.concatenate([cs, cs], 1), np.concatenate([-sn, sn], 1)

    def axial(n):
        h = n // 2
        c1, s1 = tab(row, h)
        c2, s2 = tab(col, h)
        return np.stack([np.concatenate([c1, c2], 1), np.concatenate([s1, s2], 1)], 1).astype(np.float32)

    c["c_ropeG"] = axial(64)
    c["c_ropeD"] = axial(32)
    cr, sr = tab(pos, 64)
    c["c_ropeR"] = np.stack([cr, sr], 1).astype(np.float32)
    mj = np.zeros((8, 256), np.float32)
    for m in range(8):
        h = m % 4
        mj[m, h * 64:(h + 1) * 64] = 1.0
    c["c_maskJ"] = mj
    c["c_zero"] = np.zeros((128, 4096), ml_dtypes.bfloat16)
    gf = 1.0 - 2.0 ** (-5.0 - np.arange(4, dtype=np.float64))
    gb = gf[::-1]
    M = np.zeros((36, 4, 128, 512), np.float64)
    p = np.arange(128)[:, None]
    j = np.arange(512)[None, :]
    for h in range(4):
        lf, lb = math.log(gf[h]), math.log(gb[h])
        for off in range(-15, 13):
            dlt = (128 * off + j - p).astype(np.float64)
            m = np.where(dlt > 0, np.exp(lf * np.maximum(dlt, 0)), 0.0) + np.where(dlt < 0, np.exp(lb * np.maximum(-dlt, 0)), 0.0) + np.where(dlt == 0, 2.0, 0.0)
            M[off + 15, h] = m
        for kc in range(2):
            for qc in range(4):
                cc = 128 * kc + p
                ii = 512 * qc + j
                M[28 + kc * 4 + qc, h] = np.exp(lf * (256 + ii - cc)) + np.exp(lb * (2048 - ii + cc))
    c["c_retM"] = M.astype(np.float32).astype(ml_dtypes.bfloat16)
    return c


_CACHE = {}


def kernel(**inputs):
    f = lambda k: np.ascontiguousarray(np.asarray(inputs[k], dtype=np.float32))
    L = DEPTH
    shared = {}
    for k in ("ada_w", "ada_b", "w_in", "rwkv_mu", "rwkv_g_up", "rwkv_k_k", "rwkv_k_a", "rwkv_ln_g", "rwkv_ln_b",
              "gqa_q_norm", "gqa_k_norm", "diff_norm", "w_out", "post1_g", "post1_b", "ffn_w_in", "ffn_w_out", "post2_g", "post2_b"):
        shared[k] = f(k)
    shared["rwkv_w0"] = f("rwkv_w0").reshape(L, 512)
    shared["rwkv_a0"] = f("rwkv_a0").reshape(L, 512)
    shared["rwkv_w_up"] = f("rwkv_w_up").reshape(L, 64, 256)
    shared["rwkv_a_up"] = f("rwkv_a_up").reshape(L, 64, 256)
    shared["rwkv_r_k"] = f("rwkv_r_k").reshape(L, 256)
    shared["diff_lambda"] = f("diff_lambda").reshape(L, 128)
    shared.update(host_consts())
    x, c, ctx, c_ctx = f("x"), f("c"), f("ctx"), f("c_ctx")
    in_maps = []
    for core in range(8):
        bs = slice(core * NB, (core + 1) * NB)
        m = dict(shared)
        m["xin"] = np.ascontiguousarray(np.concatenate([ctx[bs], x[bs]], axis=1))
        m["cvec"] = np.ascontiguousarray(np.concatenate([c[bs], c_ctx[None, :]], axis=0))
        in_maps.append(m)
    if "nc" not in _CACHE:
        _CACHE["nc"] = KB().build()
    res = run_bass_kernel_spmd(_CACHE["nc"], in_maps, core_ids=list(range(8)))
    return np.concatenate([r["y"] for r in res.results], axis=0).astype(np.float32)
```

```python
import math, os
KSKIP = os.environ.get('KSKIP', '')
import numpy as np
import ml_dtypes
import concourse.bass as bass
import concourse.mybir as mybir
from concourse.bass_utils import run_bass_kernel_spmd
from contextlib import ExitStack

F32 = mybir.dt.float32
BF16 = mybir.dt.bfloat16
AF = mybir.ActivationFunctionType
ALU = mybir.AluOpType
AX = mybir.AxisListType

D = 1024
NB = 2
CT = 256
SX = 2048
T = CT + SX
NT = T // 128
INW = 3264
DFF = 2816
NFF = DFF // 128
DEPTH = 4
ALPHA = (2 * DEPTH) ** 0.25
LN_EPS = 1e-5
QK_EPS = 1e-6
GN_EPS = 64e-5
CS = 16
O_RW, O_GQ, O_DF, O_RT = 0, 960, 1472, 2240


class Sched:
    def __init__(self, nc, es):
        self.nc = nc
        self.es = es
        self.eng = {"pe": nc.tensor, "act": nc.scalar, "dve": nc.vector, "pool": nc.gpsimd, "sp": nc.sync}
        self.sem = {}
        self.cnt = {}
        self.nsem = 0
        for k in self.eng:
            self.sem[k] = self._newsem()
            self.cnt[k] = 0
        self.waited = {k: {} for k in self.eng}
        self.NSLOT = 8
        self.dsem = {}
        self.dcnt = {}
        self.dnext = {}
        for q in ("sp", "pool", "act"):
            self.dsem[q] = [self._newsem() for i in range(self.NSLOT)]
            self.dcnt[q] = [0] * self.NSLOT
            self.dnext[q] = 0
        self.ninstr = 0

    def _newsem(self):
        self.nsem += 1
        return self.es.enter_context(self.nc.semaphore("sem%d" % self.nsem))

    def _wait(self, e, tok, raw=False):
        if tok is None:
            return
        sem, val, owner = tok
        if owner == e and (e == "pe" or not raw):
            return
        key = id(sem)
        w = self.waited[e]
        if w.get(key, 0) >= val:
            return
        self.eng[e].wait_ge(sem, val)
        w[key] = val

    def deps(self, e, reads, writes):
        for t in reads:
            self._wait(e, t.lastw, raw=True)
        for t in writes:
            self._wait(e, t.lastw)
            for r in t.readers.values():
                self._wait(e, r)

    def done(self, tok, reads, writes):
        for t in reads:
            t.readers[tok[2] + str(id(tok[0]))] = tok
        for t in writes:
            t.lastw = tok
            t.readers = {}

    def op(self, e, fn, reads=(), writes=()):
        self.deps(e, reads, writes)
        ins = fn(self.eng[e])
        self.cnt[e] += 1
        ins.then_inc(self.sem[e], 1)
        tok = (self.sem[e], self.cnt[e], e)
        self.done(tok, reads, writes)
        self.ninstr += 1
        return tok

    def dma(self, q, out, in_, reads=(), writes=(), **kw):
        s = self.dnext[q]
        self.dnext[q] = (s + 1) % self.NSLOT
        sem = self.dsem[q][s]
        if self.dcnt[q][s] > 0:
            self._wait(q, (sem, 16 * self.dcnt[q][s], "dma"))
        self.deps(q, reads, writes)
        ins = self.eng[q].dma_start(out=out, in_=in_, **kw)
        self.dcnt[q][s] += 1
        ins.then_inc(sem, 16)
        tok = (sem, 16 * self.dcnt[q][s], "dma")
        self.done(tok, reads, writes)
        self.ninstr += 1
        return tok

    def barrier(self):
        toks = []
        for k in self.eng:
            if self.cnt[k] > 0:
                toks.append((self.sem[k], self.cnt[k], k))
        for q in self.dsem:
            for s in range(self.NSLOT):
                if self.dcnt[q][s] > 0:
                    toks.append((self.dsem[q][s], 16 * self.dcnt[q][s], "dma"))
        for e in self.eng:
            for t in toks:
                self._wait(e, t)
        for k in self.eng:
            if self.cnt[k] > 12000:
                self.sem[k] = self._newsem()
                self.cnt[k] = 0
        for q in self.dsem:
            for s in range(self.NSLOT):
                if self.dcnt[q][s] > 1500:
                    self.dsem[q][s] = self._newsem()
                    self.dcnt[q][s] = 0


class TT:
    def __init__(self, ap):
        self.ap = ap
        self.lastw = None
        self.readers = {}

    def __getitem__(self, k):
        return self.ap[k]


class Scope:
    def __init__(self, kb):
        self.kb = kb
        self.es = ExitStack()

    def __enter__(self):
        self.es.__enter__()
        return self

    def __exit__(self, *a):
        self.kb.S.barrier()
        return self.es.__exit__(*a)

    def sb(self, shape, dt=F32, name="t"):
        self.kb.uid += 1
        return TT(self.es.enter_context(self.kb.nc.sbuf_tensor("%s_%d" % (name, self.kb.uid), list(shape), dt)))

    def ps(self, shape, dt=F32, name="p"):
        self.kb.uid += 1
        assert dt == F32
        shape = list(shape)
        free = int(np.prod(shape[1:]))
        assert free <= 512
        t = self.es.enter_context(self.kb.nc.psum_tensor("%s_%d" % (name, self.kb.uid), [128, 512], dt))
        v = t[0:shape[0], 0:free]
        if len(shape) == 3:
            v = v.rearrange("p (a b) -> p a b", a=shape[1])
        return TT(v)


def bc(ap, n):
    return ap.partition_broadcast(n)


class KB:
    def __init__(self, NL=DEPTH, dbg=(), stages=None):
        self.NL = NL
        self.dbg = set(dbg)
        self.stages = stages
        self.uid = 0
        nc = self.nc = bass.Bass("TRN2", target_bir_lowering=False)
        self.I = {}
        self.scr = {}

        def inp(name, shape, dt=F32):
            self.I[name] = nc.dram_tensor(name, list(shape), dt, kind="ExternalInput").ap()

        inp("xin", [NB, T, D])
        inp("cvec", [3, D])
        L = DEPTH
        inp("ada_w", [L, D, 6 * D]); inp("ada_b", [L, 6 * D]); inp("w_in", [L, D, INW])
        inp("rwkv_mu", [L, 960]); inp("rwkv_w0", [L, 512]); inp("rwkv_w_up", [L, 64, 256])
        inp("rwkv_a0", [L, 512]); inp("rwkv_a_up", [L, 64, 256]); inp("rwkv_g_up", [L, 64, 256])
        inp("rwkv_k_k", [L, 256]); inp("rwkv_k_a", [L, 256]); inp("rwkv_r_k", [L, 256])
        inp("rwkv_ln_g", [L, 256]); inp("rwkv_ln_b", [L, 256])
        inp("gqa_q_norm", [L, 64]); inp("gqa_k_norm", [L, 64]); inp("diff_lambda", [L, 128]); inp("diff_norm", [L, 64])
        inp("w_out", [L, D, D]); inp("post1_g", [L, D]); inp("post1_b", [L, D])
        inp("ffn_w_in", [L, D, 2 * DFF]); inp("ffn_w_out", [L, DFF, D]); inp("post2_g", [L, D]); inp("post2_b", [L, D])
        inp("c_ident", [128, 128]); inp("c_ropeG", [SX, 2, 64]); inp("c_ropeD", [SX, 2, 32]); inp("c_ropeR", [SX, 2, 64])
        inp("c_maskJ", [8, 256]); inp("c_retM", [36, 4, 128, 512], BF16); inp("c_zero", [128, 4096], BF16)
        self.out = nc.dram_tensor("y", [NB, SX, D], F32, kind="ExternalOutput").ap()

        def scr(name, shape, dt=F32):
            kind = "ExternalOutput" if name in self.dbg else "Internal"
            self.scr[name] = nc.dram_tensor(name, list(shape), dt, kind=kind).ap()

        scr("P", [NB, T, INW]); scr("Xs", [NB, T, D]); scr("X1s", [NB, T, D]); scr("modD", [3, 6 * D])
        for m in "gr":
            scr("QT" + m, [NB, 64, 4, T], BF16)
            scr("KT" + m, [NB, 64, 4, T], BF16)
        scr("QTd", [NB, 32, 8, T], BF16)
        scr("KTd", [NB, 32, 8, T], BF16)
        scr("As", [128, T, 8], BF16); scr("Rs", [128, T, 8], BF16); scr("Ws", [2, 128, T, 4])
        scr("LBs", [2, 8, T, 128], BF16); scr("LKs", [2, 8, T, 128], BF16); scr("RVs", [8, T, 256], BF16); scr("Yfull", [2, 8, T, 256])
        scr("Gt", [NB, T, 256]); scr("Bon", [NB, T, 256]); scr("Ycat", [NB, T, D])

    def build(self):
        nc = self.nc
        with ExitStack() as es:
            self.S = S = Sched(nc, es)
            with Scope(self) as g:
                self.g = g
                self.ident = g.sb([128, 128], F32, "ident")
                S.dma("sp", self.ident[:], self.I["c_ident"], writes=[self.ident])
                self.siluT = g.sb([128, 8, 3], F32, "siluT")
                self.cvals = [64.0 * QK_EPS, GN_EPS, LN_EPS, QK_EPS, 0.0]
                self.cst = g.sb([128, 8], F32, "cst")
                for i_, v_ in enumerate(self.cvals):
                    S.op("dve", lambda e: e.memset(self.cst[:, i_:i_ + 1], float(v_)), writes=[self.cst])
                self.stage_init()
                for l in range(self.NL):
                    self.layer(l)
                S.barrier()
            print("ninstr", S.ninstr, "nsem", S.nsem)
        return nc

    def rpow(self, ot, o_ap, it, i_ap, scale, bias, p):
        S = self.S
        assert p == -0.5
        bcol = self.cvals.index(float(bias))
        np_ = o_ap.shape[0]
        S.op("act", lambda e: e.activation(o_ap, i_ap, AF.Sqrt, bias=self.cst[0:np_, bcol:bcol + 1], scale=float(scale)), reads=[it, self.cst], writes=[ot])
        S.op("dve", lambda e: e.reciprocal(o_ap, o_ap), reads=[ot], writes=[ot])

    def on(self, name):
        return self.stages is None or name in self.stages

    def stage_init(self):
        S, I = self.S, self.I
        with Scope(self) as sc:
            cv = sc.sb([3, D])
            S.dma("sp", cv[:], I["cvec"], writes=[cv])
            sv = sc.sb([3, D])
            S.op("act", lambda e: e.activation(sv[:], cv[:], AF.Silu), reads=[cv], writes=[sv])
            pT = sc.ps([128, 8, 3])
            for kc in range(8):
                S.op("pe", lambda e: e.transpose(pT[:, kc, :], sv[:, kc * 128:(kc + 1) * 128], self.ident[0:3, 0:3]), reads=[sv, self.ident], writes=[pT])
            S.op("dve", lambda e: e.tensor_copy(self.siluT[:], pT[:]), reads=[pT], writes=[self.siluT])
            z = sc.sb([128, 4096], BF16)
            S.dma("sp", z[:], I["c_zero"], writes=[z])
            for name in ("LBs", "LKs"):
                flat = self.scr[name].rearrange("d m t k -> (d m t k)").rearrange("(a p f) -> a p f", p=128, f=4096)
                for a in range(flat.shape[0]):
                    S.dma("sp", flat[a], z[:], reads=[z])
            flat = self.scr["RVs"].rearrange("m t k -> (m t k)").rearrange("(a p f) -> a p f", p=128, f=4096)
            for a in range(flat.shape[0]):
                S.dma("sp", flat[a], z[:], reads=[z])

    def layer(self, l):
        self.l = l
        self.last = (l == DEPTH - 1)
        self.Xsrc = self.I["xin"] if l == 0 else self.scr["Xs"]
        if self.on("mod"):
            self.stage_mod(l)
        with Scope(self) as ls:
            self.ls = ls
            S = self.S
            self.modT = ls.sb([128, 48, 3], F32, "modT")
            for v in range(3):
                S.dma("sp", self.modT[:, :, v], self.scr["modD"][v].rearrange("(c p) -> p c", p=128), writes=[self.modT], allow_slow_non_contiguous=True)
            for c0 in (8, 32):
                S.op("dve", lambda e: e.tensor_scalar(self.modT[:, c0:c0 + 8, :], self.modT[:, c0:c0 + 8, :], 1.0, None, ALU.add), reads=[self.modT], writes=[self.modT])
            if self.on("inproj"):
                self.stage_inproj(l)
            if self.on("rwprep"):
                self.stage_rwprep(l)
            if self.on("scan"):
                self.stage_scan(l)
            if self.on("rwpost"):
                self.stage_rwpost(l)
            if self.on("attn"):
                self.stage_attn(l)
            if self.on("mix"):
                self.stage_mix(l)
            if self.on("ffn"):
                self.stage_ffn(l)

    def stage_mod(self, l):
        S, I = self.S, self.I
        with Scope(self) as sc:
            adab = sc.sb([3, 6 * D])
            S.dma("sp", adab[:], bc(I["ada_b"][l], 3), writes=[adab])
            modsb = sc.sb([3, 6 * D])
            wb = [sc.sb([128, 8, 512]) for _ in range(2)]
            pm = [sc.ps([3, 512]) for _ in range(2)]
            for n in range(12):
                w = wb[n % 2]
                S.dma("sp" if n % 2 == 0 else "pool", w[:], I["ada_w"][l, :, n * 512:(n + 1) * 512].rearrange("(c p) n -> p c n", p=128), writes=[w])
                p = pm[n % 2]
                for kc in range(8):
                    S.op("pe", lambda e: e.matmul(p[:], self.siluT[:, kc, :], w[:, kc, :], start=(kc == 0), stop=(kc == 7)), reads=[self.siluT, w], writes=[p])
                S.op("dve", lambda e: e.tensor_tensor(modsb[:, n * 512:(n + 1) * 512], p[:], adab[:, n * 512:(n + 1) * 512], ALU.add), reads=[p, adab], writes=[modsb])
            S.dma("sp", self.scr["modD"], modsb[:], reads=[modsb])

    def xT_modulated(self, sc, src_tile, pT, xmT, col0, width, v, which):
        S = self.S
        shc, scc = (0, 8) if which == 0 else (24, 32)
        for kc in range(8):
            pp = pT[kc % 2]
            S.op("pe", lambda e: e.transpose(pp[:], src_tile[:, kc * 128:(kc + 1) * 128], self.ident[:]), reads=[src_tile, self.ident], writes=[pp])
            eng = "dve" if kc % 2 == 0 else "pool"
            eng = "dve"
            S.op(eng, lambda e: e.tensor_scalar(xmT[:, kc, col0:col0 + 128], pp[:], self.modT[:, scc + kc, v:v + 1], self.modT[:, shc + kc, v:v + 1], ALU.mult, ALU.add), reads=[pp, self.modT], writes=[xmT])

    def rope(self, eng2, out, src, tab, nh, nblk, half, tmp1, tmp2, reads, tabT):
        S = self.S
        w = nblk * 2 * half
        Cb = tab[:, 0, :].unsqueeze(1).to_broadcast([128, nh, w])
        S.op("dve", lambda e: e.tensor_tensor(tmp1, src, Cb, ALU.mult), reads=reads + [tabT], writes=[self._t1])
        v5 = lambda ap: ap.rearrange("p h (b two f) -> p h b two f", b=nblk, two=2)
        sgv = tab[:, 1, :].rearrange("p (b two f) -> p b two f", b=nblk, two=2)
        for hf in range(2):
            o = v5(tmp2)[:, :, :, hf, :]
            i = v5(src)[:, :, :, 1 - hf, :]
            sg = sgv[:, :, hf, :].unsqueeze(1).to_broadcast([128, nh, nblk, half])
            S.op(eng2, lambda e: e.tensor_tensor(o, i, sg, ALU.mult), reads=reads + [tabT], writes=[self._t2])
        S.op("dve", lambda e: e.tensor_tensor(out, tmp1, tmp2, ALU.add), reads=[self._t1, self._t2], writes=[self._ro])

    def stage_inproj(self, l):
        S, I = self.S, self.I
        with Scope(self) as sc:
            win = sc.sb([128, 8, INW], BF16, "win")
            for kc in range(8):
                S.dma("pool", win[:, kc, :], I["w_in"][l, kc * 128:(kc + 1) * 128, :], writes=[win])
            rG = sc.sb([128, 16, 2, 64]); rD = sc.sb([128, 16, 2, 32]); rR = sc.sb([128, 16, 2, 64])
            S.dma("sp", rG[:], I["c_ropeG"].rearrange("(n p) a w -> p n a w", p=128), writes=[rG])
            S.dma("sp", rD[:], I["c_ropeD"].rearrange("(n p) a w -> p n a w", p=128), writes=[rD])
            S.dma("sp", rR[:], I["c_ropeR"].rearrange("(n p) a w -> p n a w", p=128), writes=[rR])
            GW = sc.sb([128, 6, 64])
            for h in range(4):
                S.dma("sp", GW[:, h, :], bc(I["gqa_q_norm"][l], 128), writes=[GW])
            for h in range(4, 6):
                S.dma("sp", GW[:, h, :], bc(I["gqa_k_norm"][l], 128), writes=[GW])
            S.op("dve", lambda e: e.tensor_scalar(GW[:, 4:6, :], GW[:, 4:6, :], 8.0, None, ALU.mult), reads=[GW], writes=[GW])
            xt = [sc.sb([128, D]) for _ in range(2)]
            pT = [sc.ps([128, 128]) for _ in range(2)]
            xmT = [sc.sb([128, 8, 128], BF16) for _ in range(2)]
            pP = [sc.ps([128, 512]) for _ in range(2)]
            Psb = [sc.sb([128, INW]) for _ in range(2)]
            pQ = [sc.ps([64, 4, 128]) for _ in range(2)]
            sq = sc.sb([128, 6, 64]); ss = sc.sb([128, 6]); r1 = sc.sb([128, 6]); qn = sc.sb([128, 6, 64])
            t1 = sc.sb([128, 512]); t2 = sc.sb([128, 512]); ro = sc.sb([128, 512])
            self._t1, self._t2, self._ro = t1, t2, ro
            QTs = [sc.sb([64, 4, 128], BF16) for _ in range(6)]
            it = 0
            for b in range(NB):
                for n in range(NT):
                    v = 2 if n < 2 else b
                    isx = n >= 2
                    rows = slice(n * 128, (n + 1) * 128)
                    x = xt[it % 2]; xm = xmT[it % 2]; P = Psb[it % 2]
                    S.dma("sp", x[:], self.Xsrc[b, rows, :], writes=[x])
                    self.xT_modulated(sc, x, pT, xm, 0, 128, v, 0)
                    for c in range(7):
                        c0 = c * 512
                        w = min(512, INW - c0)
                        pp = pP[c % 2]
                        for kc in range(8):
                            S.op("pe", lambda e: e.matmul(pp[:, :w], xm[:, kc, :], win[:, kc, c0:c0 + w], start=(kc == 0), stop=(kc == 7)), reads=[xm, win], writes=[pp])
                        S.op("act", lambda e: e.activation(P[:, c0:c0 + w], pp[:, :w], AF.Copy), reads=[pp], writes=[P])
                    S.dma("pool", self.scr["P"][b, rows, :], P[:], reads=[P])
                    qk = P[:, O_GQ:O_GQ + 384].rearrange("p (h w) -> p h w", w=64)
                    S.op("pool", lambda e: e.tensor_tensor(sq[:], qk, qk, ALU.mult), reads=[P], writes=[sq])
                    S.op("dve", lambda e: e.tensor_reduce(ss[:], sq[:], AX.X, ALU.add), reads=[sq], writes=[ss])
                    self.rpow(r1, r1[:], ss, ss[:], 1.0, 64.0 * QK_EPS, -0.5)
                    S.op("dve", lambda e: e.tensor_tensor(qn[:], qk, r1[:].unsqueeze(2).to_broadcast([128, 6, 64]), ALU.mult), reads=[P, r1], writes=[qn])
                    S.op("dve", lambda e: e.tensor_tensor(qn[:], qn[:], GW[:], ALU.mult), reads=[qn, GW], writes=[qn])
                    v3 = lambda t, nh, w: t[:, 0:nh * w].rearrange("p (h w) -> p h w", w=w)
                    if isx:
                        self.rope("pool", v3(ro, 6, 64), qn[:], rG[:, n - 2], 6, 2, 16, v3(t1, 6, 64), v3(t2, 6, 64), [qn], rG)
                        src, srcT = v3(ro, 6, 64), ro
                    else:
                        src, srcT = qn[:], qn
                    self.emit_T(src, srcT, 6, pQ, QTs, [("QTg", 0, 4, 1.0), ("KTg", 4, 2, 1.0)], b, rows)
                    dq = P[:, O_DF:O_DF + 512].rearrange("p (h w) -> p h w", w=32)
                    if isx:
                        self.rope("pool", v3(ro, 16, 32), dq, rD[:, n - 2], 16, 2, 8, v3(t1, 16, 32), v3(t2, 16, 32), [P], rD)
                        src, srcT = v3(ro, 16, 32), ro
                    else:
                        src, srcT = dq, P
                    self.emit_T(src, srcT, 16, pQ, QTs, [("QTd", 0, 8, 32 ** -0.5), ("KTd", 8, 8, 1.0)], b, rows, width=32)
                    rq = P[:, O_RT:O_RT + 512].rearrange("p (h w) -> p h w", w=64)
                    if isx:
                        self.rope("pool", v3(ro, 8, 64), rq, rR[:, n - 2], 8, 1, 32, v3(t1, 8, 64), v3(t2, 8, 64), [P], rR)
                        src, srcT = v3(ro, 8, 64), ro
                    else:
                        src, srcT = rq, P
                    self.emit_T(src, srcT, 8, pQ, QTs, [("QTr", 0, 4, 1.0), ("KTr", 4, 4, 0.125)], b, rows)
                    it += 1

    def emit_T(self, src, srcT, nh, pQ, QTs, outs, b, rows, width=64):
        S = self.S
        for (name, h0, cnt_all, scale) in outs:
            for g0 in range(0, cnt_all, 4):
                cnt = min(4, cnt_all - g0)
                self._qi = getattr(self, "_qi", 0) + 1
                pq = pQ[self._qi % 2]
                st = QTs[self._qi % 6]
                for j in range(cnt):
                    S.op("pe", lambda e: e.transpose(pq[0:width, j, :], src[:, h0 + g0 + j, :], self.ident[:]), reads=[srcT, self.ident], writes=[pq])
                S.op("act", lambda e: e.activation(st[0:width, 0:cnt, :], pq[0:width, 0:cnt, :], AF.Copy, scale=float(scale)), reads=[pq], writes=[st])
                S.dma("pool", self.scr[name][b, :, g0:g0 + cnt, rows], st[0:width, 0:cnt, :], reads=[st])

    def stage_rwprep(self, l):
        S, I = self.S, self.I
        with Scope(self) as sc:
            def btile(key, w, nrep=1):
                t = sc.sb([128, nrep * w])
                for r in range(nrep):
                    S.dma("sp", t[:, r * w:(r + 1) * w], bc(I[key][l], 128), writes=[t])
                return t
            MU = btile("rwkv_mu", 960); W0 = btile("rwkv_w0", 512); A0 = btile("rwkv_a0", 512)
            KK = btile("rwkv_k_k", 256); KA = btile("rwkv_k_a", 256); RK = btile("rwkv_r_k", 256)
            WUP = sc.sb([32, 2, 256]); AUP = sc.sb([32, 2, 256]); GUP = sc.sb([64, 256])
            S.dma("sp", WUP[:], I["rwkv_w_up"][l].rearrange("(d r) c -> r d c", d=2), writes=[WUP])
            S.dma("sp", AUP[:], I["rwkv_a_up"][l].rearrange("(d r) c -> r d c", d=2), writes=[AUP])
            S.dma("sp", GUP[:], I["rwkv_g_up"][l], writes=[GUP])
            def mkset():
                cur = sc.sb([128, 960]); prv = sc.sb([128, 960]); nxt = sc.sb([128, 960])
                tt = sc.sb([128, 960])
                pst = sc.sb([128, 1024])
                S.op("dve", lambda e: e.memset(pst[:, 0:64], 0.0), writes=[pst])
                lor = sc.sb([128, 3, 64]); lorT = sc.sb([32, 4, 128]); lorTg = sc.sb([64, 128])
                wt = sc.sb([128, 512]); e1 = sc.sb([128, 512])
                decp = sc.sb([128, 64 + 512])
                S.op("dve", lambda e: e.memset(decp[:, 0:64], 0.0), writes=[decp])
                asig = sc.sb([128, 512]); gt = sc.sb([128, 256])
                kkp = sc.sb([128, 64 + 256])
                S.op("dve", lambda e: e.memset(kkp[:, 0:64], 0.0), writes=[kkp])
                kq = sc.sb([128, 256]); ks = sc.sb([128, 4]); kr = sc.sb([128, 4])
                bd = sc.sb([128, 512], BF16); kd = sc.sb([128, 512]); tk = sc.sb([128, 512]); kd16 = sc.sb([128, 512], BF16); v16 = sc.sb([128, 256], BF16)
                rk = sc.sb([128, 256]); pr = sc.sb([128, 512]); s8 = sc.sb([128, 8]); s4 = sc.sb([128, 4]); bon = sc.sb([128, 256])
                return (cur, prv, nxt, tt, pst, lor, lorT, lorTg, wt, e1, decp, asig, gt, kkp, kq, ks, kr, bd, kd, tk, kd16, v16, rk, pr, s8, s4, bon)
            tsets = [mkset() for _ in range(2)]
            pL = sc.ps([32, 4, 128]); pW = sc.ps([128, 512]); pAl = sc.ps([128, 512]); pG = sc.ps([128, 256])
            pA = sc.ps([128, 4, 128]); pR = sc.ps([128, 4, 128]); pWt = [sc.ps([128, 4, 128]) for _ in range(2)]
            nsets = []
            for _ in range(2):
                Ast = sc.sb([128, 128, 8], BF16); Rst = sc.sb([128, 128, 8], BF16); Wst = [sc.sb([128, 128, 4]) for _ in range(2)]
                S.op("pool", lambda e: e.memset(Ast[:], 0.0), writes=[Ast])
                S.op("pool", lambda e: e.memset(Rst[:], 0.0), writes=[Rst])
                nsets.append((Ast, Rst, Wst))
            P = self.scr["P"]
            for n in range(NT):
                rows = slice(n * 128, (n + 1) * 128)
                r0 = n * 128
                Ast, Rst, Wst = nsets[n % 2]
                for b in range(NB):
                    (cur, prv, nxt, tt, pst, lor, lorT, lorTg, wt, e1, decp, asig, gt, kkp, kq, ks, kr, bd, kd, tk, kd16, v16, rk, pr, s8, s4, bon) = tsets[b]

                    def issue_loads(n_, b_):
                        cur_, prv_, nxt_ = tsets[b_][0:3]
                        q0_ = n_ * 128
                        S.dma("sp", cur_[:], P[b_, q0_:q0_ + 128, 0:960], writes=[cur_])
                        if n_ in (0, 2):
                            S.op("pool", lambda e: e.memset(prv_[0:32, :], 0.0), writes=[prv_])
                            S.dma("sp", prv_[1:128, :], P[b_, q0_:q0_ + 127, 0:960], writes=[prv_])
                        else:
                            S.dma("sp", prv_[:], P[b_, q0_ - 1:q0_ + 127, 0:960], writes=[prv_])
                        if n_ in (1, NT - 1):
                            S.op("pool", lambda e: e.memset(nxt_[96:128, :], 0.0), writes=[nxt_])
                            S.dma("sp", nxt_[0:127, :], P[b_, q0_ + 1:q0_ + 128, 0:960], writes=[nxt_])
                        else:
                            S.dma("sp", nxt_[:], P[b_, q0_ + 1:q0_ + 129, 0:960], writes=[nxt_])

                    if n == 0 and b == 0:
                        issue_loads(0, 0)
                        issue_loads(0, 1)
                    S.op("pool", lambda e: e.tensor_tensor(tt[:], prv[:], nxt[:], ALU.add), reads=[prv, nxt], writes=[tt])
                    S.op("dve", lambda e: e.scalar_tensor_tensor(tt[:], tt[:], 0.5, cur[:], ALU.mult, ALU.subtract), reads=[tt, cur], writes=[tt])
                    S.op("pool", lambda e: e.tensor_tensor(tt[:], tt[:], MU[:], ALU.mult), reads=[tt, MU], writes=[tt])
                    S.op("dve", lambda e: e.tensor_tensor(pst[:, 64:1024], tt[:], cur[:], ALU.add), reads=[tt, cur], writes=[pst])
                    if n + 1 < NT:
                        issue_loads(n + 1, b)
                    p_r = pst[:, 64:320]; p_k = pst[:, 320:576]; p_v = pst[:, 576:832]
                    S.op("act", lambda e: e.activation(lor[:, 0, :], pst[:, 832:896], AF.Tanh), reads=[pst], writes=[lor])
                    S.op("act", lambda e: e.activation(lor[:, 2, :], pst[:, 960:1024], AF.Sigmoid), reads=[pst], writes=[lor])
                    S.op("pool", lambda e: e.tensor_copy(lor[:, 1, :], pst[:, 896:960]), reads=[pst], writes=[lor])
                    if 'A' in KSKIP:
                        continue
                    for j in range(4):
                        S.op("pe", lambda e: e.transpose(pL[:, j, :], lor[:, j // 2, 32 * (j % 2):32 * (j % 2) + 32], self.ident[:]), reads=[lor, self.ident], writes=[pL])
                    S.op("dve", lambda e: e.tensor_copy(lorT[:], pL[:]), reads=[pL], writes=[lorT])
                    S.op("pe", lambda e: e.transpose(pWt[0][0:64, 0, :], lor[:, 2, :], self.ident[:]), reads=[lor, self.ident], writes=[pWt[0]])
                    S.op("dve", lambda e: e.tensor_copy(lorTg[:], pWt[0][0:64, 0, :]), reads=[pWt[0]], writes=[lorTg])
                    for d in range(2):
                        S.op("pe", lambda e: e.matmul(pW[:, d * 256:(d + 1) * 256], lorT[:, d, :], WUP[:, d, :], start=True, stop=True), reads=[lorT, WUP], writes=[pW])
                        S.op("pe", lambda e: e.matmul(pAl[:, d * 256:(d + 1) * 256], lorT[:, 2 + d, :], AUP[:, d, :], start=True, stop=True), reads=[lorT, AUP], writes=[pAl])
                    S.op("pe", lambda e: e.matmul(pG[:], lorTg[:], GUP[:], start=True, stop=True), reads=[lorTg, GUP], writes=[pG])
                    if 'B' in KSKIP:
                        continue
                    S.op("dve", lambda e: e.tensor_tensor(wt[:], pW[:], W0[:], ALU.add), reads=[pW, W0], writes=[wt])
                    S.op("act", lambda e: e.activation(e1[:], wt[:], AF.Exp, scale=-1.0), reads=[wt], writes=[e1])
                    S.op("dve", lambda e: e.tensor_scalar(e1[:], e1[:], 1.0, None, ALU.add), reads=[e1], writes=[e1])
                    S.op("dve", lambda e: e.reciprocal(e1[:], e1[:]), reads=[e1], writes=[e1])
                    S.op("act", lambda e: e.activation(decp[:, 64:576], e1[:], AF.Exp, scale=-math.exp(-0.5)), reads=[e1], writes=[decp])
                    S.op("dve", lambda e: e.tensor_tensor(wt[:], pAl[:], A0[:], ALU.add), reads=[pAl, A0], writes=[wt])
                    S.op("act", lambda e: e.activation(e1[:], wt[:], AF.Exp, scale=-1.0), reads=[wt], writes=[e1])
                    S.op("dve", lambda e: e.tensor_scalar(e1[:], e1[:], 1.0, None, ALU.add), reads=[e1], writes=[e1])
                    S.op("dve", lambda e: e.reciprocal(asig[:], e1[:]), reads=[e1], writes=[asig])
                    S.op("act", lambda e: e.activation(gt[:], pG[:], AF.Copy), reads=[pG], writes=[gt])
                    S.dma("pool", self.scr["Gt"][b, rows, :], gt[:], reads=[gt])
                    kk = kkp[:, 64:320]
                    S.op("pool", lambda e: e.tensor_tensor(kk, p_k, KK[:], ALU.mult), reads=[pst, KK], writes=[kkp])
                    S.op("pool", lambda e: e.tensor_tensor(kq[:], kk, kk, ALU.mult), reads=[kkp], writes=[kq])
                    S.op("dve", lambda e: e.tensor_reduce(ks[:], kq[:].rearrange("p (h w) -> p h w", w=64), AX.X, ALU.add), reads=[kq], writes=[ks])
                    S.op("dve", lambda e: e.tensor_scalar(kr[:], ks[:], 1e-24, None, ALU.max), reads=[ks], writes=[kr])
                    self.rpow(kr, kr[:], kr, kr[:], 1.0, 0.0, -0.5)
                    kk3 = kk.rearrange("p (h w) -> p h w", w=64)
                    S.op("dve", lambda e: e.tensor_tensor(kk3, kk3, kr[:].unsqueeze(2).to_broadcast([128, 4, 64]), ALU.mult), reads=[kkp, kr], writes=[kkp])
                    d3 = lambda t: t[:].rearrange("p (d c) -> p d c", d=2)
                    b2 = lambda ap: ap.unsqueeze(1).to_broadcast([128, 2, 256])
                    S.op("dve", lambda e: e.tensor_tensor(d3(bd), d3(asig), b2(kk), ALU.mult), reads=[asig, kkp], writes=[bd])
                    S.op("dve", lambda e: e.scalar_tensor_tensor(d3(tk), d3(asig), -1.0, b2(KA[:]), ALU.add, ALU.mult), reads=[asig, KA], writes=[tk])
                    S.op("dve", lambda e: e.scalar_tensor_tensor(d3(kd), d3(tk), 1.0, b2(p_k), ALU.add, ALU.mult), reads=[tk, pst], writes=[kd])
                    S.op("pool", lambda e: e.tensor_copy(kd16[:], kd[:]), reads=[kd], writes=[kd16])
                    S.op("pool", lambda e: e.tensor_copy(v16[:], p_v), reads=[pst], writes=[v16])
                    S.op("pool", lambda e: e.tensor_tensor(rk[:], p_r, RK[:], ALU.mult), reads=[pst, RK], writes=[rk])
                    S.op("dve", lambda e: e.tensor_tensor(d3(pr), d3(kd), b2(rk[:]), ALU.mult), reads=[kd, rk], writes=[pr])
                    S.op("dve", lambda e: e.tensor_reduce(s8[:], pr[:].rearrange("p (g w) -> p g w", w=64), AX.X, ALU.add), reads=[pr], writes=[s8])
                    S.op("dve", lambda e: e.tensor_tensor(s4[:], s8[:, 0:4], s8[:, 4:8], ALU.add), reads=[s8], writes=[s4])
                    S.op("dve", lambda e: e.tensor_tensor(bon[:].rearrange("p (h w) -> p h w", w=64), p_v.rearrange("p (h w) -> p h w", w=64), s4[:].unsqueeze(2).to_broadcast([128, 4, 64]), ALU.mult), reads=[pst, s4], writes=[bon])
                    S.dma("pool", self.scr["Bon"][b, rows, :], bon[:], reads=[bon])
                    if 'C' in KSKIP:
                        continue
                    lo, hi = b * 64, (b + 1) * 64
                    for h in range(4):
                        if b == 0:
                            ink = kkp[:, 64 + 64 * h:128 + 64 * h]; inr = pst[:, 64 + 64 * h:128 + 64 * h]
                            S.op("pe", lambda e: e.transpose(pA[0:64, h, :], ink, self.ident[:]), reads=[kkp, self.ident], writes=[pA])
                            S.op("pe", lambda e: e.transpose(pR[0:64, h, :], inr, self.ident[:]), reads=[pst, self.ident], writes=[pR])
                        else:
                            ink = kkp[:, 64 * h:128 + 64 * h]; inr = pst[:, 64 * h:128 + 64 * h]
                            S.op("pe", lambda e: e.transpose(pA[:, h, :], ink, self.ident[:]), reads=[kkp, self.ident], writes=[pA])
                            S.op("pe", lambda e: e.transpose(pR[:, h, :], inr, self.ident[:]), reads=[pst, self.ident], writes=[pR])
                    S.op("act", lambda e: e.activation(Ast[lo:hi, :, 4 * b:4 * b + 4].rearrange("p t h -> p h t"), pA[lo:hi, :, :], AF.Copy, scale=-1.0), reads=[pA], writes=[Ast])
                    S.op("act", lambda e: e.activation(Rst[lo:hi, :, 4 * b:4 * b + 4].rearrange("p t h -> p h t"), pR[lo:hi, :, :], AF.Copy), reads=[pR], writes=[Rst])
                    for d in range(2):
                        for h in range(4):
                            c0 = 64 + 256 * d + 64 * h
                            if b == 0:
                                S.op("pe", lambda e: e.transpose(pWt[d][0:64, h, :], decp[:, c0:c0 + 64], self.ident[:]), reads=[decp, self.ident], writes=[pWt[d]])
                            else:
                                S.op("pe", lambda e: e.transpose(pWt[d][:, h, :], decp[:, c0 - 64:c0 + 64], self.ident[:]), reads=[decp, self.ident], writes=[pWt[d]])
                        S.op("dve", lambda e: e.tensor_copy(Wst[d][lo:hi, :, :].rearrange("p t h -> p h t"), pWt[d][lo:hi, :, :]), reads=[pWt[d]], writes=[Wst[d]])
                    if 'D' in KSKIP:
                        continue
                    for d in range(2):
                        if 'E' in KSKIP:
                            continue
                        S.dma("sp", self.scr["LBs"][d, 4 * b:4 * b + 4, rows, lo:hi].rearrange("h t k -> t h k"), bd[:, 256 * d:256 * d + 256].rearrange("p (h k) -> p h k", k=64), reads=[bd])
                        S.dma("sp", self.scr["LKs"][d, 4 * b:4 * b + 4, rows, lo:hi].rearrange("h t k -> t h k"), kd16[:, 256 * d:256 * d + 256].rearrange("p (h k) -> p h k", k=64), reads=[kd16])
                    rv = self.scr["RVs"]
                    dst = bass.AP(rv.tensor, (4 * b) * T * 256 + r0 * 256, [[256, 128], [T * 256 + 64, 4], [1, 64]])
                    if 'F' not in KSKIP:
                        S.dma("sp", dst, v16[:].rearrange("p (h k) -> p h k", k=64), reads=[v16])
                S.dma("pool", self.scr["As"][:, rows, :], Ast[:], reads=[Ast])
                S.dma("pool", self.scr["Rs"][:, rows, :], Rst[:], reads=[Rst])
                for d in range(2):
                    S.dma("pool", self.scr["Ws"][d, :, rows, :], Wst[d][:], reads=[Wst[d]])

    def stage_scan(self, l):
        S, I = self.S, self.I
        with Scope(self) as sc:
            MJ = sc.sb([8, 256])
            S.dma("sp", MJ[:], I["c_maskJ"], writes=[MJ])
            St = [sc.sb([128, 256]) for _ in range(2)]
            Tmp = [sc.sb([128, 256]) for _ in range(2)]
            SAm = [sc.sb([8, 256], BF16) for _ in range(2)]
            for d in range(2):
                S.op("dve", lambda e: e.memset(St[d][:], 0.0), writes=[St[d]])
                S.op("dve", lambda e: e.memset(SAm[d][:], 0.0), writes=[SAm[d]])
            pSAY = [sc.ps([40, 256]) for _ in range(2)]
            pU = [sc.ps([128, 256]) for _ in range(2)]
            NBUF = 2
            bufs = []
            for i in range(NBUF):
                bb = []
                for d in range(2):
                    B = dict(AR=sc.sb([128, CS, 40], BF16), W=sc.sb([128, CS, 4]),
                             LB=sc.sb([8, CS, 128], BF16), LK=sc.sb([8, CS, 128], BF16), RV=sc.sb([8, CS, 256], BF16), Y=sc.sb([40, CS, 256]))
                    S.op("pool", lambda e: e.memset(B["AR"][:], 0.0), writes=[B["AR"]])
                    bb.append(B)
                bufs.append(bb)
            NCH = T // CS

            def rowbase(d, c):
                if d == 0:
                    return c * CS
                t0 = c * CS
                if t0 < CT:
                    return CT - CS - t0
                return (T + CT - CS) - t0

            As, Rs = self.scr["As"], self.scr["Rs"]

            def load(c):
                bb = bufs[c % NBUF]
                for d in range(2):
                    ra = rowbase(d, c)
                    rs = slice(ra, ra + CS)
                    q = "sp"
                    B = bb[d]
                    S.dma(q, B["AR"][:, :, 32:40], Rs[:, rs, :], writes=[B["AR"]])
                    if d == 0:
                        n = min(CS, T - (ra + 1))
                        S.dma(q, B["AR"][:, 0:n, 0:8], As[:, ra + 1:ra + 1 + n, :], writes=[B["AR"]])
                    elif ra == 0:
                        S.dma(q, B["AR"][:, 1:CS, 0:8], As[:, 0:CS - 1, :], writes=[B["AR"]])
                        S.dma(q, B["AR"][:, 0:1, 0:8], As[:, T - 1:T, :], writes=[B["AR"]])
                    else:
                        S.dma(q, B["AR"][:, :, 0:8], As[:, ra - 1:ra + CS - 1, :], writes=[B["AR"]])
                    S.dma(q, B["W"][:], self.scr["Ws"][d, :, rs, :], writes=[B["W"]])
                    S.dma(q, B["LB"][:], self.scr["LBs"][d, :, rs, :], writes=[B["LB"]])
                    S.dma(q, B["LK"][:], self.scr["LKs"][d, :, rs, :], writes=[B["LK"]])
                    S.dma(q, B["RV"][:], self.scr["RVs"][:, rs, :], writes=[B["RV"]])

            W3 = lambda B, i: B["W"][:, i, :].unsqueeze(2).to_broadcast([128, 4, 64])
            j4 = lambda t: t[:].rearrange("p (j v) -> p j v", j=4)
            hi16 = lambda t: t[:].bitcast(BF16).rearrange("p (n two) -> p n two", two=2)[:, :, 1]
            load(0)
            for c in range(NCH):
                if c + 1 < NCH:
                    load(c + 1)
                bb = bufs[c % NBUF]
                for s in range(CS):
                    idx = [s, CS - 1 - s]
                    for d in range(2):
                        B = bb[d]; i = idx[d]; pu = pU[d]
                        S.op("pe", lambda e: e.matmul(pu[:], B["LK"][:, i, :], B["RV"][:, i, :], start=True, stop=False), reads=[B["LK"], B["RV"]], writes=[pu])
                        S.op("pe", lambda e: e.matmul(pu[:], B["LB"][:, i, :], SAm[d][:], start=False, stop=True), reads=[B["LB"], SAm[d]], writes=[pu])
                    for d in range(2):
                        S.op("pool", lambda e: e.tensor_tensor(j4(Tmp[d]), j4(St[d]), W3(bb[d], idx[d]), ALU.mult), reads=[St[d], bb[d]["W"]], writes=[Tmp[d]])
                    for d in range(2):
                        S.op("dve", lambda e: e.tensor_tensor(St[d][:], Tmp[d][:], pU[d][:], ALU.add), reads=[Tmp[d], pU[d]], writes=[St[d]])
                    for d in range(2):
                        S.op("pe", lambda e: e.matmul(pSAY[d][:], bb[d]["AR"][:, idx[d], :], hi16(St[d]), start=True, stop=True), reads=[bb[d]["AR"], St[d]], writes=[pSAY[d]])
                    for d in range(2):
                        S.op("dve", lambda e: e.tensor_tensor(SAm[d][:], pSAY[d][0:8, :], MJ[:], ALU.mult), reads=[pSAY[d], MJ], writes=[SAm[d]])
                    for d in range(2):
                        S.op("act", lambda e: e.activation(bb[d]["Y"][32:40, idx[d], :], pSAY[d][32:40, :], AF.Copy), reads=[pSAY[d]], writes=[bb[d]["Y"]])
                for d in range(2):
                    ra = rowbase(d, c)
                    S.dma("pool", self.scr["Yfull"][d, :, ra:ra + CS, :], bb[d]["Y"][32:40, :, :], reads=[bb[d]["Y"]])

    def stage_rwpost(self, l):
        S, I = self.S, self.I
        with Scope(self) as sc:
            LNG = sc.sb([128, 256]); LNB = sc.sb([128, 256])
            S.dma("sp", LNG[:], bc(I["rwkv_ln_g"][l], 128), writes=[LNG])
            S.dma("sp", LNB[:], bc(I["rwkv_ln_b"][l], 128), writes=[LNB])
            yf = [sc.sb([128, 256]) for _ in range(2)]; yb = [sc.sb([128, 256]) for _ in range(2)]
            bo = [sc.sb([128, 256]) for _ in range(2)]; gg = [sc.sb([128, 256]) for _ in range(2)]
            y = sc.sb([128, 256]); sq = sc.sb([128, 256]); s1 = sc.sb([128, 4]); s2 = sc.sb([128, 4]); o = [sc.sb([128, 256]) for _ in range(2)]
            h3 = lambda t: t[:].rearrange("p (h w) -> p h w", w=64)
            b3 = lambda t: t[:].unsqueeze(2).to_broadcast([128, 4, 64])
            yfull = self.scr["Yfull"]
            it = 0
            for b in range(NB):
                for n in range(NT):
                    if self.last and n < 2:
                        continue
                    rows = slice(n * 128, (n + 1) * 128)
                    i = it % 2
                    for d, dstt in ((0, yf[i]), (1, yb[i])):
                        src = bass.AP(yfull.tensor, d * 8 * T * 256 + (4 * b) * T * 256 + n * 128 * 256, [[256, 128], [T * 256 + 64, 4], [1, 64]])
                        S.dma("sp", h3(dstt), src, writes=[dstt])
                    S.dma("sp", bo[i][:], self.scr["Bon"][b, rows, :], writes=[bo[i]])
                    S.dma("sp", gg[i][:], self.scr["Gt"][b, rows, :], writes=[gg[i]])
                    S.op("pool", lambda e: e.tensor_tensor(y[:], yf[i][:], yb[i][:], ALU.add), reads=[yf[i], yb[i]], writes=[y])
                    self.head_norm(y, sq, s1, s2, GN_EPS)
                    S.op("dve", lambda e: e.tensor_tensor(y[:], y[:], LNG[:], ALU.mult), reads=[y, LNG], writes=[y])
                    S.op("pool", lambda e: e.tensor_tensor(y[:], y[:], LNB[:], ALU.add), reads=[y, LNB], writes=[y])
                    S.op("pool", lambda e: e.tensor_tensor(y[:], y[:], bo[i][:], ALU.add), reads=[y, bo[i]], writes=[y])
                    S.op("dve", lambda e: e.tensor_tensor(o[i][:], y[:], gg[i][:], ALU.mult), reads=[y, gg[i]], writes=[o[i]])
                    S.dma("pool", self.scr["Ycat"][b, rows, 0:256], o[i][:], reads=[o[i]])
                    it += 1

    def head_norm(self, y, sq, s1, s2, eps, nh=4):
        S = self.S
        h3 = lambda t: t[:, 0:nh * 64].rearrange("p (h w) -> p h w", w=64)
        b3 = lambda t: t[:, 0:nh].unsqueeze(2).to_broadcast([128, nh, 64])
        S.op("dve", lambda e: e.tensor_reduce(s1[:, 0:nh], h3(y), AX.X, ALU.add), reads=[y], writes=[s1])
        S.op("dve", lambda e: e.tensor_scalar(s1[:, 0:nh], s1[:, 0:nh], -1.0 / 64, None, ALU.mult), reads=[s1], writes=[s1])
        S.op("dve", lambda e: e.tensor_tensor(h3(y), h3(y), b3(s1), ALU.add), reads=[y, s1], writes=[y])
        S.op("pool", lambda e: e.tensor_tensor(h3(sq), h3(y), h3(y), ALU.mult), reads=[y], writes=[sq])
        S.op("dve", lambda e: e.tensor_reduce(s2[:, 0:nh], h3(sq), AX.X, ALU.add), reads=[sq], writes=[s2])
        self.rpow(s2, s2[:, 0:nh], s2, s2[:, 0:nh], 1.0 / 64, eps, -0.5)
        S.op("dve", lambda e: e.tensor_tensor(h3(y), h3(y), b3(s2), ALU.mult), reads=[y, s2], writes=[y])

    def stage_attn(self, l):
        S, I = self.S, self.I
        lam_init = 0.8 - 0.6 * math.exp(-0.3 * l)
        with Scope(self) as sc:
            DL = sc.sb([128, 128]); dp = sc.sb([128, 128]); ds = sc.sb([128, 2]); lam = sc.sb([128, 1]); nlam = sc.sb([128, 1])
            S.dma("sp", DL[:], bc(I["diff_lambda"][l], 128), writes=[DL])
            S.op("dve", lambda e: e.tensor_tensor(dp[:, 0:32], DL[:, 0:32], DL[:, 32:64], ALU.mult), reads=[DL], writes=[dp])
            S.op("dve", lambda e: e.tensor_tensor(dp[:, 32:64], DL[:, 64:96], DL[:, 96:128], ALU.mult), reads=[DL], writes=[dp])
            S.op("dve", lambda e: e.tensor_reduce(ds[:], dp[:, 0:64].rearrange("p (a w) -> p a w", w=32), AX.X, ALU.add), reads=[dp], writes=[ds])
            S.op("act", lambda e: e.activation(ds[:], ds[:], AF.Exp), reads=[ds], writes=[ds])
            S.op("dve", lambda e: e.tensor_tensor(lam[:], ds[:, 0:1], ds[:, 1:2], ALU.subtract), reads=[ds], writes=[lam])
            S.op("dve", lambda e: e.tensor_scalar(nlam[:], lam[:], lam_init, -1.0, ALU.add, ALU.mult), reads=[lam], writes=[nlam])
            DN = sc.sb([128, 64])
            S.dma("sp", DN[:], bc(I["diff_norm"][l], 128), writes=[DN])
            S.op("dve", lambda e: e.tensor_scalar(DN[:], DN[:], 1.0 - lam_init, None, ALU.mult), reads=[DN], writes=[DN])
            KT = sc.sb([64, 4, T], BF16); QT = sc.sb([64, 4, T], BF16)
            KTd = sc.sb([32, 8, T], BF16); QTd = sc.sb([32, 8, T], BF16)
            V = sc.sb([128, NT, 4, 65], BF16)
            S.op("pool", lambda e: e.memset(V[:, :, :, 64:65], 1.0), writes=[V])
            pS = [sc.ps([128, 512]) for _ in range(2)]
            pO = [sc.ps([128, 4, 65]) for _ in range(4)]
            Pball = [sc.sb([128, NT, 512], BF16) for _ in range(2)]
            Mk = [sc.sb([128, 512], BF16) for _ in range(3)]
            rec = sc.sb([128, 4, 1]); o1 = sc.sb([128, 4, 64]); o2 = sc.sb([128, 4, 64])
            sq = sc.sb([128, 256]); s1 = sc.sb([128, 4]); s2 = sc.sb([128, 4])
            gate = sc.sb([128, 4, 64]); gs = sc.sb([128, 4, 64])
            osb = [sc.sb([128, 4, 64]) for _ in range(2)]
            Pd = self.scr["P"]
            cnt = {"u": 0, "o": 0, "p": 0}
            for m in "gdr":
                nk = 2 if m == "g" else 4
                nv = 2 if m == "g" else 4
                vcol = {"g": O_GQ + 384, "d": O_DF + 512, "r": O_RT + 512}[m]
                ocol = {"g": 256, "d": 512, "r": 768}[m]
                for b in range(NB):
                    if m == "d":
                        S.dma("sp", KTd[:], self.scr["KTd"][b], writes=[KTd])
                        S.dma("sp", QTd[:], self.scr["QTd"][b], writes=[QTd])
                    else:
                        S.dma("sp", KT[:, 0:nk, :], self.scr["KT" + m][b, :, 0:nk, :], writes=[KT])
                        S.dma("sp", QT[:], self.scr["QT" + m][b], writes=[QT])
                    for hv in range(nv):
                        S.dma("pool", V[:, :, hv, 0:64], Pd[b, :, vcol + hv * 64:vcol + hv * 64 + 64].rearrange("(n p) w -> p n w", p=128), writes=[V])
                    chunks = ([] if self.last else [(0, 256, 0, 2)]) + [(256 + 512 * q, 512, 0, NT) for q in range(4)]
                    for h in range(4):
                        for (q0, w, k0, k1) in chunks:
                            nj = w // 128
                            isx = q0 >= 256
                            units = [(0, 0)] if m != "d" else [(0, 0), (1, 0)]
                            pos = []
                            for (mi, rbase) in units:
                                po = pO[cnt["o"] % 4]; cnt["o"] += 1
                                pos.append(po)
                                rws = slice(rbase, rbase + (64 if m != "d" else 32))
                                hk = h // 2 if m == "g" else h
                                hvv = h // 2 if m == "g" else h
                                PB_ = Pball[cnt["o"] % 2]
                                for kt in range(k0, k1):
                                    ps_ = pS[cnt["u"] % 2]; cnt["u"] += 1
                                    pb = PB_[:, kt, :]
                                    if m == "d":
                                        S.op("pe", lambda e: e.matmul(ps_[:, :w], KTd[:, 2 * h + mi, kt * 128:(kt + 1) * 128], QTd[:, 2 * h + mi, q0:q0 + w], start=True, stop=True), reads=[KTd, QTd], writes=[ps_])
                                    else:
                                        S.op("pe", lambda e: e.matmul(ps_[:, :w], KT[rws, hk, kt * 128:(kt + 1) * 128], QT[rws, h, q0:q0 + w], start=True, stop=True), reads=[KT, QT], writes=[ps_])
                                    if m == "r":
                                        mk = Mk[cnt["p"] % 3]; cnt["p"] += 1
                                        if not isx:
                                            ti = {0: 15, 1: 14}[kt]
                                        elif kt < 2:
                                            ti = 28 + kt * 4 + (q0 - 256) // 512
                                        else:
                                            off = 4 * ((q0 - 256) // 512) - (kt - 2)
                                            ti = off + 15
                                        S.dma("sp", mk[:], I["c_retM"][ti, h], writes=[mk])
                                        S.op("dve", lambda e: e.tensor_tensor(pb[:, :w], ps_[:, :w], mk[:, :w], ALU.mult), reads=[ps_, mk], writes=[PB_])
                                    else:
                                        S.op("act", lambda e: e.activation(pb[:, :w], ps_[:, :w], AF.Exp), reads=[ps_], writes=[PB_])
                                for j in range(nj):
                                    for kt in range(k0, k1):
                                        S.op("pe", lambda e: e.matmul(po[:, j, :], PB_[:, kt, j * 128:(j + 1) * 128], V[:, kt, hvv, :], start=(kt == k0), stop=(kt == k1 - 1)), reads=[PB_, V], writes=[po])
                            ob = osb[cnt["o"] % 2]
                            dst = self.scr["Ycat"][b, q0:q0 + w, ocol + h * 64:ocol + h * 64 + 64].rearrange("(j t) v -> t j v", t=128)
                            if m == "g":
                                po = pos[0]
                                S.op("dve", lambda e: e.reciprocal(rec[:, 0:nj, :], po[:, 0:nj, 64:65]), reads=[po], writes=[rec])
                                S.op("dve", lambda e: e.tensor_tensor(ob[:, 0:nj, :], po[:, 0:nj, 0:64], rec[:, 0:nj, :].to_broadcast([128, nj, 64]), ALU.mult), reads=[po, rec], writes=[ob])
                            elif m == "d":
                                for (po, ot) in ((pos[0], o1), (pos[1], o2)):
                                    S.op("dve", lambda e: e.reciprocal(rec[:, 0:nj, :], po[:, 0:nj, 64:65]), reads=[po], writes=[rec])
                                    S.op("dve", lambda e: e.tensor_tensor(ot[:, 0:nj, :], po[:, 0:nj, 0:64], rec[:, 0:nj, :].to_broadcast([128, nj, 64]), ALU.mult), reads=[po, rec], writes=[ot])
                                S.op("dve", lambda e: e.scalar_tensor_tensor(o1[:, 0:nj, :], o2[:, 0:nj, :], nlam[:, 0:1], o1[:, 0:nj, :], ALU.mult, ALU.add), reads=[o1, o2, nlam], writes=[o1])
                                S.op("pool", lambda e: e.tensor_tensor(o2[:, 0:nj, :], o1[:, 0:nj, :], o1[:, 0:nj, :], ALU.mult), reads=[o1], writes=[o2])
                                S.op("dve", lambda e: e.tensor_reduce(s1[:, 0:nj], o2[:, 0:nj, :], AX.X, ALU.add), reads=[o2], writes=[s1])
                                self.rpow(s1, s1[:, 0:nj], s1, s1[:, 0:nj], 1.0 / 64, QK_EPS, -0.5)
                                S.op("dve", lambda e: e.tensor_tensor(o1[:, 0:nj, :], o1[:, 0:nj, :], s1[:, 0:nj].unsqueeze(2).to_broadcast([128, nj, 64]), ALU.mult), reads=[o1, s1], writes=[o1])
                                S.op("dve", lambda e: e.tensor_tensor(ob[:, 0:nj, :], o1[:, 0:nj, :], DN[:].unsqueeze(1).to_broadcast([128, nj, 64]), ALU.mult), reads=[o1, DN], writes=[ob])
                            else:
                                po = pos[0]
                                S.dma("sp", gate[:, 0:nj, :], Pd[b, q0:q0 + w, O_RT + 768 + h * 64:O_RT + 768 + h * 64 + 64].rearrange("(j t) v -> t j v", t=128), writes=[gate])
                                S.op("act", lambda e: e.activation(gs[:, 0:nj, :], gate[:, 0:nj, :], AF.Silu), reads=[gate], writes=[gs])
                                yv = TT(o1[:].rearrange("p j w -> p (j w)"))
                                S.op("act", lambda e: e.activation(o1[:, 0:nj, :], po[:, 0:nj, 0:64], AF.Copy), reads=[po], writes=[o1])
                                yv.lastw = o1.lastw
                                self.head_norm(yv, sq, s1, s2, LN_EPS, nh=nj)
                                o1.lastw = yv.lastw
                                S.op("dve", lambda e: e.tensor_tensor(ob[:, 0:nj, :], o1[:, 0:nj, :], gs[:, 0:nj, :], ALU.mult), reads=[o1, gs], writes=[ob])
                            S.dma("sp", dst, ob[:, 0:nj, :], reads=[ob])

    def layer_norm_out(self, sc, x1, G, Bt, out, tmps):
        S = self.S
        s1, s2, sq = tmps
        S.op("dve", lambda e: e.tensor_reduce(s1[:], x1[:], AX.X, ALU.add), reads=[x1], writes=[s1])
        S.op("dve", lambda e: e.tensor_scalar(s1[:], s1[:], -1.0 / D, None, ALU.mult), reads=[s1], writes=[s1])
        S.op("dve", lambda e: e.tensor_scalar(x1[:], x1[:], s1[:, 0:1], None, ALU.add), reads=[x1, s1], writes=[x1])
        S.op("pool", lambda e: e.tensor_tensor(sq[:], x1[:], x1[:], ALU.mult), reads=[x1], writes=[sq])
        S.op("dve", lambda e: e.tensor_reduce(s2[:], sq[:], AX.X, ALU.add), reads=[sq], writes=[s2])
        self.rpow(s2, s2[:], s2, s2[:], 1.0 / D, LN_EPS, -0.5)
        S.op("dve", lambda e: e.scalar_tensor_tensor(x1[:], x1[:], s2[:, 0:1], G[:], ALU.mult, ALU.mult), reads=[x1, s2, G], writes=[x1])
        S.op("pool", lambda e: e.tensor_tensor(out[:], x1[:], Bt[:], ALU.add), reads=[x1, Bt], writes=[out])

    def stage_mix(self, l):
        S, I = self.S, self.I
        with Scope(self) as sc:
            wo = sc.sb([128, 8, D], BF16)
            for kc in range(8):
                S.dma("pool", wo[:, kc, :], I["w_out"][l, kc * 128:(kc + 1) * 128, :], writes=[wo])
            G1 = sc.sb([128, 3, D])
            for v in range(3):
                S.dma("sp", G1[:, v, :], bc(self.scr["modD"][v, 2 * D:3 * D], 128), writes=[G1])
            PG = sc.sb([128, D]); PB = sc.sb([128, D])
            S.dma("sp", PG[:], bc(I["post1_g"][l], 128), writes=[PG])
            S.dma("sp", PB[:], bc(I["post1_b"][l], 128), writes=[PB])
            yc = [sc.sb([128, D]) for _ in range(2)]; xt = [sc.sb([128, D]) for _ in range(2)]
            pT = [sc.ps([128, 128]) for _ in range(2)]
            yT = [sc.sb([128, 8, 128], BF16) for _ in range(2)]
            pO = [sc.ps([128, 512]) for _ in range(2)]
            t = sc.sb([128, D]); x1 = sc.sb([128, D]); outt = [sc.sb([128, D]) for _ in range(2)]
            s1 = sc.sb([128, 1]); s2 = sc.sb([128, 1]); sq = sc.sb([128, D])
            it = 0
            for b in range(NB):
                for n in range(NT):
                    if self.last and n < 2:
                        continue
                    v = 2 if n < 2 else b
                    rows = slice(n * 128, (n + 1) * 128)
                    i = it % 2
                    S.dma("sp", yc[i][:], self.scr["Ycat"][b, rows, :], writes=[yc[i]])
                    S.dma("sp", xt[i][:], self.Xsrc[b, rows, :], writes=[xt[i]])
                    for kc in range(8):
                        pp = pT[kc % 2]
                        S.op("pe", lambda e: e.transpose(pp[:], yc[i][:, kc * 128:(kc + 1) * 128], self.ident[:]), reads=[yc[i], self.ident], writes=[pp])
                        S.op("act", lambda e: e.activation(yT[i][:, kc, :], pp[:], AF.Copy), reads=[pp], writes=[yT[i]])
                    for c in range(2):
                        po = pO[c]
                        for kc in range(8):
                            S.op("pe", lambda e: e.matmul(po[:], yT[i][:, kc, :], wo[:, kc, c * 512:(c + 1) * 512], start=(kc == 0), stop=(kc == 7)), reads=[yT[i], wo], writes=[po])
                        S.op("dve", lambda e: e.tensor_tensor(t[:, c * 512:(c + 1) * 512], po[:], G1[:, v, c * 512:(c + 1) * 512], ALU.mult), reads=[po, G1], writes=[t])
                    S.op("dve", lambda e: e.scalar_tensor_tensor(x1[:], xt[i][:], ALPHA, t[:], ALU.mult, ALU.add), reads=[xt[i], t], writes=[x1])
                    self.layer_norm_out(sc, x1, PG, PB, outt[i], (s1, s2, sq))
                    S.dma("pool", self.scr["X1s"][b, rows, :], outt[i][:], reads=[outt[i]])
                    it += 1

    def stage_ffn(self, l):
        S, I = self.S, self.I
        with Scope(self) as sc:
            w1 = sc.sb([128, 8, 2 * DFF], BF16, "w1")
            for kc in range(8):
                S.dma("pool", w1[:, kc, :], I["ffn_w_in"][l, kc * 128:(kc + 1) * 128, :], writes=[w1])
            w2 = sc.sb([128, NFF, D], BF16, "w2")
            for fc in range(NFF):
                S.dma("pool", w2[:, fc, :], I["ffn_w_out"][l, fc * 128:(fc + 1) * 128, :], writes=[w2])
            G2 = sc.sb([128, D])
            PG = sc.sb([128, D]); PB = sc.sb([128, D])
            S.dma("sp", PG[:], bc(I["post2_g"][l], 128), writes=[PG])
            S.dma("sp", PB[:], bc(I["post2_b"][l], 128), writes=[PB])
            xt = [sc.sb([128, D]) for _ in range(2)]
            pT = [sc.ps([128, 128]) for _ in range(2)]
            x1T = sc.sb([128, 8, 512], BF16)
            aT = sc.sb([128, NFF, 512], BF16)
            pU = [sc.ps([128, 512]) for _ in range(2)]; pGt = [sc.ps([128, 512]) for _ in range(2)]
            su = [sc.sb([128, 512]) for _ in range(2)]
            pF = [sc.ps([128, 512]) for _ in range(2)]
            t = sc.sb([128, D]); x2 = sc.sb([128, D]); outt = [sc.sb([128, D]) for _ in range(2)]
            s1 = sc.sb([128, 1]); s2 = sc.sb([128, 1]); sq = sc.sb([128, D])
            it = 0
            for b in range(NB):
                groups = ([] if self.last else [(0, 2, 2)]) + [(2 + 4 * q, 4, b) for q in range(4)]
                for (n0, nt, v) in groups:
                    w = nt * 128
                    S.dma("sp", G2[:], bc(self.scr["modD"][v, 5 * D:6 * D], 128), writes=[G2])
                    for j in range(nt):
                        rows = slice((n0 + j) * 128, (n0 + j + 1) * 128)
                        x = xt[it % 2]; it += 1
                        S.dma("sp", x[:], self.scr["X1s"][b, rows, :], writes=[x])
                        self.xT_modulated(sc, x, pT, x1T, j * 128, 128, v, 1)
                    for fc in range(NFF):
                        pu = pU[fc % 2]; pg = pGt[fc % 2]; s_ = su[fc % 2]
                        for kc in range(8):
                            S.op("pe", lambda e: e.matmul(pu[:, :w], w1[:, kc, fc * 128:(fc + 1) * 128], x1T[:, kc, :w], start=(kc == 0), stop=(kc == 7)), reads=[w1, x1T], writes=[pu])
                        for kc in range(8):
                            S.op("pe", lambda e: e.matmul(pg[:, :w], w1[:, kc, DFF + fc * 128:DFF + (fc + 1) * 128], x1T[:, kc, :w], start=(kc == 0), stop=(kc == 7)), reads=[w1, x1T], writes=[pg])
                        S.op("act", lambda e: e.activation(s_[:, :w], pu[:, :w], AF.Silu), reads=[pu], writes=[s_])
                        S.op("dve", lambda e: e.tensor_tensor(aT[:, fc, :w], s_[:, :w], pg[:, :w], ALU.mult), reads=[s_, pg], writes=[aT])
                    for j in range(nt):
                        rows = slice((n0 + j) * 128, (n0 + j + 1) * 128)
                        x = xt[it % 2]; it += 1
                        S.dma("sp", x[:], self.scr["X1s"][b, rows, :], writes=[x])
                        for c in range(2):
                            pf = pF[c]
                            for fc in range(NFF):
                                S.op("pe", lambda e: e.matmul(pf[:], aT[:, fc, j * 128:(j + 1) * 128], w2[:, fc, c * 512:(c + 1) * 512], start=(fc == 0), stop=(fc == NFF - 1)), reads=[aT, w2], writes=[pf])
                            S.op("dve", lambda e: e.tensor_tensor(t[:, c * 512:(c + 1) * 512], pf[:], G2[:, c * 512:(c + 1) * 512], ALU.mult), reads=[pf, G2], writes=[t])
                        S.op("dve", lambda e: e.scalar_tensor_tensor(x2[:], x[:], ALPHA, t[:], ALU.mult, ALU.add), reads=[x, t], writes=[x2])
                        o = outt[j % 2]
                        self.layer_norm_out(sc, x2, PG, PB, o, (s1, s2, sq))
                        if self.last:
                            xr = (n0 + j) * 128 - CT
                            S.dma("pool", self.out[b, xr:xr + 128, :], o[:], reads=[o])
                        else:
                            S.dma("pool", self.scr["Xs"][b, rows, :], o[:], reads=[o])


def host_consts():
    c = {}
    c["c_ident"] = np.eye(128, dtype=np.float32)
    t = np.arange(SX)
    row = (t // 64).astype(np.float32)
    col = (t % 64).astype(np.float32)
    pos = t.astype(np.float32)

    def tab(p, n):
        half = n // 2
        inv = np.power(np.float32(10000.0), -(np.arange(half, dtype=np.float32) * np.float32(2.0) / np.float32(n))).astype(np.float32)
        ang = (p[:, None] * inv[None, :]).astype(np.float32)
        cs, sn = np.cos(ang).astype(np.float32), np.sin(ang).astype(np.float32)
        return np.concatenate([cs, cs], 1), np.concatenate([-sn, sn], 1)

    def axial(n):
        h = n // 2
        c1, s1 = tab(row, h)
        c2, s2 = tab(col, h)
        return np.stack([np.concatenate([c1, c2], 1), np.concatenate([s1, s2], 1)], 1).astype(np.float32)

    c["c_ropeG"] = axial(64)
    c["c_ropeD"] = axial(32)
    cr, sr = tab(pos, 64)
    c["c_ropeR"] = np.stack([cr, sr], 1).astype(np.float32)
    mj = np.zeros((8, 256), np.float32)
    for m in range(8):
        h = m % 4
        mj[m, h * 64:(h + 1) * 64] = 1.0
    c["c_maskJ"] = mj
    c["c_zero"] = np.zeros((128, 4096), ml_dtypes.bfloat16)
    gf = 1.0 - 2.0 ** (-5.0 - np.arange(4, dtype=np.float64))
    gb = gf[::-1]
    M = np.zeros((36, 4, 128, 512), np.float64)
    p = np.arange(128)[:, None]
    j = np.arange(512)[None, :]
    for h in range(4):
        lf, lb = math.log(gf[h]), math.log(gb[h])
        for off in range(-15, 13):
            dlt = (128 * off + j - p).astype(np.float64)
            m = np.where(dlt > 0, np.exp(lf * np.maximum(dlt, 0)), 0.0) + np.where(dlt < 0, np.exp(lb * np.maximum(-dlt, 0)), 0.0) + np.where(dlt == 0, 2.0, 0.0)
            M[off + 15, h] = m
        for kc in range(2):
            for qc in range(4):
                cc = 128 * kc + p
                ii = 512 * qc + j
                M[28 + kc * 4 + qc, h] = np.exp(lf * (256 + ii - cc)) + np.exp(lb * (2048 - ii + cc))
    c["c_retM"] = M.astype(np.float32).astype(ml_dtypes.bfloat16)
    return c


_CACHE = {}


def kernel(**inputs):
    f = lambda k: np.ascontiguousarray(np.asarray(inputs[k], dtype=np.float32))
    L = DEPTH
    shared = {}
    for k in ("ada_w", "ada_b", "w_in", "rwkv_mu", "rwkv_g_up", "rwkv_k_k", "rwkv_k_a", "rwkv_ln_g", "rwkv_ln_b",
              "gqa_q_norm", "gqa_k_norm", "diff_norm", "w_out", "post1_g", "post1_b", "ffn_w_in", "ffn_w_out", "post2_g", "post2_b"):
        shared[k] = f(k)
    shared["rwkv_w0"] = f("rwkv_w0").reshape(L, 512)
    shared["rwkv_a0"] = f("rwkv_a0").reshape(L, 512)
    shared["rwkv_w_up"] = f("rwkv_w_up").reshape(L, 64, 256)
    shared["rwkv_a_up"] = f("rwkv_a_up").reshape(L, 64, 256)
    shared["rwkv_r_k"] = f("rwkv_r_k").reshape(L, 256)
    shared["diff_lambda"] = f("diff_lambda").reshape(L, 128)
    shared.update(host_consts())
    x, c, ctx, c_ctx = f("x"), f("c"), f("ctx"), f("c_ctx")
    in_maps = []
    for core in range(8):
        bs = slice(core * NB, (core + 1) * NB)
        m = dict(shared)
        m["xin"] = np.ascontiguousarray(np.concatenate([ctx[bs], x[bs]], axis=1))
        m["cvec"] = np.ascontiguousarray(np.concatenate([c[bs], c_ctx[None, :]], axis=0))
        in_maps.append(m)
    if "nc" not in _CACHE:
        _CACHE["nc"] = KB().build()
    res = run_bass_kernel_spmd(_CACHE["nc"], in_maps, core_ids=list(range(8)))
    return np.concatenate([r["y"] for r in res.results], axis=0).astype(np.float32)
```

```python
import math, os
KSKIP = os.environ.get('KSKIP', '')
import numpy as np
import ml_dtypes
import concourse.bass as bass
import concourse.mybir as mybir
from concourse.bass_utils import run_bass_kernel_spmd
from contextlib import ExitStack

F32 = mybir.dt.float32
BF16 = mybir.dt.bfloat16
AF = mybir.ActivationFunctionType
ALU = mybir.AluOpType
AX = mybir.AxisListType

D = 1024
NB = 2
CT = 256
SX = 2048
T = CT + SX
NT = T // 128
INW = 3264
DFF = 2816
NFF = DFF // 128
DEPTH = 4
ALPHA = (2 * DEPTH) ** 0.25
LN_EPS = 1e-5
QK_EPS = 1e-6
GN_EPS = 64e-5
CS = 16
O_RW, O_GQ, O_DF, O_RT = 0, 960, 1472, 2240


class Sched:
    def __init__(self, nc, es):
        self.nc = nc
        self.es = es
        self.eng = {"pe": nc.tensor, "act": nc.scalar, "dve": nc.vector, "pool": nc.gpsimd, "sp": nc.sync}
        self.sem = {}
        self.cnt = {}
        self.nsem = 0
        for k in self.eng:
            self.sem[k] = self._newsem()
            self.cnt[k] = 0
        self.waited = {k: {} for k in self.eng}
        self.NSLOT = 8
        self.dsem = {}
        self.dcnt = {}
        self.dnext = {}
        for q in ("sp", "pool", "act"):
            self.dsem[q] = [self._newsem() for i in range(self.NSLOT)]
            self.dcnt[q] = [0] * self.NSLOT
            self.dnext[q] = 0
        self.ninstr = 0

    def _newsem(self):
        self.nsem += 1
        return self.es.enter_context(self.nc.semaphore("sem%d" % self.nsem))

    def _wait(self, e, tok, raw=False):
        if tok is None:
            return
        sem, val, owner = tok
        if owner == e and (e == "pe" or not raw):
            return
        key = id(sem)
        w = self.waited[e]
        if w.get(key, 0) >= val:
            return
        self.eng[e].wait_ge(sem, val)
        w[key] = val

    def deps(self, e, reads, writes):
        for t in reads:
            self._wait(e, t.lastw, raw=True)
        for t in writes:
            self._wait(e, t.lastw)
            for r in t.readers.values():
                self._wait(e, r)

    def done(self, tok, reads, writes):
        for t in reads:
            t.readers[tok[2] + str(id(tok[0]))] = tok
        for t in writes:
            t.lastw = tok
            t.readers = {}

    def op(self, e, fn, reads=(), writes=()):
        self.deps(e, reads, writes)
        ins = fn(self.eng[e])
        self.cnt[e] += 1
        ins.then_inc(self.sem[e], 1)
        tok = (self.sem[e], self.cnt[e], e)
        self.done(tok, reads, writes)
        self.ninstr += 1
        return tok

    def dma(self, q, out, in_, reads=(), writes=(), **kw):
        s = self.dnext[q]
        self.dnext[q] = (s + 1) % self.NSLOT
        sem = self.dsem[q][s]
        if self.dcnt[q][s] > 0:
            self._wait(q, (sem, 16 * self.dcnt[q][s], "dma"))
        self.deps(q, reads, writes)
        ins = self.eng[q].dma_start(out=out, in_=in_, **kw)
        self.dcnt[q][s] += 1
        ins.then_inc(sem, 16)
        tok = (sem, 16 * self.dcnt[q][s], "dma")
        self.done(tok, reads, writes)
        self.ninstr += 1
        return tok

    def barrier(self):
        toks = []
        for k in self.eng:
            if self.cnt[k] > 0:
                toks.append((self.sem[k], self.cnt[k], k))
        for q in self.dsem:
            for s in range(self.NSLOT):
                if self.dcnt[q][s] > 0:
                    toks.append((self.dsem[q][s], 16 * self.dcnt[q][s], "dma"))
        for e in self.eng:
            for t in toks:
                self._wait(e, t)
        for k in self.eng:
            if self.cnt[k] > 12000:
                self.sem[k] = self._newsem()
                self.cnt[k] = 0
        for q in self.dsem:
            for s in range(self.NSLOT):
                if self.dcnt[q][s] > 1500:
                    self.dsem[q][s] = self._newsem()
                    self.dcnt[q][s] = 0


class TT:
    def __init__(self, ap):
        self.ap = ap
        self.lastw = None
        self.readers = {}

    def __getitem__(self, k):
        return self.ap[k]


class Scope:
    def __init__(self, kb):
        self.kb = kb
        self.es = ExitStack()

    def __enter__(self):
        self.es.__enter__()
        return self

    def __exit__(self, *a):
        self.kb.S.barrier()
        return self.es.__exit__(*a)

    def sb(self, shape, dt=F32, name="t"):
        self.kb.uid += 1
        return TT(self.es.enter_context(self.kb.nc.sbuf_tensor("%s_%d" % (name, self.kb.uid), list(shape), dt)))

    def ps(self, shape, dt=F32, name="p"):
        self.kb.uid += 1
        assert dt == F32
        shape = list(shape)
        free = int(np.prod(shape[1:]))
        assert free <= 512
        t = self.es.enter_context(self.kb.nc.psum_tensor("%s_%d" % (name, self.kb.uid), [128, 512], dt))
        v = t[0:shape[0], 0:free]
        if len(shape) == 3:
            v = v.rearrange("p (a b) -> p a b", a=shape[1])
        return TT(v)


def bc(ap, n):
    return ap.partition_broadcast(n)


class KB:
    def __init__(self, NL=DEPTH, dbg=(), stages=None):
        self.NL = NL
        self.dbg = set(dbg)
        self.stages = stages
        self.uid = 0
        nc = self.nc = bass.Bass("TRN2", target_bir_lowering=False)
        self.I = {}
        self.scr = {}

        def inp(name, shape, dt=F32):
            self.I[name] = nc.dram_tensor(name, list(shape), dt, kind="ExternalInput").ap()

        inp("xin", [NB, T, D])
        inp("cvec", [3, D])
        L = DEPTH
        inp("ada_w", [L, D, 6 * D]); inp("ada_b", [L, 6 * D]); inp("w_in", [L, D, INW])
        inp("rwkv_mu", [L, 960]); inp("rwkv_w0", [L, 512]); inp("rwkv_w_up", [L, 64, 256])
        inp("rwkv_a0", [L, 512]); inp("rwkv_a_up", [L, 64, 256]); inp("rwkv_g_up", [L, 64, 256])
        inp("rwkv_k_k", [L, 256]); inp("rwkv_k_a", [L, 256]); inp("rwkv_r_k", [L, 256])
        inp("rwkv_ln_g", [L, 256]); inp("rwkv_ln_b", [L, 256])
        inp("gqa_q_norm", [L, 64]); inp("gqa_k_norm", [L, 64]); inp("diff_lambda", [L, 128]); inp("diff_norm", [L, 64])
        inp("w_out", [L, D, D]); inp("post1_g", [L, D]); inp("post1_b", [L, D])
        inp("ffn_w_in", [L, D, 2 * DFF]); inp("ffn_w_out", [L, DFF, D]); inp("post2_g", [L, D]); inp("post2_b", [L, D])
        inp("c_ident", [128, 128]); inp("c_ropeG", [SX, 2, 64]); inp("c_ropeD", [SX, 2, 32]); inp("c_ropeR", [SX, 2, 64])
        inp("c_maskJ", [8, 256]); inp("c_retM", [36, 4, 128, 512], BF16); inp("c_zero", [128, 4096], BF16)
        self.out = nc.dram_tensor("y", [NB, SX, D], F32, kind="ExternalOutput").ap()

        def scr(name, shape, dt=F32):
            kind = "ExternalOutput" if name in self.dbg else "Internal"
            self.scr[name] = nc.dram_tensor(name, list(shape), dt, kind=kind).ap()

        scr("P", [NB, T, INW]); scr("Xs", [NB, T, D]); scr("X1s", [NB, T, D]); scr("modD", [3, 6 * D])
        for m in "gr":
            scr("QT" + m, [NB, 64, 4, T], BF16)
            scr("KT" + m, [NB, 64, 4, T], BF16)
        scr("QTd", [NB, 32, 8, T], BF16)
        scr("KTd", [NB, 32, 8, T], BF16)
        scr("As", [128, T, 8], BF16); scr("Rs", [128, T, 8], BF16); scr("Ws", [2, 128, T, 4])
        scr("LBs", [2, 8, T, 128], BF16); scr("LKs", [2, 8, T, 128], BF16); scr("RVs", [8, T, 256], BF16); scr("Yfull", [2, 8, T, 256])
        scr("Gt", [NB, T, 256]); scr("Bon", [NB, T, 256]); scr("Ycat", [NB, T, D])

    def build(self):
        nc = self.nc
        with ExitStack() as es:
            self.S = S = Sched(nc, es)
            with Scope(self) as g:
                self.g = g
                self.ident = g.sb([128, 128], F32, "ident")
                S.dma("sp", self.ident[:], self.I["c_ident"], writes=[self.ident])
                self.siluT = g.sb([128, 8, 3], F32, "siluT")
                self.cvals = [64.0 * QK_EPS, GN_EPS, LN_EPS, QK_EPS, 0.0]
                self.cst = g.sb([128, 8], F32, "cst")
                for i_, v_ in enumerate(self.cvals):
                    S.op("dve", lambda e: e.memset(self.cst[:, i_:i_ + 1], float(v_)), writes=[self.cst])
                self.stage_init()
                for l in range(self.NL):
                    self.layer(l)
                S.barrier()
            print("ninstr", S.ninstr, "nsem", S.nsem)
        return nc

    def rpow(self, ot, o_ap, it, i_ap, scale, bias, p):
        S = self.S
        assert p == -0.5
        bcol = self.cvals.index(float(bias))
        np_ = o_ap.shape[0]
        S.op("act", lambda e: e.activation(o_ap, i_ap, AF.Sqrt, bias=self.cst[0:np_, bcol:bcol + 1], scale=float(scale)), reads=[it, self.cst], writes=[ot])
        S.op("dve", lambda e: e.reciprocal(o_ap, o_ap), reads=[ot], writes=[ot])

    def on(self, name):
        return self.stages is None or name in self.stages

    def stage_init(self):
        S, I = self.S, self.I
        with Scope(self) as sc:
            cv = sc.sb([3, D])
            S.dma("sp", cv[:], I["cvec"], writes=[cv])
            sv = sc.sb([3, D])
            S.op("act", lambda e: e.activation(sv[:], cv[:], AF.Silu), reads=[cv], writes=[sv])
            pT = sc.ps([128, 8, 3])
            for kc in range(8):
                S.op("pe", lambda e: e.transpose(pT[:, kc, :], sv[:, kc * 128:(kc + 1) * 128], self.ident[0:3, 0:3]), reads=[sv, self.ident], writes=[pT])
            S.op("dve", lambda e: e.tensor_copy(self.siluT[:], pT[:]), reads=[pT], writes=[self.siluT])
            z = sc.sb([128, 4096], BF16)
            S.dma("sp", z[:], I["c_zero"], writes=[z])
            for name in ("LBs", "LKs"):
                flat = self.scr[name].rearrange("d m t k -> (d m t k)").rearrange("(a p f) -> a p f", p=128, f=4096)
                for a in range(flat.shape[0]):
                    S.dma("sp", flat[a], z[:], reads=[z])
            flat = self.scr["RVs"].rearrange("m t k -> (m t k)").rearrange("(a p f) -> a p f", p=128, f=4096)
            for a in range(flat.shape[0]):
                S.dma("sp", flat[a], z[:], reads=[z])

    def layer(self, l):
        self.l = l
        self.last = (l == DEPTH - 1)
        self.Xsrc = self.I["xin"] if l == 0 else self.scr["Xs"]
        if self.on("mod"):
            self.stage_mod(l)
        with Scope(self) as ls:
            self.ls = ls
            S = self.S
            self.modT = ls.sb([128, 48, 3], F32, "modT")
            for v in range(3):
                S.dma("sp", self.modT[:, :, v], self.scr["modD"][v].rearrange("(c p) -> p c", p=128), writes=[self.modT], allow_slow_non_contiguous=True)
            for c0 in (8, 32):
                S.op("dve", lambda e: e.tensor_scalar(self.modT[:, c0:c0 + 8, :], self.modT[:, c0:c0 + 8, :], 1.0, None, ALU.add), reads=[self.modT], writes=[self.modT])
            if self.on("inproj"):
                self.stage_inproj(l)
            if self.on("rwprep"):
                self.stage_rwprep(l)
            if self.on("scan"):
                self.stage_scan(l)
            if self.on("rwpost"):
                self.stage_rwpost(l)
            if self.on("attn"):
                self.stage_attn(l)
            with Scope(self) as mf:
                self.w1 = mf.sb([128, 8, 2 * DFF], BF16, "w1")
                self.w2 = mf.sb([128, NFF, D], BF16, "w2")
                I = self.I
                self.wload = []
                if self.on("ffn"):
                    for kc in range(8):
                        self.wload.append((self.w1, self.w1[:, kc, :], I["ffn_w_in"][l, kc * 128:(kc + 1) * 128, :]))
                    for fc in range(NFF):
                        self.wload.append((self.w2, self.w2[:, fc, :], I["ffn_w_out"][l, fc * 128:(fc + 1) * 128, :]))
                if self.on("mix"):
                    self.stage_mix(l)
                self.pump_wload(1000)
                if self.on("ffn"):
                    self.stage_ffn(l)

    def stage_mod(self, l):
        S, I = self.S, self.I
        with Scope(self) as sc:
            adab = sc.sb([3, 6 * D])
            S.dma("sp", adab[:], bc(I["ada_b"][l], 3), writes=[adab])
            modsb = sc.sb([3, 6 * D])
            wb = [sc.sb([128, 8, 512]) for _ in range(2)]
            pm = [sc.ps([3, 512]) for _ in range(2)]
            for n in range(12):
                w = wb[n % 2]
                S.dma("sp" if n % 2 == 0 else "pool", w[:], I["ada_w"][l, :, n * 512:(n + 1) * 512].rearrange("(c p) n -> p c n", p=128), writes=[w])
                p = pm[n % 2]
                for kc in range(8):
                    S.op("pe", lambda e: e.matmul(p[:], self.siluT[:, kc, :], w[:, kc, :], start=(kc == 0), stop=(kc == 7)), reads=[self.siluT, w], writes=[p])
                S.op("dve", lambda e: e.tensor_tensor(modsb[:, n * 512:(n + 1) * 512], p[:], adab[:, n * 512:(n + 1) * 512], ALU.add), reads=[p, adab], writes=[modsb])
            S.dma("sp", self.scr["modD"], modsb[:], reads=[modsb])

    def xT_modulated(self, sc, src_tile, pT, xmT, col0, width, v, which):
        S = self.S
        shc, scc = (0, 8) if which == 0 else (24, 32)
        for kc in range(8):
            pp = pT[kc % 2]
            S.op("pe", lambda e: e.transpose(pp[:], src_tile[:, kc * 128:(kc + 1) * 128], self.ident[:]), reads=[src_tile, self.ident], writes=[pp])
            eng = "dve" if kc % 2 == 0 else "pool"
            eng = "dve"
            S.op(eng, lambda e: e.tensor_scalar(xmT[:, kc, col0:col0 + 128], pp[:], self.modT[:, scc + kc, v:v + 1], self.modT[:, shc + kc, v:v + 1], ALU.mult, ALU.add), reads=[pp, self.modT], writes=[xmT])

    def rope(self, eng2, out, src, tab, nh, nblk, half, tmp1, tmp2, reads, tabT):
        S = self.S
        w = nblk * 2 * half
        Cb = tab[:, 0, :].unsqueeze(1).to_broadcast([128, nh, w])
        S.op("dve", lambda e: e.tensor_tensor(tmp1, src, Cb, ALU.mult), reads=reads + [tabT], writes=[self._t1])
        v5 = lambda ap: ap.rearrange("p h (b two f) -> p h b two f", b=nblk, two=2)
        sgv = tab[:, 1, :].rearrange("p (b two f) -> p b two f", b=nblk, two=2)
        for hf in range(2):
            o = v5(tmp2)[:, :, :, hf, :]
            i = v5(src)[:, :, :, 1 - hf, :]
            sg = sgv[:, :, hf, :].unsqueeze(1).to_broadcast([128, nh, nblk, half])
            S.op(eng2, lambda e: e.tensor_tensor(o, i, sg, ALU.mult), reads=reads + [tabT], writes=[self._t2])
        S.op("dve", lambda e: e.tensor_tensor(out, tmp1, tmp2, ALU.add), reads=[self._t1, self._t2], writes=[self._ro])

    def stage_inproj(self, l):
        S, I = self.S, self.I
        with Scope(self) as sc:
            win = sc.sb([128, 8, INW], BF16, "win")
            for kc in range(8):
                S.dma("pool", win[:, kc, :], I["w_in"][l, kc * 128:(kc + 1) * 128, :], writes=[win])
            rG = sc.sb([128, 16, 2, 64]); rD = sc.sb([128, 16, 2, 32]); rR = sc.sb([128, 16, 2, 64])
            S.dma("sp", rG[:], I["c_ropeG"].rearrange("(n p) a w -> p n a w", p=128), writes=[rG])
            S.dma("sp", rD[:], I["c_ropeD"].rearrange("(n p) a w -> p n a w", p=128), writes=[rD])
            S.dma("sp", rR[:], I["c_ropeR"].rearrange("(n p) a w -> p n a w", p=128), writes=[rR])
            GW = sc.sb([128, 6, 64])
            for h in range(4):
                S.dma("sp", GW[:, h, :], bc(I["gqa_q_norm"][l], 128), writes=[GW])
            for h in range(4, 6):
                S.dma("sp", GW[:, h, :], bc(I["gqa_k_norm"][l], 128), writes=[GW])
            S.op("dve", lambda e: e.tensor_scalar(GW[:, 4:6, :], GW[:, 4:6, :], 8.0, None, ALU.mult), reads=[GW], writes=[GW])
            xt = [sc.sb([128, D]) for _ in range(2)]
            pT = [sc.ps([128, 128]) for _ in range(2)]
            xmT = [sc.sb([128, 8, 128], BF16) for _ in range(2)]
            pP = [sc.ps([128, 512]) for _ in range(2)]
            Psb = [sc.sb([128, INW]) for _ in range(2)]
            pQ = [sc.ps([64, 4, 128]) for _ in range(2)]
            sq = sc.sb([128, 6, 64]); ss = sc.sb([128, 6]); r1 = sc.sb([128, 6]); qn = sc.sb([128, 6, 64])
            t1 = sc.sb([128, 512]); t2 = sc.sb([128, 512]); ro = sc.sb([128, 512])
            self._t1, self._t2, self._ro = t1, t2, ro
            QTs = [sc.sb([64, 4, 128], BF16) for _ in range(6)]
            it = 0
            for b in range(NB):
                for n in range(NT):
                    v = 2 if n < 2 else b
                    isx = n >= 2
                    rows = slice(n * 128, (n + 1) * 128)
                    x = xt[it % 2]; xm = xmT[it % 2]; P = Psb[it % 2]
                    S.dma("sp", x[:], self.Xsrc[b, rows, :], writes=[x])
                    self.xT_modulated(sc, x, pT, xm, 0, 128, v, 0)
                    for c in range(7):
                        c0 = c * 512
                        w = min(512, INW - c0)
                        pp = pP[c % 2]
                        for kc in range(8):
                            S.op("pe", lambda e: e.matmul(pp[:, :w], xm[:, kc, :], win[:, kc, c0:c0 + w], start=(kc == 0), stop=(kc == 7)), reads=[xm, win], writes=[pp])
                        S.op("act", lambda e: e.activation(P[:, c0:c0 + w], pp[:, :w], AF.Copy), reads=[pp], writes=[P])
                    S.dma("pool", self.scr["P"][b, rows, :], P[:], reads=[P])
                    qk = P[:, O_GQ:O_GQ + 384].rearrange("p (h w) -> p h w", w=64)
                    S.op("pool", lambda e: e.tensor_tensor(sq[:], qk, qk, ALU.mult), reads=[P], writes=[sq])
                    S.op("dve", lambda e: e.tensor_reduce(ss[:], sq[:], AX.X, ALU.add), reads=[sq], writes=[ss])
                    self.rpow(r1, r1[:], ss, ss[:], 1.0, 64.0 * QK_EPS, -0.5)
                    S.op("dve", lambda e: e.tensor_tensor(qn[:], qk, r1[:].unsqueeze(2).to_broadcast([128, 6, 64]), ALU.mult), reads=[P, r1], writes=[qn])
                    S.op("dve", lambda e: e.tensor_tensor(qn[:], qn[:], GW[:], ALU.mult), reads=[qn, GW], writes=[qn])
                    v3 = lambda t, nh, w: t[:, 0:nh * w].rearrange("p (h w) -> p h w", w=w)
                    if isx:
                        self.rope("pool", v3(ro, 6, 64), qn[:], rG[:, n - 2], 6, 2, 16, v3(t1, 6, 64), v3(t2, 6, 64), [qn], rG)
                        src, srcT = v3(ro, 6, 64), ro
                    else:
                        src, srcT = qn[:], qn
                    self.emit_T(src, srcT, 6, pQ, QTs, [("QTg", 0, 4, 1.0), ("KTg", 4, 2, 1.0)], b, rows)
                    dq = P[:, O_DF:O_DF + 512].rearrange("p (h w) -> p h w", w=32)
                    if isx:
                        self.rope("pool", v3(ro, 16, 32), dq, rD[:, n - 2], 16, 2, 8, v3(t1, 16, 32), v3(t2, 16, 32), [P], rD)
                        src, srcT = v3(ro, 16, 32), ro
                    else:
                        src, srcT = dq, P
                    self.emit_T(src, srcT, 16, pQ, QTs, [("QTd", 0, 8, 32 ** -0.5), ("KTd", 8, 8, 1.0)], b, rows, width=32)
                    rq = P[:, O_RT:O_RT + 512].rearrange("p (h w) -> p h w", w=64)
                    if isx:
                        self.rope("pool", v3(ro, 8, 64), rq, rR[:, n - 2], 8, 1, 32, v3(t1, 8, 64), v3(t2, 8, 64), [P], rR)
                        src, srcT = v3(ro, 8, 64), ro
                    else:
                        src, srcT = rq, P
                    self.emit_T(src, srcT, 8, pQ, QTs, [("QTr", 0, 4, 1.0), ("KTr", 4, 4, 0.125)], b, rows)
                    it += 1

    def emit_T(self, src, srcT, nh, pQ, QTs, outs, b, rows, width=64):
        S = self.S
        for (name, h0, cnt_all, scale) in outs:
            for g0 in range(0, cnt_all, 4):
                cnt = min(4, cnt_all - g0)
                self._qi = getattr(self, "_qi", 0) + 1
                pq = pQ[self._qi % 2]
                st = QTs[self._qi % 6]
                for j in range(cnt):
                    S.op("pe", lambda e: e.transpose(pq[0:width, j, :], src[:, h0 + g0 + j, :], self.ident[:]), reads=[srcT, self.ident], writes=[pq])
                S.op("act", lambda e: e.activation(st[0:width, 0:cnt, :], pq[0:width, 0:cnt, :], AF.Copy, scale=float(scale)), reads=[pq], writes=[st])
                S.dma("pool", self.scr[name][b, :, g0:g0 + cnt, rows], st[0:width, 0:cnt, :], reads=[st])

    def stage_rwprep(self, l):
        S, I = self.S, self.I
        with Scope(self) as sc:
            def btile(key, w, nrep=1):
                t = sc.sb([128, nrep * w])
                for r in range(nrep):
                    S.dma("sp", t[:, r * w:(r + 1) * w], bc(I[key][l], 128), writes=[t])
                return t
            MU = btile("rwkv_mu", 960); W0 = btile("rwkv_w0", 512); A0 = btile("rwkv_a0", 512)
            KK = btile("rwkv_k_k", 256); KA = btile("rwkv_k_a", 256); RK = btile("rwkv_r_k", 256)
            WUP = sc.sb([32, 2, 256]); AUP = sc.sb([32, 2, 256]); GUP = sc.sb([64, 256])
            S.dma("sp", WUP[:], I["rwkv_w_up"][l].rearrange("(d r) c -> r d c", d=2), writes=[WUP])
            S.dma("sp", AUP[:], I["rwkv_a_up"][l].rearrange("(d r) c -> r d c", d=2), writes=[AUP])
            S.dma("sp", GUP[:], I["rwkv_g_up"][l], writes=[GUP])
            def mkset():
                cur = sc.sb([128, 960]); prv = sc.sb([128, 960]); nxt = sc.sb([128, 960])
                tt = sc.sb([128, 960])
                pst = sc.sb([128, 1024])
                S.op("dve", lambda e: e.memset(pst[:, 0:64], 0.0), writes=[pst])
                lor = sc.sb([128, 3, 64]); lorT = sc.sb([32, 4, 128]); lorTg = sc.sb([64, 128])
                wt = sc.sb([128, 512]); e1 = sc.sb([128, 512])
                decp = sc.sb([128, 64 + 512])
                S.op("dve", lambda e: e.memset(decp[:, 0:64], 0.0), writes=[decp])
                asig = sc.sb([128, 512]); gt = sc.sb([128, 256])
                kkp = sc.sb([128, 64 + 256])
                S.op("dve", lambda e: e.memset(kkp[:, 0:64], 0.0), writes=[kkp])
                kq = sc.sb([128, 256]); ks = sc.sb([128, 4]); kr = sc.sb([128, 4])
                bd = sc.sb([128, 512], BF16); kd = sc.sb([128, 512]); tk = sc.sb([128, 512]); kd16 = sc.sb([128, 512], BF16); v16 = sc.sb([128, 256], BF16)
                rk = sc.sb([128, 256]); pr = sc.sb([128, 512]); s8 = sc.sb([128, 8]); s4 = sc.sb([128, 4]); bon = sc.sb([128, 256])
                return (cur, prv, nxt, tt, pst, lor, lorT, lorTg, wt, e1, decp, asig, gt, kkp, kq, ks, kr, bd, kd, tk, kd16, v16, rk, pr, s8, s4, bon)
            tsets = [mkset() for _ in range(2)]
            pL = sc.ps([32, 4, 128]); pW = sc.ps([128, 512]); pAl = sc.ps([128, 512]); pG = sc.ps([128, 256])
            pA = sc.ps([128, 4, 128]); pR = sc.ps([128, 4, 128]); pWt = [sc.ps([128, 4, 128]) for _ in range(2)]
            nsets = []
            for _ in range(2):
                Ast = sc.sb([128, 128, 8], BF16); Rst = sc.sb([128, 128, 8], BF16); Wst = [sc.sb([128, 128, 4]) for _ in range(2)]
                S.op("pool", lambda e: e.memset(Ast[:], 0.0), writes=[Ast])
                S.op("pool", lambda e: e.memset(Rst[:], 0.0), writes=[Rst])
                nsets.append((Ast, Rst, Wst))
            P = self.scr["P"]
            def body(n, b):
                rows = slice(n * 128, (n + 1) * 128)
                r0 = n * 128
                Ast, Rst, Wst = nsets[n % 2]
                (cur, prv, nxt, tt, pst, lor, lorT, lorTg, wt, e1, decp, asig, gt, kkp, kq, ks, kr, bd, kd, tk, kd16, v16, rk, pr, s8, s4, bon) = tsets[b]

                def issue_loads(n_, b_):
                    cur_, prv_, nxt_ = tsets[b_][0:3]
                    q0_ = n_ * 128
                    S.dma("sp", cur_[:], P[b_, q0_:q0_ + 128, 0:960], writes=[cur_])
                    if n_ in (0, 2):
                        S.op("pool", lambda e: e.memset(prv_[0:32, :], 0.0), writes=[prv_])
                        S.dma("sp", prv_[1:128, :], P[b_, q0_:q0_ + 127, 0:960], writes=[prv_])
                    else:
                        S.dma("sp", prv_[:], P[b_, q0_ - 1:q0_ + 127, 0:960], writes=[prv_])
                    if n_ in (1, NT - 1):
                        S.op("pool", lambda e: e.memset(nxt_[96:128, :], 0.0), writes=[nxt_])
                        S.dma("sp", nxt_[0:127, :], P[b_, q0_ + 1:q0_ + 128, 0:960], writes=[nxt_])
                    else:
                        S.dma("sp", nxt_[:], P[b_, q0_ + 1:q0_ + 129, 0:960], writes=[nxt_])

                if n == 0 and b == 0:
                    issue_loads(0, 0)
                    issue_loads(0, 1)
                S.op("pool", lambda e: e.tensor_tensor(tt[:], prv[:], nxt[:], ALU.add), reads=[prv, nxt], writes=[tt])
                yield
                S.op("dve", lambda e: e.scalar_tensor_tensor(tt[:], tt[:], 0.5, cur[:], ALU.mult, ALU.subtract), reads=[tt, cur], writes=[tt])
                yield
                S.op("pool", lambda e: e.tensor_tensor(tt[:], tt[:], MU[:], ALU.mult), reads=[tt, MU], writes=[tt])
                yield
                S.op("dve", lambda e: e.tensor_tensor(pst[:, 64:1024], tt[:], cur[:], ALU.add), reads=[tt, cur], writes=[pst])
                yield
                if n + 1 < NT:
                    issue_loads(n + 1, b)
                p_r = pst[:, 64:320]; p_k = pst[:, 320:576]; p_v = pst[:, 576:832]
                S.op("act", lambda e: e.activation(lor[:, 0, :], pst[:, 832:896], AF.Tanh), reads=[pst], writes=[lor])
                yield
                S.op("act", lambda e: e.activation(lor[:, 2, :], pst[:, 960:1024], AF.Sigmoid), reads=[pst], writes=[lor])
                yield
                S.op("pool", lambda e: e.tensor_copy(lor[:, 1, :], pst[:, 896:960]), reads=[pst], writes=[lor])
                yield
                if 'A' in KSKIP:
                    return
                for j in range(4):
                    S.op("pe", lambda e: e.transpose(pL[:, j, :], lor[:, j // 2, 32 * (j % 2):32 * (j % 2) + 32], self.ident[:]), reads=[lor, self.ident], writes=[pL])
                    yield
                S.op("dve", lambda e: e.tensor_copy(lorT[:], pL[:]), reads=[pL], writes=[lorT])
                yield
                S.op("pe", lambda e: e.transpose(pG[0:64, 0:128], lor[:, 2, :], self.ident[:]), reads=[lor, self.ident], writes=[pG])
                yield
                S.op("dve", lambda e: e.tensor_copy(lorTg[:], pG[0:64, 0:128]), reads=[pG], writes=[lorTg])
                yield
                for d in range(2):
                    S.op("pe", lambda e: e.matmul(pW[:, d * 256:(d + 1) * 256], lorT[:, d, :], WUP[:, d, :], start=True, stop=True), reads=[lorT, WUP], writes=[pW])
                    yield
                    S.op("pe", lambda e: e.matmul(pAl[:, d * 256:(d + 1) * 256], lorT[:, 2 + d, :], AUP[:, d, :], start=True, stop=True), reads=[lorT, AUP], writes=[pAl])
                    yield
                S.op("pe", lambda e: e.matmul(pG[:], lorTg[:], GUP[:], start=True, stop=True), reads=[lorTg, GUP], writes=[pG])
                yield
                if 'B' in KSKIP:
                    return
                S.op("dve", lambda e: e.tensor_tensor(wt[:], pW[:], W0[:], ALU.add), reads=[pW, W0], writes=[wt])
                yield
                S.op("act", lambda e: e.activation(e1[:], wt[:], AF.Exp, scale=-1.0), reads=[wt], writes=[e1])
                yield
                S.op("dve", lambda e: e.tensor_scalar(e1[:], e1[:], 1.0, None, ALU.add), reads=[e1], writes=[e1])
                yield
                S.op("dve", lambda e: e.reciprocal(e1[:], e1[:]), reads=[e1], writes=[e1])
                yield
                S.op("act", lambda e: e.activation(decp[:, 64:576], e1[:], AF.Exp, scale=-math.exp(-0.5)), reads=[e1], writes=[decp])
                yield
                S.op("dve", lambda e: e.tensor_tensor(wt[:], pAl[:], A0[:], ALU.add), reads=[pAl, A0], writes=[wt])
                yield
                S.op("act", lambda e: e.activation(e1[:], wt[:], AF.Exp, scale=-1.0), reads=[wt], writes=[e1])
                yield
                S.op("dve", lambda e: e.tensor_scalar(e1[:], e1[:], 1.0, None, ALU.add), reads=[e1], writes=[e1])
                yield
                S.op("dve", lambda e: e.reciprocal(asig[:], e1[:]), reads=[e1], writes=[asig])
                yield
                S.op("act", lambda e: e.activation(gt[:], pG[:], AF.Copy), reads=[pG], writes=[gt])
                yield
                S.dma("pool", self.scr["Gt"][b, rows, :], gt[:], reads=[gt])
                yield
                yield "MID"
                kk = kkp[:, 64:320]
                S.op("pool", lambda e: e.tensor_tensor(kk, p_k, KK[:], ALU.mult), reads=[pst, KK], writes=[kkp])
                yield
                S.op("pool", lambda e: e.tensor_tensor(kq[:], kk, kk, ALU.mult), reads=[kkp], writes=[kq])
                yield
                S.op("dve", lambda e: e.tensor_reduce(ks[:], kq[:].rearrange("p (h w) -> p h w", w=64), AX.X, ALU.add), reads=[kq], writes=[ks])
                yield
                S.op("dve", lambda e: e.tensor_scalar(kr[:], ks[:], 1e-24, None, ALU.max), reads=[ks], writes=[kr])
                yield
                self.rpow(kr, kr[:], kr, kr[:], 1.0, 0.0, -0.5)
                yield
                kk3 = kk.rearrange("p (h w) -> p h w", w=64)
                S.op("dve", lambda e: e.tensor_tensor(kk3, kk3, kr[:].unsqueeze(2).to_broadcast([128, 4, 64]), ALU.mult), reads=[kkp, kr], writes=[kkp])
                yield
                d3 = lambda t: t[:].rearrange("p (d c) -> p d c", d=2)
                b2 = lambda ap: ap.unsqueeze(1).to_broadcast([128, 2, 256])
                S.op("dve", lambda e: e.tensor_tensor(d3(bd), d3(asig), b2(kk), ALU.mult), reads=[asig, kkp], writes=[bd])
                yield
                S.op("dve", lambda e: e.scalar_tensor_tensor(d3(tk), d3(asig), -1.0, b2(KA[:]), ALU.add, ALU.mult), reads=[asig, KA], writes=[tk])
                yield
                S.op("dve", lambda e: e.scalar_tensor_tensor(d3(kd), d3(tk), 1.0, b2(p_k), ALU.add, ALU.mult), reads=[tk, pst], writes=[kd])
                yield
                S.op("pool", lambda e: e.tensor_copy(kd16[:], kd[:]), reads=[kd], writes=[kd16])
                yield
                S.op("pool", lambda e: e.tensor_copy(v16[:], p_v), reads=[pst], writes=[v16])
                yield
                S.op("pool", lambda e: e.tensor_tensor(rk[:], p_r, RK[:], ALU.mult), reads=[pst, RK], writes=[rk])
                yield
                S.op("dve", lambda e: e.tensor_tensor(d3(pr), d3(kd), b2(rk[:]), ALU.mult), reads=[kd, rk], writes=[pr])
                yield
                S.op("dve", lambda e: e.tensor_reduce(s8[:], pr[:].rearrange("p (g w) -> p g w", w=64), AX.X, ALU.add), reads=[pr], writes=[s8])
                yield
                S.op("dve", lambda e: e.tensor_tensor(s4[:], s8[:, 0:4], s8[:, 4:8], ALU.add), reads=[s8], writes=[s4])
                yield
                S.op("dve", lambda e: e.tensor_tensor(bon[:].rearrange("p (h w) -> p h w", w=64), p_v.rearrange("p (h w) -> p h w", w=64), s4[:].unsqueeze(2).to_broadcast([128, 4, 64]), ALU.mult), reads=[pst, s4], writes=[bon])
                yield
                S.dma("pool", self.scr["Bon"][b, rows, :], bon[:], reads=[bon])
                yield
                if 'C' in KSKIP:
                    return
                lo, hi = b * 64, (b + 1) * 64
                for h in range(4):
                    if b == 0:
                        ink = kkp[:, 64 + 64 * h:128 + 64 * h]; inr = pst[:, 64 + 64 * h:128 + 64 * h]
                        S.op("pe", lambda e: e.transpose(pA[0:64, h, :], ink, self.ident[:]), reads=[kkp, self.ident], writes=[pA])
                        yield
                        S.op("pe", lambda e: e.transpose(pR[0:64, h, :], inr, self.ident[:]), reads=[pst, self.ident], writes=[pR])
                        yield
                    else:
                        ink = kkp[:, 64 * h:128 + 64 * h]; inr = pst[:, 64 * h:128 + 64 * h]
                        S.op("pe", lambda e: e.transpose(pA[:, h, :], ink, self.ident[:]), reads=[kkp, self.ident], writes=[pA])
                        yield
                        S.op("pe", lambda e: e.transpose(pR[:, h, :], inr, self.ident[:]), reads=[pst, self.ident], writes=[pR])
                        yield
                S.op("act", lambda e: e.activation(Ast[lo:hi, :, 4 * b:4 * b + 4].rearrange("p t h -> p h t"), pA[lo:hi, :, :], AF.Copy, scale=-1.0), reads=[pA], writes=[Ast])
                yield
                S.op("act", lambda e: e.activation(Rst[lo:hi, :, 4 * b:4 * b + 4].rearrange("p t h -> p h t"), pR[lo:hi, :, :], AF.Copy), reads=[pR], writes=[Rst])
                yield
                for d in range(2):
                    for h in range(4):
                        c0 = 64 + 256 * d + 64 * h
                        if b == 0:
                            S.op("pe", lambda e: e.transpose(pWt[d][0:64, h, :], decp[:, c0:c0 + 64], self.ident[:]), reads=[decp, self.ident], writes=[pWt[d]])
                            yield
                        else:
                            S.op("pe", lambda e: e.transpose(pWt[d][:, h, :], decp[:, c0 - 64:c0 + 64], self.ident[:]), reads=[decp, self.ident], writes=[pWt[d]])
                            yield
                    S.op("dve", lambda e: e.tensor_copy(Wst[d][lo:hi, :, :].rearrange("p t h -> p h t"), pWt[d][lo:hi, :, :]), reads=[pWt[d]], writes=[Wst[d]])
                    yield
                if 'D' in KSKIP:
                    return
                for d in range(2):
                    if 'E' in KSKIP:
                        return
                    S.dma("sp", self.scr["LBs"][d, 4 * b:4 * b + 4, rows, lo:hi].rearrange("h t k -> t h k"), bd[:, 256 * d:256 * d + 256].rearrange("p (h k) -> p h k", k=64), reads=[bd])
                    yield
                    S.dma("sp", self.scr["LKs"][d, 4 * b:4 * b + 4, rows, lo:hi].rearrange("h t k -> t h k"), kd16[:, 256 * d:256 * d + 256].rearrange("p (h k) -> p h k", k=64), reads=[kd16])
                    yield
                rv = self.scr["RVs"]
                dst = bass.AP(rv.tensor, (4 * b) * T * 256 + r0 * 256, [[256, 128], [T * 256 + 64, 4], [1, 64]])
                if 'F' not in KSKIP:
                    S.dma("sp", dst, v16[:].rearrange("p (h k) -> p h k", k=64), reads=[v16])
                    yield
                if b == NB - 1:
                    S.dma("pool", self.scr["As"][:, rows, :], Ast[:], reads=[Ast])
                    S.dma("pool", self.scr["Rs"][:, rows, :], Rst[:], reads=[Rst])
                    for d in range(2):
                        S.dma("pool", self.scr["Ws"][d, :, rows, :], Wst[d][:], reads=[Wst[d]])
                yield

            def run_to_mid(g):
                for v in g:
                    if v == "MID":
                        return

            old = None
            for n in range(NT):
                for b in range(NB):
                    new = body(n, b)
                    if old is None or 'S' in KSKIP:
                        if old is not None:
                            for _ in old:
                                pass
                        run_to_mid(new)
                        old = new
                        continue
                    new_mid = False
                    old_done = False
                    while not old_done:
                        try:
                            next(old)
                        except StopIteration:
                            old_done = True
                        if not new_mid:
                            if next(new, "END") == "MID":
                                new_mid = True
                    if not new_mid:
                        run_to_mid(new)
                    old = new
            for _ in old:
                pass

    def stage_scan(self, l):
        S, I = self.S, self.I
        with Scope(self) as sc:
            MJ = sc.sb([8, 256])
            S.dma("sp", MJ[:], I["c_maskJ"], writes=[MJ])
            St = [sc.sb([128, 256]) for _ in range(2)]
            Tmp = [sc.sb([128, 256]) for _ in range(2)]
            SAm = [sc.sb([8, 256], BF16) for _ in range(2)]
            for d in range(2):
                S.op("dve", lambda e: e.memset(St[d][:], 0.0), writes=[St[d]])
                S.op("dve", lambda e: e.memset(SAm[d][:], 0.0), writes=[SAm[d]])
            pSAY = [sc.ps([40, 256]) for _ in range(2)]
            pU = [sc.ps([128, 256]) for _ in range(2)]
            NBUF = 2
            bufs = []
            for i in range(NBUF):
                bb = []
                for d in range(2):
                    B = dict(AR=sc.sb([128, CS, 40], BF16), W=sc.sb([128, CS, 4]),
                             LB=sc.sb([8, CS, 128], BF16), LK=sc.sb([8, CS, 128], BF16), RV=sc.sb([8, CS, 256], BF16), Y=sc.sb([40, CS, 256]))
                    S.op("pool", lambda e: e.memset(B["AR"][:], 0.0), writes=[B["AR"]])
                    bb.append(B)
                bufs.append(bb)
            NCH = T // CS

            def rowbase(d, c):
                if d == 0:
                    return c * CS
                t0 = c * CS
                if t0 < CT:
                    return CT - CS - t0
                return (T + CT - CS) - t0

            As, Rs = self.scr["As"], self.scr["Rs"]

            def load(c):
                bb = bufs[c % NBUF]
                for d in range(2):
                    ra = rowbase(d, c)
                    rs = slice(ra, ra + CS)
                    q = "sp"
                    B = bb[d]
                    S.dma(q, B["AR"][:, :, 32:40], Rs[:, rs, :], writes=[B["AR"]])
                    if d == 0:
                        n = min(CS, T - (ra + 1))
                        S.dma(q, B["AR"][:, 0:n, 0:8], As[:, ra + 1:ra + 1 + n, :], writes=[B["AR"]])
                    elif ra == 0:
                        S.dma(q, B["AR"][:, 1:CS, 0:8], As[:, 0:CS - 1, :], writes=[B["AR"]])
                        S.dma(q, B["AR"][:, 0:1, 0:8], As[:, T - 1:T, :], writes=[B["AR"]])
                    else:
                        S.dma(q, B["AR"][:, :, 0:8], As[:, ra - 1:ra + CS - 1, :], writes=[B["AR"]])
                    S.dma(q, B["W"][:], self.scr["Ws"][d, :, rs, :], writes=[B["W"]])
                    S.dma(q, B["LB"][:], self.scr["LBs"][d, :, rs, :], writes=[B["LB"]])
                    S.dma(q, B["LK"][:], self.scr["LKs"][d, :, rs, :], writes=[B["LK"]])
                    S.dma(q, B["RV"][:], self.scr["RVs"][:, rs, :], writes=[B["RV"]])

            W3 = lambda B, i: B["W"][:, i, :].unsqueeze(2).to_broadcast([128, 4, 64])
            j4 = lambda t: t[:].rearrange("p (j v) -> p j v", j=4)
            hi16 = lambda t: t[:].bitcast(BF16).rearrange("p (n two) -> p n two", two=2)[:, :, 1]
            load(0)
            for c in range(NCH):
                if c + 1 < NCH:
                    load(c + 1)
                bb = bufs[c % NBUF]
                for s in range(CS):
                    idx = [s, CS - 1 - s]
                    for d in range(2):
                        B = bb[d]; i = idx[d]; pu = pU[d]
                        S.op("pe", lambda e: e.matmul(pu[:], B["LK"][:, i, :], B["RV"][:, i, :], start=True, stop=False), reads=[B["LK"], B["RV"]], writes=[pu])
                        S.op("pe", lambda e: e.matmul(pu[:], B["LB"][:, i, :], SAm[d][:], start=False, stop=True), reads=[B["LB"], SAm[d]], writes=[pu])
                    for d in range(2):
                        S.op("pool", lambda e: e.tensor_tensor(j4(Tmp[d]), j4(St[d]), W3(bb[d], idx[d]), ALU.mult), reads=[St[d], bb[d]["W"]], writes=[Tmp[d]])
                    for d in range(2):
                        S.op("dve", lambda e: e.tensor_tensor(St[d][:], Tmp[d][:], pU[d][:], ALU.add), reads=[Tmp[d], pU[d]], writes=[St[d]])
                    for d in range(2):
                        S.op("pe", lambda e: e.matmul(pSAY[d][:], bb[d]["AR"][:, idx[d], :], hi16(St[d]), start=True, stop=True), reads=[bb[d]["AR"], St[d]], writes=[pSAY[d]])
                    for d in range(2):
                        S.op("dve", lambda e: e.tensor_tensor(SAm[d][:], pSAY[d][0:8, :], MJ[:], ALU.mult), reads=[pSAY[d], MJ], writes=[SAm[d]])
                    for d in range(2):
                        S.op("act", lambda e: e.activation(bb[d]["Y"][32:40, idx[d], :], pSAY[d][32:40, :], AF.Copy), reads=[pSAY[d]], writes=[bb[d]["Y"]])
                for d in range(2):
                    ra = rowbase(d, c)
                    S.dma("pool", self.scr["Yfull"][d, :, ra:ra + CS, :], bb[d]["Y"][32:40, :, :], reads=[bb[d]["Y"]])

    def stage_rwpost(self, l):
        S, I = self.S, self.I
        with Scope(self) as sc:
            LNG = sc.sb([128, 256]); LNB = sc.sb([128, 256])
            S.dma("sp", LNG[:], bc(I["rwkv_ln_g"][l], 128), writes=[LNG])
            S.dma("sp", LNB[:], bc(I["rwkv_ln_b"][l], 128), writes=[LNB])
            yf = [sc.sb([128, 256]) for _ in range(2)]; yb = [sc.sb([128, 256]) for _ in range(2)]
            bo = [sc.sb([128, 256]) for _ in range(2)]; gg = [sc.sb([128, 256]) for _ in range(2)]
            y = sc.sb([128, 256]); sq = sc.sb([128, 256]); s1 = sc.sb([128, 4]); s2 = sc.sb([128, 4]); o = [sc.sb([128, 256]) for _ in range(2)]
            h3 = lambda t: t[:].rearrange("p (h w) -> p h w", w=64)
            b3 = lambda t: t[:].unsqueeze(2).to_broadcast([128, 4, 64])
            yfull = self.scr["Yfull"]
            it = 0
            for b in range(NB):
                for n in range(NT):
                    if self.last and n < 2:
                        continue
                    rows = slice(n * 128, (n + 1) * 128)
                    i = it % 2
                    for d, dstt in ((0, yf[i]), (1, yb[i])):
                        src = bass.AP(yfull.tensor, d * 8 * T * 256 + (4 * b) * T * 256 + n * 128 * 256, [[256, 128], [T * 256 + 64, 4], [1, 64]])
                        S.dma("sp", h3(dstt), src, writes=[dstt])
                    S.dma("sp", bo[i][:], self.scr["Bon"][b, rows, :], writes=[bo[i]])
                    S.dma("sp", gg[i][:], self.scr["Gt"][b, rows, :], writes=[gg[i]])
                    S.op("pool", lambda e: e.tensor_tensor(y[:], yf[i][:], yb[i][:], ALU.add), reads=[yf[i], yb[i]], writes=[y])
                    self.head_norm(y, sq, s1, s2, GN_EPS)
                    S.op("dve", lambda e: e.tensor_tensor(y[:], y[:], LNG[:], ALU.mult), reads=[y, LNG], writes=[y])
                    S.op("pool", lambda e: e.tensor_tensor(y[:], y[:], LNB[:], ALU.add), reads=[y, LNB], writes=[y])
                    S.op("pool", lambda e: e.tensor_tensor(y[:], y[:], bo[i][:], ALU.add), reads=[y, bo[i]], writes=[y])
                    S.op("dve", lambda e: e.tensor_tensor(o[i][:], y[:], gg[i][:], ALU.mult), reads=[y, gg[i]], writes=[o[i]])
                    S.dma("pool", self.scr["Ycat"][b, rows, 0:256], o[i][:], reads=[o[i]])
                    it += 1

    def head_norm(self, y, sq, s1, s2, eps, nh=4):
        S = self.S
        h3 = lambda t: t[:, 0:nh * 64].rearrange("p (h w) -> p h w", w=64)
        b3 = lambda t: t[:, 0:nh].unsqueeze(2).to_broadcast([128, nh, 64])
        S.op("dve", lambda e: e.tensor_reduce(s1[:, 0:nh], h3(y), AX.X, ALU.add), reads=[y], writes=[s1])
        S.op("dve", lambda e: e.tensor_scalar(s1[:, 0:nh], s1[:, 0:nh], -1.0 / 64, None, ALU.mult), reads=[s1], writes=[s1])
        S.op("dve", lambda e: e.tensor_tensor(h3(y), h3(y), b3(s1), ALU.add), reads=[y, s1], writes=[y])
        S.op("pool", lambda e: e.tensor_tensor(h3(sq), h3(y), h3(y), ALU.mult), reads=[y], writes=[sq])
        S.op("dve", lambda e: e.tensor_reduce(s2[:, 0:nh], h3(sq), AX.X, ALU.add), reads=[sq], writes=[s2])
        self.rpow(s2, s2[:, 0:nh], s2, s2[:, 0:nh], 1.0 / 64, eps, -0.5)
        S.op("dve", lambda e: e.tensor_tensor(h3(y), h3(y), b3(s2), ALU.mult), reads=[y, s2], writes=[y])

    def stage_attn(self, l):
        S, I = self.S, self.I
        lam_init = 0.8 - 0.6 * math.exp(-0.3 * l)
        with Scope(self) as sc:
            DL = sc.sb([128, 128]); dp = sc.sb([128, 128]); ds = sc.sb([128, 2]); lam = sc.sb([128, 1]); nlam = sc.sb([128, 1])
            S.dma("sp", DL[:], bc(I["diff_lambda"][l], 128), writes=[DL])
            S.op("dve", lambda e: e.tensor_tensor(dp[:, 0:32], DL[:, 0:32], DL[:, 32:64], ALU.mult), reads=[DL], writes=[dp])
            S.op("dve", lambda e: e.tensor_tensor(dp[:, 32:64], DL[:, 64:96], DL[:, 96:128], ALU.mult), reads=[DL], writes=[dp])
            S.op("dve", lambda e: e.tensor_reduce(ds[:], dp[:, 0:64].rearrange("p (a w) -> p a w", w=32), AX.X, ALU.add), reads=[dp], writes=[ds])
            S.op("act", lambda e: e.activation(ds[:], ds[:], AF.Exp), reads=[ds], writes=[ds])
            S.op("dve", lambda e: e.tensor_tensor(lam[:], ds[:, 0:1], ds[:, 1:2], ALU.subtract), reads=[ds], writes=[lam])
            S.op("dve", lambda e: e.tensor_scalar(nlam[:], lam[:], lam_init, -1.0, ALU.add, ALU.mult), reads=[lam], writes=[nlam])
            DN = sc.sb([128, 64])
            S.dma("sp", DN[:], bc(I["diff_norm"][l], 128), writes=[DN])
            S.op("dve", lambda e: e.tensor_scalar(DN[:], DN[:], 1.0 - lam_init, None, ALU.mult), reads=[DN], writes=[DN])
            KT = sc.sb([64, 4, T], BF16); QT = sc.sb([64, 4, T], BF16)
            KTd = sc.sb([32, 8, T], BF16); QTd = sc.sb([32, 8, T], BF16)
            V = sc.sb([128, NT, 4, 65], BF16)
            S.op("pool", lambda e: e.memset(V[:, :, :, 64:65], 1.0), writes=[V])
            pS = [sc.ps([128, 512]) for _ in range(2)]
            pO = [sc.ps([128, 4, 65]) for _ in range(4)]
            Pball = [sc.sb([128, NT, 512], BF16) for _ in range(2)]
            Mk = [sc.sb([128, 512], BF16) for _ in range(3)]
            rec = sc.sb([128, 4, 1]); o1 = sc.sb([128, 4, 64]); o2 = sc.sb([128, 4, 64])
            sq = sc.sb([128, 256]); s1 = sc.sb([128, 4]); s2 = sc.sb([128, 4])
            gate = sc.sb([128, 4, 64]); gs = sc.sb([128, 4, 64])
            osb = [sc.sb([128, 4, 64]) for _ in range(2)]
            Pd = self.scr["P"]
            cnt = {"u": 0, "o": 0, "p": 0}
            for m in "gdr":
                nk = 2 if m == "g" else 4
                nv = 2 if m == "g" else 4
                vcol = {"g": O_GQ + 384, "d": O_DF + 512, "r": O_RT + 512}[m]
                ocol = {"g": 256, "d": 512, "r": 768}[m]
                for b in range(NB):
                    if m == "d":
                        S.dma("sp", KTd[:], self.scr["KTd"][b], writes=[KTd])
                        S.dma("sp", QTd[:], self.scr["QTd"][b], writes=[QTd])
                    else:
                        S.dma("sp", KT[:, 0:nk, :], self.scr["KT" + m][b, :, 0:nk, :], writes=[KT])
                        S.dma("sp", QT[:], self.scr["QT" + m][b], writes=[QT])
                    for hv in range(nv):
                        S.dma("pool", V[:, :, hv, 0:64], Pd[b, :, vcol + hv * 64:vcol + hv * 64 + 64].rearrange("(n p) w -> p n w", p=128), writes=[V])
                    chunks = ([] if self.last else [(0, 256, 0, 2)]) + [(256 + 512 * q, 512, 0, NT) for q in range(4)]
                    for h in range(4):
                        for (q0, w, k0, k1) in chunks:
                            nj = w // 128
                            isx = q0 >= 256
                            units = [(0, 0)] if m != "d" else [(0, 0), (1, 0)]
                            pos = []
                            for (mi, rbase) in units:
                                po = pO[cnt["o"] % 4]; cnt["o"] += 1
                                pos.append(po)
                                rws = slice(rbase, rbase + (64 if m != "d" else 32))
                                hk = h // 2 if m == "g" else h
                                hvv = h // 2 if m == "g" else h
                                PB_ = Pball[cnt["o"] % 2]
                                for kt in range(k0, k1):
                                    ps_ = pS[cnt["u"] % 2]; cnt["u"] += 1
                                    pb = PB_[:, kt, :]
                                    if m == "d":
                                        S.op("pe", lambda e: e.matmul(ps_[:, :w], KTd[:, 2 * h + mi, kt * 128:(kt + 1) * 128], QTd[:, 2 * h + mi, q0:q0 + w], start=True, stop=True), reads=[KTd, QTd], writes=[ps_])
                                    else:
                                        S.op("pe", lambda e: e.matmul(ps_[:, :w], KT[rws, hk, kt * 128:(kt + 1) * 128], QT[rws, h, q0:q0 + w], start=True, stop=True), reads=[KT, QT], writes=[ps_])
                                    if m == "r":
                                        mk = Mk[cnt["p"] % 3]; cnt["p"] += 1
                                        if not isx:
                                            ti = {0: 15, 1: 14}[kt]
                                        elif kt < 2:
                                            ti = 28 + kt * 4 + (q0 - 256) // 512
                                        else:
                                            off = 4 * ((q0 - 256) // 512) - (kt - 2)
                                            ti = off + 15
                                        S.dma("sp", mk[:], I["c_retM"][ti, h], writes=[mk])
                                        S.op("dve", lambda e: e.tensor_tensor(pb[:, :w], ps_[:, :w], mk[:, :w], ALU.mult), reads=[ps_, mk], writes=[PB_])
                                    else:
                                        S.op("act", lambda e: e.activation(pb[:, :w], ps_[:, :w], AF.Exp), reads=[ps_], writes=[PB_])
                                for j in range(nj):
                                    for kt in range(k0, k1):
                                        S.op("pe", lambda e: e.matmul(po[:, j, :], PB_[:, kt, j * 128:(j + 1) * 128], V[:, kt, hvv, :], start=(kt == k0), stop=(kt == k1 - 1)), reads=[PB_, V], writes=[po])
                            ob = osb[cnt["o"] % 2]
                            dst = self.scr["Ycat"][b, q0:q0 + w, ocol + h * 64:ocol + h * 64 + 64].rearrange("(j t) v -> t j v", t=128)
                            if m == "g":
                                po = pos[0]
                                S.op("dve", lambda e: e.reciprocal(rec[:, 0:nj, :], po[:, 0:nj, 64:65]), reads=[po], writes=[rec])
                                S.op("dve", lambda e: e.tensor_tensor(ob[:, 0:nj, :], po[:, 0:nj, 0:64], rec[:, 0:nj, :].to_broadcast([128, nj, 64]), ALU.mult), reads=[po, rec], writes=[ob])
                            elif m == "d":
                                for (po, ot) in ((pos[0], o1), (pos[1], o2)):
                                    S.op("dve", lambda e: e.reciprocal(rec[:, 0:nj, :], po[:, 0:nj, 64:65]), reads=[po], writes=[rec])
                                    S.op("dve", lambda e: e.tensor_tensor(ot[:, 0:nj, :], po[:, 0:nj, 0:64], rec[:, 0:nj, :].to_broadcast([128, nj, 64]), ALU.mult), reads=[po, rec], writes=[ot])
                                S.op("dve", lambda e: e.scalar_tensor_tensor(o1[:, 0:nj, :], o2[:, 0:nj, :], nlam[:, 0:1], o1[:, 0:nj, :], ALU.mult, ALU.add), reads=[o1, o2, nlam], writes=[o1])
                                S.op("pool", lambda e: e.tensor_tensor(o2[:, 0:nj, :], o1[:, 0:nj, :], o1[:, 0:nj, :], ALU.mult), reads=[o1], writes=[o2])
                                S.op("dve", lambda e: e.tensor_reduce(s1[:, 0:nj], o2[:, 0:nj, :], AX.X, ALU.add), reads=[o2], writes=[s1])
                                self.rpow(s1, s1[:, 0:nj], s1, s1[:, 0:nj], 1.0 / 64, QK_EPS, -0.5)
                                S.op("dve", lambda e: e.tensor_tensor(o1[:, 0:nj, :], o1[:, 0:nj, :], s1[:, 0:nj].unsqueeze(2).to_broadcast([128, nj, 64]), ALU.mult), reads=[o1, s1], writes=[o1])
                                S.op("dve", lambda e: e.tensor_tensor(ob[:, 0:nj, :], o1[:, 0:nj, :], DN[:].unsqueeze(1).to_broadcast([128, nj, 64]), ALU.mult), reads=[o1, DN], writes=[ob])
                            else:
                                po = pos[0]
                                S.dma("sp", gate[:, 0:nj, :], Pd[b, q0:q0 + w, O_RT + 768 + h * 64:O_RT + 768 + h * 64 + 64].rearrange("(j t) v -> t j v", t=128), writes=[gate])
                                S.op("act", lambda e: e.activation(gs[:, 0:nj, :], gate[:, 0:nj, :], AF.Silu), reads=[gate], writes=[gs])
                                yv = TT(o1[:].rearrange("p j w -> p (j w)"))
                                S.op("act", lambda e: e.activation(o1[:, 0:nj, :], po[:, 0:nj, 0:64], AF.Copy), reads=[po], writes=[o1])
                                yv.lastw = o1.lastw
                                self.head_norm(yv, sq, s1, s2, LN_EPS, nh=nj)
                                o1.lastw = yv.lastw
                                S.op("dve", lambda e: e.tensor_tensor(ob[:, 0:nj, :], o1[:, 0:nj, :], gs[:, 0:nj, :], ALU.mult), reads=[o1, gs], writes=[ob])
                            S.dma("sp", dst, ob[:, 0:nj, :], reads=[ob])

    def layer_norm_out(self, sc, x1, G, Bt, out, tmps):
        S = self.S
        s1, s2, sq = tmps
        S.op("dve", lambda e: e.tensor_reduce(s1[:], x1[:], AX.X, ALU.add), reads=[x1], writes=[s1])
        S.op("dve", lambda e: e.tensor_scalar(s1[:], s1[:], -1.0 / D, None, ALU.mult), reads=[s1], writes=[s1])
        S.op("dve", lambda e: e.tensor_scalar(x1[:], x1[:], s1[:, 0:1], None, ALU.add), reads=[x1, s1], writes=[x1])
        S.op("pool", lambda e: e.tensor_tensor(sq[:], x1[:], x1[:], ALU.mult), reads=[x1], writes=[sq])
        S.op("dve", lambda e: e.tensor_reduce(s2[:], sq[:], AX.X, ALU.add), reads=[sq], writes=[s2])
        self.rpow(s2, s2[:], s2, s2[:], 1.0 / D, LN_EPS, -0.5)
        S.op("dve", lambda e: e.scalar_tensor_tensor(x1[:], x1[:], s2[:, 0:1], G[:], ALU.mult, ALU.mult), reads=[x1, s2, G], writes=[x1])
        S.op("pool", lambda e: e.tensor_tensor(out[:], x1[:], Bt[:], ALU.add), reads=[x1, Bt], writes=[out])

    def pump_wload(self, k):
        for _ in range(k):
            if not self.wload:
                return
            t, o, i = self.wload.pop(0)
            self.S.dma("pool", o, i, writes=[t])

    def stage_mix(self, l):
        S, I = self.S, self.I
        with Scope(self) as sc:
            wo = sc.sb([128, 8, D], BF16)
            for kc in range(8):
                S.dma("pool", wo[:, kc, :], I["w_out"][l, kc * 128:(kc + 1) * 128, :], writes=[wo])
            G1 = sc.sb([128, D])
            g1v = [None]
            PG = sc.sb([128, D]); PB = sc.sb([128, D])
            S.dma("sp", PG[:], bc(I["post1_g"][l], 128), writes=[PG])
            S.dma("sp", PB[:], bc(I["post1_b"][l], 128), writes=[PB])
            yc = [sc.sb([128, D]) for _ in range(2)]; xt = [sc.sb([128, D]) for _ in range(2)]
            pT = [sc.ps([128, 128]) for _ in range(2)]
            yT = [sc.sb([128, 8, 128], BF16) for _ in range(2)]
            pO = [sc.ps([128, 512]) for _ in range(2)]
            t = sc.sb([128, D]); x1 = sc.sb([128, D]); outt = [sc.sb([128, D]) for _ in range(2)]
            s1 = sc.sb([128, 1]); s2 = sc.sb([128, 1]); sq = sc.sb([128, D])
            it = 0
            for b in range(NB):
                for n in range(NT):
                    if self.last and n < 2:
                        continue
                    v = 2 if n < 2 else b
                    if g1v[0] != v:
                        S.dma("sp", G1[:], bc(self.scr["modD"][v, 2 * D:3 * D], 128), writes=[G1])
                        g1v[0] = v
                    rows = slice(n * 128, (n + 1) * 128)
                    i = it % 2
                    self.pump_wload(1)
                    S.dma("sp", yc[i][:], self.scr["Ycat"][b, rows, :], writes=[yc[i]])
                    S.dma("sp", xt[i][:], self.Xsrc[b, rows, :], writes=[xt[i]])
                    for kc in range(8):
                        pp = pT[kc % 2]
                        S.op("pe", lambda e: e.transpose(pp[:], yc[i][:, kc * 128:(kc + 1) * 128], self.ident[:]), reads=[yc[i], self.ident], writes=[pp])
                        S.op("act", lambda e: e.activation(yT[i][:, kc, :], pp[:], AF.Copy), reads=[pp], writes=[yT[i]])
                    for c in range(2):
                        po = pO[c]
                        for kc in range(8):
                            S.op("pe", lambda e: e.matmul(po[:], yT[i][:, kc, :], wo[:, kc, c * 512:(c + 1) * 512], start=(kc == 0), stop=(kc == 7)), reads=[yT[i], wo], writes=[po])
                        S.op("dve", lambda e: e.tensor_tensor(t[:, c * 512:(c + 1) * 512], po[:], G1[:, c * 512:(c + 1) * 512], ALU.mult), reads=[po, G1], writes=[t])
                    S.op("dve", lambda e: e.scalar_tensor_tensor(x1[:], xt[i][:], ALPHA, t[:], ALU.mult, ALU.add), reads=[xt[i], t], writes=[x1])
                    self.layer_norm_out(sc, x1, PG, PB, outt[i], (s1, s2, sq))
                    S.dma("pool", self.scr["X1s"][b, rows, :], outt[i][:], reads=[outt[i]])
                    it += 1

    def stage_ffn(self, l):
        S, I = self.S, self.I
        with Scope(self) as sc:
            w1, w2 = self.w1, self.w2
            G2 = sc.sb([128, D])
            PG = sc.sb([128, D]); PB = sc.sb([128, D])
            S.dma("sp", PG[:], bc(I["post2_g"][l], 128), writes=[PG])
            S.dma("sp", PB[:], bc(I["post2_b"][l], 128), writes=[PB])
            xt = [sc.sb([128, D]) for _ in range(2)]
            pT = [sc.ps([128, 128]) for _ in range(2)]
            x1T = sc.sb([128, 8, 512], BF16)
            aT = sc.sb([128, NFF, 512], BF16)
            pU = [sc.ps([128, 512]) for _ in range(2)]; pGt = [sc.ps([128, 512]) for _ in range(2)]
            su = [sc.sb([128, 512]) for _ in range(2)]
            pF = [sc.ps([128, 512]) for _ in range(2)]
            t = sc.sb([128, D]); x2 = sc.sb([128, D]); outt = [sc.sb([128, D]) for _ in range(2)]
            s1 = sc.sb([128, 1]); s2 = sc.sb([128, 1]); sq = sc.sb([128, D])
            it = 0
            for b in range(NB):
                groups = ([] if self.last else [(0, 2, 2)]) + [(2 + 4 * q, 4, b) for q in range(4)]
                for (n0, nt, v) in groups:
                    w = nt * 128
                    S.dma("sp", G2[:], bc(self.scr["modD"][v, 5 * D:6 * D], 128), writes=[G2])
                    for j in range(nt):
                        rows = slice((n0 + j) * 128, (n0 + j + 1) * 128)
                        x = xt[it % 2]; it += 1
                        S.dma("sp", x[:], self.scr["X1s"][b, rows, :], writes=[x])
                        self.xT_modulated(sc, x, pT, x1T, j * 128, 128, v, 1)
                    for fc in range(NFF):
                        pu = pU[fc % 2]; pg = pGt[fc % 2]; s_ = su[fc % 2]
                        for kc in range(8):
                            S.op("pe", lambda e: e.matmul(pu[:, :w], w1[:, kc, fc * 128:(fc + 1) * 128], x1T[:, kc, :w], start=(kc == 0), stop=(kc == 7)), reads=[w1, x1T], writes=[pu])
                        for kc in range(8):
                            S.op("pe", lambda e: e.matmul(pg[:, :w], w1[:, kc, DFF + fc * 128:DFF + (fc + 1) * 128], x1T[:, kc, :w], start=(kc == 0), stop=(kc == 7)), reads=[w1, x1T], writes=[pg])
                        S.op("act", lambda e: e.activation(s_[:, :w], pu[:, :w], AF.Silu), reads=[pu], writes=[s_])
                        S.op("dve", lambda e: e.tensor_tensor(aT[:, fc, :w], s_[:, :w], pg[:, :w], ALU.mult), reads=[s_, pg], writes=[aT])
                    for j in range(nt):
                        rows = slice((n0 + j) * 128, (n0 + j + 1) * 128)
                        x = xt[it % 2]; it += 1
                        S.dma("sp", x[:], self.scr["X1s"][b, rows, :], writes=[x])
                        for c in range(2):
                            pf = pF[c]
                            for fc in range(NFF):
                                S.op("pe", lambda e: e.matmul(pf[:], aT[:, fc, j * 128:(j + 1) * 128], w2[:, fc, c * 512:(c + 1) * 512], start=(fc == 0), stop=(fc == NFF - 1)), reads=[aT, w2], writes=[pf])
                            S.op("dve", lambda e: e.tensor_tensor(t[:, c * 512:(c + 1) * 512], pf[:], G2[:, c * 512:(c + 1) * 512], ALU.mult), reads=[pf, G2], writes=[t])
                        S.op("dve", lambda e: e.scalar_tensor_tensor(x2[:], x[:], ALPHA, t[:], ALU.mult, ALU.add), reads=[x, t], writes=[x2])
                        o = outt[j % 2]
                        self.layer_norm_out(sc, x2, PG, PB, o, (s1, s2, sq))
                        if self.last:
                            xr = (n0 + j) * 128 - CT
                            S.dma("pool", self.out[b, xr:xr + 128, :], o[:], reads=[o])
                        else:
                            S.dma("pool", self.scr["Xs"][b, rows, :], o[:], reads=[o])


def host_consts():
    c = {}
    c["c_ident"] = np.eye(128, dtype=np.float32)
    t = np.arange(SX)
    row = (t // 64).astype(np.float32)
    col = (t % 64).astype(np.float32)
    pos = t.astype(np.float32)

    def tab(p, n):
        half = n // 2
        inv = np.power(np.float32(10000.0), -(np.arange(half, dtype=np.float32) * np.float32(2.0) / np.float32(n))).astype(np.float32)
        ang = (p[:, None] * inv[None, :]).astype(np.float32)
        cs, sn = np.cos(ang).astype(np.float32), np.sin(ang).astype(np.float32)
        return np.concatenate([cs, cs], 1), np.concatenate([-sn, sn], 1)

    def axial(n):
        h = n // 2
        c1, s1 = tab(row, h)
        c2, s2 = tab(col, h)
        return np.stack([np.concatenate([c1, c2], 1), np.concatenate([s1, s2], 1)], 1).astype(np.float32)

    c["c_ropeG"] = axial(64)
    c["c_ropeD"] = axial(32)
    cr, sr = tab(pos, 64)
    c["c_ropeR"] = np.stack([cr, sr], 1).astype(np.float32)
    mj = np.zeros((8, 256), np.float32)
    for m in range(8):
        h = m % 4
        mj[m, h * 64:(h + 1) * 64] = 1.0
    c["c_maskJ"] = mj
    c["c_zero"] = np.zeros((128, 4096), ml_dtypes.bfloat16)
    gf = 1.0 - 2.0 ** (-5.0 - np.arange(4, dtype=np.float64))
    gb = gf[::-1]
    M = np.zeros((36, 4, 128, 512), np.float64)
    p = np.arange(128)[:, None]
    j = np.arange(512)[None, :]
    for h in range(4):
        lf, lb = math.log(gf[h]), math.log(gb[h])
        for off in range(-15, 13):
            dlt = (128 * off + j - p).astype(np.float64)
            m = np.where(dlt > 0, np.exp(lf * np.maximum(dlt, 0)), 0.0) + np.where(dlt < 0, np.exp(lb * np.maximum(-dlt, 0)), 0.0) + np.where(dlt == 0, 2.0, 0.0)
            M[off + 15, h] = m
        for kc in range(2):
            for qc in range(4):
                cc = 128 * kc + p
                ii = 512 * qc + j
                M[28 + kc * 4 + qc, h] = np.exp(lf * (256 + ii - cc)) + np.exp(lb * (2048 - ii + cc))
    c["c_retM"] = M.astype(np.float32).astype(ml_dtypes.bfloat16)
    return c


_CACHE = {}


def kernel(**inputs):
    f = lambda k: np.ascontiguousarray(np.asarray(inputs[k], dtype=np.float32))
    L = DEPTH
    shared = {}
    for k in ("ada_w", "ada_b", "w_in", "rwkv_mu", "rwkv_g_up", "rwkv_k_k", "rwkv_k_a", "rwkv_ln_g", "rwkv_ln_b",
              "gqa_q_norm", "gqa_k_norm", "diff_norm", "w_out", "post1_g", "post1_b", "ffn_w_in", "ffn_w_out", "post2_g", "post2_b"):
        shared[k] = f(k)
    shared["rwkv_w0"] = f("rwkv_w0").reshape(L, 512)
    shared["rwkv_a0"] = f("rwkv_a0").reshape(L, 512)
    shared["rwkv_w_up"] = f("rwkv_w_up").reshape(L, 64, 256)
    shared["rwkv_a_up"] = f("rwkv_a_up").reshape(L, 64, 256)
    shared["rwkv_r_k"] = f("rwkv_r_k").reshape(L, 256)
    shared["diff_lambda"] = f("diff_lambda").reshape(L, 128)
    shared.update(host_consts())
    x, c, ctx, c_ctx = f("x"), f("c"), f("ctx"), f("c_ctx")
    in_maps = []
    for core in range(8):
        bs = slice(core * NB, (core + 1) * NB)
        m = dict(shared)
        m["xin"] = np.ascontiguousarray(np.concatenate([ctx[bs], x[bs]], axis=1))
        m["cvec"] = np.ascontiguousarray(np.concatenate([c[bs], c_ctx[None, :]], axis=0))
        in_maps.append(m)
    if "nc" not in _CACHE:
        _CACHE["nc"] = KB().build()
    res = run_bass_kernel_spmd(_CACHE["nc"], in_maps, core_ids=list(range(8)))
    return np.concatenate([r["y"] for r in res.results], axis=0).astype(np.float32)
```

```python
import math, os
KSKIP = os.environ.get('KSKIP', '')
import numpy as np
import ml_dtypes
import concourse.bass as bass
import concourse.mybir as mybir
from concourse.bass_utils import run_bass_kernel_spmd
from contextlib import ExitStack

F32 = mybir.dt.float32
BF16 = mybir.dt.bfloat16
AF = mybir.ActivationFunctionType
ALU = mybir.AluOpType
AX = mybir.AxisListType

D = 1024
NB = 2
CT = 256
SX = 2048
T = CT + SX
NT = T // 128
INW = 3264
DFF = 2816
NFF = DFF // 128
DEPTH = 4
ALPHA = (2 * DEPTH) ** 0.25
LN_EPS = 1e-5
QK_EPS = 1e-6
GN_EPS = 64e-5
CS = 16
O_RW, O_GQ, O_DF, O_RT = 0, 960, 1472, 2240


class Sched:
    def __init__(self, nc, es):
        self.nc = nc
        self.es = es
        self.eng = {"pe": nc.tensor, "act": nc.scalar, "dve": nc.vector, "pool": nc.gpsimd, "sp": nc.sync}
        self.sem = {}
        self.cnt = {}
        self.nsem = 0
        for k in self.eng:
            self.sem[k] = self._newsem()
            self.cnt[k] = 0
        self.waited = {k: {} for k in self.eng}
        self.NSLOT = 8
        self.dsem = {}
        self.dcnt = {}
        self.dnext = {}
        for q in ("sp", "pool", "act"):
            self.dsem[q] = [self._newsem() for i in range(self.NSLOT)]
            self.dcnt[q] = [0] * self.NSLOT
            self.dnext[q] = 0
        self.ninstr = 0

    def _newsem(self):
        self.nsem += 1
        return self.es.enter_context(self.nc.semaphore("sem%d" % self.nsem))

    def _wait(self, e, tok, raw=False):
        if tok is None:
            return
        sem, val, owner = tok
        if owner == e and (e == "pe" or not raw):
            return
        key = id(sem)
        w = self.waited[e]
        if w.get(key, 0) >= val:
            return
        self.eng[e].wait_ge(sem, val)
        w[key] = val

    def deps(self, e, reads, writes):
        for t in reads:
            self._wait(e, t.lastw, raw=True)
        for t in writes:
            self._wait(e, t.lastw)
            for r in t.readers.values():
                self._wait(e, r)

    def done(self, tok, reads, writes):
        for t in reads:
            t.readers[tok[2] + str(id(tok[0]))] = tok
        for t in writes:
            t.lastw = tok
            t.readers = {}

    def op(self, e, fn, reads=(), writes=()):
        self.deps(e, reads, writes)
        ins = fn(self.eng[e])
        self.cnt[e] += 1
        ins.then_inc(self.sem[e], 1)
        tok = (self.sem[e], self.cnt[e], e)
        self.done(tok, reads, writes)
        self.ninstr += 1
        return tok

    def dma(self, q, out, in_, reads=(), writes=(), **kw):
        s = self.dnext[q]
        self.dnext[q] = (s + 1) % self.NSLOT
        sem = self.dsem[q][s]
        if self.dcnt[q][s] > 0:
            self._wait(q, (sem, 16 * self.dcnt[q][s], "dma"))
        self.deps(q, reads, writes)
        ins = self.eng[q].dma_start(out=out, in_=in_, **kw)
        self.dcnt[q][s] += 1
        ins.then_inc(sem, 16)
        tok = (sem, 16 * self.dcnt[q][s], "dma")
        self.done(tok, reads, writes)
        self.ninstr += 1
        return tok

    def barrier(self):
        toks = []
        for k in self.eng:
            if self.cnt[k] > 0:
                toks.append((self.sem[k], self.cnt[k], k))
        for q in self.dsem:
            for s in range(self.NSLOT):
                if self.dcnt[q][s] > 0:
                    toks.append((self.dsem[q][s], 16 * self.dcnt[q][s], "dma"))
        for e in self.eng:
            for t in toks:
                self._wait(e, t)
        for k in self.eng:
            if self.cnt[k] > 12000:
                self.sem[k] = self._newsem()
                self.cnt[k] = 0
        for q in self.dsem:
            for s in range(self.NSLOT):
                if self.dcnt[q][s] > 1500:
                    self.dsem[q][s] = self._newsem()
                    self.dcnt[q][s] = 0


class TT:
    def __init__(self, ap):
        self.ap = ap
        self.lastw = None
        self.readers = {}

    def __getitem__(self, k):
        return self.ap[k]


class Scope:
    def __init__(self, kb):
        self.kb = kb
        self.es = ExitStack()

    def __enter__(self):
        self.es.__enter__()
        return self

    def __exit__(self, *a):
        self.kb.S.barrier()
        return self.es.__exit__(*a)

    def sb(self, shape, dt=F32, name="t"):
        self.kb.uid += 1
        return TT(self.es.enter_context(self.kb.nc.sbuf_tensor("%s_%d" % (name, self.kb.uid), list(shape), dt)))

    def ps(self, shape, dt=F32, name="p"):
        self.kb.uid += 1
        assert dt == F32
        shape = list(shape)
        free = int(np.prod(shape[1:]))
        assert free <= 512
        t = self.es.enter_context(self.kb.nc.psum_tensor("%s_%d" % (name, self.kb.uid), [128, 512], dt))
        v = t[0:shape[0], 0:free]
        if len(shape) == 3:
            v = v.rearrange("p (a b) -> p a b", a=shape[1])
        return TT(v)


def bc(ap, n):
    return ap.partition_broadcast(n)


class KB:
    def __init__(self, NL=DEPTH, dbg=(), stages=None):
        self.NL = NL
        self.dbg = set(dbg)
        self.stages = stages
        self.uid = 0
        nc = self.nc = bass.Bass("TRN2", target_bir_lowering=False)
        self.I = {}
        self.scr = {}

        def inp(name, shape, dt=F32):
            self.I[name] = nc.dram_tensor(name, list(shape), dt, kind="ExternalInput").ap()

        inp("xin", [NB, T, D])
        inp("cvec", [3, D])
        L = DEPTH
        inp("ada_w", [L, D, 6 * D]); inp("ada_b", [L, 6 * D]); inp("w_in", [L, D, INW])
        inp("rwkv_mu", [L, 960]); inp("rwkv_w0", [L, 512]); inp("rwkv_w_up", [L, 64, 256])
        inp("rwkv_a0", [L, 512]); inp("rwkv_a_up", [L, 64, 256]); inp("rwkv_g_up", [L, 64, 256])
        inp("rwkv_k_k", [L, 256]); inp("rwkv_k_a", [L, 256]); inp("rwkv_r_k", [L, 256])
        inp("rwkv_ln_g", [L, 256]); inp("rwkv_ln_b", [L, 256])
        inp("gqa_q_norm", [L, 64]); inp("gqa_k_norm", [L, 64]); inp("diff_lambda", [L, 128]); inp("diff_norm", [L, 64])
        inp("w_out", [L, D, D]); inp("post1_g", [L, D]); inp("post1_b", [L, D])
        inp("ffn_w_in", [L, D, 2 * DFF]); inp("ffn_w_out", [L, DFF, D]); inp("post2_g", [L, D]); inp("post2_b", [L, D])
        inp("c_ident", [128, 128]); inp("c_ropeG", [SX, 2, 64]); inp("c_ropeD", [SX, 2, 32]); inp("c_ropeR", [SX, 2, 64])
        inp("c_maskJ", [8, 256]); inp("c_retM", [36, 4, 128, 512], BF16); inp("c_zero", [128, 4096], BF16)
        self.out = nc.dram_tensor("y", [NB, SX, D], F32, kind="ExternalOutput").ap()

        def scr(name, shape, dt=F32):
            kind = "ExternalOutput" if name in self.dbg else "Internal"
            self.scr[name] = nc.dram_tensor(name, list(shape), dt, kind=kind).ap()

        scr("P", [NB, T, INW]); scr("Xs", [NB, T, D]); scr("X1s", [NB, T, D]); scr("modD", [3, 6 * D])
        scr("QTg", [NB, 128, 2, T], BF16); scr("KTg", [NB, 128, 1, T], BF16)
        scr("QTr", [NB, 128, 2, T], BF16); scr("KTr", [NB, 128, 2, T], BF16)
        scr("QTd", [NB, 128, 4, T], BF16); scr("KTd", [NB, 128, 4, T], BF16)
        scr("As", [128, T, 8], BF16); scr("Rs", [128, T, 8], BF16); scr("Ws", [2, 128, T, 4])
        scr("LBs", [2, 8, T, 128], BF16); scr("LKs", [2, 8, T, 128], BF16); scr("RVs", [8, T, 256], BF16); scr("Yfull", [2, 8, T, 256])
        scr("Gt", [NB, T, 256]); scr("Bon", [NB, T, 256]); scr("Ycat", [NB, T, D])

    def build(self):
        nc = self.nc
        with ExitStack() as es:
            self.S = S = Sched(nc, es)
            with Scope(self) as g:
                self.g = g
                self.ident = g.sb([128, 128], F32, "ident")
                S.dma("sp", self.ident[:], self.I["c_ident"], writes=[self.ident])
                self.siluT = g.sb([128, 8, 3], F32, "siluT")
                self.cvals = [64.0 * QK_EPS, GN_EPS, LN_EPS, QK_EPS, 0.0]
                self.cst = g.sb([128, 8], F32, "cst")
                for i_, v_ in enumerate(self.cvals):
                    S.op("dve", lambda e: e.memset(self.cst[:, i_:i_ + 1], float(v_)), writes=[self.cst])
                self.stage_init()
                for l in range(self.NL):
                    self.layer(l)
                S.barrier()
            print("ninstr", S.ninstr, "nsem", S.nsem)
        return nc

    def rpow(self, ot, o_ap, it, i_ap, scale, bias, p):
        S = self.S
        assert p == -0.5
        bcol = self.cvals.index(float(bias))
        np_ = o_ap.shape[0]
        S.op("act", lambda e: e.activation(o_ap, i_ap, AF.Sqrt, bias=self.cst[0:np_, bcol:bcol + 1], scale=float(scale)), reads=[it, self.cst], writes=[ot])
        S.op("dve", lambda e: e.reciprocal(o_ap, o_ap), reads=[ot], writes=[ot])

    def on(self, name):
        return self.stages is None or name in self.stages

    def stage_init(self):
        S, I = self.S, self.I
        with Scope(self) as sc:
            cv = sc.sb([3, D])
            S.dma("sp", cv[:], I["cvec"], writes=[cv])
            sv = sc.sb([3, D])
            S.op("act", lambda e: e.activation(sv[:], cv[:], AF.Silu), reads=[cv], writes=[sv])
            pT = sc.ps([128, 8, 3])
            for kc in range(8):
                S.op("pe", lambda e: e.transpose(pT[:, kc, :], sv[:, kc * 128:(kc + 1) * 128], self.ident[0:3, 0:3]), reads=[sv, self.ident], writes=[pT])
            S.op("dve", lambda e: e.tensor_copy(self.siluT[:], pT[:]), reads=[pT], writes=[self.siluT])
            z = sc.sb([128, 4096], BF16)
            S.dma("sp", z[:], I["c_zero"], writes=[z])
            for name in ("LBs", "LKs"):
                flat = self.scr[name].rearrange("d m t k -> (d m t k)").rearrange("(a p f) -> a p f", p=128, f=4096)
                for a in range(flat.shape[0]):
                    S.dma("sp", flat[a], z[:], reads=[z])
            flat = self.scr["RVs"].rearrange("m t k -> (m t k)").rearrange("(a p f) -> a p f", p=128, f=4096)
            for a in range(flat.shape[0]):
                S.dma("sp", flat[a], z[:], reads=[z])

    def layer(self, l):
        self.l = l
        self.last = (l == DEPTH - 1)
        self.Xsrc = self.I["xin"] if l == 0 else self.scr["Xs"]
        if self.on("mod"):
            self.stage_mod(l)
        with Scope(self) as ls:
            self.ls = ls
            S = self.S
            self.modT = ls.sb([128, 48, 3], F32, "modT")
            for v in range(3):
                S.dma("sp", self.modT[:, :, v], self.scr["modD"][v].rearrange("(c p) -> p c", p=128), writes=[self.modT], allow_slow_non_contiguous=True)
            for c0 in (8, 32):
                S.op("dve", lambda e: e.tensor_scalar(self.modT[:, c0:c0 + 8, :], self.modT[:, c0:c0 + 8, :], 1.0, None, ALU.add), reads=[self.modT], writes=[self.modT])
            if self.on("inproj"):
                self.stage_inproj(l)
            if self.on("rwprep"):
                self.stage_rwprep(l)
            if self.on("scan"):
                self.stage_scan(l)
            if self.on("rwpost"):
                self.stage_rwpost(l)
            if self.on("attn"):
                self.stage_attn(l)
            with Scope(self) as mf:
                self.w1 = mf.sb([128, 8, 2 * DFF], BF16, "w1")
                self.w2 = mf.sb([128, NFF, D], BF16, "w2")
                I = self.I
                self.wload = []
                if self.on("ffn"):
                    for kc in range(8):
                        self.wload.append((self.w1, self.w1[:, kc, :], I["ffn_w_in"][l, kc * 128:(kc + 1) * 128, :]))
                    for fc in range(NFF):
                        self.wload.append((self.w2, self.w2[:, fc, :], I["ffn_w_out"][l, fc * 128:(fc + 1) * 128, :]))
                if self.on("mix"):
                    self.stage_mix(l)
                self.pump_wload(1000)
                if self.on("ffn"):
                    self.stage_ffn(l)

    def stage_mod(self, l):
        S, I = self.S, self.I
        with Scope(self) as sc:
            adab = sc.sb([3, 6 * D])
            S.dma("sp", adab[:], bc(I["ada_b"][l], 3), writes=[adab])
            modsb = sc.sb([3, 6 * D])
            wb = [sc.sb([128, 8, 512]) for _ in range(2)]
            pm = [sc.ps([3, 512]) for _ in range(2)]
            for n in range(12):
                w = wb[n % 2]
                S.dma("sp" if n % 2 == 0 else "pool", w[:], I["ada_w"][l, :, n * 512:(n + 1) * 512].rearrange("(c p) n -> p c n", p=128), writes=[w])
                p = pm[n % 2]
                for kc in range(8):
                    S.op("pe", lambda e: e.matmul(p[:], self.siluT[:, kc, :], w[:, kc, :], start=(kc == 0), stop=(kc == 7)), reads=[self.siluT, w], writes=[p])
                S.op("dve", lambda e: e.tensor_tensor(modsb[:, n * 512:(n + 1) * 512], p[:], adab[:, n * 512:(n + 1) * 512], ALU.add), reads=[p, adab], writes=[modsb])
            S.dma("sp", self.scr["modD"], modsb[:], reads=[modsb])

    def xT_modulated(self, sc, src_tile, pT, xmT, col0, width, v, which):
        S = self.S
        shc, scc = (0, 8) if which == 0 else (24, 32)
        for kc in range(8):
            pp = pT[kc % 2]
            S.op("pe", lambda e: e.transpose(pp[:], src_tile[:, kc * 128:(kc + 1) * 128], self.ident[:]), reads=[src_tile, self.ident], writes=[pp])
            eng = "dve" if kc % 2 == 0 else "pool"
            eng = "dve"
            S.op(eng, lambda e: e.tensor_scalar(xmT[:, kc, col0:col0 + 128], pp[:], self.modT[:, scc + kc, v:v + 1], self.modT[:, shc + kc, v:v + 1], ALU.mult, ALU.add), reads=[pp, self.modT], writes=[xmT])

    def rope(self, eng2, out, src, tab, nh, nblk, half, tmp1, tmp2, reads, tabT):
        S = self.S
        w = nblk * 2 * half
        Cb = tab[:, 0, :].unsqueeze(1).to_broadcast([128, nh, w])
        S.op("dve", lambda e: e.tensor_tensor(tmp1, src, Cb, ALU.mult), reads=reads + [tabT], writes=[self._t1])
        v5 = lambda ap: ap.rearrange("p h (b two f) -> p h b two f", b=nblk, two=2)
        sgv = tab[:, 1, :].rearrange("p (b two f) -> p b two f", b=nblk, two=2)
        for hf in range(2):
            o = v5(tmp2)[:, :, :, hf, :]
            i = v5(src)[:, :, :, 1 - hf, :]
            sg = sgv[:, :, hf, :].unsqueeze(1).to_broadcast([128, nh, nblk, half])
            S.op(eng2, lambda e: e.tensor_tensor(o, i, sg, ALU.mult), reads=reads + [tabT], writes=[self._t2])
        S.op("dve", lambda e: e.tensor_tensor(out, tmp1, tmp2, ALU.add), reads=[self._t1, self._t2], writes=[self._ro])

    def stage_inproj(self, l):
        S, I = self.S, self.I
        with Scope(self) as sc:
            win = sc.sb([128, 8, INW], BF16, "win")
            for kc in range(8):
                S.dma("pool", win[:, kc, :], I["w_in"][l, kc * 128:(kc + 1) * 128, :], writes=[win])
            rG = sc.sb([128, 16, 2, 64]); rD = sc.sb([128, 16, 2, 32]); rR = sc.sb([128, 16, 2, 64])
            S.dma("sp", rG[:], I["c_ropeG"].rearrange("(n p) a w -> p n a w", p=128), writes=[rG])
            S.dma("sp", rD[:], I["c_ropeD"].rearrange("(n p) a w -> p n a w", p=128), writes=[rD])
            S.dma("sp", rR[:], I["c_ropeR"].rearrange("(n p) a w -> p n a w", p=128), writes=[rR])
            GW = sc.sb([128, 6, 64])
            for h in range(4):
                S.dma("sp", GW[:, h, :], bc(I["gqa_q_norm"][l], 128), writes=[GW])
            for h in range(4, 6):
                S.dma("sp", GW[:, h, :], bc(I["gqa_k_norm"][l], 128), writes=[GW])
            S.op("dve", lambda e: e.tensor_scalar(GW[:, 4:6, :], GW[:, 4:6, :], 8.0, None, ALU.mult), reads=[GW], writes=[GW])
            xt = [sc.sb([128, D]) for _ in range(2)]
            pT = [sc.ps([128, 128]) for _ in range(2)]
            xmT = [sc.sb([128, 8, 128], BF16) for _ in range(2)]
            pP = [sc.ps([128, 512]) for _ in range(2)]
            Psb = [sc.sb([128, INW]) for _ in range(2)]
            pQ = [sc.ps([64, 4, 128]) for _ in range(2)]
            sq = sc.sb([128, 6, 64]); ss = sc.sb([128, 6]); r1 = sc.sb([128, 6]); qn = sc.sb([128, 6, 64])
            t1 = sc.sb([128, 512]); t2 = sc.sb([128, 512]); ro = sc.sb([128, 512])
            self._t1, self._t2, self._ro = t1, t2, ro
            QTs = [sc.sb([64, 4, 128], BF16) for _ in range(6)]
            it = 0
            for b in range(NB):
                for n in range(NT):
                    v = 2 if n < 2 else b
                    isx = n >= 2
                    rows = slice(n * 128, (n + 1) * 128)
                    x = xt[it % 2]; xm = xmT[it % 2]; P = Psb[it % 2]
                    S.dma("sp", x[:], self.Xsrc[b, rows, :], writes=[x])
                    self.xT_modulated(sc, x, pT, xm, 0, 128, v, 0)
                    for c in range(7):
                        c0 = c * 512
                        w = min(512, INW - c0)
                        pp = pP[c % 2]
                        for kc in range(8):
                            S.op("pe", lambda e: e.matmul(pp[:, :w], xm[:, kc, :], win[:, kc, c0:c0 + w], start=(kc == 0), stop=(kc == 7)), reads=[xm, win], writes=[pp])
                        S.op("act", lambda e: e.activation(P[:, c0:c0 + w], pp[:, :w], AF.Copy), reads=[pp], writes=[P])
                    S.dma("pool", self.scr["P"][b, rows, :], P[:], reads=[P])
                    qk = P[:, O_GQ:O_GQ + 384].rearrange("p (h w) -> p h w", w=64)
                    S.op("pool", lambda e: e.tensor_tensor(sq[:], qk, qk, ALU.mult), reads=[P], writes=[sq])
                    S.op("dve", lambda e: e.tensor_reduce(ss[:], sq[:], AX.X, ALU.add), reads=[sq], writes=[ss])
                    self.rpow(r1, r1[:], ss, ss[:], 1.0, 64.0 * QK_EPS, -0.5)
                    S.op("dve", lambda e: e.tensor_tensor(qn[:], qk, r1[:].unsqueeze(2).to_broadcast([128, 6, 64]), ALU.mult), reads=[P, r1], writes=[qn])
                    S.op("dve", lambda e: e.tensor_tensor(qn[:], qn[:], GW[:], ALU.mult), reads=[qn, GW], writes=[qn])
                    v3 = lambda t, nh, w: t[:, 0:nh * w].rearrange("p (h w) -> p h w", w=w)
                    if isx:
                        self.rope("pool", v3(ro, 6, 64), qn[:], rG[:, n - 2], 6, 2, 16, v3(t1, 6, 64), v3(t2, 6, 64), [qn], rG)
                        src, srcT = v3(ro, 6, 64), ro
                    else:
                        src, srcT = qn[:], qn
                    self.emit_T(src, srcT, 6, pQ, QTs, [("QTg", [0, 1], 1.0, 0, 0), ("QTg", [2, 3], 1.0, 64, 0),
                                                         ("KTg", [4], 1.0, 0, 0), ("KTg", [5], 1.0, 64, 0)], b, rows)
                    dq = P[:, O_DF:O_DF + 512].rearrange("p (h w) -> p h w", w=32)
                    if isx:
                        self.rope("pool", v3(ro, 16, 32), dq, rD[:, n - 2], 16, 2, 8, v3(t1, 16, 32), v3(t2, 16, 32), [P], rD)
                        src, srcT = v3(ro, 16, 32), ro
                    else:
                        src, srcT = dq, P
                    self.emit_T(src, srcT, 16, pQ, QTs, [("QTd", [0, 2, 4, 6], 32 ** -0.5, 0, 0), ("QTd", [1, 3, 5, 7], 32 ** -0.5, 64, 0),
                                                          ("KTd", [8, 10, 12, 14], 1.0, 0, 0), ("KTd", [9, 11, 13, 15], 1.0, 64, 0)], b, rows, width=32)
                    rq = P[:, O_RT:O_RT + 512].rearrange("p (h w) -> p h w", w=64)
                    if isx:
                        self.rope("pool", v3(ro, 8, 64), rq, rR[:, n - 2], 8, 1, 32, v3(t1, 8, 64), v3(t2, 8, 64), [P], rR)
                        src, srcT = v3(ro, 8, 64), ro
                    else:
                        src, srcT = rq, P
                    self.emit_T(src, srcT, 8, pQ, QTs, [("QTr", [0, 1], 1.0, 0, 0), ("QTr", [2, 3], 1.0, 64, 0),
                                                         ("KTr", [4, 5], 0.125, 0, 0), ("KTr", [6, 7], 0.125, 64, 0)], b, rows)
                    it += 1

    def emit_T(self, src, srcT, nh, pQ, QTs, outs, b, rows, width=64):
        S = self.S
        for (name, idxs, scale, p0, slot0) in outs:
            cnt = len(idxs)
            self._qi = getattr(self, "_qi", 0) + 1
            pq = pQ[self._qi % 2]
            st = QTs[self._qi % 6]
            for j, hi in enumerate(idxs):
                S.op("pe", lambda e: e.transpose(pq[0:width, j, :], src[:, hi, :], self.ident[:]), reads=[srcT, self.ident], writes=[pq])
            S.op("act", lambda e: e.activation(st[0:width, 0:cnt, :], pq[0:width, 0:cnt, :], AF.Copy, scale=float(scale)), reads=[pq], writes=[st])
            S.dma("pool", self.scr[name][b, p0:p0 + width, slot0:slot0 + cnt, rows], st[0:width, 0:cnt, :], reads=[st])

    def stage_rwprep(self, l):
        S, I = self.S, self.I
        with Scope(self) as sc:
            def btile(key, w, nrep=1):
                t = sc.sb([128, nrep * w])
                for r in range(nrep):
                    S.dma("sp", t[:, r * w:(r + 1) * w], bc(I[key][l], 128), writes=[t])
                return t
            MU = btile("rwkv_mu", 960); W0 = btile("rwkv_w0", 512); A0 = btile("rwkv_a0", 512)
            KK = btile("rwkv_k_k", 256); KA = btile("rwkv_k_a", 256); RK = btile("rwkv_r_k", 256)
            WUP = sc.sb([32, 2, 256]); AUP = sc.sb([32, 2, 256]); GUP = sc.sb([64, 256])
            S.dma("sp", WUP[:], I["rwkv_w_up"][l].rearrange("(d r) c -> r d c", d=2), writes=[WUP])
            S.dma("sp", AUP[:], I["rwkv_a_up"][l].rearrange("(d r) c -> r d c", d=2), writes=[AUP])
            S.dma("sp", GUP[:], I["rwkv_g_up"][l], writes=[GUP])
            def mkset():
                cur = sc.sb([128, 960]); prv = sc.sb([128, 960]); nxt = sc.sb([128, 960])
                tt = sc.sb([128, 960])
                pst = sc.sb([128, 1024])
                S.op("dve", lambda e: e.memset(pst[:, 0:64], 0.0), writes=[pst])
                lor = sc.sb([128, 3, 64]); lorT = sc.sb([32, 4, 128]); lorTg = sc.sb([64, 128])
                wt = sc.sb([128, 512]); e1 = sc.sb([128, 512])
                decp = sc.sb([128, 64 + 512])
                S.op("dve", lambda e: e.memset(decp[:, 0:64], 0.0), writes=[decp])
                asig = sc.sb([128, 512]); gt = sc.sb([128, 256])
                kkp = sc.sb([128, 64 + 256])
                S.op("dve", lambda e: e.memset(kkp[:, 0:64], 0.0), writes=[kkp])
                kq = sc.sb([128, 256]); ks = sc.sb([128, 4]); kr = sc.sb([128, 4])
                bd = sc.sb([128, 512], BF16); kd = sc.sb([128, 512]); tk = sc.sb([128, 512]); kd16 = sc.sb([128, 512], BF16); v16 = sc.sb([128, 256], BF16)
                rk = sc.sb([128, 256]); pr = sc.sb([128, 512]); s8 = sc.sb([128, 8]); s4 = sc.sb([128, 4]); bon = sc.sb([128, 256])
                return (cur, prv, nxt, tt, pst, lor, lorT, lorTg, wt, e1, decp, asig, gt, kkp, kq, ks, kr, bd, kd, tk, kd16, v16, rk, pr, s8, s4, bon)
            tsets = [mkset() for _ in range(2)]
            pL = sc.ps([32, 4, 128]); pW = sc.ps([128, 512]); pAl = sc.ps([128, 512]); pG = sc.ps([128, 256])
            pA = sc.ps([128, 4, 128]); pR = sc.ps([128, 4, 128]); pWt = [sc.ps([128, 4, 128]) for _ in range(2)]
            nsets = []
            for _ in range(2):
                Ast = sc.sb([128, 128, 8], BF16); Rst = sc.sb([128, 128, 8], BF16); Wst = [sc.sb([128, 128, 4]) for _ in range(2)]
                S.op("pool", lambda e: e.memset(Ast[:], 0.0), writes=[Ast])
                S.op("pool", lambda e: e.memset(Rst[:], 0.0), writes=[Rst])
                nsets.append((Ast, Rst, Wst))
            P = self.scr["P"]
            def body(n, b):
                rows = slice(n * 128, (n + 1) * 128)
                r0 = n * 128
                Ast, Rst, Wst = nsets[n % 2]
                (cur, prv, nxt, tt, pst, lor, lorT, lorTg, wt, e1, decp, asig, gt, kkp, kq, ks, kr, bd, kd, tk, kd16, v16, rk, pr, s8, s4, bon) = tsets[b]

                def issue_loads(n_, b_):
                    cur_, prv_, nxt_ = tsets[b_][0:3]
                    q0_ = n_ * 128
                    S.dma("sp", cur_[:], P[b_, q0_:q0_ + 128, 0:960], writes=[cur_])
                    if n_ in (0, 2):
                        S.op("pool", lambda e: e.memset(prv_[0:32, :], 0.0), writes=[prv_])
                        S.dma("sp", prv_[1:128, :], P[b_, q0_:q0_ + 127, 0:960], writes=[prv_])
                    else:
                        S.dma("sp", prv_[:], P[b_, q0_ - 1:q0_ + 127, 0:960], writes=[prv_])
                    if n_ in (1, NT - 1):
                        S.op("pool", lambda e: e.memset(nxt_[96:128, :], 0.0), writes=[nxt_])
                        S.dma("sp", nxt_[0:127, :], P[b_, q0_ + 1:q0_ + 128, 0:960], writes=[nxt_])
                    else:
                        S.dma("sp", nxt_[:], P[b_, q0_ + 1:q0_ + 129, 0:960], writes=[nxt_])

                if n == 0 and b == 0:
                    issue_loads(0, 0)
                    issue_loads(0, 1)
                S.op("pool", lambda e: e.tensor_tensor(tt[:], prv[:], nxt[:], ALU.add), reads=[prv, nxt], writes=[tt])
                yield
                S.op("dve", lambda e: e.scalar_tensor_tensor(tt[:], tt[:], 0.5, cur[:], ALU.mult, ALU.subtract), reads=[tt, cur], writes=[tt])
                yield
                S.op("pool", lambda e: e.tensor_tensor(tt[:], tt[:], MU[:], ALU.mult), reads=[tt, MU], writes=[tt])
                yield
                S.op("dve", lambda e: e.tensor_tensor(pst[:, 64:1024], tt[:], cur[:], ALU.add), reads=[tt, cur], writes=[pst])
                yield
                if n + 1 < NT:
                    issue_loads(n + 1, b)
                p_r = pst[:, 64:320]; p_k = pst[:, 320:576]; p_v = pst[:, 576:832]
                S.op("act", lambda e: e.activation(lor[:, 0, :], pst[:, 832:896], AF.Tanh), reads=[pst], writes=[lor])
                yield
                S.op("act", lambda e: e.activation(lor[:, 2, :], pst[:, 960:1024], AF.Sigmoid), reads=[pst], writes=[lor])
                yield
                S.op("pool", lambda e: e.tensor_copy(lor[:, 1, :], pst[:, 896:960]), reads=[pst], writes=[lor])
                yield
                if 'A' in KSKIP:
                    return
                for j in range(4):
                    S.op("pe", lambda e: e.transpose(pL[:, j, :], lor[:, j // 2, 32 * (j % 2):32 * (j % 2) + 32], self.ident[:]), reads=[lor, self.ident], writes=[pL])
                    yield
                S.op("dve", lambda e: e.tensor_copy(lorT[:], pL[:]), reads=[pL], writes=[lorT])
                yield
                S.op("pe", lambda e: e.transpose(pG[0:64, 0:128], lor[:, 2, :], self.ident[:]), reads=[lor, self.ident], writes=[pG])
                yield
                S.op("dve", lambda e: e.tensor_copy(lorTg[:], pG[0:64, 0:128]), reads=[pG], writes=[lorTg])
                yield
                for d in range(2):
                    S.op("pe", lambda e: e.matmul(pW[:, d * 256:(d + 1) * 256], lorT[:, d, :], WUP[:, d, :], start=True, stop=True), reads=[lorT, WUP], writes=[pW])
                    yield
                    S.op("pe", lambda e: e.matmul(pAl[:, d * 256:(d + 1) * 256], lorT[:, 2 + d, :], AUP[:, d, :], start=True, stop=True), reads=[lorT, AUP], writes=[pAl])
                    yield
                S.op("pe", lambda e: e.matmul(pG[:], lorTg[:], GUP[:], start=True, stop=True), reads=[lorTg, GUP], writes=[pG])
                yield
                if 'B' in KSKIP:
                    return
                S.op("dve", lambda e: e.tensor_tensor(wt[:], pW[:], W0[:], ALU.add), reads=[pW, W0], writes=[wt])
                yield
                S.op("act", lambda e: e.activation(e1[:], wt[:], AF.Exp, scale=-1.0), reads=[wt], writes=[e1])
                yield
                S.op("dve", lambda e: e.tensor_scalar(e1[:], e1[:], 1.0, None, ALU.add), reads=[e1], writes=[e1])
                yield
                S.op("dve", lambda e: e.reciprocal(e1[:], e1[:]), reads=[e1], writes=[e1])
                yield
                S.op("act", lambda e: e.activation(decp[:, 64:576], e1[:], AF.Exp, scale=-math.exp(-0.5)), reads=[e1], writes=[decp])
                yield
                S.op("dve", lambda e: e.tensor_tensor(wt[:], pAl[:], A0[:], ALU.add), reads=[pAl, A0], writes=[wt])
                yield
                S.op("act", lambda e: e.activation(e1[:], wt[:], AF.Exp, scale=-1.0), reads=[wt], writes=[e1])
                yield
                S.op("dve", lambda e: e.tensor_scalar(e1[:], e1[:], 1.0, None, ALU.add), reads=[e1], writes=[e1])
                yield
                S.op("dve", lambda e: e.reciprocal(asig[:], e1[:]), reads=[e1], writes=[asig])
                yield
                S.op("act", lambda e: e.activation(gt[:], pG[:], AF.Copy), reads=[pG], writes=[gt])
                yield
                S.dma("pool", self.scr["Gt"][b, rows, :], gt[:], reads=[gt])
                yield
                yield "MID"
                kk = kkp[:, 64:320]
                S.op("pool", lambda e: e.tensor_tensor(kk, p_k, KK[:], ALU.mult), reads=[pst, KK], writes=[kkp])
                yield
                S.op("pool", lambda e: e.tensor_tensor(kq[:], kk, kk, ALU.mult), reads=[kkp], writes=[kq])
                yield
                S.op("dve", lambda e: e.tensor_reduce(ks[:], kq[:].rearrange("p (h w) -> p h w", w=64), AX.X, ALU.add), reads=[kq], writes=[ks])
                yield
                S.op("dve", lambda e: e.tensor_scalar(kr[:], ks[:], 1e-24, None, ALU.max), reads=[ks], writes=[kr])
                yield
                self.rpow(kr, kr[:], kr, kr[:], 1.0, 0.0, -0.5)
                yield
                kk3 = kk.rearrange("p (h w) -> p h w", w=64)
                S.op("dve", lambda e: e.tensor_tensor(kk3, kk3, kr[:].unsqueeze(2).to_broadcast([128, 4, 64]), ALU.mult), reads=[kkp, kr], writes=[kkp])
                yield
                d3 = lambda t: t[:].rearrange("p (d c) -> p d c", d=2)
                b2 = lambda ap: ap.unsqueeze(1).to_broadcast([128, 2, 256])
                S.op("dve", lambda e: e.tensor_tensor(d3(bd), d3(asig), b2(kk), ALU.mult), reads=[asig, kkp], writes=[bd])
                yield
                S.op("dve", lambda e: e.scalar_tensor_tensor(d3(tk), d3(asig), -1.0, b2(KA[:]), ALU.add, ALU.mult), reads=[asig, KA], writes=[tk])
                yield
                S.op("dve", lambda e: e.scalar_tensor_tensor(d3(kd), d3(tk), 1.0, b2(p_k), ALU.add, ALU.mult), reads=[tk, pst], writes=[kd])
                yield
                S.op("pool", lambda e: e.tensor_copy(kd16[:], kd[:]), reads=[kd], writes=[kd16])
                yield
                S.op("pool", lambda e: e.tensor_copy(v16[:], p_v), reads=[pst], writes=[v16])
                yield
                S.op("pool", lambda e: e.tensor_tensor(rk[:], p_r, RK[:], ALU.mult), reads=[pst, RK], writes=[rk])
                yield
                S.op("dve", lambda e: e.tensor_tensor(d3(pr), d3(kd), b2(rk[:]), ALU.mult), reads=[kd, rk], writes=[pr])
                yield
                S.op("dve", lambda e: e.tensor_reduce(s8[:], pr[:].rearrange("p (g w) -> p g w", w=64), AX.X, ALU.add), reads=[pr], writes=[s8])
                yield
                S.op("dve", lambda e: e.tensor_tensor(s4[:], s8[:, 0:4], s8[:, 4:8], ALU.add), reads=[s8], writes=[s4])
                yield
                S.op("dve", lambda e: e.tensor_tensor(bon[:].rearrange("p (h w) -> p h w", w=64), p_v.rearrange("p (h w) -> p h w", w=64), s4[:].unsqueeze(2).to_broadcast([128, 4, 64]), ALU.mult), reads=[pst, s4], writes=[bon])
                yield
                S.dma("pool", self.scr["Bon"][b, rows, :], bon[:], reads=[bon])
                yield
                if 'C' in KSKIP:
                    return
                lo, hi = b * 64, (b + 1) * 64
                for h in range(4):
                    if b == 0:
                        ink = kkp[:, 64 + 64 * h:128 + 64 * h]; inr = pst[:, 64 + 64 * h:128 + 64 * h]
                        S.op("pe", lambda e: e.transpose(pA[0:64, h, :], ink, self.ident[:]), reads=[kkp, self.ident], writes=[pA])
                        yield
                        S.op("pe", lambda e: e.transpose(pR[0:64, h, :], inr, self.ident[:]), reads=[pst, self.ident], writes=[pR])
                        yield
                    else:
                        ink = kkp[:, 64 * h:128 + 64 * h]; inr = pst[:, 64 * h:128 + 64 * h]
                        S.op("pe", lambda e: e.transpose(pA[:, h, :], ink, self.ident[:]), reads=[kkp, self.ident], writes=[pA])
                        yield
                        S.op("pe", lambda e: e.transpose(pR[:, h, :], inr, self.ident[:]), reads=[pst, self.ident], writes=[pR])
                        yield
                S.op("act", lambda e: e.activation(Ast[lo:hi, :, 4 * b:4 * b + 4].rearrange("p t h -> p h t"), pA[lo:hi, :, :], AF.Copy, scale=-1.0), reads=[pA], writes=[Ast])
                yield
                S.op("act", lambda e: e.activation(Rst[lo:hi, :, 4 * b:4 * b + 4].rearrange("p t h -> p h t"), pR[lo:hi, :, :], AF.Copy), reads=[pR], writes=[Rst])
                yield
                for d in range(2):
                    for h in range(4):
                        c0 = 64 + 256 * d + 64 * h
                        if b == 0:
                            S.op("pe", lambda e: e.transpose(pWt[d][0:64, h, :], decp[:, c0:c0 + 64], self.ident[:]), reads=[decp, self.ident], writes=[pWt[d]])
                            yield
                        else:
                            S.op("pe", lambda e: e.transpose(pWt[d][:, h, :], decp[:, c0 - 64:c0 + 64], self.ident[:]), reads=[decp, self.ident], writes=[pWt[d]])
                            yield
                    S.op("dve", lambda e: e.tensor_copy(Wst[d][lo:hi, :, :].rearrange("p t h -> p h t"), pWt[d][lo:hi, :, :]), reads=[pWt[d]], writes=[Wst[d]])
                    yield
                if 'D' in KSKIP:
                    return
                for d in range(2):
                    if 'E' in KSKIP:
                        return
                    S.dma("sp", self.scr["LBs"][d, 4 * b:4 * b + 4, rows, lo:hi].rearrange("h t k -> t h k"), bd[:, 256 * d:256 * d + 256].rearrange("p (h k) -> p h k", k=64), reads=[bd])
                    yield
                    S.dma("sp", self.scr["LKs"][d, 4 * b:4 * b + 4, rows, lo:hi].rearrange("h t k -> t h k"), kd16[:, 256 * d:256 * d + 256].rearrange("p (h k) -> p h k", k=64), reads=[kd16])
                    yield
                rv = self.scr["RVs"]
                dst = bass.AP(rv.tensor, (4 * b) * T * 256 + r0 * 256, [[256, 128], [T * 256 + 64, 4], [1, 64]])
                if 'F' not in KSKIP:
                    S.dma("sp", dst, v16[:].rearrange("p (h k) -> p h k", k=64), reads=[v16])
                    yield
                if b == NB - 1:
                    S.dma("pool", self.scr["As"][:, rows, :], Ast[:], reads=[Ast])
                    S.dma("pool", self.scr["Rs"][:, rows, :], Rst[:], reads=[Rst])
                    for d in range(2):
                        S.dma("pool", self.scr["Ws"][d, :, rows, :], Wst[d][:], reads=[Wst[d]])
                yield

            def run_to_mid(g):
                for v in g:
                    if v == "MID":
                        return

            old = None
            for n in range(NT):
                for b in range(NB):
                    new = body(n, b)
                    if old is None or 'S' in KSKIP:
                        if old is not None:
                            for _ in old:
                                pass
                        run_to_mid(new)
                        old = new
                        continue
                    new_mid = False
                    old_done = False
                    while not old_done:
                        try:
                            next(old)
                        except StopIteration:
                            old_done = True
                        if not new_mid:
                            if next(new, "END") == "MID":
                                new_mid = True
                    if not new_mid:
                        run_to_mid(new)
                    old = new
            for _ in old:
                pass

    def stage_scan(self, l):
        S, I = self.S, self.I
        with Scope(self) as sc:
            MJ = sc.sb([8, 256])
            S.dma("sp", MJ[:], I["c_maskJ"], writes=[MJ])
            St = [sc.sb([128, 256]) for _ in range(2)]
            Tmp = [sc.sb([128, 256]) for _ in range(2)]
            SAm = [sc.sb([8, 256], BF16) for _ in range(2)]
            for d in range(2):
                S.op("dve", lambda e: e.memset(St[d][:], 0.0), writes=[St[d]])
                S.op("dve", lambda e: e.memset(SAm[d][:], 0.0), writes=[SAm[d]])
            pSAY = [sc.ps([40, 256]) for _ in range(2)]
            pU = [sc.ps([128, 256]) for _ in range(2)]
            NBUF = 2
            bufs = []
            for i in range(NBUF):
                bb = []
                for d in range(2):
                    B = dict(AR=sc.sb([128, CS, 40], BF16), W=sc.sb([128, CS, 4]),
                             LB=sc.sb([8, CS, 128], BF16), LK=sc.sb([8, CS, 128], BF16), RV=sc.sb([8, CS, 256], BF16), Y=sc.sb([40, CS, 256]))
                    S.op("pool", lambda e: e.memset(B["AR"][:], 0.0), writes=[B["AR"]])
                    bb.append(B)
                bufs.append(bb)
            NCH = T // CS

            def rowbase(d, c):
                if d == 0:
                    return c * CS
                t0 = c * CS
                if t0 < CT:
                    return CT - CS - t0
                return (T + CT - CS) - t0

            As, Rs = self.scr["As"], self.scr["Rs"]

            def load(c):
                bb = bufs[c % NBUF]
                for d in range(2):
                    ra = rowbase(d, c)
                    rs = slice(ra, ra + CS)
                    q = "sp"
                    B = bb[d]
                    S.dma(q, B["AR"][:, :, 32:40], Rs[:, rs, :], writes=[B["AR"]])
                    if d == 0:
                        n = min(CS, T - (ra + 1))
                        S.dma(q, B["AR"][:, 0:n, 0:8], As[:, ra + 1:ra + 1 + n, :], writes=[B["AR"]])
                    elif ra == 0:
                        S.dma(q, B["AR"][:, 1:CS, 0:8], As[:, 0:CS - 1, :], writes=[B["AR"]])
                        S.dma(q, B["AR"][:, 0:1, 0:8], As[:, T - 1:T, :], writes=[B["AR"]])
                    else:
                        S.dma(q, B["AR"][:, :, 0:8], As[:, ra - 1:ra + CS - 1, :], writes=[B["AR"]])
                    S.dma(q, B["W"][:], self.scr["Ws"][d, :, rs, :], writes=[B["W"]])
                    S.dma(q, B["LB"][:], self.scr["LBs"][d, :, rs, :], writes=[B["LB"]])
                    S.dma(q, B["LK"][:], self.scr["LKs"][d, :, rs, :], writes=[B["LK"]])
                    S.dma(q, B["RV"][:], self.scr["RVs"][:, rs, :], writes=[B["RV"]])

            W3 = lambda B, i: B["W"][:, i, :].unsqueeze(2).to_broadcast([128, 4, 64])
            j4 = lambda t: t[:].rearrange("p (j v) -> p j v", j=4)
            hi16 = lambda t: t[:].bitcast(BF16).rearrange("p (n two) -> p n two", two=2)[:, :, 1]
            load(0)
            for c in range(NCH):
                if c + 1 < NCH:
                    load(c + 1)
                bb = bufs[c % NBUF]
                for s in range(CS):
                    idx = [s, CS - 1 - s]
                    for d in range(2):
                        B = bb[d]; i = idx[d]; pu = pU[d]
                        S.op("pe", lambda e: e.matmul(pu[:], B["LK"][:, i, :], B["RV"][:, i, :], start=True, stop=False), reads=[B["LK"], B["RV"]], writes=[pu])
                        S.op("pe", lambda e: e.matmul(pu[:], B["LB"][:, i, :], SAm[d][:], start=False, stop=True), reads=[B["LB"], SAm[d]], writes=[pu])
                    for d in range(2):
                        S.op("pool", lambda e: e.tensor_tensor(j4(Tmp[d]), j4(St[d]), W3(bb[d], idx[d]), ALU.mult), reads=[St[d], bb[d]["W"]], writes=[Tmp[d]])
                    for d in range(2):
                        S.op("dve", lambda e: e.tensor_tensor(St[d][:], Tmp[d][:], pU[d][:], ALU.add), reads=[Tmp[d], pU[d]], writes=[St[d]])
                    for d in range(2):
                        S.op("pe", lambda e: e.matmul(pSAY[d][:], bb[d]["AR"][:, idx[d], :], hi16(St[d]), start=True, stop=True), reads=[bb[d]["AR"], St[d]], writes=[pSAY[d]])
                    for d in range(2):
                        S.op("dve", lambda e: e.tensor_tensor(SAm[d][:], pSAY[d][0:8, :], MJ[:], ALU.mult), reads=[pSAY[d], MJ], writes=[SAm[d]])
                    for d in range(2):
                        S.op("act", lambda e: e.activation(bb[d]["Y"][32:40, idx[d], :], pSAY[d][32:40, :], AF.Copy), reads=[pSAY[d]], writes=[bb[d]["Y"]])
                for d in range(2):
                    ra = rowbase(d, c)
                    S.dma("pool", self.scr["Yfull"][d, :, ra:ra + CS, :], bb[d]["Y"][32:40, :, :], reads=[bb[d]["Y"]])

    def stage_rwpost(self, l):
        S, I = self.S, self.I
        with Scope(self) as sc:
            LNG = sc.sb([128, 256]); LNB = sc.sb([128, 256])
            S.dma("sp", LNG[:], bc(I["rwkv_ln_g"][l], 128), writes=[LNG])
            S.dma("sp", LNB[:], bc(I["rwkv_ln_b"][l], 128), writes=[LNB])
            yf = [sc.sb([128, 256]) for _ in range(2)]; yb = [sc.sb([128, 256]) for _ in range(2)]
            bo = [sc.sb([128, 256]) for _ in range(2)]; gg = [sc.sb([128, 256]) for _ in range(2)]
            y = sc.sb([128, 256]); sq = sc.sb([128, 256]); s1 = sc.sb([128, 4]); s2 = sc.sb([128, 4]); o = [sc.sb([128, 256]) for _ in range(2)]
            h3 = lambda t: t[:].rearrange("p (h w) -> p h w", w=64)
            b3 = lambda t: t[:].unsqueeze(2).to_broadcast([128, 4, 64])
            yfull = self.scr["Yfull"]
            it = 0
            for b in range(NB):
                for n in range(NT):
                    if self.last and n < 2:
                        continue
                    rows = slice(n * 128, (n + 1) * 128)
                    i = it % 2
                    for d, dstt in ((0, yf[i]), (1, yb[i])):
                        src = bass.AP(yfull.tensor, d * 8 * T * 256 + (4 * b) * T * 256 + n * 128 * 256, [[256, 128], [T * 256 + 64, 4], [1, 64]])
                        S.dma("sp", h3(dstt), src, writes=[dstt])
                    S.dma("sp", bo[i][:], self.scr["Bon"][b, rows, :], writes=[bo[i]])
                    S.dma("sp", gg[i][:], self.scr["Gt"][b, rows, :], writes=[gg[i]])
                    S.op("pool", lambda e: e.tensor_tensor(y[:], yf[i][:], yb[i][:], ALU.add), reads=[yf[i], yb[i]], writes=[y])
                    self.head_norm(y, sq, s1, s2, GN_EPS)
                    S.op("dve", lambda e: e.tensor_tensor(y[:], y[:], LNG[:], ALU.mult), reads=[y, LNG], writes=[y])
                    S.op("pool", lambda e: e.tensor_tensor(y[:], y[:], LNB[:], ALU.add), reads=[y, LNB], writes=[y])
                    S.op("pool", lambda e: e.tensor_tensor(y[:], y[:], bo[i][:], ALU.add), reads=[y, bo[i]], writes=[y])
                    S.op("dve", lambda e: e.tensor_tensor(o[i][:], y[:], gg[i][:], ALU.mult), reads=[y, gg[i]], writes=[o[i]])
                    S.dma("pool", self.scr["Ycat"][b, rows, 0:256], o[i][:], reads=[o[i]])
                    it += 1

    def head_norm(self, y, sq, s1, s2, eps, nh=4):
        S = self.S
        h3 = lambda t: t[:, 0:nh * 64].rearrange("p (h w) -> p h w", w=64)
        b3 = lambda t: t[:, 0:nh].unsqueeze(2).to_broadcast([128, nh, 64])
        S.op("dve", lambda e: e.tensor_reduce(s1[:, 0:nh], h3(y), AX.X, ALU.add), reads=[y], writes=[s1])
        S.op("dve", lambda e: e.tensor_scalar(s1[:, 0:nh], s1[:, 0:nh], -1.0 / 64, None, ALU.mult), reads=[s1], writes=[s1])
        S.op("dve", lambda e: e.tensor_tensor(h3(y), h3(y), b3(s1), ALU.add), reads=[y, s1], writes=[y])
        S.op("pool", lambda e: e.tensor_tensor(h3(sq), h3(y), h3(y), ALU.mult), reads=[y], writes=[sq])
        S.op("dve", lambda e: e.tensor_reduce(s2[:, 0:nh], h3(sq), AX.X, ALU.add), reads=[sq], writes=[s2])
        self.rpow(s2, s2[:, 0:nh], s2, s2[:, 0:nh], 1.0 / 64, eps, -0.5)
        S.op("dve", lambda e: e.tensor_tensor(h3(y), h3(y), b3(s2), ALU.mult), reads=[y, s2], writes=[y])

    def stage_attn(self, l):
        S, I = self.S, self.I
        lam_init = 0.8 - 0.6 * math.exp(-0.3 * l)
        with Scope(self) as sc:
            DL = sc.sb([128, 128]); dp = sc.sb([128, 128]); ds = sc.sb([128, 2]); lam = sc.sb([128, 1]); nlam = sc.sb([128, 1])
            S.dma("sp", DL[:], bc(I["diff_lambda"][l], 128), writes=[DL])
            S.op("dve", lambda e: e.tensor_tensor(dp[:, 0:32], DL[:, 0:32], DL[:, 32:64], ALU.mult), reads=[DL], writes=[dp])
            S.op("dve", lambda e: e.tensor_tensor(dp[:, 32:64], DL[:, 64:96], DL[:, 96:128], ALU.mult), reads=[DL], writes=[dp])
            S.op("dve", lambda e: e.tensor_reduce(ds[:], dp[:, 0:64].rearrange("p (a w) -> p a w", w=32), AX.X, ALU.add), reads=[dp], writes=[ds])
            S.op("act", lambda e: e.activation(ds[:], ds[:], AF.Exp), reads=[ds], writes=[ds])
            S.op("dve", lambda e: e.tensor_tensor(lam[:], ds[:, 0:1], ds[:, 1:2], ALU.subtract), reads=[ds], writes=[lam])
            S.op("dve", lambda e: e.tensor_scalar(nlam[:], lam[:], lam_init, -1.0, ALU.add, ALU.mult), reads=[lam], writes=[nlam])
            DN = sc.sb([128, 64])
            S.dma("sp", DN[:], bc(I["diff_norm"][l], 128), writes=[DN])
            S.op("dve", lambda e: e.tensor_scalar(DN[:], DN[:], 1.0 - lam_init, None, ALU.mult), reads=[DN], writes=[DN])
            KT = sc.sb([128, 4, T], BF16); QT = sc.sb([128, 4, T], BF16)
            V = sc.sb([128, NT, 4, 65], BF16)
            S.op("pool", lambda e: e.memset(V[:, :, :, 64:65], 1.0), writes=[V])
            pS = [sc.ps([128, 512]) for _ in range(4)]
            pO = [sc.ps([128, 4, 65]) for _ in range(4)]
            Pball = [sc.sb([128, NT, 512], BF16) for _ in range(4)]
            Mk = [sc.sb([128, 512], BF16) for _ in range(4)]
            rec = sc.sb([128, 4, 1]); o1 = sc.sb([128, 4, 64]); o2 = sc.sb([128, 4, 64])
            sq = sc.sb([128, 256]); s1 = sc.sb([128, 4]); s2 = sc.sb([128, 4])
            gate = sc.sb([128, 4, 64]); gs = sc.sb([128, 4, 64])
            osb = [sc.sb([128, 4, 64]) for _ in range(3)]
            Pd = self.scr["P"]
            cnt = {"u": 0, "o": 0, "p": 0, "s": 0}
            for m in "gdr":
                ns_q = {"g": 2, "d": 4, "r": 2}[m]
                ns_k = {"g": 1, "d": 4, "r": 2}[m]
                nv = 2 if m == "g" else 4
                kw = 32 if m == "d" else 64
                vcol = {"g": O_GQ + 384, "d": O_DF + 512, "r": O_RT + 512}[m]
                ocol = {"g": 256, "d": 512, "r": 768}[m]
                for b in range(NB):
                    S.dma("sp", KT[:, 0:ns_k, :], self.scr["KT" + m][b], writes=[KT])
                    S.dma("sp", QT[:, 0:ns_q, :], self.scr["QT" + m][b], writes=[QT])
                    for hv in range(nv):
                        S.dma("pool", V[:, :, hv, 0:64], Pd[b, :, vcol + hv * 64:vcol + hv * 64 + 64].rearrange("(n p) w -> p n w", p=128), writes=[V])
                    chunks = ([] if self.last else [(0, 256, 0, 2)]) + [(256 + 512 * q, 512, 0, NT) for q in range(4)]
                    if m == "g":
                        pairs = [[(0, s_, 0, 0, s_), (64, s_, 0, 1, 2 + s_)] for s_ in range(2)]
                    elif m == "r":
                        pairs = [[(0, s_, s_, s_, s_), (64, s_, s_, 2 + s_, 2 + s_)] for s_ in range(2)]
                    else:
                        pairs = [[(0, h_, h_, h_, h_), (64, h_, h_, h_, h_)] for h_ in range(4)]
                    for pair in pairs:
                        for (q0, w, k0, k1) in chunks:
                            nj = w // 128
                            isx = q0 >= 256
                            pos = []; pbs = []
                            for u in range(2):
                                pos.append(pO[cnt["o"] % 4]); pbs.append(Pball[cnt["o"] % 4]); cnt["o"] += 1
                            for kt in range(k0, k1):
                                pss = []
                                for u in range(2):
                                    (pb0, qs, ks_, hv_, ho) = pair[u]
                                    ps_ = pS[cnt["u"] % 4]; cnt["u"] += 1
                                    pss.append(ps_)
                                    S.op("pe", lambda e: e.matmul(ps_[:, :w], KT[pb0:pb0 + kw, ks_, kt * 128:(kt + 1) * 128], QT[pb0:pb0 + kw, qs, q0:q0 + w], start=True, stop=True), reads=[KT, QT], writes=[ps_])
                                for u in range(2):
                                    (pb0, qs, ks_, hv_, ho) = pair[u]
                                    ps_ = pss[u]; PB_ = pbs[u]
                                    pb = PB_[:, kt, :]
                                    if m == "r":
                                        mk = Mk[cnt["p"] % 4]; cnt["p"] += 1
                                        if not isx:
                                            ti = {0: 15, 1: 14}[kt]
                                        elif kt < 2:
                                            ti = 28 + kt * 4 + (q0 - 256) // 512
                                        else:
                                            off = 4 * ((q0 - 256) // 512) - (kt - 2)
                                            ti = off + 15
                                        S.dma("sp", mk[:], I["c_retM"][ti, ho], writes=[mk])
                                        S.op("dve", lambda e: e.tensor_tensor(pb[:, :w], ps_[:, :w], mk[:, :w], ALU.mult), reads=[ps_, mk], writes=[PB_])
                                    else:
                                        S.op("act", lambda e: e.activation(pb[:, :w], ps_[:, :w], AF.Exp), reads=[ps_], writes=[PB_])
                            for u in range(2):
                                (pb0, qs, ks_, hv_, ho) = pair[u]
                                po = pos[u]; PB_ = pbs[u]
                                for j in range(nj):
                                    for kt in range(k0, k1):
                                        S.op("pe", lambda e: e.matmul(po[:, j, :], PB_[:, kt, j * 128:(j + 1) * 128], V[:, kt, hv_, :], start=(kt == k0), stop=(kt == k1 - 1)), reads=[PB_, V], writes=[po])
                            dstf = lambda ho: self.scr["Ycat"][b, q0:q0 + w, ocol + ho * 64:ocol + ho * 64 + 64].rearrange("(j t) v -> t j v", t=128)
                            if m == "g":
                                for u in range(2):
                                    po = pos[u]; ho = pair[u][4]
                                    ob = osb[cnt["s"] % 3]; cnt["s"] += 1
                                    S.op("dve", lambda e: e.reciprocal(rec[:, 0:nj, :], po[:, 0:nj, 64:65]), reads=[po], writes=[rec])
                                    S.op("dve", lambda e: e.tensor_tensor(ob[:, 0:nj, :], po[:, 0:nj, 0:64], rec[:, 0:nj, :].to_broadcast([128, nj, 64]), ALU.mult), reads=[po, rec], writes=[ob])
                                    S.dma("sp", dstf(ho), ob[:, 0:nj, :], reads=[ob])
                            elif m == "d":
                                ho = pair[0][4]
                                ob = osb[cnt["s"] % 3]; cnt["s"] += 1
                                for (po, ot) in ((pos[0], o1), (pos[1], o2)):
                                    S.op("dve", lambda e: e.reciprocal(rec[:, 0:nj, :], po[:, 0:nj, 64:65]), reads=[po], writes=[rec])
                                    S.op("dve", lambda e: e.tensor_tensor(ot[:, 0:nj, :], po[:, 0:nj, 0:64], rec[:, 0:nj, :].to_broadcast([128, nj, 64]), ALU.mult), reads=[po, rec], writes=[ot])
                                S.op("dve", lambda e: e.scalar_tensor_tensor(o1[:, 0:nj, :], o2[:, 0:nj, :], nlam[:, 0:1], o1[:, 0:nj, :], ALU.mult, ALU.add), reads=[o1, o2, nlam], writes=[o1])
                                S.op("pool", lambda e: e.tensor_tensor(o2[:, 0:nj, :], o1[:, 0:nj, :], o1[:, 0:nj, :], ALU.mult), reads=[o1], writes=[o2])
                                S.op("dve", lambda e: e.tensor_reduce(s1[:, 0:nj], o2[:, 0:nj, :], AX.X, ALU.add), reads=[o2], writes=[s1])
                                self.rpow(s1, s1[:, 0:nj], s1, s1[:, 0:nj], 1.0 / 64, QK_EPS, -0.5)
                                S.op("dve", lambda e: e.tensor_tensor(o1[:, 0:nj, :], o1[:, 0:nj, :], s1[:, 0:nj].unsqueeze(2).to_broadcast([128, nj, 64]), ALU.mult), reads=[o1, s1], writes=[o1])
                                S.op("dve", lambda e: e.tensor_tensor(ob[:, 0:nj, :], o1[:, 0:nj, :], DN[:].unsqueeze(1).to_broadcast([128, nj, 64]), ALU.mult), reads=[o1, DN], writes=[ob])
                                S.dma("sp", dstf(ho), ob[:, 0:nj, :], reads=[ob])
                            else:
                                for u in range(2):
                                    po = pos[u]; ho = pair[u][4]
                                    ob = osb[cnt["s"] % 3]; cnt["s"] += 1
                                    S.dma("sp", gate[:, 0:nj, :], Pd[b, q0:q0 + w, O_RT + 768 + ho * 64:O_RT + 768 + ho * 64 + 64].rearrange("(j t) v -> t j v", t=128), writes=[gate])
                                    S.op("act", lambda e: e.activation(gs[:, 0:nj, :], gate[:, 0:nj, :], AF.Silu), reads=[gate], writes=[gs])
                                    yv = TT(o1[:].rearrange("p j w -> p (j w)"))
                                    S.op("act", lambda e: e.activation(o1[:, 0:nj, :], po[:, 0:nj, 0:64], AF.Copy), reads=[po], writes=[o1])
                                    yv.lastw = o1.lastw
                                    yv.readers = o1.readers
                                    self.head_norm(yv, sq, s1, s2, LN_EPS, nh=nj)
                                    o1.lastw = yv.lastw
                                    o1.readers = yv.readers
                                    S.op("dve", lambda e: e.tensor_tensor(ob[:, 0:nj, :], o1[:, 0:nj, :], gs[:, 0:nj, :], ALU.mult), reads=[o1, gs], writes=[ob])
                                    S.dma("sp", dstf(ho), ob[:, 0:nj, :], reads=[ob])

    def layer_norm_out(self, sc, x1, G, Bt, out, tmps):
        S = self.S
        s1, s2, sq = tmps
        S.op("dve", lambda e: e.tensor_reduce(s1[:], x1[:], AX.X, ALU.add), reads=[x1], writes=[s1])
        S.op("dve", lambda e: e.tensor_scalar(s1[:], s1[:], -1.0 / D, None, ALU.mult), reads=[s1], writes=[s1])
        S.op("dve", lambda e: e.tensor_scalar(x1[:], x1[:], s1[:, 0:1], None, ALU.add), reads=[x1, s1], writes=[x1])
        S.op("pool", lambda e: e.tensor_tensor(sq[:], x1[:], x1[:], ALU.mult), reads=[x1], writes=[sq])
        S.op("dve", lambda e: e.tensor_reduce(s2[:], sq[:], AX.X, ALU.add), reads=[sq], writes=[s2])
        self.rpow(s2, s2[:], s2, s2[:], 1.0 / D, LN_EPS, -0.5)
        S.op("dve", lambda e: e.scalar_tensor_tensor(x1[:], x1[:], s2[:, 0:1], G[:], ALU.mult, ALU.mult), reads=[x1, s2, G], writes=[x1])
        S.op("pool", lambda e: e.tensor_tensor(out[:], x1[:], Bt[:], ALU.add), reads=[x1, Bt], writes=[out])

    def pump_wload(self, k):
        for _ in range(k):
            if not self.wload:
                return
            t, o, i = self.wload.pop(0)
            self.S.dma("pool", o, i, writes=[t])

    def stage_mix(self, l):
        S, I = self.S, self.I
        with Scope(self) as sc:
            wo = sc.sb([128, 8, D], BF16)
            for kc in range(8):
                S.dma("pool", wo[:, kc, :], I["w_out"][l, kc * 128:(kc + 1) * 128, :], writes=[wo])
            G1 = sc.sb([128, D])
            g1v = [None]
            PG = sc.sb([128, D]); PB = sc.sb([128, D])
            S.dma("sp", PG[:], bc(I["post1_g"][l], 128), writes=[PG])
            S.dma("sp", PB[:], bc(I["post1_b"][l], 128), writes=[PB])
            yc = [sc.sb([128, D]) for _ in range(2)]; xt = [sc.sb([128, D]) for _ in range(2)]
            pT = [sc.ps([128, 128]) for _ in range(2)]
            yT = [sc.sb([128, 8, 128], BF16) for _ in range(2)]
            pO = [sc.ps([128, 512]) for _ in range(2)]
            t = sc.sb([128, D]); x1 = sc.sb([128, D]); outt = [sc.sb([128, D]) for _ in range(2)]
            s1 = sc.sb([128, 1]); s2 = sc.sb([128, 1]); sq = sc.sb([128, D])
            it = 0
            for b in range(NB):
                for n in range(NT):
                    if self.last and n < 2:
                        continue
                    v = 2 if n < 2 else b
                    if g1v[0] != v:
                        S.dma("sp", G1[:], bc(self.scr["modD"][v, 2 * D:3 * D], 128), writes=[G1])
                        g1v[0] = v
                    rows = slice(n * 128, (n + 1) * 128)
                    i = it % 2
                    self.pump_wload(1)
                    S.dma("sp", yc[i][:], self.scr["Ycat"][b, rows, :], writes=[yc[i]])
                    S.dma("sp", xt[i][:], self.Xsrc[b, rows, :], writes=[xt[i]])
                    for kc in range(8):
                        pp = pT[kc % 2]
                        S.op("pe", lambda e: e.transpose(pp[:], yc[i][:, kc * 128:(kc + 1) * 128], self.ident[:]), reads=[yc[i], self.ident], writes=[pp])
                        S.op("act", lambda e: e.activation(yT[i][:, kc, :], pp[:], AF.Copy), reads=[pp], writes=[yT[i]])
                    for c in range(2):
                        po = pO[c]
                        for kc in range(8):
                            S.op("pe", lambda e: e.matmul(po[:], yT[i][:, kc, :], wo[:, kc, c * 512:(c + 1) * 512], start=(kc == 0), stop=(kc == 7)), reads=[yT[i], wo], writes=[po])
                        S.op("dve", lambda e: e.tensor_tensor(t[:, c * 512:(c + 1) * 512], po[:], G1[:, c * 512:(c + 1) * 512], ALU.mult), reads=[po, G1], writes=[t])
                    S.op("dve", lambda e: e.scalar_tensor_tensor(x1[:], xt[i][:], ALPHA, t[:], ALU.mult, ALU.add), reads=[xt[i], t], writes=[x1])
                    self.layer_norm_out(sc, x1, PG, PB, outt[i], (s1, s2, sq))
                    S.dma("pool", self.scr["X1s"][b, rows, :], outt[i][:], reads=[outt[i]])
                    it += 1

    def stage_ffn(self, l):
        S, I = self.S, self.I
        with Scope(self) as sc:
            w1, w2 = self.w1, self.w2
            G2 = sc.sb([128, D])
            PG = sc.sb([128, D]); PB = sc.sb([128, D])
            S.dma("sp", PG[:], bc(I["post2_g"][l], 128), writes=[PG])
            S.dma("sp", PB[:], bc(I["post2_b"][l], 128), writes=[PB])
            xt = [sc.sb([128, D]) for _ in range(2)]
            pT = [sc.ps([128, 128]) for _ in range(2)]
            x1T = sc.sb([128, 8, 512], BF16)
            aT = sc.sb([128, NFF, 512], BF16)
            pU = [sc.ps([128, 512]) for _ in range(2)]; pGt = [sc.ps([128, 512]) for _ in range(2)]
            su = [sc.sb([128, 512]) for _ in range(2)]
            pF = [sc.ps([128, 512]) for _ in range(2)]
            t = sc.sb([128, D]); x2 = sc.sb([128, D]); outt = [sc.sb([128, D]) for _ in range(2)]
            s1 = sc.sb([128, 1]); s2 = sc.sb([128, 1]); sq = sc.sb([128, D])
            it = 0
            for b in range(NB):
                groups = ([] if self.last else [(0, 2, 2)]) + [(2 + 4 * q, 4, b) for q in range(4)]
                for (n0, nt, v) in groups:
                    w = nt * 128
                    S.dma("sp", G2[:], bc(self.scr["modD"][v, 5 * D:6 * D], 128), writes=[G2])
                    for j in range(nt):
                        rows = slice((n0 + j) * 128, (n0 + j + 1) * 128)
                        x = xt[it % 2]; it += 1
                        S.dma("sp", x[:], self.scr["X1s"][b, rows, :], writes=[x])
                        self.xT_modulated(sc, x, pT, x1T, j * 128, 128, v, 1)
                    for fc in range(NFF):
                        pu = pU[fc % 2]; pg = pGt[fc % 2]; s_ = su[fc % 2]
                        for kc in range(8):
                            S.op("pe", lambda e: e.matmul(pu[:, :w], w1[:, kc, fc * 128:(fc + 1) * 128], x1T[:, kc, :w], start=(kc == 0), stop=(kc == 7)), reads=[w1, x1T], writes=[pu])
                        for kc in range(8):
                            S.op("pe", lambda e: e.matmul(pg[:, :w], w1[:, kc, DFF + fc * 128:DFF + (fc + 1) * 128], x1T[:, kc, :w], start=(kc == 0), stop=(kc == 7)), reads=[w1, x1T], writes=[pg])
                        S.op("act", lambda e: e.activation(s_[:, :w], pu[:, :w], AF.Silu), reads=[pu], writes=[s_])
                        S.op("dve", lambda e: e.tensor_tensor(aT[:, fc, :w], s_[:, :w], pg[:, :w], ALU.mult), reads=[s_, pg], writes=[aT])
                    for j in range(nt):
                        rows = slice((n0 + j) * 128, (n0 + j + 1) * 128)
                        x = xt[it % 2]; it += 1
                        S.dma("sp", x[:], self.scr["X1s"][b, rows, :], writes=[x])
                        for c in range(2):
                            pf = pF[c]
                            for fc in range(NFF):
                                S.op("pe", lambda e: e.matmul(pf[:], aT[:, fc, j * 128:(j + 1) * 128], w2[:, fc, c * 512:(c + 1) * 512], start=(fc == 0), stop=(fc == NFF - 1)), reads=[aT, w2], writes=[pf])
                            S.op("dve", lambda e: e.tensor_tensor(t[:, c * 512:(c + 1) * 512], pf[:], G2[:, c * 512:(c + 1) * 512], ALU.mult), reads=[pf, G2], writes=[t])
                        S.op("dve", lambda e: e.scalar_tensor_tensor(x2[:], x[:], ALPHA, t[:], ALU.mult, ALU.add), reads=[x, t], writes=[x2])
                        o = outt[j % 2]
                        self.layer_norm_out(sc, x2, PG, PB, o, (s1, s2, sq))
                        if self.last:
                            xr = (n0 + j) * 128 - CT
                            S.dma("pool", self.out[b, xr:xr + 128, :], o[:], reads=[o])
                        else:
                            S.dma("pool", self.scr["Xs"][b, rows, :], o[:], reads=[o])


def host_consts():
    c = {}
    c["c_ident"] = np.eye(128, dtype=np.float32)
    t = np.arange(SX)
    row = (t // 64).astype(np.float32)
    col = (t % 64).astype(np.float32)
    pos = t.astype(np.float32)

    def tab(p, n):
        half = n // 2
        inv = np.power(np.float32(10000.0), -(np.arange(half, dtype=np.float32) * np.float32(2.0) / np.float32(n))).astype(np.float32)
        ang = (p[:, None] * inv[None, :]).astype(np.float32)
        cs, sn = np.cos(ang).astype(np.float32), np.sin(ang).astype(np.float32)
        return np.concatenate([cs, cs], 1), np.concatenate([-sn, sn], 1)

    def axial(n):
        h = n // 2
        c1, s1 = tab(row, h)
        c2, s2 = tab(col, h)
        return np.stack([np.concatenate([c1, c2], 1), np.concatenate([s1, s2], 1)], 1).astype(np.float32)

    c["c_ropeG"] = axial(64)
    c["c_ropeD"] = axial(32)
    cr, sr = tab(pos, 64)
    c["c_ropeR"] = np.stack([cr, sr], 1).astype(np.float32)
    mj = np.zeros((8, 256), np.float32)
    for m in range(8):
        h = m % 4
        mj[m, h * 64:(h + 1) * 64] = 1.0
    c["c_maskJ"] = mj
    c["c_zero"] = np.zeros((128, 4096), ml_dtypes.bfloat16)
    gf = 1.0 - 2.0 ** (-5.0 - np.arange(4, dtype=np.float64))
    gb = gf[::-1]
    M = np.zeros((36, 4, 128, 512), np.float64)
    p = np.arange(128)[:, None]
    j = np.arange(512)[None, :]
    for h in range(4):
        lf, lb = math.log(gf[h]), math.log(gb[h])
        for off in range(-15, 13):
            dlt = (128 * off + j - p).astype(np.float64)
            m = np.where(dlt > 0, np.exp(lf * np.maximum(dlt, 0)), 0.0) + np.where(dlt < 0, np.exp(lb * np.maximum(-dlt, 0)), 0.0) + np.where(dlt == 0, 2.0, 0.0)
            M[off + 15, h] = m
        for kc in range(2):
            for qc in range(4):
                cc = 128 * kc + p
                ii = 512 * qc + j
                M[28 + kc * 4 + qc, h] = np.exp(lf * (256 + ii - cc)) + np.exp(lb * (2048 - ii + cc))
    c["c_retM"] = M.astype(np.float32).astype(ml_dtypes.bfloat16)
    return c


_CACHE = {}


def kernel(**inputs):
    f = lambda k: np.ascontiguousarray(np.asarray(inputs[k], dtype=np.float32))
    L = DEPTH
    shared = {}
    for k in ("ada_w", "ada_b", "w_in", "rwkv_mu", "rwkv_g_up", "rwkv_k_k", "rwkv_k_a", "rwkv_ln_g", "rwkv_ln_b",
              "gqa_q_norm", "gqa_k_norm", "diff_norm", "w_out", "post1_g", "post1_b", "ffn_w_in", "ffn_w_out", "post2_g", "post2_b"):
        shared[k] = f(k)
    shared["rwkv_w0"] = f("rwkv_w0").reshape(L, 512)
    shared["rwkv_a0"] = f("rwkv_a0").reshape(L, 512)
    shared["rwkv_w_up"] = f("rwkv_w_up").reshape(L, 64, 256)
    shared["rwkv_a_up"] = f("rwkv_a_up").reshape(L, 64, 256)
    shared["rwkv_r_k"] = f("rwkv_r_k").reshape(L, 256)
    shared["diff_lambda"] = f("diff_lambda").reshape(L, 128)
    shared.update(host_consts())
    x, c, ctx, c_ctx = f("x"), f("c"), f("ctx"), f("c_ctx")
    in_maps = []
    for core in range(8):
        bs = slice(core * NB, (core + 1) * NB)
        m = dict(shared)
        m["xin"] = np.ascontiguousarray(np.concatenate([ctx[bs], x[bs]], axis=1))
        m["cvec"] = np.ascontiguousarray(np.concatenate([c[bs], c_ctx[None, :]], axis=0))
        in_maps.append(m)
    if "nc" not in _CACHE:
        _CACHE["nc"] = KB().build()
    res = run_bass_kernel_spmd(_CACHE["nc"], in_maps, core_ids=list(range(8)))
    return np.concatenate([r["y"] for r in res.results], axis=0).astype(np.float32)
```
